# Optimizing a Trainium2 kernel written in Bass

```python
import math
import jax, jax.numpy as jnp
from jax import lax
import numpy as np

D_MODEL = 1024
BATCH = 16
SEQ = 2048
DEPTH = 2

N_META = 16
ROPE_THETA = 10000.0
NORM_EPS = 1e-6

ATT_HEAD_DIM = 64
ATT_HEADS = D_MODEL // (2 * ATT_HEAD_DIM)
ATT_QK_W = ATT_HEADS * 2 * ATT_HEAD_DIM
ATT_V_DIM = 2 * ATT_HEAD_DIM
ATT_W = ATT_HEADS * ATT_V_DIM
ATT_BLOCK = 128

RWKV_HEAD_DIM = 64
RWKV_HEADS = D_MODEL // RWKV_HEAD_DIM
RWKV_W = RWKV_HEADS * RWKV_HEAD_DIM
RWKV_DECAY_RANK = 64
RWKV_A_RANK = 64
RWKV_GN_EPS = 64e-5
RWKV_IN_W = 4 * RWKV_W + RWKV_DECAY_RANK + RWKV_A_RANK
RWKV_SPLITS = (RWKV_W, 2 * RWKV_W, 3 * RWKV_W, 4 * RWKV_W, 4 * RWKV_W + RWKV_DECAY_RANK)

HGRN_EXPAND = 128
HGRN_HEADS = D_MODEL // HGRN_EXPAND
HGRN_W = HGRN_HEADS * HGRN_EXPAND
HGRN_V_DIM = HGRN_W // HGRN_HEADS
HGRN_CHUNK = 64

N_BRANCH = 3
IN_WIDTHS = (ATT_QK_W, ATT_QK_W, ATT_W, ATT_W, RWKV_IN_W, HGRN_W, HGRN_W, HGRN_W, HGRN_W, N_BRANCH * D_MODEL)
IN_SPLITS = tuple(sum(IN_WIDTHS[:i + 1]) for i in range(len(IN_WIDTHS) - 1))
IN_W = sum(IN_WIDTHS)

kernel_name = "hybrid_diffattn_rwkv7_hgrn2_gated_merge"


def rms_norm(x, w, eps=NORM_EPS):
    xf = x.astype(jnp.float32)
    y = xf * lax.rsqrt(jnp.mean(xf * xf, axis=-1, keepdims=True) + eps)
    return (y * w.astype(jnp.float32)).astype(x.dtype)


def head_layer_norm(x, w, b, eps):
    xf = x.astype(jnp.float32)
    mu = jnp.mean(xf, axis=-1, keepdims=True)
    var = jnp.mean(jnp.square(xf - mu), axis=-1, keepdims=True)
    y = (xf - mu) * lax.rsqrt(var + eps)
    return y * w.reshape(x.shape[-2:]).astype(jnp.float32) + b.reshape(x.shape[-2:]).astype(jnp.float32)


def rotary_tables(n_pos, dim, dtype):
    inv = 1.0 / (ROPE_THETA ** (jnp.arange(0, dim, 2, dtype=jnp.float32) / dim))
    ang = jnp.arange(n_pos, dtype=jnp.float32)[:, None] * inv[None, :]
    ang = jnp.concatenate([ang, ang], axis=-1)
    return jnp.cos(ang).astype(dtype), jnp.sin(ang).astype(dtype)


def apply_rope(x, cos, sin):
    c = cos[None, :, None, None, :]
    s = sin[None, :, None, None, :]
    x1, x2 = jnp.split(x, 2, axis=-1)
    return x * c + jnp.concatenate([-x2, x1], axis=-1) * s


def token_shift(z):
    return jnp.pad(z[:, :-1], ((0, 0), (1, 0), (0, 0)))


def diff_attention(q, k, v, lam, cos, sin):
    B, L = q.shape[0], q.shape[1]
    pad = (-L) % ATT_BLOCK
    P = L + pad
    q = apply_rope(q, cos, sin)
    k = apply_rope(k, cos, sin)
    q = jnp.pad(jnp.transpose(q, (0, 2, 3, 1, 4)), ((0, 0), (0, 0), (0, 0), (pad, 0), (0, 0)))
    k = jnp.pad(jnp.transpose(k, (0, 2, 3, 1, 4)), ((0, 0), (0, 0), (0, 0), (pad, 0), (0, 0)))
    v = jnp.pad(jnp.transpose(v, (0, 2, 1, 3)), ((0, 0), (0, 0), (pad, 0), (0, 0)))
    key_pos = jnp.arange(P)
    scale = ATT_HEAD_DIM ** -0.5

    def block(n):
        start = n * ATT_BLOCK
        qb = lax.dynamic_slice_in_dim(q, start, ATT_BLOCK, axis=3)
        s = jnp.einsum('bhgqd,bhgkd->bhgqk', qb, k, preferred_element_type=jnp.float32) * scale
        q_pos = start + jnp.arange(ATT_BLOCK)
        allowed = (key_pos[None, :] <= q_pos[:, None]) & (key_pos[None, :] >= pad)
        p = jax.nn.softmax(jnp.where(allowed, s, -1e30), axis=-1)
        w = p[:, :, 0] - lam * p[:, :, 1]
        return jnp.einsum('bhqk,bhkv->bhqv', w.astype(v.dtype), v)

    o = lax.map(block, jnp.arange(P // ATT_BLOCK))
    o = jnp.transpose(o, (1, 0, 3, 2, 4)).reshape(B, P, ATT_HEADS, ATT_V_DIM)
    return o[:, pad:]


def rwkv7_recurrence(r, w, k, v, a, b):
    B, L, H, N = r.shape

    def step(S, inp):
        r_t, w_t, k_t, v_t, a_t, b_t = inp
        sa = jnp.einsum('bhvk,bhk->bhv', S, a_t)
        S = S * w_t[:, :, None, :] + sa[..., None] * b_t[:, :, None, :] + v_t[..., None] * k_t[:, :, None, :]
        return S, jnp.einsum('bhvk,bhk->bhv', S, r_t)

    xs = tuple(jnp.moveaxis(t.astype(jnp.float32), 1, 0) for t in (r, w, k, v, a, b))
    _, o = lax.scan(step, jnp.zeros((B, H, N, N), jnp.float32), xs)
    return jnp.moveaxis(o, 0, 1)


def hgrn2_chunked(q, k, v, log_f):
    B, L, H, K = q.shape
    V = v.shape[-1]
    C = HGRN_CHUNK
    pad = (-L) % C
    P = L + pad
    n = P // C

    def chunks(t):
        t = jnp.pad(t.astype(jnp.float32), ((0, 0), (pad, 0), (0, 0), (0, 0)))
        return t.reshape(B, n, C, H, t.shape[-1]).transpose(1, 0, 3, 2, 4)

    causal = jnp.tril(jnp.ones((C, C), dtype=bool))

    def step(S, inp):
        qc, kc, vc, gc = inp
        lam = jnp.cumsum(gc, axis=2)
        o_inter = jnp.einsum('bhtk,bhkv->bhtv', qc * jnp.exp(lam), S)
        rel = lam[:, :, :, None, :] - lam[:, :, None, :, :]
        decay = jnp.exp(jnp.where(causal[:, :, None], rel, -jnp.inf))
        att = jnp.einsum('bhtk,bhsk,bhtsk->bhts', qc, kc, decay)
        o = o_inter + jnp.einsum('bhts,bhsv->bhtv', att, vc)
        lam_end = lam[:, :, -1:, :]
        S = jnp.exp(lam_end[:, :, 0, :])[..., None] * S + jnp.einsum('bhsk,bhsv->bhkv', kc * jnp.exp(lam_end - lam), vc)
        return S, o

    _, o = lax.scan(step, jnp.zeros((B, H, K, V), jnp.float32), (chunks(q), chunks(k), chunks(v), chunks(log_f)))
    o = o.transpose(1, 0, 3, 2, 4).reshape(B, P, H, V)
    return o[:, pad:]


def hybrid_layer(h, l, cos, sin, pre_w, post_w, w_in, lq1, lk1, lq2, lk2, att_norm_w,
                 rwkv_mu, rwkv_w0, rwkv_w_up, rwkv_a0, rwkv_a_up, rwkv_k_k, rwkv_k_a, rwkv_r_k,
                 rwkv_gn_w, rwkv_gn_b, hgrn_lb, hgrn_norm_w, w_att_out, w_rwkv_out, w_hgrn_out, w_o):
    B, L, _ = h.shape
    f32 = jnp.float32
    u = rms_norm(h, pre_w)
    z = u @ w_in
    aq, ak, av, ag, rz, hq, hf, hi, hg, mg = jnp.split(z, IN_SPLITS, axis=-1)

    lam_init = 0.8 - 0.6 * math.exp(-0.3 * l)
    lam = (jnp.exp(jnp.sum(lq1.astype(f32) * lk1.astype(f32))) - jnp.exp(jnp.sum(lq2.astype(f32) * lk2.astype(f32))) + lam_init)
    o_att = diff_attention(aq.reshape(B, L, ATT_HEADS, 2, ATT_HEAD_DIM), ak.reshape(B, L, ATT_HEADS, 2, ATT_HEAD_DIM),
                           av.reshape(B, L, ATT_HEADS, ATT_V_DIM), lam, cos, sin)
    o_att = rms_norm(o_att, att_norm_w) * (1.0 - lam_init)
    o_att = (o_att.reshape(B, L, ATT_W) * jax.nn.silu(ag)).astype(h.dtype)

    rz = rz + (token_shift(rz) - rz) * rwkv_mu
    rr, rk, rv, rg, rwd, rad = jnp.split(rz, RWKV_SPLITS, axis=-1)
    w_log = -jax.nn.softplus(-(rwkv_w0 + jnp.tanh(rwd) @ rwkv_w_up)) - 0.5
    decay = jnp.exp(-jnp.exp(w_log.astype(f32)))
    a = jax.nn.sigmoid((rwkv_a0 + rad @ rwkv_a_up).astype(f32))
    heads = lambda t: t.reshape(B, L, RWKV_HEADS, RWKV_HEAD_DIM)
    kk = heads((rk * rwkv_k_k).astype(f32))
    kk = kk / jnp.maximum(jnp.sqrt(jnp.sum(kk * kk, axis=-1, keepdims=True)), 1e-12)
    rk = rk.astype(f32) * (1.0 + (a - 1.0) * rwkv_k_a.astype(f32))
    r_h, k_h, v_h, a_h = heads(rr.astype(f32)), heads(rk), heads(rv.astype(f32)), heads(a)
    o = rwkv7_recurrence(r_h, heads(decay), k_h, v_h, -kk, kk * a_h)
    o = head_layer_norm(o, rwkv_gn_w, rwkv_gn_b, RWKV_GN_EPS)
    o = o + jnp.sum(r_h * k_h * rwkv_r_k.astype(f32), axis=-1, keepdims=True) * v_h
    o_rwkv = (o.reshape(B, L, RWKV_W) * jax.nn.silu(rg.astype(f32))).astype(h.dtype)

    lb = hgrn_lb.reshape(HGRN_HEADS, HGRN_EXPAND)
    f_gate = lb + (1.0 - lb) * jax.nn.sigmoid(hf.astype(f32).reshape(B, L, HGRN_HEADS, HGRN_EXPAND))
    o = hgrn2_chunked(jax.nn.silu(hq).reshape(B, L, HGRN_HEADS, HGRN_EXPAND), 1.0 - f_gate,
                      hi.reshape(B, L, HGRN_HEADS, HGRN_V_DIM), jnp.log(f_gate))
    o = rms_norm(o, hgrn_norm_w) * jax.nn.silu(hg.astype(f32)).reshape(B, L, HGRN_HEADS, HGRN_V_DIM)
    o_hgrn = o.reshape(B, L, HGRN_W).astype(h.dtype)

    g_att, g_rwkv, g_hgrn = jnp.split(jax.nn.sigmoid(mg), N_BRANCH, axis=-1)
    y = g_att * (o_att @ w_att_out) + g_rwkv * (o_rwkv @ w_rwkv_out) + g_hgrn * (o_hgrn @ w_hgrn_out)
    return h + rms_norm(y @ w_o, post_w)


def setup_inputs(seed: int = 0) -> dict:
    key = jax.random.key(seed)
    ks = jax.random.split(key, 26)
    f32 = jnp.float32

    def nrm(k, shape, scale):
        return jax.random.normal(k, shape, f32) * scale

    return {
        "x": nrm(ks[0], (BATCH, SEQ, D_MODEL), 1.0),
        "meta_tokens": nrm(ks[1], (N_META, D_MODEL), 1.0),
        "pre_norm_w": 1.0 + nrm(ks[2], (DEPTH, D_MODEL), 0.05),
        "post_norm_w": 1.0 + nrm(ks[3], (DEPTH, D_MODEL), 0.05),
        "w_in": nrm(ks[4], (DEPTH, D_MODEL, IN_W), D_MODEL ** -0.5),
        "lambda_q1": nrm(ks[5], (DEPTH, ATT_HEAD_DIM), 0.1),
        "lambda_k1": nrm(ks[6], (DEPTH, ATT_HEAD_DIM), 0.1),
        "lambda_q2": nrm(ks[7], (DEPTH, ATT_HEAD_DIM), 0.1),
        "lambda_k2": nrm(ks[8], (DEPTH, ATT_HEAD_DIM), 0.1),
        "att_norm_w": 1.0 + nrm(ks[9], (DEPTH, ATT_V_DIM), 0.05),
        "rwkv_mu": jax.random.uniform(ks[10], (DEPTH, RWKV_IN_W), f32),
        "rwkv_w0": jax.random.uniform(ks[11], (DEPTH, RWKV_W), f32, minval=-6.0, maxval=1.0),
        "rwkv_w_up": nrm(ks[12], (DEPTH, RWKV_DECAY_RANK, RWKV_W), 0.5 * RWKV_DECAY_RANK ** -0.5),
        "rwkv_a0": nrm(ks[13], (DEPTH, RWKV_W), 0.1),
        "rwkv_a_up": nrm(ks[14], (DEPTH, RWKV_A_RANK, RWKV_W), 0.5 * RWKV_A_RANK ** -0.5),
        "rwkv_k_k": 0.85 + nrm(ks[15], (DEPTH, RWKV_W), 0.05),
        "rwkv_k_a": 1.0 + nrm(ks[16], (DEPTH, RWKV_W), 0.05),
        "rwkv_r_k": nrm(ks[17], (DEPTH, RWKV_HEADS, RWKV_HEAD_DIM), 0.1),
        "rwkv_gn_w": 1.0 + nrm(ks[18], (DEPTH, RWKV_W), 0.05),
        "rwkv_gn_b": nrm(ks[19], (DEPTH, RWKV_W), 0.01),
        "hgrn_lower_bounds": nrm(ks[20], (DEPTH, HGRN_W), 0.1),
        "hgrn_norm_w": 1.0 + nrm(ks[21], (DEPTH, HGRN_V_DIM), 0.05),
        "w_att_out": nrm(ks[22], (DEPTH, ATT_W, D_MODEL), ATT_W ** -0.5),
        "w_rwkv_out": nrm(ks[23], (DEPTH, RWKV_W, D_MODEL), RWKV_W ** -0.5),
        "w_hgrn_out": nrm(ks[24], (DEPTH, HGRN_W, D_MODEL), HGRN_W ** -0.5),
        "w_o": nrm(ks[25], (DEPTH, D_MODEL, D_MODEL), D_MODEL ** -0.5),
    }


def reference(x, meta_tokens, pre_norm_w, post_norm_w, w_in, lambda_q1, lambda_k1, lambda_q2, lambda_k2,
              att_norm_w, rwkv_mu, rwkv_w0, rwkv_w_up, rwkv_a0, rwkv_a_up, rwkv_k_k, rwkv_k_a, rwkv_r_k,
              rwkv_gn_w, rwkv_gn_b, hgrn_lower_bounds, hgrn_norm_w, w_att_out, w_rwkv_out, w_hgrn_out, w_o):
    B = x.shape[0]
    meta = jnp.broadcast_to(meta_tokens[None].astype(x.dtype), (B, N_META, D_MODEL))
    h = jnp.concatenate([meta, x], axis=1)
    L = h.shape[1]
    cos, sin = rotary_tables(L, ATT_HEAD_DIM, x.dtype)
    lbs = jnp.cumsum(jax.nn.softmax(hgrn_lower_bounds.astype(jnp.float32), axis=0), axis=0)
    lbs = lbs - lbs[0]
    for l in range(DEPTH):
        h = hybrid_layer(h, l, cos, sin, pre_norm_w[l], post_norm_w[l], w_in[l],
                         lambda_q1[l], lambda_k1[l], lambda_q2[l], lambda_k2[l], att_norm_w[l],
                         rwkv_mu[l], rwkv_w0[l], rwkv_w_up[l], rwkv_a0[l], rwkv_a_up[l], rwkv_k_k[l],
                         rwkv_k_a[l], rwkv_r_k[l], rwkv_gn_w[l], rwkv_gn_b[l], lbs[l], hgrn_norm_w[l],
                         w_att_out[l], w_rwkv_out[l], w_hgrn_out[l], w_o[l])
    return h[:, N_META:]
```

```python
import numpy as np
import ml_dtypes
from contextlib import ExitStack
import concourse.bass as bass
import concourse.mybir as mybir
from concourse.bass_utils import run_bass_kernel_spmd

F32 = mybir.dt.float32
BF16 = mybir.dt.bfloat16
AF = mybir.ActivationFunctionType
ALU = mybir.AluOpType
AX = mybir.AxisListType

D = 1024
NS = 2
NT = 17
P = NT * 128
PAD = 112
ROWS = NS * P
IN_W = 15488
RW_OFF = 4096
RW_W = 4224
HG_OFF = RW_OFF + RW_W
MG_OFF = HG_OFF + 4096
DEPTH = 2
EPS = 1e-6
GN_EPS = 64e-5


def zr(Z, a, b):
    if isinstance(Z, list):
        si = a // P
        assert (b - 1) // P == si
        return Z[si][a - si * P:b - si * P]
    return Z[a:b]


class Tl:
    __slots__ = ("t", "w", "r", "name", "multi", "wd", "excl")

    def __init__(self, t, name="", multi=False, excl=False):
        self.excl = excl
        self.t = t
        self.w = None
        self.r = {}
        self.wd = {}
        self.multi = multi
        self.name = name

    def __getitem__(self, idx):
        return self.t[idx]


class Prog:
    def __init__(self, nc, es, same_engine_sync=True):
        self.nc = nc
        self.es = es
        self.E = {"pe": nc.tensor, "dve": nc.vector, "act": nc.scalar, "pool": nc.gpsimd, "sp": nc.sync}
        self.sem = {e: es.enter_context(nc.semaphore("sem_" + e)) for e in ("pe", "dve", "act", "pool")}
        self.cnt = {e: 0 for e in self.sem}
        self.epoch = {e: 0 for e in self.sem}
        self.old = []
        self.waited = {e: {} for e in self.E}
        self.same = same_engine_sync
        self.dq = {}
        for q, n in (("sp", 40),):
            self.dq[q] = {"sems": [es.enter_context(nc.semaphore("dsem_%s%d" % (q, i))) for i in range(n)],
                          "val": [0] * n, "next": 0}
        self.n_inst = 0
        self.mute = False

    def tile(self, name, shape, dt, psum=False, multi=False, es=None):
        es = es or self.es
        self.n_tiles = getattr(self, "n_tiles", 0) + 1
        name = "%s_%d" % (name, self.n_tiles)
        if psum:
            t = es.enter_context(self.nc.psum_tensor("pt_" + name, shape, dt))
        else:
            t = es.enter_context(self.nc.sbuf_tensor("sb_" + name, shape, dt))
        return Tl(t, name, multi, excl=psum)

    def _wait(self, e, tok):
        if tok is None:
            return
        sem, val, key, owner = tok
        if owner == e and (e == "pe" or not self.same):
            return
        if self.waited[e].get(key, 0) >= val:
            return
        self.E[e].wait_ge(sem, val)
        self.waited[e][key] = val

    def _deps(self, e, r, w):
        for t in r:
            if t.multi:
                for tok in t.wd.values():
                    self._wait(e, tok)
            else:
                self._wait(e, t.w)
        for t in w:
            if not t.multi:
                self._wait(e, t.w)
            for tok in t.r.values():
                self._wait(e, tok)

    def _record(self, tok, r, w):
        for t in r:
            t.r[tok[2]] = tok
        for t in w:
            if t.multi:
                t.wd[tok[2]] = tok
            else:
                t.w = tok
                t.r = {}

    def barrier(self):
        self.mute = False
        toks = [(self.sem[e], self.cnt[e], "c_%s_%d" % (e, self.epoch[e]), e) for e in self.sem if self.cnt[e] > 0]
        toks += self.old
        for q, dq in self.dq.items():
            for i, v in enumerate(dq["val"]):
                if v > 0:
                    toks.append((dq["sems"][i], v, "d_%s%d" % (q, i), None))
        for e in self.E:
            for tok in toks:
                if tok[3] == e:
                    continue
                self._wait(e, tok)

    def op(self, e, fn, r=(), w=()):
        if self.mute:
            return None
        if any(t.excl for t in r):
            w = list(w) + [t for t in r if t.excl and t not in w]
            r = [t for t in r if not t.excl]
        self._deps(e, r, w)
        if self.cnt[e] >= 16000:
            self.old.append((self.sem[e], self.cnt[e], "c_%s_%d" % (e, self.epoch[e]), None))
            self.epoch[e] += 1
            self.sem[e] = self.es.enter_context(self.nc.semaphore("sem_%s_%d" % (e, self.epoch[e])))
            self.cnt[e] = 0
        ins = fn()
        self.cnt[e] += 1
        ins.then_inc(self.sem[e], 1)
        tok = (self.sem[e], self.cnt[e], "c_%s_%d" % (e, self.epoch[e]), e)
        self._record(tok, r, w)
        self.n_inst += 1
        return tok

    def dma(self, out, in_, r=(), w=(), q="sp", **kw):
        if self.mute:
            return None
        q = "sp"
        dq = self.dq[q]
        i = dq["next"]
        dq["next"] = (i + 1) % len(dq["sems"])
        key = "d_%s%d" % (q, i)
        if dq["val"][i] > 0:
            self._wait(q, (dq["sems"][i], dq["val"][i], key, None))
        self._deps(q, r, w)
        dq["val"][i] += 16
        self.E[q].dma_start(out=out, in_=in_, **kw).then_inc(dq["sems"][i], 16)
        tok = (dq["sems"][i], dq["val"][i], key, None)
        self._record(tok, r, w)
        self.n_inst += 1
        return tok

    def finish(self, toks):
        for tok in toks:
            self._wait("sp", tok)


def phase_proj(p, nc, H, Z, w_in_l, prew_bc, C, ntiles=NS * NT, ncolblk=None):
    uT = C["uT"]
    ident = C["ident"]
    for tt in range(ntiles):
        ht = C["ht"][tt % 2]
        p.dma(ht[:], H[tt * 128:(tt + 1) * 128, :], r=([C["Ht"]] if "Ht" in C else []), w=[ht])
        sq = C["sq"]
        ss = C["ss"][tt % 2]
        p.op("act", lambda: nc.scalar.activation(out=sq[:], in_=ht[:], func=AF.Square, accum_out=ss[:, 0:1]),
             r=[ht], w=[sq, ss])
        p.op("dve", lambda: nc.vector.tensor_scalar(out=ss[:, 1:2], in0=ss[:, 0:1], scalar1=1.0 / D, scalar2=EPS,
                                                    op0=ALU.mult, op1=ALU.add), r=[ss], w=[ss])
        p.op("pool", lambda: nc.gpsimd.tensor_tensor(out=ss[:, 3:4], in0=ss[:, 1:2], in1=C["mhalf"][:], op=ALU.pow), r=[ss, C["mhalf"]], w=[ss])
        ut = C["ut"][tt % 2]
        p.op("dve", lambda: nc.vector.scalar_tensor_tensor(out=ut[:], in0=ht[:], scalar=ss[:, 3:4], in1=prew_bc[:],
                                                           op0=ALU.mult, op1=ALU.mult), r=[ht, ss, prew_bc], w=[ut])
        for half in range(2):
            pst = C["pst"][half]
            for k4 in range(4):
                kc = half * 4 + k4
                p.op("pe", lambda: nc.tensor.transpose(pst[:, k4 * 128:(k4 + 1) * 128], ut[:, kc * 128:(kc + 1) * 128],
                                                       ident[:]), r=[ut, ident], w=[pst])
            dst = uT.t[:, half * 4:(half + 1) * 4, tt * 128:(tt + 1) * 128]
            src = pst.t[:].rearrange("p (k t) -> p k t", k=4)
            if half == 0:
                p.op("act", lambda: nc.scalar.copy(out=dst, in_=src), r=[pst], w=[uT])
            else:
                p.op("dve", lambda: nc.vector.tensor_copy(out=dst, in_=src), r=[pst], w=[uT])
    wv = w_in_l.rearrange("(kc p) n -> p kc n", p=128)
    ncb = (IN_W + 511) // 512 if ncolblk is None else ncolblk
    ev = 0
    def load_w(cb):
        c0 = cb * 512
        cw = min(512, IN_W - c0)
        wst = C["wst"][cb % 2]
        wbf = C["wbf"][cb % 2]
        p.dma(wst[:, :, 0:cw], wv[:, :, c0:c0 + cw], w=[wst], q="sp")
        p.op("pool", lambda: nc.gpsimd.tensor_copy(out=wbf[:, 0:4, 0:cw], in_=wst[:, 0:4, 0:cw]), r=[wst], w=[wbf])
        p.op("pool", lambda: nc.gpsimd.tensor_copy(out=wbf[:, 4:8, 0:cw], in_=wst[:, 4:8, 0:cw]), r=[wst], w=[wbf])

    load_w(0)
    for cb in range(ncb):
        c0 = cb * 512
        cw = min(512, IN_W - c0)
        wbf = C["wbf"][cb % 2]
        if cb + 1 < ncb:
            load_w(cb + 1)
        for tt in range(ntiles):
            psz = C["psz"][ev % 4]
            zo = C["zo"][ev % 4]
            for kc in range(8):
                p.op("pe", lambda: nc.tensor.matmul(psz[:, 0:cw], lhsT=uT[:, kc, tt * 128:(tt + 1) * 128],
                                                    rhs=wbf[:, kc, 0:cw], start=(kc == 0), stop=(kc == 7)),
                     r=[uT, wbf], w=[psz])
            if ev % 2 == 0:
                p.op("act", lambda: nc.scalar.copy(out=zo[:, 0:cw], in_=psz[:, 0:cw]), r=[psz], w=[zo])
            else:
                p.op("dve", lambda: nc.vector.tensor_copy(out=zo[:, 0:cw], in_=psz[:, 0:cw]), r=[psz], w=[zo])
            p.dma(zr(Z, tt * 128, (tt + 1) * 128)[:, c0:c0 + cw], zo[:, 0:cw], r=[zo], w=[C["Zt"]], q="sp")
            ev += 1


def load_bc(p, nc, tl, row_ap, q="sp"):
    p.dma(tl[:], row_ap.partition_broadcast(128), w=[tl], q=q)


def phase_merge(p, nc, l, DR, G, Hin, Hout, out_final=None, tiles=None):
    PS = G["PS"]
    ident = G["ident"]
    Z = DR["Z"]
    with ExitStack() as es:
        T = lambda name, shape, dt, **kw: p.tile("mg_" + name, shape, dt, es=es, **kw)
        wst = T("wst", [128, 8, 1024], F32)
        W = [T("w%d" % i, [128, 8, 1024], BF16) for i in range(4)]
        names = ["w_att_out", "w_rwkv_out", "w_hgrn_out", "w_o"]
        for i in range(4):
            p.dma(wst[:], DR[names[i]][l].rearrange("(kc p) n -> p kc n", p=128), w=[wst])
            p.op("pool", lambda: nc.gpsimd.tensor_copy(out=W[i][:, 0:4, :], in_=wst[:, 0:4, :]), r=[wst], w=[W[i]])
            p.op("act", lambda: nc.scalar.copy(out=W[i][:, 4:8, :], in_=wst[:, 4:8, :]), r=[wst], w=[W[i]])
        mhalf = T("mhalf", [128, 1], F32)
        p.op("pool", lambda: nc.gpsimd.memset(mhalf[:], -0.5), w=[mhalf])
        postw = T("postw", [128, D], F32)
        load_bc(p, nc, postw, DR["post_norm_w"][l:l + 1, :])
        oT = [[T("oT%d_%d" % (b, i), [128, 8, 128], BF16) for i in range(3)] for b in range(3)]
        mg = [T("mgt%d" % i, [128, 3072], F32) for i in range(3)]
        hin = [T("hin%d" % i, [128, D], F32) for i in range(3)]
        ys = [T("y%d" % i, [128, D], F32) for i in range(2)]
        tmp = [T("tmp%d" % i, [128, 512], F32) for i in range(2)]
        yT = T("yT", [128, 8, 128], BF16)
        hn = [T("hn%d" % i, [128, D], F32) for i in range(2)]
        sq = T("sq", [128, 512], F32)
        st = [T("st%d" % i, [128, 8], F32) for i in range(2)]
        OTs = [DR["OT_att"], DR["OT_rwkv"], DR["OT_hgrn"]]
        tl_list = list(range(NS * NT)) if tiles is None else tiles
        cnt = 0
        def stage0(it, tt):
            r0 = tt * 128
            for b in range(3):
                p.dma(oT[b][it % 3][:], OTs[b].rearrange("(kc p) t -> p kc t", p=128)[:, :, r0:r0 + 128],
                      r=[DR["OT_t"][b]], w=[oT[b][it % 3]], q="act")
            m = mg[it % 3]
            p.dma(m[:], zr(Z, r0, r0 + 128)[:, MG_OFF:MG_OFF + 3072], r=[DR["Zt"]], w=[m])
            hi_ = hin[it % 3]
            p.dma(hi_[:], Hin[r0:r0 + 128, :], r=[DR["Ht"]], w=[hi_])

        def stage1(it, tt):
            nonlocal cnt
            y = ys[it % 2]
            r0 = tt * 128
            m = mg[it % 3]
            hi_ = hin[it % 3]
            p.op("act", lambda: nc.scalar.activation(out=m[:], in_=m[:], func=AF.Sigmoid), r=[m], w=[m])
            for b in range(3):
                ob = oT[b][it % 3]
                for half in range(2):
                    ps = PS[cnt % 2]
                    cnt += 1
                    for kc in range(8):
                        p.op("pe", lambda: nc.tensor.matmul(ps[:], lhsT=ob[:, kc, :], rhs=W[b][:, kc, half * 512:(half + 1) * 512],
                                                            start=(kc == 0), stop=(kc == 7)), r=[ob, W[b]], w=[ps])
                    gsl = m[:, b * 1024 + half * 512: b * 1024 + (half + 1) * 512]
                    ysl = y[:, half * 512:(half + 1) * 512]
                    if b == 0:
                        p.op("dve", lambda: nc.vector.tensor_tensor(out=ysl, in0=ps[:], in1=gsl, op=ALU.mult), r=[ps, m], w=[y])
                    else:
                        t_ = tmp[cnt % 2]
                        p.op("dve", lambda: nc.vector.tensor_tensor(out=t_[:], in0=ps[:], in1=gsl, op=ALU.mult), r=[ps, m], w=[t_])
                        p.op("pool", lambda: nc.gpsimd.tensor_tensor(out=ysl, in0=ysl, in1=t_[:], op=ALU.add), r=[t_, y], w=[y])

        def stage2(it, tt):
            y = ys[it % 2]
            r0 = tt * 128
            hi_ = hin[it % 3]
            for half in range(2):
                pst = PS[2 + half]
                for k4 in range(4):
                    kc = half * 4 + k4
                    p.op("pe", lambda: nc.tensor.transpose(pst[:, k4 * 128:(k4 + 1) * 128], y[:, kc * 128:(kc + 1) * 128], ident[:]),
                         r=[y, ident], w=[pst])
                src = pst.t[:].rearrange("p (k t) -> p k t", k=4)
                if half == 0:
                    p.op("act", lambda: nc.scalar.copy(out=yT[:, 0:4, :], in_=src), r=[pst], w=[yT])
                else:
                    p.op("dve", lambda: nc.vector.tensor_copy(out=yT[:, 4:8, :], in_=src), r=[pst], w=[yT])
            s_ = st[it % 2]
            h_ = hn[it % 2]
            pso = [PS[4 + (it % 2) * 2], PS[5 + (it % 2) * 2]]
            for half in range(2):
                for kc in range(8):
                    p.op("pe", lambda: nc.tensor.matmul(pso[half][:], lhsT=yT[:, kc, :], rhs=W[3][:, kc, half * 512:(half + 1) * 512],
                                                        start=(kc == 0), stop=(kc == 7)), r=[yT, W[3]], w=[pso[half]])
                p.op("act", lambda: nc.scalar.activation(out=sq[:], in_=pso[half][:], func=AF.Square, accum_out=s_[:, half:half + 1]),
                     r=[pso[half]], w=[sq, s_])
            p.op("dve", lambda: nc.vector.tensor_tensor(out=s_[:, 2:3], in0=s_[:, 0:1], in1=s_[:, 1:2], op=ALU.add), r=[s_], w=[s_])
            p.op("dve", lambda: nc.vector.tensor_scalar(out=s_[:, 3:4], in0=s_[:, 2:3], scalar1=1.0 / D, scalar2=EPS,
                                                        op0=ALU.mult, op1=ALU.add), r=[s_], w=[s_])
            p.op("pool", lambda: nc.gpsimd.tensor_tensor(out=s_[:, 5:6], in0=s_[:, 3:4], in1=mhalf[:], op=ALU.pow), r=[s_, mhalf], w=[s_])
            for half in range(2):
                hs = h_[:, half * 512:(half + 1) * 512]
                p.op("dve", lambda: nc.vector.scalar_tensor_tensor(out=hs, in0=pso[half][:], scalar=s_[:, 5:6],
                                                                   in1=postw[:, half * 512:(half + 1) * 512],
                                                                   op0=ALU.mult, op1=ALU.mult), r=[pso[half], s_, postw], w=[h_])
            p.op("pool", lambda: nc.gpsimd.tensor_tensor(out=h_[:], in0=h_[:], in1=hi_[:], op=ALU.add), r=[h_, hi_], w=[h_])
            n = tt % NT
            if n == 0:
                p.op("pool", lambda: nc.gpsimd.memset(h_[0:96, :], 0.0), r=[], w=[h_])
                p.op("pool", lambda: nc.gpsimd.memset(h_[96:112, :], 0.0), r=[], w=[h_])
            if out_final is None:
                p.dma(Hout[r0:r0 + 128, :], h_[:], r=[h_], w=[DR["Ht2"]])
            else:
                s = tt // NT
                if n == 0:
                    pass
                else:
                    p.dma(out_final[s, (n - 1) * 128:n * 128, :], h_[:], r=[h_], w=[DR["Outt"]])
        for it, tt in enumerate(tl_list):
            if it == 0:
                stage0(0, tt)
                if len(tl_list) > 1:
                    stage0(1, tl_list[1])
                stage1(0, tt)
            if it + 2 < len(tl_list):
                stage0(it + 2, tl_list[it + 2])
            if it + 1 < len(tl_list):
                stage1(it + 1, tl_list[it + 1])
            stage2(it, tt)
        p.barrier()


def phase_att(p, nc, l, DR, G, pairs=None, qts=None):
    import math as _m
    PS = G["PS"]
    ident = G["ident"]
    tri = G["tri_le"]
    Z = DR["Z"]
    lam_init = 0.8 - 0.6 * _m.exp(-0.3 * l)
    with ExitStack() as es:
        T = lambda name, shape, dt, **kw: p.tile("at_" + name, shape, dt, es=es, **kw)
        cos2 = T("cos2", [128, NT, 128], F32)
        sin2 = T("sin2", [128, NT, 128], F32)
        p.dma(cos2[:], DR["cos2"], w=[cos2])
        p.dma(sin2[:], DR["sin2"], w=[sin2])
        lv = T("lv", [128, 4, 64], F32)
        for i, n in enumerate(("lambda_q1", "lambda_k1", "lambda_q2", "lambda_k2")):
            p.dma(lv[:, i, :], DR[n][l:l + 1, :].partition_broadcast(128), w=[lv])
        lt = T("lt", [128, 2, 64], F32)
        ls = T("ls", [128, 8], F32)
        p.op("dve", lambda: nc.vector.tensor_tensor(out=lt[:, 0, :], in0=lv[:, 0, :], in1=lv[:, 1, :], op=ALU.mult), r=[lv], w=[lt])
        p.op("dve", lambda: nc.vector.tensor_tensor(out=lt[:, 1, :], in0=lv[:, 2, :], in1=lv[:, 3, :], op=ALU.mult), r=[lv], w=[lt])
        p.op("dve", lambda: nc.vector.tensor_reduce(out=ls[:, 0:2], in_=lt[:], axis=AX.X, op=ALU.add), r=[lt], w=[ls])
        p.op("act", lambda: nc.scalar.activation(out=ls[:, 2:4], in_=ls[:, 0:2], func=AF.Exp), r=[ls], w=[ls])
        p.op("dve", lambda: nc.vector.tensor_tensor(out=ls[:, 4:5], in0=ls[:, 3:4], in1=ls[:, 2:3], op=ALU.subtract), r=[ls], w=[ls])
        p.op("dve", lambda: nc.vector.tensor_scalar(out=ls[:, 5:6], in0=ls[:, 4:5], scalar1=-lam_init, scalar2=None, op0=ALU.add), r=[ls], w=[ls])
        neg_lam = ls[:, 5:6]
        normw = T("normw", [128, 128], F32)
        load_bc(p, nc, normw, DR["att_norm_w"][l:l + 1, :])
        p.op("dve", lambda: nc.vector.tensor_scalar(out=normw[:], in0=normw[:], scalar1=(1.0 - lam_init), scalar2=None, op0=ALU.mult),
             r=[normw], w=[normw])
        SETS = []
        for kb in range(2):
            Bf = {}
            for nm in ("qraw", "kraw", "vraw", "graw"):
                Bf[nm] = T("%s%d" % (nm, kb), [128, NT, 128], F32)
            Bf["qT"] = T("qT%d" % kb, [128, P], BF16)
            Bf["kT"] = T("kT%d" % kb, [128, P], BF16)
            Bf["v1"] = T("v1%d" % kb, [128, NT, 130], BF16)
            v1_ = Bf["v1"]
            p.op("pool", lambda: nc.gpsimd.memset(v1_[:, :, 128:130], 1.0), w=[v1_])
            p.op("pool", lambda: nc.gpsimd.memset(v1_[0:96, 0, 128:130], 0.0), w=[v1_])
            p.op("pool", lambda: nc.gpsimd.memset(v1_[96:112, 0, 128:130], 0.0), w=[v1_])
            SETS.append(Bf)
        t1 = T("t1", [128, NT, 128], F32)
        t2 = T("t2", [128, NT, 128], F32)
        obuf = T("obuf", [128, NT, 128], F32)
        oTb = T("oTb", [128, P], BF16)
        pT = [T("pT%d" % i, [128, 512], BF16) for i in range(5)]
        SB = [PS[0], PS[1], PS[6]]
        mhalf = T("mhalf", [128, 1], F32)
        p.op("pool", lambda: nc.gpsimd.memset(mhalf[:], -0.5), w=[mhalf])
        o_ = [T("o%d" % i, [128, 128], F32) for i in range(2)]
        sq = T("sq", [128, 128], F32)
        rs = [T("rs%d" % i, [128, 12], F32) for i in range(2)]
        pr = [(s, j) for s in range(NS) for j in range(8)] if pairs is None else pairs

        def setup_load(s, j, Bf):
            qraw, kraw, vraw, graw = (Bf[k] for k in ("qraw", "kraw", "vraw", "graw"))
            zs = zr(Z, s * P, (s + 1) * P).rearrange("(n p) c -> p n c", p=128)
            p.dma(qraw[:], zs[:, :, j * 128:(j + 1) * 128], r=[DR["Zt"]], w=[qraw])
            p.dma(kraw[:], zs[:, :, 1024 + j * 128:1024 + (j + 1) * 128], r=[DR["Zt"]], w=[kraw], q="act")
            p.dma(vraw[:], zs[:, :, 2048 + j * 128:2048 + (j + 1) * 128], r=[DR["Zt"]], w=[vraw])
            p.dma(graw[:], zs[:, :, 3072 + j * 128:3072 + (j + 1) * 128], r=[DR["Zt"]], w=[graw], q="act")

        def setup(s, j, Bf):
            qraw, kraw, vraw, graw, qT, kT, v1 = (Bf[k] for k in ("qraw", "kraw", "vraw", "graw", "qT", "kT", "v1"))
            for (raw, dstT) in ((qraw, qT), (kraw, kT)):
                E = nc.vector
                p.op("dve", lambda: E.tensor_tensor(out=t1[:], in0=raw[:], in1=cos2[:], op=ALU.mult), r=[raw, cos2], w=[t1])
                yield
                rv = raw.t[:].rearrange("p n (g h d) -> p (n g) h d", g=2, h=2)
                sv = sin2.t[:].rearrange("p n (g h d) -> p (n g) h d", g=2, h=2)
                tv = t2.t[:].rearrange("p n (g h d) -> p (n g) h d", g=2, h=2)
                p.op("dve", lambda: E.tensor_tensor(out=tv[:, :, 0, :], in0=rv[:, :, 1, :], in1=sv[:, :, 0, :], op=ALU.mult), r=[raw, sin2], w=[t2])
                yield
                p.op("dve", lambda: E.tensor_tensor(out=tv[:, :, 1, :], in0=rv[:, :, 0, :], in1=sv[:, :, 1, :], op=ALU.mult), r=[raw, sin2], w=[t2])
                yield
                p.op("dve", lambda: E.tensor_tensor(out=t1[:], in0=t1[:], in1=t2[:], op=ALU.add), r=[t1, t2], w=[t1])
                yield
                for n0 in range(0, NT, 4):
                    nn = min(4, NT - n0)
                    pst = PS[7]
                    for i in range(nn):
                        p.op("pe", lambda: nc.tensor.transpose(pst[:, i * 128:(i + 1) * 128], t1[:, n0 + i, :], ident[:]), r=[t1, ident], w=[pst])
                    p.op("dve", lambda: nc.vector.tensor_copy(out=dstT[:, n0 * 128:(n0 + nn) * 128], in_=pst[:, 0:nn * 128]), r=[pst], w=[dstT])
                    yield
            p.op("dve", lambda: nc.vector.tensor_copy(out=v1[:, :, 0:128], in_=vraw[:]), r=[vraw], w=[v1])
            yield
            p.op("act", lambda: nc.scalar.activation(out=graw[:], in_=graw[:], func=AF.Silu), r=[graw], w=[graw])
            yield

        def mainloop(s, j, Bf, inj):
            qT, kT, v1, graw = Bf["qT"], Bf["kT"], Bf["v1"], Bf["graw"]
            groups = []
            for qt in (range(NT) if qts is None else qts):
                for k0 in range(0, qt + 1, 4):
                    for g in range(2):
                        groups.append((qt, g, k0, min(k0 + 4, qt + 1)))

            def emit_scores(grp, gi):
                qt, g, k0, k1 = grp
                pss = SB[gi % 3]
                for i, kt in enumerate(range(k0, k1)):
                    p.op("pe", lambda: nc.tensor.matmul(pss[:, i * 128:(i + 1) * 128], lhsT=kT[g * 64:(g + 1) * 64, kt * 128:(kt + 1) * 128],
                                                        rhs=qT[g * 64:(g + 1) * 64, qt * 128:(qt + 1) * 128], start=True, stop=True),
                         r=[kT, qT], w=[pss])
                pt = pT[gi % 5]
                n = k1 - k0
                p.op("act", lambda: nc.scalar.activation(out=pt[:, 0:n * 128], in_=pss[:, 0:n * 128], func=AF.Exp, scale=0.125), r=[pss], w=[pt])
                if k1 - 1 == qt:
                    i = qt - k0
                    p.op("pool", lambda: nc.gpsimd.tensor_tensor(out=pt[:, i * 128:(i + 1) * 128], in0=pt[:, i * 128:(i + 1) * 128], in1=tri[:], op=ALU.mult),
                         r=[pt, tri], w=[pt])

            def emit_pv(grp, gi):
                qt, g, k0, k1 = grp
                pt = pT[gi % 5]
                pso = PS[2 + (qt % 2) * 2 + g]
                for i, kt in enumerate(range(k0, k1)):
                    p.op("pe", lambda: nc.tensor.matmul(pso[:, 0:129], lhsT=pt[:, i * 128:(i + 1) * 128], rhs=v1[:, kt, 0:129],
                                                        start=(kt == 0), stop=(kt == qt)), r=[pt, v1], w=[pso])
                if k1 - 1 == qt and g == 1:
                    epilogue(qt)
                    if qt >= 3:
                        inj()

            def epilogue(qt):
                O1 = PS[2 + (qt % 2) * 2]
                O2 = PS[2 + (qt % 2) * 2 + 1]
                r_ = rs[qt % 2]
                o = o_[qt % 2]
                p.op("dve", lambda: nc.vector.tensor_scalar(out=r_[:, 0:1], in0=O1[:, 128:129], scalar1=1e-30, scalar2=None, op0=ALU.max), r=[O1], w=[r_])
                p.op("dve", lambda: nc.vector.tensor_scalar(out=r_[:, 1:2], in0=O2[:, 128:129], scalar1=1e-30, scalar2=None, op0=ALU.max), r=[O2], w=[r_])
                p.op("dve", lambda: nc.vector.reciprocal(out=r_[:, 2:4], in_=r_[:, 0:2]), r=[r_], w=[r_])
                p.op("dve", lambda: nc.vector.tensor_tensor(out=r_[:, 4:5], in0=r_[:, 3:4], in1=neg_lam, op=ALU.mult), r=[r_, ls], w=[r_])
                p.op("dve", lambda: nc.vector.tensor_scalar(out=o[:], in0=O1[:, 0:128], scalar1=r_[:, 2:3], scalar2=None, op0=ALU.mult), r=[O1, r_], w=[o])
                p.op("dve", lambda: nc.vector.scalar_tensor_tensor(out=o[:], in0=O2[:, 0:128], scalar=r_[:, 4:5], in1=o[:], op0=ALU.mult, op1=ALU.add),
                     r=[O2, r_, o], w=[o])
                p.op("act", lambda: nc.scalar.activation(out=sq[:], in_=o[:], func=AF.Square, accum_out=r_[:, 5:6]), r=[o], w=[sq, r_])
                p.op("dve", lambda: nc.vector.tensor_scalar(out=r_[:, 6:7], in0=r_[:, 5:6], scalar1=1.0 / 128, scalar2=EPS, op0=ALU.mult, op1=ALU.add), r=[r_], w=[r_])
                p.op("pool", lambda: nc.gpsimd.tensor_tensor(out=r_[:, 8:9], in0=r_[:, 6:7], in1=mhalf[:], op=ALU.pow), r=[r_, mhalf], w=[r_])
                p.op("dve", lambda: nc.vector.scalar_tensor_tensor(out=o[:], in0=o[:], scalar=r_[:, 8:9], in1=normw[:], op0=ALU.mult, op1=ALU.mult),
                     r=[o, r_, normw], w=[o])
                p.op("pool", lambda: nc.gpsimd.tensor_tensor(out=obuf[:, qt, :], in0=o[:], in1=graw[:, qt, :], op=ALU.mult), r=[o, graw], w=[obuf])

            AHEAD = 2
            for gi, grp in enumerate(groups):
                emit_scores(grp, gi)
                if gi >= AHEAD:
                    emit_pv(groups[gi - AHEAD], gi - AHEAD)
            for gi in range(max(0, len(groups) - AHEAD), len(groups)):
                emit_pv(groups[gi], gi)

        def finish(s, j):
            for n0 in range(0, NT, 4):
                nn = min(4, NT - n0)
                pst = PS[7]
                for i in range(nn):
                    p.op("pe", lambda: nc.tensor.transpose(pst[:, i * 128:(i + 1) * 128], obuf[:, n0 + i, :], ident[:]), r=[obuf, ident], w=[pst])
                if (n0 // 4) % 2 == 0:
                    p.op("act", lambda: nc.scalar.copy(out=oTb[:, n0 * 128:(n0 + nn) * 128], in_=pst[:, 0:nn * 128]), r=[pst], w=[oTb])
                else:
                    p.op("dve", lambda: nc.vector.tensor_copy(out=oTb[:, n0 * 128:(n0 + nn) * 128], in_=pst[:, 0:nn * 128]), r=[pst], w=[oTb])
            p.dma(DR["OT_att"][j * 128:(j + 1) * 128, s * P:(s + 1) * P], oTb[:], r=[oTb], w=[DR["OT_t"][0]])

        def run_all(gen):
            for _ in gen:
                pass

        setup_load(pr[0][0], pr[0][1], SETS[0])
        run_all(setup(pr[0][0], pr[0][1], SETS[0]))
        for k, (s, j) in enumerate(pr):
            if k + 1 < len(pr):
                setup_load(pr[k + 1][0], pr[k + 1][1], SETS[(k + 1) % 2])
            bg = setup(pr[k + 1][0], pr[k + 1][1], SETS[(k + 1) % 2]) if k + 1 < len(pr) else iter(())

            def inj(n=2):
                for _ in range(n):
                    try:
                        next(bg)
                    except StopIteration:
                        return
            mainloop(s, j, SETS[k % 2], inj)
            run_all(bg)
            finish(s, j)
        p.barrier()


def phase_hgrn(p, nc, l, DR, G, seqs=None, zoff=HG_OFF, ntl=NT, pipelined=True):
    PS = G["PS"]
    ident = G["ident"]
    b_le, b_gt, bind = G["b_le"], G["b_gt"], G["bind"]
    Z = DR["Z"]
    with ExitStack() as es:
        T = lambda name, shape, dt, **kw: p.tile("hg_" + name, shape, dt, es=es, **kw)
        hnw = T("hnw", [128, 128], F32)
        load_bc(p, nc, hnw, DR["hgrn_norm_w"][l:l + 1, :])
        identb = T("identb", [128, 128], BF16)
        p.op("pool", lambda: nc.gpsimd.tensor_copy(out=identb[:], in_=ident[:]), r=[ident], w=[identb])
        mhalf = T("mhalf", [128, 8], F32)
        p.op("pool", lambda: nc.gpsimd.memset(mhalf[:], -0.5), w=[mhalf])
        if l > 0:
            lb = T("lb", [128, 1024], F32)
            oml = T("oml", [128, 1024], F32)
            x0 = T("x0", [128, 1024], F32)
            load_bc(p, nc, x0, DR["hgrn_lower_bounds"][0:1, :])
            load_bc(p, nc, lb, DR["hgrn_lower_bounds"][1:2, :])
            p.op("dve", lambda: nc.vector.tensor_tensor(out=x0[:], in0=lb[:], in1=x0[:], op=ALU.subtract), r=[lb, x0], w=[x0])
            p.op("act", lambda: nc.scalar.activation(out=lb[:], in_=x0[:], func=AF.Sigmoid), r=[x0], w=[lb])
            p.op("act", lambda: nc.scalar.activation(out=oml[:], in_=x0[:], func=AF.Sigmoid, scale=-1.0), r=[x0], w=[oml])
        zts = [T("z%d" % i, [128, 4096], F32) for i in range(2)]
        f = T("f", [128, 1024], F32)
        kf = T("kf", [128, 1024], F32)
        logf = T("logf", [128, 1024], F32)
        ex = T("ex", [128, 1024], F32)
        SETS = []
        for k in range(2):
            B = {}
            for nm in ("qtb", "ktb", "kh", "khz", "vb"):
                B[nm] = T("%s%d" % (nm, k), [128, 1024], BF16)
            B["gs"] = T("gs%d" % k, [128, 1024], F32)
            B["gC"] = T("gC%d" % k, [128, 32], F32)
            SETS.append(B)
        qkT = [T("qkT%d" % j, [128, 256], BF16) for j in range(8)]
        qz = [T("qz%d" % j, [128, 64], BF16) for j in range(8)]
        for j in range(8):
            p.op("pool", lambda: nc.gpsimd.memset(qz[j][:], 0.0), w=[qz[j]])
        attm = [T("attm%d" % j, [128, 128], BF16) for j in range(8)]
        S = [T("S%d" % j, [128, 128], F32) for j in range(8)]
        Sb = [T("Sb%d" % j, [128, 128], BF16) for j in range(8)]
        osb = T("osb", [128, 1024], F32)
        obuf = T("obuf", [128, 1024], F32)
        oTb = T("oTb", [128, 8, 128], BF16)
        st = T("st", [128, 4, 8], F32)

        def v8(ap):
            return ap.rearrange("p (h d) -> p h d", d=128)

        def prep_load(s, n, z_):
            r0 = s * P + n * 128
            p.dma(z_[:], zr(Z, r0, r0 + 128)[:, zoff:zoff + 4096], r=[DR["Zt"]], w=[z_])
            yield

        def prep(s, n, B, z_):
            r0 = s * P + n * 128
            p.op("act", lambda: nc.scalar.activation(out=f[:], in_=z_[:, 1024:2048], func=AF.Sigmoid), r=[z_], w=[f])
            yield
            if l > 0:
                p.op("dve", lambda: nc.vector.tensor_tensor(out=f[:], in0=f[:], in1=oml[:], op=ALU.mult), r=[f, oml], w=[f])
                yield
                p.op("pool", lambda: nc.gpsimd.tensor_tensor(out=f[:], in0=f[:], in1=lb[:], op=ALU.add), r=[f, lb], w=[f])
                yield
            p.op("dve", lambda: nc.vector.tensor_scalar(out=kf[:], in0=f[:], scalar1=-1.0, scalar2=1.0, op0=ALU.mult, op1=ALU.add), r=[f], w=[kf])
            p.op("act", lambda: nc.scalar.activation(out=logf[:], in_=f[:], func=AF.Ln), r=[f], w=[logf])
            yield
            p.op("act", lambda: nc.scalar.activation(out=z_[:, 0:1024], in_=z_[:, 0:1024], func=AF.Silu), r=[z_], w=[z_])
            p.op("act", lambda: nc.scalar.activation(out=B["gs"][:], in_=z_[:, 3072:4096], func=AF.Silu), r=[z_], w=[B["gs"]])
            yield
            p.op("pool", lambda: nc.gpsimd.tensor_copy(out=B["vb"][:], in_=z_[:, 2048:3072]), r=[z_], w=[B["vb"]])
            yield
            for j in range(8):
                p.op("pe", lambda: nc.tensor.matmul(PS[4][:, j * 4:(j + 1) * 4], lhsT=logf[:, j * 128:(j + 1) * 128], rhs=bind[:, 0:4], start=True, stop=True),
                     r=[logf, bind], w=[PS[4]])
            p.op("act", lambda: nc.scalar.activation(out=B["gC"][:], in_=PS[4][:, 0:32], func=AF.Exp), r=[PS[4]], w=[B["gC"]])
            yield
            for half in range(2):
                hs = slice(half * 512, (half + 1) * 512)
                pl = PS[4 + half]
                p.op("pe", lambda: nc.tensor.matmul(pl[:], lhsT=b_le[:], rhs=logf[:, hs], start=True, stop=True), r=[b_le, logf], w=[pl])
                p.op("act", lambda: nc.scalar.activation(out=ex[:, hs], in_=pl[:], func=AF.Exp), r=[pl], w=[ex])
                p.op("dve", lambda: nc.vector.tensor_tensor(out=B["qtb"][:, hs], in0=z_[:, hs], in1=ex[:, hs], op=ALU.mult), r=[z_, ex], w=[B["qtb"]])
                p.op("act", lambda: nc.scalar.activation(out=ex[:, hs], in_=pl[:], func=AF.Exp, scale=-1.0), r=[pl], w=[ex])
                p.op("dve", lambda: nc.vector.tensor_tensor(out=B["ktb"][:, hs], in0=kf[:, hs], in1=ex[:, hs], op=ALU.mult), r=[kf, ex], w=[B["ktb"]])
                p.op("pe", lambda: nc.tensor.matmul(pl[:], lhsT=b_gt[:], rhs=logf[:, hs], start=True, stop=True), r=[b_gt, logf], w=[pl])
                p.op("act", lambda: nc.scalar.activation(out=ex[:, hs], in_=pl[:], func=AF.Exp), r=[pl], w=[ex])
                p.op("dve", lambda: nc.vector.tensor_tensor(out=B["kh"][:, hs], in0=kf[:, hs], in1=ex[:, hs], op=ALU.mult), r=[kf, ex], w=[B["kh"]])
                yield
            p.op("pool", lambda: nc.gpsimd.tensor_scalar(out=B["khz"][:], in0=B["kh"][:], scalar1=bind[:, 3:4], scalar2=None, op0=ALU.mult), r=[B["kh"], bind], w=[B["khz"]])
            yield

        def main(s, n, B, inj):
            if n == 0:
                for j in range(8):
                    p.op("pool", lambda: nc.gpsimd.memset(S[j][:], 0.0), w=[S[j]])
                    p.op("pool", lambda: nc.gpsimd.memset(Sb[j][:], 0.0), w=[Sb[j]])
            vb, kh, khz, gC = B["vb"], B["kh"], B["khz"], B["gC"]
            for j in range(8):
                js = slice(j * 128, (j + 1) * 128)
                pw = PS[4 + (j % 2)]
                pwb = pw.t[:].bitcast(BF16)
                p.op("pe", lambda: nc.tensor.transpose(pwb[:, 0:128], B["qtb"][:, js], identb[:]), r=[B["qtb"], identb], w=[pw])
                p.op("pe", lambda: nc.tensor.transpose(pwb[:, 128:256], B["ktb"][:, js], identb[:]), r=[B["ktb"], identb], w=[pw])
                p.op("act", lambda: nc.scalar.copy(out=qkT[j][:], in_=pwb[:, 0:256]), r=[pw], w=[qkT[j]])
                p.op("pool", lambda: nc.gpsimd.tensor_copy(out=qz[j][:, 32:64], in_=qkT[j][:, 96:128]), r=[qkT[j]], w=[qz[j]])
                p.op("pe", lambda: nc.tensor.matmul(pw[:, 256:384], lhsT=qkT[j][:, 128:256], rhs=qkT[j][:, 0:128], start=True, stop=True),
                     r=[qkT[j]], w=[pw])
                p.op("dve", lambda: nc.vector.tensor_tensor(out=attm[j][:], in0=pw[:, 256:384], in1=b_le[:], op=ALU.mult), r=[pw, b_le], w=[attm[j]])
                if j % 2 == 1:
                    inj()
            for j in range(8):
                js = slice(j * 128, (j + 1) * 128)
                po = PS[6 + j // 4]
                p.op("pe", lambda: nc.tensor.matmul(po[:, (j % 4) * 128:(j % 4 + 1) * 128], lhsT=attm[j][:], rhs=vb[:, js],
                                                    start=(j % 4 == 0), stop=False, skip_group_check=True), r=[attm[j], vb], w=[po])
            for c in range(4):
                cs = slice(c * 32, (c + 1) * 32)
                for j in range(8):
                    js = slice(j * 128, (j + 1) * 128)
                    po = PS[6 + j // 4]
                    if c < 3:
                        p.op("pe", lambda: nc.tensor.matmul(po[cs, (j % 4) * 128:(j % 4 + 1) * 128], lhsT=qkT[j][:, cs], rhs=Sb[j][:],
                                                            start=False, stop=False, skip_group_check=True), r=[qkT[j], Sb[j]], w=[po])
                    else:
                        p.op("pe", lambda: nc.tensor.matmul(po[64:128, (j % 4) * 128:(j % 4 + 1) * 128], lhsT=qz[j][:], rhs=Sb[j][:],
                                                            start=False, stop=True, skip_group_check=True), r=[qz[j], Sb[j]], w=[po])
                    pss = PS[j % 4]
                    pc = slice((j // 4) * 128, (j // 4 + 1) * 128)
                    if c < 3:
                        p.op("pe", lambda: nc.tensor.matmul(pss[:, pc], lhsT=kh[cs, js], rhs=vb[cs, js],
                                                            start=True, stop=True), r=[kh, vb], w=[pss])
                    else:
                        p.op("pe", lambda: nc.tensor.matmul(pss[:, pc], lhsT=khz[64:128, js], rhs=vb[64:128, js],
                                                            start=True, stop=True), r=[khz, vb], w=[pss])
                    p.op("dve", lambda: nc.vector.scalar_tensor_tensor(out=S[j][:], in0=S[j][:], scalar=gC[:, j * 4 + c:j * 4 + c + 1],
                                                                       in1=pss[:, pc], op0=ALU.mult, op1=ALU.add),
                         r=[S[j], gC, pss], w=[S[j]])
                    p.op("act", lambda: nc.scalar.copy(out=Sb[j][:], in_=S[j][:]), r=[S[j]], w=[Sb[j]])
                    if j % 4 == 3:
                        inj()
            for half in range(2):
                hs = slice(half * 512, (half + 1) * 512)
                p.op("act", lambda: nc.scalar.copy(out=osb[:, hs], in_=PS[6 + half][:]), r=[PS[6 + half]], w=[osb])

        def outp(s, n, B):
            r0 = s * P + n * 128
            p.op("pool", lambda: nc.gpsimd.tensor_tensor(out=obuf[:], in0=osb[:], in1=osb[:], op=ALU.mult), r=[osb], w=[obuf])
            yield
            p.op("dve", lambda: nc.vector.tensor_reduce(out=st[:, 0, :], in_=v8(obuf.t[:]), axis=AX.X, op=ALU.add), r=[obuf], w=[st])
            p.op("dve", lambda: nc.vector.tensor_scalar(out=st[:, 1, :], in0=st[:, 0, :], scalar1=1.0 / 128, scalar2=EPS, op0=ALU.mult, op1=ALU.add), r=[st], w=[st])
            p.op("pool", lambda: nc.gpsimd.tensor_tensor(out=st[:, 2, :], in0=st[:, 1, :], in1=mhalf[:], op=ALU.pow), r=[st, mhalf], w=[st])
            yield
            p.op("dve", lambda: nc.vector.tensor_tensor(out=v8(osb.t[:]), in0=v8(osb.t[:]), in1=st[:, 2, :].unsqueeze(2).broadcast_to([128, 8, 128]), op=ALU.mult),
                 r=[osb, st], w=[osb])
            yield
            p.op("pool", lambda: nc.gpsimd.tensor_tensor(out=v8(osb.t[:]), in0=v8(osb.t[:]), in1=hnw.t[:].unsqueeze(1).broadcast_to([128, 8, 128]), op=ALU.mult),
                 r=[osb, hnw], w=[osb])
            yield
            p.op("pool", lambda: nc.gpsimd.tensor_tensor(out=obuf[:], in0=osb[:], in1=B["gs"][:], op=ALU.mult), r=[osb, B["gs"]], w=[obuf])
            yield
            for half in range(2):
                pst = PS[4 + half]
                for k4 in range(4):
                    kc = half * 4 + k4
                    p.op("pe", lambda: nc.tensor.transpose(pst[:, k4 * 128:(k4 + 1) * 128], obuf[:, kc * 128:(kc + 1) * 128], ident[:]), r=[obuf, ident], w=[pst])
                src_ = pst.t[:].rearrange("p (k t) -> p k t", k=4)
                if half == 0:
                    p.op("act", lambda: nc.scalar.copy(out=oTb[:, 0:4, :], in_=src_), r=[pst], w=[oTb])
                else:
                    p.op("dve", lambda: nc.vector.tensor_copy(out=oTb[:, 4:8, :], in_=src_), r=[pst], w=[oTb])
                yield
            p.dma(DR["OT_hgrn"].rearrange("(kc p) t -> p kc t", p=128)[:, :, r0:r0 + 128], oTb[:], r=[oTb], w=[DR["OT_t"][2]])
            yield

        tiles = [(s, n) for s in (range(NS) if seqs is None else seqs) for n in range(ntl)]

        def run_all(gen):
            for _ in gen:
                pass

        def chain(*gens):
            for g in gens:
                if g is not None:
                    for _ in g:
                        yield

        if not pipelined:
            for idx, (s, n) in enumerate(tiles):
                B = SETS[idx % 2]
                run_all(prep_load(s, n, zts[idx % 2]))
                run_all(prep(s, n, B, zts[idx % 2]))
                main(s, n, B, lambda: None)
                run_all(outp(s, n, B))
        else:
            run_all(prep_load(tiles[0][0], tiles[0][1], zts[0]))
            run_all(prep(tiles[0][0], tiles[0][1], SETS[0], zts[0]))
            if len(tiles) > 1:
                run_all(prep_load(tiles[1][0], tiles[1][1], zts[1]))
            for idx, (s, n) in enumerate(tiles):
                B = SETS[idx % 2]
                g_out = outp(tiles[idx - 1][0], tiles[idx - 1][1], SETS[(idx - 1) % 2]) if idx > 0 else None
                g_prep = prep(tiles[idx + 1][0], tiles[idx + 1][1], SETS[(idx + 1) % 2], zts[(idx + 1) % 2]) if idx + 1 < len(tiles) else None
                g_load = prep_load(tiles[idx + 2][0], tiles[idx + 2][1], zts[idx % 2]) if idx + 2 < len(tiles) else None
                bg = chain(g_out, g_prep, g_load)

                def inj(k=2):
                    for _ in range(k):
                        try:
                            next(bg)
                        except StopIteration:
                            return
                main(s, n, B, inj)
                run_all(bg)
            run_all(outp(tiles[-1][0], tiles[-1][1], SETS[(len(tiles) - 1) % 2]))
        p.barrier()


C0 = -0.6065306597126334


def phase_rwkv(p, nc, l, DR, G, seqs=None, zoff=RW_OFF, ntl=NT, dbg=None, stop=None, pipelined=True):
    PS = G["PS"]
    ident = G["ident"]
    tri_le, tri_lt, tri_gt, ones = G["tri_le"], G["tri_lt"], G["tri_gt"], G["ones"]
    Z = DR["Z"]

    def bc(ap16, n=16):
        return ap16.unsqueeze(2).broadcast_to([128, n, 64])

    def v3(ap):
        return ap.rearrange("p (h d) -> p h d", d=64)

    with ExitStack() as es:
        T = lambda name, shape, dt, **kw: p.tile("rw_" + name, shape, dt, es=es, **kw)
        mu = T("mu", [128, RW_W], F32)
        load_bc(p, nc, mu, DR["rwkv_mu"][l:l + 1, :])
        prm = {}
        for i, n_ in enumerate(("rwkv_w0", "rwkv_a0", "rwkv_k_k", "rwkv_k_a", "rwkv_gn_w", "rwkv_gn_b")):
            prm[n_] = T(n_, [128, 1024], F32)
            load_bc(p, nc, prm[n_], DR[n_][l:l + 1, :], q=("sp" if i % 2 == 0 else "act"))
        prm["rwkv_r_k"] = T("rwkv_r_k", [128, 1024], F32)
        load_bc(p, nc, prm["rwkv_r_k"], DR["rwkv_r_k"][l:l + 1].rearrange("o h d -> o (h d)"))
        w_up = T("w_up", [64, 1024], F32)
        a_up = T("a_up", [64, 1024], F32)
        p.dma(w_up[:], DR["rwkv_w_up"][l], w=[w_up])
        p.dma(a_up[:], DR["rwkv_a_up"][l], w=[a_up])
        mA = T("mA", [128, 384], F32)
        mB = T("mB", [128, 256], F32)
        p.op("pool", lambda: nc.gpsimd.tensor_copy(out=mA[:, 0:128], in_=tri_lt[:]), r=[tri_lt], w=[mA])
        p.op("pool", lambda: nc.gpsimd.tensor_copy(out=mA[:, 128:256], in_=tri_gt[:]), r=[tri_gt], w=[mA])
        p.op("pool", lambda: nc.gpsimd.tensor_copy(out=mA[:, 256:384], in_=tri_lt[:]), r=[tri_lt], w=[mA])
        p.op("pool", lambda: nc.gpsimd.tensor_copy(out=mB[:, 0:128], in_=tri_le[:]), r=[tri_le], w=[mB])
        p.op("pool", lambda: nc.gpsimd.tensor_copy(out=mB[:, 128:256], in_=tri_le[:]), r=[tri_le], w=[mB])
        identb = T("identb", [128, 128], BF16)
        p.op("pool", lambda: nc.gpsimd.tensor_copy(out=identb[:], in_=ident[:]), r=[ident], w=[identb])
        mhalf16 = T("mhalf16", [128, 16], F32)
        p.op("pool", lambda: nc.gpsimd.memset(mhalf16[:], -0.5), w=[mhalf16])
        zc = T("zc", [128, RW_W], F32)
        zp = T("zp", [128, RW_W], F32)
        sw = T("sw", [128, 1024], F32)
        a_ = T("a_", [128, 1024], F32)
        kk = T("kk", [128, 1024], F32)
        kp = T("kp", [128, 1024], F32)
        b_ = T("b_", [128, 1024], F32)
        twT = T("twT", [64, 256], F32)
        ex = zp.t[:, 0:1024]
        tq = zp.t[:, 3072:4096]
        SETS = []
        for k in range(2):
            B = {}
            for nm in ("rtb", "ktb", "atb", "btb", "khat", "bhat", "vb"):
                B[nm] = T("%s%d" % (nm, k), [128, 1024], BF16)
            B["gs"] = T("gs%d" % k, [128, 1024], F32)
            B["gC"] = T("gC%d" % k, [128, 16], F32)
            B["sm"] = T("sm%d" % k, [128, 4, 16], F32)
            SETS.append(B)
        sm2 = T("sm2", [128, 8, 16], F32)
        fT = [T("fT%d" % i, [128, 4, 128], BF16) for i in range(8)]
        Ar = [T("Ar%d" % h, [128, 256], BF16) for h in range(16)]
        Aak = [T("Aak%d" % i, [128, 128], BF16) for i in range(8)]
        PP = [[T("PP%d_%d" % (i, k), [128, 256], BF16) for k in range(2)] for i in range(8)]
        X = [[T("X%d_%d" % (i, k), [128, 128], BF16) for k in range(2)] for i in range(8)]
        Gt = [T("Gt%d" % i, [128, 64], BF16) for i in range(8)]
        Wt = T("Wt", [128, 1024], F32)
        YT = [T("YT%d" % i, [128, 128], BF16) for i in range(8)]
        ST = T("ST", [128, 8, 64], F32)
        STbd = T("STbd", [128, 8, 128], BF16)
        Ub = T("Ub", [128, 1024], BF16)
        osb = T("osb", [128, 1024], F32)
        obuf = T("obuf", [128, 1024], F32)
        oTb = T("oTb", [128, 8, 128], BF16)

        def prep_load(s, n):
            r0 = s * P + n * 128
            p.dma(zc[:], zr(Z, r0, r0 + 128)[:, zoff:zoff + RW_W], r=[DR["Zt"]], w=[zc])
            if n == 0:
                p.op("pool", lambda: nc.gpsimd.memset(zp[0:1, :], 0.0), w=[zp])
                p.dma(zp[1:128, :], zr(Z, r0, r0 + 127)[:, zoff:zoff + RW_W], r=[DR["Zt"]], w=[zp], q="act")
            else:
                p.dma(zp[:], zr(Z, r0 - 1, r0 + 127)[:, zoff:zoff + RW_W], r=[DR["Zt"]], w=[zp], q="act")
            yield

        def prep(s, n, B):
            r0 = s * P + n * 128
            hs_ = slice(4096, RW_W)
            p.op("dve", lambda: nc.vector.tensor_tensor(out=zp[:, hs_], in0=zp[:, hs_], in1=zc[:, hs_], op=ALU.subtract), r=[zp, zc], w=[zp])
            p.op("dve", lambda: nc.vector.tensor_tensor(out=zp[:, hs_], in0=zp[:, hs_], in1=mu[:, hs_], op=ALU.mult), r=[zp, mu], w=[zp])
            p.op("dve", lambda: nc.vector.tensor_tensor(out=zc[:, hs_], in0=zc[:, hs_], in1=zp[:, hs_], op=ALU.add), r=[zp, zc], w=[zc])
            p.op("act", lambda: nc.scalar.activation(out=zc[:, 4096:4160], in_=zc[:, 4096:4160], func=AF.Tanh), r=[zc], w=[zc])
            yield
            h1 = slice(0, 2560)
            h2 = slice(2560, 4096)
            for k3 in range(3):
                for (sl, eng) in ((h1, "dve"), (h2, "pool")):
                    E = nc.vector if eng == "dve" else nc.gpsimd
                    if k3 == 0:
                        p.op(eng, lambda: E.tensor_tensor(out=zp[:, sl], in0=zp[:, sl], in1=zc[:, sl], op=ALU.subtract), r=[zp, zc], w=[zp])
                    elif k3 == 1:
                        p.op(eng, lambda: E.tensor_tensor(out=zp[:, sl], in0=zp[:, sl], in1=mu[:, sl], op=ALU.mult), r=[zp, mu], w=[zp])
                    else:
                        p.op(eng, lambda: E.tensor_tensor(out=zc[:, sl], in0=zc[:, sl], in1=zp[:, sl], op=ALU.add), r=[zp, zc], w=[zc])
                    yield
            rr = zc.t[:, 0:1024]
            rk = zc.t[:, 1024:2048]
            rv = zc.t[:, 2048:3072]
            rg = zc.t[:, 3072:4096]
            p.op("pe", lambda: nc.tensor.transpose(PS[6][0:64, 0:128], zc[:, 4096:4160], ident[:]), r=[zc, ident], w=[PS[6]])
            p.op("pe", lambda: nc.tensor.transpose(PS[6][0:64, 128:256], zc[:, 4160:4224], ident[:]), r=[zc, ident], w=[PS[6]])
            p.op("act", lambda: nc.scalar.copy(out=twT[:], in_=PS[6][0:64, 0:256]), r=[PS[6]], w=[twT])
            yield
            for half in range(2):
                hs = slice(half * 512, (half + 1) * 512)
                p.op("pe", lambda: nc.tensor.matmul(PS[6][:], lhsT=twT[:, 0:128], rhs=w_up[:, hs], start=True, stop=True), r=[twT, w_up], w=[PS[6]])
                p.op("dve", lambda: nc.vector.tensor_tensor(out=sw[:, hs], in0=PS[6][:], in1=prm["rwkv_w0"][:, hs], op=ALU.add),
                     r=[PS[6], prm["rwkv_w0"]], w=[sw])
                yield
            for half in range(2):
                hs = slice(half * 512, (half + 1) * 512)
                p.op("pe", lambda: nc.tensor.matmul(PS[7][:], lhsT=twT[:, 128:256], rhs=a_up[:, hs], start=True, stop=True), r=[twT, a_up], w=[PS[7]])
                p.op("dve", lambda: nc.vector.tensor_tensor(out=a_[:, hs], in0=PS[7][:], in1=prm["rwkv_a0"][:, hs], op=ALU.add),
                     r=[PS[7], prm["rwkv_a0"]], w=[a_])
                yield
            p.op("act", lambda: nc.scalar.activation(out=sw[:], in_=sw[:], func=AF.Sigmoid), r=[sw], w=[sw])
            p.op("act", lambda: nc.scalar.activation(out=a_[:], in_=a_[:], func=AF.Sigmoid), r=[a_], w=[a_])
            yield
            p.op("dve", lambda: nc.vector.tensor_tensor(out=kk[:], in0=rk, in1=prm["rwkv_k_k"][:], op=ALU.mult), r=[zc, prm["rwkv_k_k"]], w=[kk])
            yield
            p.op("pool", lambda: nc.gpsimd.tensor_tensor(out=kp[:], in0=kk[:], in1=kk[:], op=ALU.mult), r=[kk], w=[kp])
            yield
            p.op("dve", lambda: nc.vector.tensor_reduce(out=sm2[:, 0, :], in_=v3(kp.t[:]), axis=AX.X, op=ALU.add), r=[kp], w=[sm2])
            p.op("dve", lambda: nc.vector.tensor_scalar(out=sm2[:, 1, :], in0=sm2[:, 0, :], scalar1=1e-24, scalar2=None, op0=ALU.max), r=[sm2], w=[sm2])
            p.op("pool", lambda: nc.gpsimd.tensor_tensor(out=sm2[:, 2, :], in0=sm2[:, 1, :], in1=mhalf16[:], op=ALU.pow), r=[sm2, mhalf16], w=[sm2])
            yield
            p.op("dve", lambda: nc.vector.tensor_tensor(out=v3(kk.t[:]), in0=v3(kk.t[:]), in1=bc(sm2[:, 2, :]), op=ALU.mult), r=[kk, sm2], w=[kk])
            yield
            p.op("dve", lambda: nc.vector.scalar_tensor_tensor(out=kp[:], in0=a_[:], scalar=-1.0, in1=prm["rwkv_k_a"][:], op0=ALU.add, op1=ALU.mult),
                 r=[a_, prm["rwkv_k_a"]], w=[kp])
            yield
            p.op("dve", lambda: nc.vector.scalar_tensor_tensor(out=kp[:], in0=kp[:], scalar=1.0, in1=rk, op0=ALU.add, op1=ALU.mult), r=[kp, zc], w=[kp])
            yield
            p.op("pool", lambda: nc.gpsimd.tensor_tensor(out=b_[:], in0=kk[:], in1=a_[:], op=ALU.mult), r=[kk, a_], w=[b_])
            yield
            p.op("pool", lambda: nc.gpsimd.tensor_tensor(out=tq, in0=rr, in1=kp[:], op=ALU.mult), r=[zc, kp], w=[zp])
            yield
            p.op("pool", lambda: nc.gpsimd.tensor_tensor(out=tq, in0=tq, in1=prm["rwkv_r_k"][:], op=ALU.mult), r=[zp, prm["rwkv_r_k"]], w=[zp])
            yield
            p.op("dve", lambda: nc.vector.tensor_reduce(out=B["sm"][:, 3, :], in_=v3(tq), axis=AX.X, op=ALU.add), r=[zp], w=[B["sm"]])
            p.op("pool", lambda: nc.gpsimd.tensor_copy(out=B["vb"][:], in_=rv), r=[zc], w=[B["vb"]])
            yield
            p.op("act", lambda: nc.scalar.activation(out=B["gs"][:], in_=rg, func=AF.Silu), r=[zc], w=[B["gs"]])
            yield
            for _ in range(8):
                yield
            for half in range(2):
                hs = slice(half * 512, (half + 1) * 512)
                pl = PS[6 + half]
                p.op("pe", lambda: nc.tensor.matmul(pl[:], lhsT=tri_le[:], rhs=sw[:, hs], start=True, stop=True), r=[tri_le, sw], w=[pl])
                p.op("act", lambda: nc.scalar.activation(out=ex[:, hs], in_=pl[:], func=AF.Exp, scale=C0), r=[pl], w=[zp])
                p.op("dve", lambda: nc.vector.tensor_tensor(out=B["rtb"][:, hs], in0=rr[:, hs], in1=ex[:, hs], op=ALU.mult), r=[zc, zp], w=[B["rtb"]])
                p.op("dve", lambda: nc.vector.tensor_tensor(out=tq[:, hs], in0=pl[:], in1=sw[:, hs], op=ALU.subtract), r=[pl, sw], w=[zp])
                p.op("act", lambda: nc.scalar.activation(out=tq[:, hs], in_=tq[:, hs], func=AF.Exp, scale=C0), r=[zp], w=[zp])
                p.op("dve", lambda: nc.vector.scalar_tensor_tensor(out=B["atb"][:, hs], in0=kk[:, hs], scalar=-1.0, in1=tq[:, hs], op0=ALU.mult, op1=ALU.mult),
                     r=[kk, zp], w=[B["atb"]])
                p.op("act", lambda: nc.scalar.activation(out=ex[:, hs], in_=pl[:], func=AF.Exp, scale=-C0), r=[pl], w=[zp])
                p.op("pool", lambda: nc.gpsimd.tensor_tensor(out=B["btb"][:, hs], in0=b_[:, hs], in1=ex[:, hs], op=ALU.mult), r=[b_, zp], w=[B["btb"]])
                p.op("dve", lambda: nc.vector.tensor_tensor(out=B["ktb"][:, hs], in0=kp[:, hs], in1=ex[:, hs], op=ALU.mult), r=[kp, zp], w=[B["ktb"]])
                p.op("pe", lambda: nc.tensor.matmul(pl[:], lhsT=tri_gt[:], rhs=sw[:, hs], start=True, stop=True), r=[tri_gt, sw], w=[pl])
                p.op("act", lambda: nc.scalar.activation(out=ex[:, hs], in_=pl[:], func=AF.Exp, scale=C0), r=[pl], w=[zp])
                p.op("pool", lambda: nc.gpsimd.tensor_tensor(out=B["khat"][:, hs], in0=kp[:, hs], in1=ex[:, hs], op=ALU.mult), r=[kp, zp], w=[B["khat"]])
                p.op("dve", lambda: nc.vector.tensor_tensor(out=B["bhat"][:, hs], in0=b_[:, hs], in1=ex[:, hs], op=ALU.mult), r=[b_, zp], w=[B["bhat"]])
                yield
            for pp in range(8):
                p.op("pe", lambda: nc.tensor.matmul(PS[6][:, pp * 2:pp * 2 + 2], lhsT=sw[:, pp * 128:(pp + 1) * 128], rhs=ones[:, 0:2],
                                                    start=True, stop=True), r=[sw, ones], w=[PS[6]])
            p.op("act", lambda: nc.scalar.activation(out=B["gC"][:], in_=PS[6][:, 0:16], func=AF.Exp, scale=C0), r=[PS[6]], w=[B["gC"]])
            yield

        def main(s, n, B, inj):
            if n == 0:
                p.op("pool", lambda: nc.gpsimd.memset(ST[:], 0.0), w=[ST])
                p.op("pool", lambda: nc.gpsimd.memset(STbd[:], 0.0), w=[STbd])
            src = (B["rtb"], B["ktb"], B["atb"], B["btb"])
            for g8 in range(2):
                for pi in range(4):
                    pp = g8 * 4 + pi
                    cs = slice(pp * 128, (pp + 1) * 128)
                    pw = PS[6 + pi // 2]
                    pwb = pw.t[:].bitcast(BF16)
                    off = (pi % 2) * 512
                    for k in range(4):
                        p.op("pe", lambda: nc.tensor.transpose(pwb[:, off + k * 128:off + (k + 1) * 128], src[k][:, cs], identb[:]), r=[src[k], identb], w=[pw])
                    if pi % 2 == 0:
                        p.op("act", lambda: nc.scalar.copy(out=fT[pp].t[:].rearrange("p a t -> p (a t)"), in_=pwb[:, off:off + 512]), r=[pw], w=[fT[pp]])
                    else:
                        p.op("dve", lambda: nc.vector.tensor_copy(out=fT[pp].t[:].rearrange("p a t -> p (a t)"), in_=pwb[:, off:off + 512]), r=[pw], w=[fT[pp]])
                for i in range(8):
                    h = g8 * 8 + i
                    pp, e = h // 2, h % 2
                    es_ = slice(e * 64, (e + 1) * 64)
                    rT, kT, aT, bT = (fT[pp][es_, k, :] for k in range(4))
                    pa = PS[i]
                    p.op("pe", lambda: nc.tensor.matmul(pa[:, 0:128], lhsT=bT, rhs=aT, start=True, stop=True), r=[fT[pp]], w=[pa])
                    p.op("pe", lambda: nc.tensor.matmul(pa[:, 128:256], lhsT=aT, rhs=bT, start=True, stop=True), r=[fT[pp]], w=[pa])
                    p.op("pe", lambda: nc.tensor.matmul(pa[:, 256:384], lhsT=kT, rhs=aT, start=True, stop=True), r=[fT[pp]], w=[pa])
                    p.op("dve", lambda: nc.vector.tensor_tensor(out=PP[i][0][:], in0=pa[:, 0:256], in1=mA[:, 0:256], op=ALU.mult), r=[pa, mA], w=[PP[i][0]])
                    p.op("dve", lambda: nc.vector.tensor_tensor(out=Aak[i][:], in0=pa[:, 256:384], in1=mA[:, 256:384], op=ALU.mult), r=[pa, mA], w=[Aak[i]])
                    p.op("dve", lambda: nc.vector.tensor_tensor(out=X[i][0][:], in0=PP[i][0][:, 0:128], in1=identb[:], op=ALU.add), r=[PP[i][0], identb], w=[X[i][0]])
                inj()
                for i in range(8):
                    h = g8 * 8 + i
                    pp, e = h // 2, h % 2
                    es_ = slice(e * 64, (e + 1) * 64)
                    rT, kT, aT, bT = (fT[pp][es_, k, :] for k in range(4))
                    pa = PS[i]
                    p.op("pe", lambda: nc.tensor.matmul(pa[:, 0:128], lhsT=bT, rhs=rT, start=True, stop=True), r=[fT[pp]], w=[pa])
                    p.op("pe", lambda: nc.tensor.matmul(pa[:, 128:256], lhsT=kT, rhs=rT, start=True, stop=True), r=[fT[pp]], w=[pa])
                    p.op("dve", lambda: nc.vector.tensor_tensor(out=Ar[h][:], in0=pa[:, 0:256], in1=mB[:], op=ALU.mult), r=[pa, mB], w=[Ar[h]])
                inj()
                for lv in range(1, 7):
                    cur, nxt = (lv - 1) % 2, lv % 2
                    for i in range(8):
                        pq = PS[i]
                        Pm, PTm = PP[i][cur][:, 0:128], PP[i][cur][:, 128:256]
                        if lv < 6:
                            p.op("pe", lambda: nc.tensor.matmul(pq[:, 0:128], lhsT=PTm, rhs=Pm, start=True, stop=True), r=[PP[i][cur]], w=[pq])
                        p.op("pe", lambda: nc.tensor.matmul(pq[:, 128:256], lhsT=Pm, rhs=PTm, start=True, stop=True), r=[PP[i][cur]], w=[pq])
                        if i % 2 == 0:
                            p.op("act", lambda: nc.scalar.copy(out=PP[i][nxt][:], in_=pq[:, 0:256]), r=[pq], w=[PP[i][nxt]])
                        else:
                            p.op("dve", lambda: nc.vector.tensor_copy(out=PP[i][nxt][:], in_=pq[:, 0:256]), r=[pq], w=[PP[i][nxt]])
                    inj()
                    for i in range(8):
                        px = PS[i]
                        p.op("pe", lambda: nc.tensor.matmul(px[:, 256:384], lhsT=identb[:], rhs=X[i][cur][:], start=True, stop=False), r=[identb, X[i][cur]], w=[px])
                        p.op("pe", lambda: nc.tensor.matmul(px[:, 256:384], lhsT=PP[i][nxt][:, 128:256], rhs=X[i][cur][:], start=False, stop=True),
                             r=[PP[i][nxt], X[i][cur]], w=[px])
                        if i % 2 == 1:
                            p.op("act", lambda: nc.scalar.copy(out=X[i][nxt][:], in_=px[:, 256:384]), r=[px], w=[X[i][nxt]])
                        else:
                            p.op("dve", lambda: nc.vector.tensor_copy(out=X[i][nxt][:], in_=px[:, 256:384]), r=[px], w=[X[i][nxt]])
                    inj()
                for i in range(8):
                    h = g8 * 8 + i
                    hc = slice(h * 64, (h + 1) * 64)
                    p.op("pe", lambda: nc.tensor.matmul(PS[i][:, 0:64], lhsT=Aak[i][:], rhs=B["vb"][:, hc], start=True, stop=True), r=[Aak[i], B["vb"]], w=[PS[i]])
                    if i % 2 == 0:
                        p.op("act", lambda: nc.scalar.copy(out=Gt[i][:], in_=PS[i][:, 0:64]), r=[PS[i]], w=[Gt[i]])
                    else:
                        p.op("dve", lambda: nc.vector.tensor_copy(out=Gt[i][:], in_=PS[i][:, 0:64]), r=[PS[i]], w=[Gt[i]])
                for i in range(8):
                    h = g8 * 8 + i
                    pp, e = h // 2, h % 2
                    hc = slice(h * 64, (h + 1) * 64)
                    py = PS[i]
                    p.op("pe", lambda: nc.tensor.matmul(py[:, 64:128], lhsT=X[i][0][:], rhs=Gt[i][:], start=True, stop=True),
                         r=[X[i][0], Gt[i]], w=[py])
                    p.op("pe", lambda: nc.tensor.matmul(py[:, 128:256], lhsT=B["atb"][:, pp * 128:(pp + 1) * 128], rhs=X[i][0][:], start=True, stop=True),
                         r=[B["atb"], X[i][0]], w=[py])
                    p.op("act", lambda: nc.scalar.copy(out=Wt[:, hc], in_=py[:, 64:128]), r=[py], w=[Wt])
                    p.op("dve", lambda: nc.vector.tensor_copy(out=YT[pp][e * 64:(e + 1) * 64, :], in_=py[e * 64:(e + 1) * 64, 128:256]), r=[py], w=[YT[pp]])
                inj()
            for pp in range(8):
                p.op("pe", lambda: nc.tensor.matmul(PS[pp // 4][:, (pp % 4) * 128:(pp % 4 + 1) * 128], lhsT=YT[pp][:], rhs=STbd[:, pp, :], start=True, stop=True),
                     r=[YT[pp], STbd], w=[PS[pp // 4]])
            for half in range(2):
                hs = slice(half * 512, (half + 1) * 512)
                p.op("dve", lambda: nc.vector.tensor_tensor(out=Ub[:, hs], in0=PS[half][:], in1=Wt[:, hs], op=ALU.add), r=[PS[half], Wt], w=[Ub])
            for pp in range(8):
                po = PS[2 + pp // 4]
                p.op("pe", lambda: nc.tensor.matmul(po[:, (pp % 4) * 128:(pp % 4 + 1) * 128], lhsT=fT[pp][:, 0, :], rhs=STbd[:, pp, :],
                                                    start=(pp % 4 == 0), stop=False, skip_group_check=True), r=[fT[pp], STbd], w=[po])
            for h in range(16):
                hc = slice(h * 64, (h + 1) * 64)
                po = PS[2 + h // 8]
                oc = slice((h % 8) * 64, (h % 8 + 1) * 64)
                p.op("pe", lambda: nc.tensor.matmul(po[:, oc], lhsT=Ar[h][:, 128:256], rhs=B["vb"][:, hc], start=False, stop=False, skip_group_check=True),
                     r=[Ar[h], B["vb"]], w=[po])
                p.op("pe", lambda: nc.tensor.matmul(po[:, oc], lhsT=Ar[h][:, 0:128], rhs=Ub[:, hc], start=False, stop=True, skip_group_check=True),
                     r=[Ar[h], Ub], w=[po])
            for h in range(16):
                pp, e = h // 2, h % 2
                es_ = slice(e * 64, (e + 1) * 64)
                hc = slice(h * 64, (h + 1) * 64)
                p.op("pe", lambda: nc.tensor.matmul(PS[4][es_, pp * 64:(pp + 1) * 64], lhsT=B["bhat"][:, hc], rhs=Ub[:, hc], start=(pp == 0), stop=False, skip_group_check=True),
                     r=[B["bhat"], Ub], w=[PS[4]])
                p.op("pe", lambda: nc.tensor.matmul(PS[4][es_, pp * 64:(pp + 1) * 64], lhsT=B["khat"][:, hc], rhs=B["vb"][:, hc], start=False, stop=True, skip_group_check=True),
                     r=[B["khat"], B["vb"]], w=[PS[4]])
            gC = B["gC"]
            p.op("dve", lambda: nc.vector.tensor_tensor(out=ST[:], in0=ST[:], in1=gC.t[:, 0:16].rearrange("p (a b) -> p a b", b=2)[:, :, 0:1].broadcast_to([128, 8, 64]), op=ALU.mult),
                 r=[ST, gC], w=[ST])
            p.op("dve", lambda: nc.vector.tensor_tensor(out=ST.t[:].rearrange("p a v -> p (a v)"), in0=ST.t[:].rearrange("p a v -> p (a v)"), in1=PS[4][:], op=ALU.add),
                 r=[ST, PS[4]], w=[ST])
            p.op("act", lambda: nc.scalar.copy(out=STbd[0:64, :, 0:64], in_=ST[0:64, :, :]), r=[ST], w=[STbd])
            p.op("pool", lambda: nc.gpsimd.tensor_copy(out=STbd[64:128, :, 64:128], in_=ST[64:128, :, :]), r=[ST], w=[STbd])
            for half in range(2):
                hs = slice(half * 512, (half + 1) * 512)
                p.op("act", lambda: nc.scalar.copy(out=osb[:, hs], in_=PS[2 + half][:]), r=[PS[2 + half]], w=[osb])

        def outp(s, n, B):
            r0 = s * P + n * 128
            sm = sm2
            if dbg is not None:
                p.dma(dbg[r0:r0 + 128, :], osb[:], r=[osb], w=[DR["dbgt"]])
            p.op("dve", lambda: nc.vector.tensor_reduce(out=sm[:, 4, :], in_=v3(osb.t[:]), axis=AX.X, op=ALU.add), r=[osb], w=[sm])
            yield
            p.op("pool", lambda: nc.gpsimd.tensor_tensor(out=obuf[:], in0=osb[:], in1=osb[:], op=ALU.mult), r=[osb], w=[obuf])
            yield
            p.op("dve", lambda: nc.vector.tensor_reduce(out=sm[:, 5, :], in_=v3(obuf.t[:]), axis=AX.X, op=ALU.add), r=[obuf], w=[sm])
            p.op("dve", lambda: nc.vector.tensor_scalar(out=sm[:, 4, :], in0=sm[:, 4, :], scalar1=1.0 / 64, scalar2=None, op0=ALU.mult), r=[sm], w=[sm])
            p.op("dve", lambda: nc.vector.tensor_tensor(out=sm[:, 6, :], in0=sm[:, 4, :], in1=sm[:, 4, :], op=ALU.mult), r=[sm], w=[sm])
            p.op("dve", lambda: nc.vector.scalar_tensor_tensor(out=sm[:, 5, :], in0=sm[:, 5, :], scalar=1.0 / 64, in1=sm[:, 6, :], op0=ALU.mult, op1=ALU.subtract),
                 r=[sm], w=[sm])
            yield
            p.op("dve", lambda: nc.vector.tensor_scalar(out=sm[:, 5, :], in0=sm[:, 5, :], scalar1=GN_EPS, scalar2=None, op0=ALU.add), r=[sm], w=[sm])
            p.op("pool", lambda: nc.gpsimd.tensor_tensor(out=sm[:, 7, :], in0=sm[:, 5, :], in1=mhalf16[:], op=ALU.pow), r=[sm, mhalf16], w=[sm])
            yield
            p.op("dve", lambda: nc.vector.tensor_tensor(out=v3(osb.t[:]), in0=v3(osb.t[:]), in1=bc(sm[:, 4, :]), op=ALU.subtract), r=[osb, sm], w=[osb])
            yield
            p.op("dve", lambda: nc.vector.tensor_tensor(out=v3(osb.t[:]), in0=v3(osb.t[:]), in1=bc(sm[:, 7, :]), op=ALU.mult), r=[osb, sm], w=[osb])
            yield
            p.op("pool", lambda: nc.gpsimd.tensor_tensor(out=osb[:], in0=osb[:], in1=prm["rwkv_gn_w"][:], op=ALU.mult), r=[osb, prm["rwkv_gn_w"]], w=[osb])
            yield
            p.op("pool", lambda: nc.gpsimd.tensor_tensor(out=osb[:], in0=osb[:], in1=prm["rwkv_gn_b"][:], op=ALU.add), r=[osb, prm["rwkv_gn_b"]], w=[osb])
            yield
            p.op("dve", lambda: nc.vector.tensor_tensor(out=v3(obuf.t[:]), in0=v3(B["vb"].t[:]), in1=bc(B["sm"][:, 3, :]), op=ALU.mult), r=[B["vb"], B["sm"]], w=[obuf])
            yield
            p.op("pool", lambda: nc.gpsimd.tensor_tensor(out=obuf[:], in0=obuf[:], in1=osb[:], op=ALU.add), r=[obuf, osb], w=[obuf])
            yield
            p.op("pool", lambda: nc.gpsimd.tensor_tensor(out=obuf[:], in0=obuf[:], in1=B["gs"][:], op=ALU.mult), r=[obuf, B["gs"]], w=[obuf])
            yield
            for _ in range(4):
                yield
            for half in range(2):
                pst = PS[6 + half]
                for k4 in range(4):
                    kc = half * 4 + k4
                    p.op("pe", lambda: nc.tensor.transpose(pst[:, k4 * 128:(k4 + 1) * 128], obuf[:, kc * 128:(kc + 1) * 128], ident[:]), r=[obuf, ident], w=[pst])
                src_ = pst.t[:].rearrange("p (k t) -> p k t", k=4)
                if half == 0:
                    p.op("act", lambda: nc.scalar.copy(out=oTb[:, 0:4, :], in_=src_), r=[pst], w=[oTb])
                else:
                    p.op("dve", lambda: nc.vector.tensor_copy(out=oTb[:, 4:8, :], in_=src_), r=[pst], w=[oTb])
                yield
            p.dma(DR["OT_rwkv"].rearrange("(kc p) t -> p kc t", p=128)[:, :, r0:r0 + 128], oTb[:], r=[oTb], w=[DR["OT_t"][1]])
            yield

        tiles = [(s, n) for s in (range(NS) if seqs is None else seqs) for n in range(ntl)]

        def run_all(gen):
            for _ in gen:
                pass

        def chain(*gens):
            for g in gens:
                if g is not None:
                    for _ in g:
                        yield

        if not pipelined:
            for idx, (s, n) in enumerate(tiles):
                B = SETS[idx % 2]
                run_all(prep_load(s, n))
                run_all(prep(s, n, B))
                main(s, n, B, lambda: None)
                run_all(outp(s, n, B))
        else:
            run_all(prep_load(tiles[0][0], tiles[0][1]))
            run_all(prep(tiles[0][0], tiles[0][1], SETS[0]))
            for idx, (s, n) in enumerate(tiles):
                B = SETS[idx % 2]
                g_out = outp(tiles[idx - 1][0], tiles[idx - 1][1], SETS[(idx - 1) % 2]) if idx > 0 else None
                g_load = prep_load(tiles[idx + 1][0], tiles[idx + 1][1]) if idx + 1 < len(tiles) else None
                g_prep = prep(tiles[idx + 1][0], tiles[idx + 1][1], SETS[(idx + 1) % 2]) if idx + 1 < len(tiles) else None
                bg = chain(g_load, g_out, g_prep)

                def inj(k=4):
                    for _ in range(k):
                        try:
                            next(bg)
                        except StopIteration:
                            return
                main(s, n, B, inj)
                run_all(bg)
            run_all(outp(tiles[-1][0], tiles[-1][1], SETS[(len(tiles) - 1) % 2]))
        p.barrier()


def host_consts():
    c = {}
    i = np.arange(128)
    c["ident"] = np.eye(128, dtype=np.float32)
    c["tri_le"] = (i[:, None] <= i[None, :]).astype(np.float32)
    c["tri_lt"] = (i[:, None] < i[None, :]).astype(np.float32)
    c["tri_gt"] = (i[:, None] > i[None, :]).astype(np.float32)
    same = (i[:, None] // 32) == (i[None, :] // 32)
    c["b_le"] = (same & (i[:, None] <= i[None, :])).astype(np.float32)
    c["b_gt"] = (same & (i[:, None] > i[None, :])).astype(np.float32)
    bind = np.zeros((128, 128), np.float32)
    bind[i, i // 32] = 1.0
    c["bind"] = bind
    c["ones"] = np.ones((128, 128), np.float32)
    t = (np.arange(NT)[None, :] * 128 + np.arange(128)[:, None]).astype(np.float32) - PAD
    inv = (1.0 / (10000.0 ** (np.arange(0, 64, 2, dtype=np.float32) / 64))).astype(np.float32)
    ang = (t[:, :, None] * inv[None, None, :]).astype(np.float32)
    cs, sn = np.cos(ang).astype(np.float32), np.sin(ang).astype(np.float32)
    cos2 = np.zeros((128, NT, 2, 2, 32), np.float32)
    sin2 = np.zeros((128, NT, 2, 2, 32), np.float32)
    cos2[:] = cs[:, :, None, None, :]
    sin2[:, :, :, 0, :] = -sn[:, :, None, :]
    sin2[:, :, :, 1, :] = sn[:, :, None, :]
    c["cos2"] = cos2.reshape(128, NT, 128)
    c["sin2"] = sin2.reshape(128, NT, 128)
    return c


PARAM_SHAPES = {
    "pre_norm_w": [2, D], "post_norm_w": [2, D], "w_in": [2, D, IN_W],
    "lambda_q1": [2, 64], "lambda_k1": [2, 64], "lambda_q2": [2, 64], "lambda_k2": [2, 64], "att_norm_w": [2, 128],
    "rwkv_mu": [2, RW_W], "rwkv_w0": [2, 1024], "rwkv_w_up": [2, 64, 1024], "rwkv_a0": [2, 1024], "rwkv_a_up": [2, 64, 1024],
    "rwkv_k_k": [2, 1024], "rwkv_k_a": [2, 1024], "rwkv_r_k": [2, 16, 64], "rwkv_gn_w": [2, 1024], "rwkv_gn_b": [2, 1024],
    "hgrn_lower_bounds": [2, 1024], "hgrn_norm_w": [2, 128],
    "w_att_out": [2, D, D], "w_rwkv_out": [2, D, D], "w_hgrn_out": [2, D, D], "w_o": [2, D, D],
}
CONST_NAMES = ("ident", "tri_le", "tri_lt", "tri_gt", "b_le", "b_gt", "bind", "ones")


def build_program(nlayers=DEPTH):
    nc = bass.Bass("TRN2", target_bir_lowering=False)
    DR = {}
    H0 = nc.dram_tensor("h0", [ROWS, D], F32, kind="ExternalInput").ap()
    for n, sh in PARAM_SHAPES.items():
        DR[n] = nc.dram_tensor(n, sh, F32, kind="ExternalInput").ap()
    cd = {n: nc.dram_tensor(n, [128, 128], F32, kind="ExternalInput").ap() for n in CONST_NAMES}
    DR["cos2"] = nc.dram_tensor("cos2", [128, NT, 128], F32, kind="ExternalInput").ap()
    DR["sin2"] = nc.dram_tensor("sin2", [128, NT, 128], F32, kind="ExternalInput").ap()
    DR["Z"] = [nc.dram_tensor("Zscr%d" % i, [P, IN_W], F32, kind="Internal").ap() for i in range(NS)]
    for n in ("OT_att", "OT_rwkv", "OT_hgrn"):
        DR[n] = nc.dram_tensor(n + "_scr", [D, ROWS], BF16, kind="Internal").ap()
    Hs = nc.dram_tensor("Hscr", [ROWS, D], F32, kind="Internal").ap()
    OUT = nc.dram_tensor("out", [NS, 2048, D], F32, kind="ExternalOutput").ap()
    DR["OT_t"] = [Tl(None, "ot%d" % i, multi=True) for i in range(3)]
    DR["Zt"] = Tl(None, "Z", multi=True)
    DR["Outt"] = Tl(None, "out", multi=True)
    H_t = [Tl(None, "H0", multi=True), Tl(None, "Hs", multi=True)]
    with ExitStack() as es:
        p = Prog(nc, es)
        G = {}
        for i, n in enumerate(CONST_NAMES):
            G[n] = p.tile(n, [128, 128], F32)
            p.dma(G[n][:], cd[n], w=[G[n]], q=("sp" if i % 2 == 0 else "act"))
        G["PS"] = [p.tile("ps%d" % i, [128, 512], F32, psum=True) for i in range(8)]
        PS = G["PS"]
        for l in range(nlayers):
            Hin = H0 if l == 0 else Hs
            Hin_t = H_t[0] if l == 0 else H_t[1]
            last = (l == nlayers - 1)
            with ExitStack() as es2:
                T = lambda name, shape, dt, **kw: p.tile("pj_" + name, shape, dt, es=es2, **kw)
                C = {"ident": G["ident"]}
                C["uT"] = T("uT", [128, 8, ROWS], BF16, multi=True)
                C["ht"] = [T("ht%d" % i, [128, D], F32) for i in range(2)]
                C["ut"] = [T("ut%d" % i, [128, D], F32) for i in range(2)]
                C["sq"] = T("sq", [128, D], F32)
                C["ss"] = [T("ss%d" % i, [128, 8], F32) for i in range(2)]
                C["pst"] = PS[0:2]
                C["psz"] = PS[2:6]
                C["wst"] = [T("wst%d" % i, [128, 8, 512], F32) for i in range(2)]
                C["wbf"] = [T("wbf%d" % i, [128, 8, 512], BF16) for i in range(2)]
                C["zo"] = [T("zo%d" % i, [128, 512], F32) for i in range(4)]
                C["Zt"] = DR["Zt"]
                C["Ht"] = Hin_t
                C["mhalf"] = T("mhalf", [128, 1], F32)
                p.op("pool", lambda: nc.gpsimd.memset(C["mhalf"][:], -0.5), w=[C["mhalf"]])
                prew_bc = T("prew_bc", [128, D], F32)
                load_bc(p, nc, prew_bc, DR["pre_norm_w"][l:l + 1, :])
                phase_proj(p, nc, Hin, DR["Z"], DR["w_in"][l], prew_bc, C)
                p.barrier()
            phase_att(p, nc, l, DR, G)
            phase_hgrn(p, nc, l, DR, G)
            phase_rwkv(p, nc, l, DR, G)
            DR["Ht"] = Hin_t
            DR["Ht2"] = H_t[1]
            phase_merge(p, nc, l, DR, G, Hin, Hs, out_final=(OUT if last else None))
        p.barrier()
        print("n_inst", p.n_inst)
    return nc


_NC_CACHE = {}


def kernel(**inputs):
    x = np.asarray(inputs["x"], dtype=np.float32)
    meta = np.asarray(inputs["meta_tokens"], dtype=np.float32)
    B = x.shape[0]
    ncores = B // NS
    if "nc" not in _NC_CACHE:
        _NC_CACHE["nc"] = build_program()
    nc = _NC_CACHE["nc"]
    consts = host_consts()
    shared = {n: np.ascontiguousarray(np.asarray(inputs[n], dtype=np.float32)) for n in PARAM_SHAPES}
    shared.update(consts)
    in_maps = []
    for c in range(ncores):
        h0 = np.zeros((NS, P, D), np.float32)
        h0[:, PAD:PAD + 16] = meta[None]
        h0[:, PAD + 16:] = x[c * NS:(c + 1) * NS]
        m = dict(shared)
        m["h0"] = h0.reshape(ROWS, D)
        in_maps.append(m)
    res = run_bass_kernel_spmd(nc, in_maps, core_ids=list(range(ncores)))
    out = np.concatenate([np.asarray(r["out"], dtype=np.float32) for r in res.results], axis=0)
    return out
```

```python
import numpy as np
import ml_dtypes
from contextlib import ExitStack
import concourse.bass as bass
import concourse.mybir as mybir
from concourse.bass_utils import run_bass_kernel_spmd

F32 = mybir.dt.float32
BF16 = mybir.dt.bfloat16
AF = mybir.ActivationFunctionType
ALU = mybir.AluOpType
AX = mybir.AxisListType

D = 1024
NS = 2
NT = 17
P = NT * 128
PAD = 112
ROWS = NS * P
IN_W = 15488
RW_OFF = 4096
RW_W = 4224
HG_OFF = RW_OFF + RW_W
MG_OFF = HG_OFF + 4096
DEPTH = 2
EPS = 1e-6
GN_EPS = 64e-5


def zr(Z, a, b):
    if isinstance(Z, list):
        si = a // P
        assert (b - 1) // P == si
        return Z[si][a - si * P:b - si * P]
    return Z[a:b]


class Tl:
    __slots__ = ("t", "w", "r", "name", "multi", "wd", "excl")

    def __init__(self, t, name="", multi=False, excl=False):
        self.excl = excl
        self.t = t
        self.w = None
        self.r = {}
        self.wd = {}
        self.multi = multi
        self.name = name

    def __getitem__(self, idx):
        return self.t[idx]


class Prog:
    def __init__(self, nc, es, same_engine_sync=True):
        self.nc = nc
        self.es = es
        self.E = {"pe": nc.tensor, "dve": nc.vector, "act": nc.scalar, "pool": nc.gpsimd, "sp": nc.sync}
        self.sem = {e: es.enter_context(nc.semaphore("sem_" + e)) for e in ("pe", "dve", "act", "pool")}
        self.cnt = {e: 0 for e in self.sem}
        self.epoch = {e: 0 for e in self.sem}
        self.old = []
        self.waited = {e: {} for e in self.E}
        self.same = same_engine_sync
        self.dq = {}
        for q, n in (("sp", 40),):
            self.dq[q] = {"sems": [es.enter_context(nc.semaphore("dsem_%s%d" % (q, i))) for i in range(n)],
                          "val": [0] * n, "next": 0}
        self.n_inst = 0
        self.mute = False

    def tile(self, name, shape, dt, psum=False, multi=False, es=None):
        es = es or self.es
        self.n_tiles = getattr(self, "n_tiles", 0) + 1
        name = "%s_%d" % (name, self.n_tiles)
        if psum:
            t = es.enter_context(self.nc.psum_tensor("pt_" + name, shape, dt))
        else:
            t = es.enter_context(self.nc.sbuf_tensor("sb_" + name, shape, dt))
        return Tl(t, name, multi, excl=psum)

    def _wait(self, e, tok):
        if tok is None:
            return
        sem, val, key, owner = tok
        if owner == e and (e == "pe" or not self.same):
            return
        if self.waited[e].get(key, 0) >= val:
            return
        self.E[e].wait_ge(sem, val)
        self.waited[e][key] = val

    def _deps(self, e, r, w):
        for t in r:
            if t.multi:
                for tok in t.wd.values():
                    self._wait(e, tok)
            else:
                self._wait(e, t.w)
        for t in w:
            if not t.multi:
                self._wait(e, t.w)
            for tok in t.r.values():
                self._wait(e, tok)

    def _record(self, tok, r, w):
        for t in r:
            t.r[tok[2]] = tok
        for t in w:
            if t.multi:
                t.wd[tok[2]] = tok
            else:
                t.w = tok
                t.r = {}

    def barrier(self):
        self.mute = False
        toks = [(self.sem[e], self.cnt[e], "c_%s_%d" % (e, self.epoch[e]), e) for e in self.sem if self.cnt[e] > 0]
        toks += self.old
        for q, dq in self.dq.items():
            for i, v in enumerate(dq["val"]):
                if v > 0:
                    toks.append((dq["sems"][i], v, "d_%s%d" % (q, i), None))
        for e in self.E:
            for tok in toks:
                if tok[3] == e:
                    continue
                self._wait(e, tok)

    def op(self, e, fn, r=(), w=()):
        if self.mute:
            return None
        if any(t.excl for t in r):
            w = list(w) + [t for t in r if t.excl and t not in w]
            r = [t for t in r if not t.excl]
        self._deps(e, r, w)
        if self.cnt[e] >= 16000:
            self.old.append((self.sem[e], self.cnt[e], "c_%s_%d" % (e, self.epoch[e]), None))
            self.epoch[e] += 1
            self.sem[e] = self.es.enter_context(self.nc.semaphore("sem_%s_%d" % (e, self.epoch[e])))
            self.cnt[e] = 0
        ins = fn()
        self.cnt[e] += 1
        ins.then_inc(self.sem[e], 1)
        tok = (self.sem[e], self.cnt[e], "c_%s_%d" % (e, self.epoch[e]), e)
        self._record(tok, r, w)
        self.n_inst += 1
        return tok

    def dma(self, out, in_, r=(), w=(), q="sp", **kw):
        if self.mute:
            return None
        q = "sp"
        dq = self.dq[q]
        i = dq["next"]
        dq["next"] = (i + 1) % len(dq["sems"])
        key = "d_%s%d" % (q, i)
        if dq["val"][i] > 0:
            self._wait(q, (dq["sems"][i], dq["val"][i], key, None))
        self._deps(q, r, w)
        dq["val"][i] += 16
        self.E[q].dma_start(out=out, in_=in_, **kw).then_inc(dq["sems"][i], 16)
        tok = (dq["sems"][i], dq["val"][i], key, None)
        self._record(tok, r, w)
        self.n_inst += 1
        return tok

    def finish(self, toks):
        for tok in toks:
            self._wait("sp", tok)


def phase_proj(p, nc, H, Z, w_in_l, prew_bc, C, ntiles=NS * NT, ncolblk=None):
    uT = C["uT"]
    ident = C["ident"]
    for tt in range(ntiles):
        ht = C["ht"][tt % 2]
        p.dma(ht[:], H[tt * 128:(tt + 1) * 128, :], r=([C["Ht"]] if "Ht" in C else []), w=[ht])
        sq = C["sq"]
        ss = C["ss"][tt % 2]
        p.op("act", lambda: nc.scalar.activation(out=sq[:], in_=ht[:], func=AF.Square, accum_out=ss[:, 0:1]),
             r=[ht], w=[sq, ss])
        p.op("dve", lambda: nc.vector.tensor_scalar(out=ss[:, 1:2], in0=ss[:, 0:1], scalar1=1.0 / D, scalar2=EPS,
                                                    op0=ALU.mult, op1=ALU.add), r=[ss], w=[ss])
        p.op("pool", lambda: nc.gpsimd.tensor_tensor(out=ss[:, 3:4], in0=ss[:, 1:2], in1=C["mhalf"][:], op=ALU.pow), r=[ss, C["mhalf"]], w=[ss])
        ut = C["ut"][tt % 2]
        p.op("dve", lambda: nc.vector.scalar_tensor_tensor(out=ut[:], in0=ht[:], scalar=ss[:, 3:4], in1=prew_bc[:],
                                                           op0=ALU.mult, op1=ALU.mult), r=[ht, ss, prew_bc], w=[ut])
        for half in range(2):
            pst = C["pst"][half]
            for k4 in range(4):
                kc = half * 4 + k4
                p.op("pe", lambda: nc.tensor.transpose(pst[:, k4 * 128:(k4 + 1) * 128], ut[:, kc * 128:(kc + 1) * 128],
                                                       ident[:]), r=[ut, ident], w=[pst])
            dst = uT.t[:, half * 4:(half + 1) * 4, tt * 128:(tt + 1) * 128]
            src = pst.t[:].rearrange("p (k t) -> p k t", k=4)
            if half == 0:
                p.op("act", lambda: nc.scalar.copy(out=dst, in_=src), r=[pst], w=[uT])
            else:
                p.op("dve", lambda: nc.vector.tensor_copy(out=dst, in_=src), r=[pst], w=[uT])
    wv = w_in_l.rearrange("(kc p) n -> p kc n", p=128)
    ncb = (IN_W + 511) // 512 if ncolblk is None else ncolblk
    ev = 0
    def load_w(cb):
        c0 = cb * 512
        cw = min(512, IN_W - c0)
        wst = C["wst"][cb % 2]
        wbf = C["wbf"][cb % 2]
        p.dma(wst[:, :, 0:cw], wv[:, :, c0:c0 + cw], w=[wst], q="sp")
        p.op("pool", lambda: nc.gpsimd.tensor_copy(out=wbf[:, 0:4, 0:cw], in_=wst[:, 0:4, 0:cw]), r=[wst], w=[wbf])
        p.op("pool", lambda: nc.gpsimd.tensor_copy(out=wbf[:, 4:8, 0:cw], in_=wst[:, 4:8, 0:cw]), r=[wst], w=[wbf])

    load_w(0)
    for cb in range(ncb):
        c0 = cb * 512
        cw = min(512, IN_W - c0)
        wbf = C["wbf"][cb % 2]
        if cb + 1 < ncb:
            load_w(cb + 1)
        for tt in range(ntiles):
            psz = C["psz"][ev % 4]
            zo = C["zo"][ev % 4]
            for kc in range(8):
                p.op("pe", lambda: nc.tensor.matmul(psz[:, 0:cw], lhsT=uT[:, kc, tt * 128:(tt + 1) * 128],
                                                    rhs=wbf[:, kc, 0:cw], start=(kc == 0), stop=(kc == 7)),
                     r=[uT, wbf], w=[psz])
            if ev % 2 == 0:
                p.op("act", lambda: nc.scalar.copy(out=zo[:, 0:cw], in_=psz[:, 0:cw]), r=[psz], w=[zo])
            else:
                p.op("dve", lambda: nc.vector.tensor_copy(out=zo[:, 0:cw], in_=psz[:, 0:cw]), r=[psz], w=[zo])
            p.dma(zr(Z, tt * 128, (tt + 1) * 128)[:, c0:c0 + cw], zo[:, 0:cw], r=[zo], w=[C["Zt"]], q="sp")
            ev += 1


def load_bc(p, nc, tl, row_ap, q="sp"):
    p.dma(tl[:], row_ap.partition_broadcast(128), w=[tl], q=q)


def phase_merge(p, nc, l, DR, G, Hin, Hout, out_final=None, tiles=None):
    PS = G["PS"]
    ident = G["ident"]
    Z = DR["Z"]
    with ExitStack() as es:
        T = lambda name, shape, dt, **kw: p.tile("mg_" + name, shape, dt, es=es, **kw)
        wst = T("wst", [128, 8, 1024], F32)
        W = [T("w%d" % i, [128, 8, 1024], BF16) for i in range(4)]
        names = ["w_att_out", "w_rwkv_out", "w_hgrn_out", "w_o"]
        for i in range(4):
            p.dma(wst[:], DR[names[i]][l].rearrange("(kc p) n -> p kc n", p=128), w=[wst])
            p.op("pool", lambda: nc.gpsimd.tensor_copy(out=W[i][:, 0:4, :], in_=wst[:, 0:4, :]), r=[wst], w=[W[i]])
            p.op("act", lambda: nc.scalar.copy(out=W[i][:, 4:8, :], in_=wst[:, 4:8, :]), r=[wst], w=[W[i]])
        mhalf = T("mhalf", [128, 1], F32)
        p.op("pool", lambda: nc.gpsimd.memset(mhalf[:], -0.5), w=[mhalf])
        postw = T("postw", [128, D], F32)
        load_bc(p, nc, postw, DR["post_norm_w"][l:l + 1, :])
        oT = [[T("oT%d_%d" % (b, i), [128, 8, 128], BF16) for i in range(3)] for b in range(3)]
        mg = [T("mgt%d" % i, [128, 3072], F32) for i in range(3)]
        hin = [T("hin%d" % i, [128, D], F32) for i in range(3)]
        ys = [T("y%d" % i, [128, D], F32) for i in range(2)]
        tmp = [T("tmp%d" % i, [128, 512], F32) for i in range(2)]
        yT = T("yT", [128, 8, 128], BF16)
        hn = [T("hn%d" % i, [128, D], F32) for i in range(2)]
        sq = T("sq", [128, 512], F32)
        st = [T("st%d" % i, [128, 8], F32) for i in range(2)]
        OTs = [DR["OT_att"], DR["OT_rwkv"], DR["OT_hgrn"]]
        tl_list = list(range(NS * NT)) if tiles is None else tiles
        cnt = 0
        def stage0(it, tt):
            r0 = tt * 128
            for b in range(3):
                p.dma(oT[b][it % 3][:], OTs[b].rearrange("(kc p) t -> p kc t", p=128)[:, :, r0:r0 + 128],
                      r=[DR["OT_t"][b]], w=[oT[b][it % 3]], q="act")
            m = mg[it % 3]
            p.dma(m[:], zr(Z, r0, r0 + 128)[:, MG_OFF:MG_OFF + 3072], r=[DR["Zt"]], w=[m])
            hi_ = hin[it % 3]
            p.dma(hi_[:], Hin[r0:r0 + 128, :], r=[DR["Ht"]], w=[hi_])

        def stage1(it, tt):
            nonlocal cnt
            y = ys[it % 2]
            r0 = tt * 128
            m = mg[it % 3]
            hi_ = hin[it % 3]
            p.op("act", lambda: nc.scalar.activation(out=m[:], in_=m[:], func=AF.Sigmoid), r=[m], w=[m])
            for b in range(3):
                ob = oT[b][it % 3]
                for half in range(2):
                    ps = PS[cnt % 2]
                    cnt += 1
                    for kc in range(8):
                        p.op("pe", lambda: nc.tensor.matmul(ps[:], lhsT=ob[:, kc, :], rhs=W[b][:, kc, half * 512:(half + 1) * 512],
                                                            start=(kc == 0), stop=(kc == 7)), r=[ob, W[b]], w=[ps])
                    gsl = m[:, b * 1024 + half * 512: b * 1024 + (half + 1) * 512]
                    ysl = y[:, half * 512:(half + 1) * 512]
                    if b == 0:
                        p.op("dve", lambda: nc.vector.tensor_tensor(out=ysl, in0=ps[:], in1=gsl, op=ALU.mult), r=[ps, m], w=[y])
                    else:
                        t_ = tmp[cnt % 2]
                        p.op("dve", lambda: nc.vector.tensor_tensor(out=t_[:], in0=ps[:], in1=gsl, op=ALU.mult), r=[ps, m], w=[t_])
                        p.op("pool", lambda: nc.gpsimd.tensor_tensor(out=ysl, in0=ysl, in1=t_[:], op=ALU.add), r=[t_, y], w=[y])

        def stage2(it, tt):
            y = ys[it % 2]
            r0 = tt * 128
            hi_ = hin[it % 3]
            for half in range(2):
                pst = PS[2 + half]
                for k4 in range(4):
                    kc = half * 4 + k4
                    p.op("pe", lambda: nc.tensor.transpose(pst[:, k4 * 128:(k4 + 1) * 128], y[:, kc * 128:(kc + 1) * 128], ident[:]),
                         r=[y, ident], w=[pst])
                src = pst.t[:].rearrange("p (k t) -> p k t", k=4)
                if half == 0:
                    p.op("act", lambda: nc.scalar.copy(out=yT[:, 0:4, :], in_=src), r=[pst], w=[yT])
                else:
                    p.op("dve", lambda: nc.vector.tensor_copy(out=yT[:, 4:8, :], in_=src), r=[pst], w=[yT])
            s_ = st[it % 2]
            h_ = hn[it % 2]
            pso = [PS[4 + (it % 2) * 2], PS[5 + (it % 2) * 2]]
            for half in range(2):
                for kc in range(8):
                    p.op("pe", lambda: nc.tensor.matmul(pso[half][:], lhsT=yT[:, kc, :], rhs=W[3][:, kc, half * 512:(half + 1) * 512],
                                                        start=(kc == 0), stop=(kc == 7)), r=[yT, W[3]], w=[pso[half]])
                p.op("act", lambda: nc.scalar.activation(out=sq[:], in_=pso[half][:], func=AF.Square, accum_out=s_[:, half:half + 1]),
                     r=[pso[half]], w=[sq, s_])
            p.op("dve", lambda: nc.vector.tensor_tensor(out=s_[:, 2:3], in0=s_[:, 0:1], in1=s_[:, 1:2], op=ALU.add), r=[s_], w=[s_])
            p.op("dve", lambda: nc.vector.tensor_scalar(out=s_[:, 3:4], in0=s_[:, 2:3], scalar1=1.0 / D, scalar2=EPS,
                                                        op0=ALU.mult, op1=ALU.add), r=[s_], w=[s_])
            p.op("pool", lambda: nc.gpsimd.tensor_tensor(out=s_[:, 5:6], in0=s_[:, 3:4], in1=mhalf[:], op=ALU.pow), r=[s_, mhalf], w=[s_])
            for half in range(2):
                hs = h_[:, half * 512:(half + 1) * 512]
                p.op("dve", lambda: nc.vector.scalar_tensor_tensor(out=hs, in0=pso[half][:], scalar=s_[:, 5:6],
                                                                   in1=postw[:, half * 512:(half + 1) * 512],
                                                                   op0=ALU.mult, op1=ALU.mult), r=[pso[half], s_, postw], w=[h_])
            p.op("pool", lambda: nc.gpsimd.tensor_tensor(out=h_[:], in0=h_[:], in1=hi_[:], op=ALU.add), r=[h_, hi_], w=[h_])
            n = tt % NT
            if n == 0:
                p.op("pool", lambda: nc.gpsimd.memset(h_[0:96, :], 0.0), r=[], w=[h_])
                p.op("pool", lambda: nc.gpsimd.memset(h_[96:112, :], 0.0), r=[], w=[h_])
            if out_final is None:
                p.dma(Hout[r0:r0 + 128, :], h_[:], r=[h_], w=[DR["Ht2"]])
            else:
                s = tt // NT
                if n == 0:
                    pass
                else:
                    p.dma(out_final[s, (n - 1) * 128:n * 128, :], h_[:], r=[h_], w=[DR["Outt"]])
        for it, tt in enumerate(tl_list):
            if it == 0:
                stage0(0, tt)
                if len(tl_list) > 1:
                    stage0(1, tl_list[1])
                stage1(0, tt)
            if it + 2 < len(tl_list):
                stage0(it + 2, tl_list[it + 2])
            if it + 1 < len(tl_list):
                stage1(it + 1, tl_list[it + 1])
            stage2(it, tt)
        p.barrier()


def phase_att(p, nc, l, DR, G, pairs=None, qts=None):
    import math as _m
    PS = G["PS"]
    ident = G["ident"]
    tri = G["tri_le"]
    Z = DR["Z"]
    lam_init = 0.8 - 0.6 * _m.exp(-0.3 * l)
    with ExitStack() as es:
        T = lambda name, shape, dt, **kw: p.tile("at_" + name, shape, dt, es=es, **kw)
        cos2 = T("cos2", [128, NT, 128], F32)
        sin2 = T("sin2", [128, NT, 128], F32)
        p.dma(cos2[:], DR["cos2"], w=[cos2])
        p.dma(sin2[:], DR["sin2"], w=[sin2])
        lv = T("lv", [128, 4, 64], F32)
        for i, n in enumerate(("lambda_q1", "lambda_k1", "lambda_q2", "lambda_k2")):
            p.dma(lv[:, i, :], DR[n][l:l + 1, :].partition_broadcast(128), w=[lv])
        lt = T("lt", [128, 2, 64], F32)
        ls = T("ls", [128, 8], F32)
        p.op("dve", lambda: nc.vector.tensor_tensor(out=lt[:, 0, :], in0=lv[:, 0, :], in1=lv[:, 1, :], op=ALU.mult), r=[lv], w=[lt])
        p.op("dve", lambda: nc.vector.tensor_tensor(out=lt[:, 1, :], in0=lv[:, 2, :], in1=lv[:, 3, :], op=ALU.mult), r=[lv], w=[lt])
        p.op("dve", lambda: nc.vector.tensor_reduce(out=ls[:, 0:2], in_=lt[:], axis=AX.X, op=ALU.add), r=[lt], w=[ls])
        p.op("act", lambda: nc.scalar.activation(out=ls[:, 2:4], in_=ls[:, 0:2], func=AF.Exp), r=[ls], w=[ls])
        p.op("dve", lambda: nc.vector.tensor_tensor(out=ls[:, 4:5], in0=ls[:, 3:4], in1=ls[:, 2:3], op=ALU.subtract), r=[ls], w=[ls])
        p.op("dve", lambda: nc.vector.tensor_scalar(out=ls[:, 5:6], in0=ls[:, 4:5], scalar1=-lam_init, scalar2=None, op0=ALU.add), r=[ls], w=[ls])
        neg_lam = ls[:, 5:6]
        normw = T("normw", [128, 128], F32)
        load_bc(p, nc, normw, DR["att_norm_w"][l:l + 1, :])
        p.op("dve", lambda: nc.vector.tensor_scalar(out=normw[:], in0=normw[:], scalar1=(1.0 - lam_init), scalar2=None, op0=ALU.mult),
             r=[normw], w=[normw])
        SETS = []
        for kb in range(2):
            Bf = {}
            for nm in ("qraw", "kraw", "vraw", "graw"):
                Bf[nm] = T("%s%d" % (nm, kb), [128, NT, 128], F32)
            Bf["qT"] = T("qT%d" % kb, [128, P], BF16)
            Bf["kT"] = T("kT%d" % kb, [128, P], BF16)
            Bf["v1"] = T("v1%d" % kb, [128, NT, 130], BF16)
            v1_ = Bf["v1"]
            p.op("pool", lambda: nc.gpsimd.memset(v1_[:, :, 128:130], 1.0), w=[v1_])
            p.op("pool", lambda: nc.gpsimd.memset(v1_[0:96, 0, 128:130], 0.0), w=[v1_])
            p.op("pool", lambda: nc.gpsimd.memset(v1_[96:112, 0, 128:130], 0.0), w=[v1_])
            SETS.append(Bf)
        t1 = T("t1", [128, NT, 128], F32)
        t2 = T("t2", [128, NT, 128], F32)
        obufs = [T("obuf%d" % i, [128, NT, 128], F32) for i in range(2)]
        oTb = T("oTb", [128, P], BF16)
        pT = [T("pT%d" % i, [128, 512], BF16) for i in range(5)]
        SB = [PS[0], PS[1], PS[6]]
        mhalf = T("mhalf", [128, 1], F32)
        p.op("pool", lambda: nc.gpsimd.memset(mhalf[:], -0.5), w=[mhalf])
        o_ = [T("o%d" % i, [128, 128], F32) for i in range(2)]
        sq = T("sq", [128, 128], F32)
        rs = [T("rs%d" % i, [128, 12], F32) for i in range(2)]
        pr = [(s, j) for s in range(NS) for j in range(8)] if pairs is None else pairs

        def setup_load(s, j, Bf):
            qraw, kraw, vraw, graw = (Bf[k] for k in ("qraw", "kraw", "vraw", "graw"))
            zs = zr(Z, s * P, (s + 1) * P).rearrange("(n p) c -> p n c", p=128)
            p.dma(qraw[:], zs[:, :, j * 128:(j + 1) * 128], r=[DR["Zt"]], w=[qraw])
            p.dma(kraw[:], zs[:, :, 1024 + j * 128:1024 + (j + 1) * 128], r=[DR["Zt"]], w=[kraw], q="act")
            p.dma(vraw[:], zs[:, :, 2048 + j * 128:2048 + (j + 1) * 128], r=[DR["Zt"]], w=[vraw])
            p.dma(graw[:], zs[:, :, 3072 + j * 128:3072 + (j + 1) * 128], r=[DR["Zt"]], w=[graw], q="act")

        def setup(s, j, Bf):
            qraw, kraw, vraw, graw, qT, kT, v1 = (Bf[k] for k in ("qraw", "kraw", "vraw", "graw", "qT", "kT", "v1"))
            for (raw, dstT) in ((qraw, qT), (kraw, kT)):
                E = nc.vector
                p.op("dve", lambda: E.tensor_tensor(out=t1[:], in0=raw[:], in1=cos2[:], op=ALU.mult), r=[raw, cos2], w=[t1])
                yield
                rv = raw.t[:].rearrange("p n (g h d) -> p (n g) h d", g=2, h=2)
                sv = sin2.t[:].rearrange("p n (g h d) -> p (n g) h d", g=2, h=2)
                tv = t2.t[:].rearrange("p n (g h d) -> p (n g) h d", g=2, h=2)
                p.op("dve", lambda: E.tensor_tensor(out=tv[:, :, 0, :], in0=rv[:, :, 1, :], in1=sv[:, :, 0, :], op=ALU.mult), r=[raw, sin2], w=[t2])
                yield
                p.op("dve", lambda: E.tensor_tensor(out=tv[:, :, 1, :], in0=rv[:, :, 0, :], in1=sv[:, :, 1, :], op=ALU.mult), r=[raw, sin2], w=[t2])
                yield
                p.op("dve", lambda: E.tensor_tensor(out=t1[:], in0=t1[:], in1=t2[:], op=ALU.add), r=[t1, t2], w=[t1])
                yield
                for n0 in range(0, NT, 4):
                    nn = min(4, NT - n0)
                    pst = PS[7]
                    for i in range(nn):
                        p.op("pe", lambda: nc.tensor.transpose(pst[:, i * 128:(i + 1) * 128], t1[:, n0 + i, :], ident[:]), r=[t1, ident], w=[pst])
                    p.op("dve", lambda: nc.vector.tensor_copy(out=dstT[:, n0 * 128:(n0 + nn) * 128], in_=pst[:, 0:nn * 128]), r=[pst], w=[dstT])
                    yield
            p.op("dve", lambda: nc.vector.tensor_copy(out=v1[:, :, 0:128], in_=vraw[:]), r=[vraw], w=[v1])
            yield
            p.op("act", lambda: nc.scalar.activation(out=graw[:], in_=graw[:], func=AF.Silu), r=[graw], w=[graw])
            yield

        def mainloop(s, j, Bf, inj, obuf):
            qT, kT, v1, graw = Bf["qT"], Bf["kT"], Bf["v1"], Bf["graw"]
            groups = []
            for qt in (range(NT) if qts is None else qts):
                for k0 in range(0, qt + 1, 4):
                    for g in range(2):
                        groups.append((qt, g, k0, min(k0 + 4, qt + 1)))

            def emit_scores(grp, gi):
                qt, g, k0, k1 = grp
                pss = SB[gi % 3]
                for i, kt in enumerate(range(k0, k1)):
                    p.op("pe", lambda: nc.tensor.matmul(pss[:, i * 128:(i + 1) * 128], lhsT=kT[g * 64:(g + 1) * 64, kt * 128:(kt + 1) * 128],
                                                        rhs=qT[g * 64:(g + 1) * 64, qt * 128:(qt + 1) * 128], start=True, stop=True),
                         r=[kT, qT], w=[pss])
                pt = pT[gi % 5]
                n = k1 - k0
                p.op("act", lambda: nc.scalar.activation(out=pt[:, 0:n * 128], in_=pss[:, 0:n * 128], func=AF.Exp, scale=0.125), r=[pss], w=[pt])
                if k1 - 1 == qt:
                    i = qt - k0
                    p.op("pool", lambda: nc.gpsimd.tensor_tensor(out=pt[:, i * 128:(i + 1) * 128], in0=pt[:, i * 128:(i + 1) * 128], in1=tri[:], op=ALU.mult),
                         r=[pt, tri], w=[pt])

            def emit_pv(grp, gi):
                qt, g, k0, k1 = grp
                pt = pT[gi % 5]
                pso = PS[2 + (qt % 2) * 2 + g]
                for i, kt in enumerate(range(k0, k1)):
                    p.op("pe", lambda: nc.tensor.matmul(pso[:, 0:129], lhsT=pt[:, i * 128:(i + 1) * 128], rhs=v1[:, kt, 0:129],
                                                        start=(kt == 0), stop=(kt == qt)), r=[pt, v1], w=[pso])
                if k1 - 1 == qt and g == 1:
                    epilogue(qt)
                    if qt >= 3:
                        inj()

            def epilogue(qt):
                O1 = PS[2 + (qt % 2) * 2]
                O2 = PS[2 + (qt % 2) * 2 + 1]
                r_ = rs[qt % 2]
                o = o_[qt % 2]
                p.op("dve", lambda: nc.vector.tensor_scalar(out=r_[:, 0:1], in0=O1[:, 128:129], scalar1=1e-30, scalar2=None, op0=ALU.max), r=[O1], w=[r_])
                p.op("dve", lambda: nc.vector.tensor_scalar(out=r_[:, 1:2], in0=O2[:, 128:129], scalar1=1e-30, scalar2=None, op0=ALU.max), r=[O2], w=[r_])
                p.op("dve", lambda: nc.vector.reciprocal(out=r_[:, 2:4], in_=r_[:, 0:2]), r=[r_], w=[r_])
                p.op("dve", lambda: nc.vector.tensor_tensor(out=r_[:, 4:5], in0=r_[:, 3:4], in1=neg_lam, op=ALU.mult), r=[r_, ls], w=[r_])
                p.op("dve", lambda: nc.vector.tensor_scalar(out=o[:], in0=O1[:, 0:128], scalar1=r_[:, 2:3], scalar2=None, op0=ALU.mult), r=[O1, r_], w=[o])
                p.op("dve", lambda: nc.vector.scalar_tensor_tensor(out=o[:], in0=O2[:, 0:128], scalar=r_[:, 4:5], in1=o[:], op0=ALU.mult, op1=ALU.add),
                     r=[O2, r_, o], w=[o])
                p.op("act", lambda: nc.scalar.activation(out=sq[:], in_=o[:], func=AF.Square, accum_out=r_[:, 5:6]), r=[o], w=[sq, r_])
                p.op("dve", lambda: nc.vector.tensor_scalar(out=r_[:, 6:7], in0=r_[:, 5:6], scalar1=1.0 / 128, scalar2=EPS, op0=ALU.mult, op1=ALU.add), r=[r_], w=[r_])
                p.op("pool", lambda: nc.gpsimd.tensor_tensor(out=r_[:, 8:9], in0=r_[:, 6:7], in1=mhalf[:], op=ALU.pow), r=[r_, mhalf], w=[r_])
                p.op("dve", lambda: nc.vector.scalar_tensor_tensor(out=o[:], in0=o[:], scalar=r_[:, 8:9], in1=normw[:], op0=ALU.mult, op1=ALU.mult),
                     r=[o, r_, normw], w=[o])
                p.op("pool", lambda: nc.gpsimd.tensor_tensor(out=obuf[:, qt, :], in0=o[:], in1=graw[:, qt, :], op=ALU.mult), r=[o, graw], w=[obuf])

            AHEAD = 2
            for gi, grp in enumerate(groups):
                emit_scores(grp, gi)
                if gi >= AHEAD:
                    emit_pv(groups[gi - AHEAD], gi - AHEAD)
            for gi in range(max(0, len(groups) - AHEAD), len(groups)):
                emit_pv(groups[gi], gi)

        def finish(s, j, obuf):
            for _ in range(3):
                yield
            for n0 in range(0, NT, 4):
                nn = min(4, NT - n0)
                pst = PS[7]
                for i in range(nn):
                    p.op("pe", lambda: nc.tensor.transpose(pst[:, i * 128:(i + 1) * 128], obuf[:, n0 + i, :], ident[:]), r=[obuf, ident], w=[pst])
                if (n0 // 4) % 2 == 0:
                    p.op("act", lambda: nc.scalar.copy(out=oTb[:, n0 * 128:(n0 + nn) * 128], in_=pst[:, 0:nn * 128]), r=[pst], w=[oTb])
                else:
                    p.op("dve", lambda: nc.vector.tensor_copy(out=oTb[:, n0 * 128:(n0 + nn) * 128], in_=pst[:, 0:nn * 128]), r=[pst], w=[oTb])
                yield
            p.dma(DR["OT_att"][j * 128:(j + 1) * 128, s * P:(s + 1) * P], oTb[:], r=[oTb], w=[DR["OT_t"][0]])
            yield

        def run_all(gen):
            for _ in gen:
                pass

        setup_load(pr[0][0], pr[0][1], SETS[0])
        run_all(setup(pr[0][0], pr[0][1], SETS[0]))
        def chain(*gens):
            for g_ in gens:
                if g_ is not None:
                    for _ in g_:
                        yield

        for k, (s, j) in enumerate(pr):
            if k + 1 < len(pr):
                setup_load(pr[k + 1][0], pr[k + 1][1], SETS[(k + 1) % 2])
            g_fin = finish(pr[k - 1][0], pr[k - 1][1], obufs[(k - 1) % 2]) if k > 0 else None
            g_set = setup(pr[k + 1][0], pr[k + 1][1], SETS[(k + 1) % 2]) if k + 1 < len(pr) else None
            bg = chain(g_fin, g_set)

            def inj(n=3):
                for _ in range(n):
                    try:
                        next(bg)
                    except StopIteration:
                        return
            mainloop(s, j, SETS[k % 2], inj, obufs[k % 2])
            run_all(bg)
        run_all(finish(pr[-1][0], pr[-1][1], obufs[(len(pr) - 1) % 2]))
        p.barrier()


def phase_hgrn(p, nc, l, DR, G, seqs=None, zoff=HG_OFF, ntl=NT, pipelined=True):
    PS = G["PS"]
    ident = G["ident"]
    b_le, b_gt, bind = G["b_le"], G["b_gt"], G["bind"]
    Z = DR["Z"]
    with ExitStack() as es:
        T = lambda name, shape, dt, **kw: p.tile("hg_" + name, shape, dt, es=es, **kw)
        hnw = T("hnw", [128, 128], F32)
        load_bc(p, nc, hnw, DR["hgrn_norm_w"][l:l + 1, :])
        identb = T("identb", [128, 128], BF16)
        p.op("pool", lambda: nc.gpsimd.tensor_copy(out=identb[:], in_=ident[:]), r=[ident], w=[identb])
        mhalf = T("mhalf", [128, 8], F32)
        p.op("pool", lambda: nc.gpsimd.memset(mhalf[:], -0.5), w=[mhalf])
        if l > 0:
            lb = T("lb", [128, 1024], F32)
            oml = T("oml", [128, 1024], F32)
            x0 = T("x0", [128, 1024], F32)
            load_bc(p, nc, x0, DR["hgrn_lower_bounds"][0:1, :])
            load_bc(p, nc, lb, DR["hgrn_lower_bounds"][1:2, :])
            p.op("dve", lambda: nc.vector.tensor_tensor(out=x0[:], in0=lb[:], in1=x0[:], op=ALU.subtract), r=[lb, x0], w=[x0])
            p.op("act", lambda: nc.scalar.activation(out=lb[:], in_=x0[:], func=AF.Sigmoid), r=[x0], w=[lb])
            p.op("act", lambda: nc.scalar.activation(out=oml[:], in_=x0[:], func=AF.Sigmoid, scale=-1.0), r=[x0], w=[oml])
        zts = [T("z%d" % i, [128, 4096], F32) for i in range(2)]
        f = T("f", [128, 1024], F32)
        kf = T("kf", [128, 1024], F32)
        logf = T("logf", [128, 1024], F32)
        ex = T("ex", [128, 1024], F32)
        SETS = []
        for k in range(2):
            B = {}
            for nm in ("qtb", "ktb", "kh", "khz", "vb"):
                B[nm] = T("%s%d" % (nm, k), [128, 1024], BF16)
            B["gs"] = T("gs%d" % k, [128, 1024], F32)
            B["gC"] = T("gC%d" % k, [128, 32], F32)
            SETS.append(B)
        qkT = [T("qkT%d" % j, [128, 256], BF16) for j in range(8)]
        qz = [T("qz%d" % j, [128, 64], BF16) for j in range(8)]
        for j in range(8):
            p.op("pool", lambda: nc.gpsimd.memset(qz[j][:], 0.0), w=[qz[j]])
        attm = [T("attm%d" % j, [128, 128], BF16) for j in range(8)]
        S = [T("S%d" % j, [128, 128], F32) for j in range(8)]
        Sb = [T("Sb%d" % j, [128, 128], BF16) for j in range(8)]
        osb = T("osb", [128, 1024], F32)
        obuf = T("obuf", [128, 1024], F32)
        oTb = T("oTb", [128, 8, 128], BF16)
        st = T("st", [128, 4, 8], F32)

        def v8(ap):
            return ap.rearrange("p (h d) -> p h d", d=128)

        def prep_load(s, n, z_):
            r0 = s * P + n * 128
            p.dma(z_[:], zr(Z, r0, r0 + 128)[:, zoff:zoff + 4096], r=[DR["Zt"]], w=[z_])
            yield

        def prep(s, n, B, z_):
            r0 = s * P + n * 128
            p.op("act", lambda: nc.scalar.activation(out=f[:], in_=z_[:, 1024:2048], func=AF.Sigmoid), r=[z_], w=[f])
            yield
            if l > 0:
                p.op("dve", lambda: nc.vector.tensor_tensor(out=f[:], in0=f[:], in1=oml[:], op=ALU.mult), r=[f, oml], w=[f])
                yield
                p.op("pool", lambda: nc.gpsimd.tensor_tensor(out=f[:], in0=f[:], in1=lb[:], op=ALU.add), r=[f, lb], w=[f])
                yield
            p.op("dve", lambda: nc.vector.tensor_scalar(out=kf[:], in0=f[:], scalar1=-1.0, scalar2=1.0, op0=ALU.mult, op1=ALU.add), r=[f], w=[kf])
            p.op("act", lambda: nc.scalar.activation(out=logf[:], in_=f[:], func=AF.Ln), r=[f], w=[logf])
            yield
            p.op("act", lambda: nc.scalar.activation(out=z_[:, 0:1024], in_=z_[:, 0:1024], func=AF.Silu), r=[z_], w=[z_])
            p.op("act", lambda: nc.scalar.activation(out=B["gs"][:], in_=z_[:, 3072:4096], func=AF.Silu), r=[z_], w=[B["gs"]])
            yield
            p.op("pool", lambda: nc.gpsimd.tensor_copy(out=B["vb"][:], in_=z_[:, 2048:3072]), r=[z_], w=[B["vb"]])
            yield
            for j in range(8):
                p.op("pe", lambda: nc.tensor.matmul(PS[4][:, j * 4:(j + 1) * 4], lhsT=logf[:, j * 128:(j + 1) * 128], rhs=bind[:, 0:4], start=True, stop=True),
                     r=[logf, bind], w=[PS[4]])
            p.op("act", lambda: nc.scalar.activation(out=B["gC"][:], in_=PS[4][:, 0:32], func=AF.Exp), r=[PS[4]], w=[B["gC"]])
            yield
            for half in range(2):
                hs = slice(half * 512, (half + 1) * 512)
                pl = PS[4 + half]
                p.op("pe", lambda: nc.tensor.matmul(pl[:], lhsT=b_le[:], rhs=logf[:, hs], start=True, stop=True), r=[b_le, logf], w=[pl])
                p.op("act", lambda: nc.scalar.activation(out=ex[:, hs], in_=pl[:], func=AF.Exp), r=[pl], w=[ex])
                p.op("dve", lambda: nc.vector.tensor_tensor(out=B["qtb"][:, hs], in0=z_[:, hs], in1=ex[:, hs], op=ALU.mult), r=[z_, ex], w=[B["qtb"]])
                p.op("act", lambda: nc.scalar.activation(out=ex[:, hs], in_=pl[:], func=AF.Exp, scale=-1.0), r=[pl], w=[ex])
                p.op("dve", lambda: nc.vector.tensor_tensor(out=B["ktb"][:, hs], in0=kf[:, hs], in1=ex[:, hs], op=ALU.mult), r=[kf, ex], w=[B["ktb"]])
                p.op("pe", lambda: nc.tensor.matmul(pl[:], lhsT=b_gt[:], rhs=logf[:, hs], start=True, stop=True), r=[b_gt, logf], w=[pl])
                p.op("act", lambda: nc.scalar.activation(out=ex[:, hs], in_=pl[:], func=AF.Exp), r=[pl], w=[ex])
                p.op("dve", lambda: nc.vector.tensor_tensor(out=B["kh"][:, hs], in0=kf[:, hs], in1=ex[:, hs], op=ALU.mult), r=[kf, ex], w=[B["kh"]])
                yield
            p.op("pool", lambda: nc.gpsimd.tensor_scalar(out=B["khz"][:], in0=B["kh"][:], scalar1=bind[:, 3:4], scalar2=None, op0=ALU.mult), r=[B["kh"], bind], w=[B["khz"]])
            yield

        def main(s, n, B, inj):
            if n == 0:
                for j in range(8):
                    p.op("pool", lambda: nc.gpsimd.memset(S[j][:], 0.0), w=[S[j]])
                    p.op("pool", lambda: nc.gpsimd.memset(Sb[j][:], 0.0), w=[Sb[j]])
            vb, kh, khz, gC = B["vb"], B["kh"], B["khz"], B["gC"]
            for j in range(8):
                js = slice(j * 128, (j + 1) * 128)
                pw = PS[4 + (j % 2)]
                pwb = pw.t[:].bitcast(BF16)
                p.op("pe", lambda: nc.tensor.transpose(pwb[:, 0:128], B["qtb"][:, js], identb[:]), r=[B["qtb"], identb], w=[pw])
                p.op("pe", lambda: nc.tensor.transpose(pwb[:, 128:256], B["ktb"][:, js], identb[:]), r=[B["ktb"], identb], w=[pw])
                p.op("act", lambda: nc.scalar.copy(out=qkT[j][:], in_=pwb[:, 0:256]), r=[pw], w=[qkT[j]])
                p.op("pool", lambda: nc.gpsimd.tensor_copy(out=qz[j][:, 32:64], in_=qkT[j][:, 96:128]), r=[qkT[j]], w=[qz[j]])
                p.op("pe", lambda: nc.tensor.matmul(pw[:, 256:384], lhsT=qkT[j][:, 128:256], rhs=qkT[j][:, 0:128], start=True, stop=True),
                     r=[qkT[j]], w=[pw])
                p.op("dve", lambda: nc.vector.tensor_tensor(out=attm[j][:], in0=pw[:, 256:384], in1=b_le[:], op=ALU.mult), r=[pw, b_le], w=[attm[j]])
                if j % 2 == 1:
                    inj()
            for j in range(8):
                js = slice(j * 128, (j + 1) * 128)
                po = PS[6 + j // 4]
                p.op("pe", lambda: nc.tensor.matmul(po[:, (j % 4) * 128:(j % 4 + 1) * 128], lhsT=attm[j][:], rhs=vb[:, js],
                                                    start=(j % 4 == 0), stop=False, skip_group_check=True), r=[attm[j], vb], w=[po])
            for c in range(4):
                cs = slice(c * 32, (c + 1) * 32)
                for j in range(8):
                    js = slice(j * 128, (j + 1) * 128)
                    po = PS[6 + j // 4]
                    if c < 3:
                        p.op("pe", lambda: nc.tensor.matmul(po[cs, (j % 4) * 128:(j % 4 + 1) * 128], lhsT=qkT[j][:, cs], rhs=Sb[j][:],
                                                            start=False, stop=False, skip_group_check=True), r=[qkT[j], Sb[j]], w=[po])
                    else:
                        p.op("pe", lambda: nc.tensor.matmul(po[64:128, (j % 4) * 128:(j % 4 + 1) * 128], lhsT=qz[j][:], rhs=Sb[j][:],
                                                            start=False, stop=True, skip_group_check=True), r=[qz[j], Sb[j]], w=[po])
                    pss = PS[j % 4]
                    pc = slice((j // 4) * 128, (j // 4 + 1) * 128)
                    if c < 3:
                        p.op("pe", lambda: nc.tensor.matmul(pss[:, pc], lhsT=kh[cs, js], rhs=vb[cs, js],
                                                            start=True, stop=True), r=[kh, vb], w=[pss])
                    else:
                        p.op("pe", lambda: nc.tensor.matmul(pss[:, pc], lhsT=khz[64:128, js], rhs=vb[64:128, js],
                                                            start=True, stop=True), r=[khz, vb], w=[pss])
                    p.op("dve", lambda: nc.vector.scalar_tensor_tensor(out=S[j][:], in0=S[j][:], scalar=gC[:, j * 4 + c:j * 4 + c + 1],
                                                                       in1=pss[:, pc], op0=ALU.mult, op1=ALU.add),
                         r=[S[j], gC, pss], w=[S[j]])
                    p.op("act", lambda: nc.scalar.copy(out=Sb[j][:], in_=S[j][:]), r=[S[j]], w=[Sb[j]])
                    if j % 4 == 3:
                        inj()
            for half in range(2):
                hs = slice(half * 512, (half + 1) * 512)
                p.op("act", lambda: nc.scalar.copy(out=osb[:, hs], in_=PS[6 + half][:]), r=[PS[6 + half]], w=[osb])

        def outp(s, n, B):
            r0 = s * P + n * 128
            p.op("pool", lambda: nc.gpsimd.tensor_tensor(out=obuf[:], in0=osb[:], in1=osb[:], op=ALU.mult), r=[osb], w=[obuf])
            yield
            p.op("dve", lambda: nc.vector.tensor_reduce(out=st[:, 0, :], in_=v8(obuf.t[:]), axis=AX.X, op=ALU.add), r=[obuf], w=[st])
            p.op("dve", lambda: nc.vector.tensor_scalar(out=st[:, 1, :], in0=st[:, 0, :], scalar1=1.0 / 128, scalar2=EPS, op0=ALU.mult, op1=ALU.add), r=[st], w=[st])
            p.op("pool", lambda: nc.gpsimd.tensor_tensor(out=st[:, 2, :], in0=st[:, 1, :], in1=mhalf[:], op=ALU.pow), r=[st, mhalf], w=[st])
            yield
            p.op("dve", lambda: nc.vector.tensor_tensor(out=v8(osb.t[:]), in0=v8(osb.t[:]), in1=st[:, 2, :].unsqueeze(2).broadcast_to([128, 8, 128]), op=ALU.mult),
                 r=[osb, st], w=[osb])
            yield
            p.op("pool", lambda: nc.gpsimd.tensor_tensor(out=v8(osb.t[:]), in0=v8(osb.t[:]), in1=hnw.t[:].unsqueeze(1).broadcast_to([128, 8, 128]), op=ALU.mult),
                 r=[osb, hnw], w=[osb])
            yield
            p.op("pool", lambda: nc.gpsimd.tensor_tensor(out=obuf[:], in0=osb[:], in1=B["gs"][:], op=ALU.mult), r=[osb, B["gs"]], w=[obuf])
            yield
            for half in range(2):
                pst = PS[4 + half]
                for k4 in range(4):
                    kc = half * 4 + k4
                    p.op("pe", lambda: nc.tensor.transpose(pst[:, k4 * 128:(k4 + 1) * 128], obuf[:, kc * 128:(kc + 1) * 128], ident[:]), r=[obuf, ident], w=[pst])
                src_ = pst.t[:].rearrange("p (k t) -> p k t", k=4)
                if half == 0:
                    p.op("act", lambda: nc.scalar.copy(out=oTb[:, 0:4, :], in_=src_), r=[pst], w=[oTb])
                else:
                    p.op("dve", lambda: nc.vector.tensor_copy(out=oTb[:, 4:8, :], in_=src_), r=[pst], w=[oTb])
                yield
            p.dma(DR["OT_hgrn"].rearrange("(kc p) t -> p kc t", p=128)[:, :, r0:r0 + 128], oTb[:], r=[oTb], w=[DR["OT_t"][2]])
            yield

        tiles = [(s, n) for s in (range(NS) if seqs is None else seqs) for n in range(ntl)]

        def run_all(gen):
            for _ in gen:
                pass

        def chain(*gens):
            for g in gens:
                if g is not None:
                    for _ in g:
                        yield

        if not pipelined:
            for idx, (s, n) in enumerate(tiles):
                B = SETS[idx % 2]
                run_all(prep_load(s, n, zts[idx % 2]))
                run_all(prep(s, n, B, zts[idx % 2]))
                main(s, n, B, lambda: None)
                run_all(outp(s, n, B))
        else:
            run_all(prep_load(tiles[0][0], tiles[0][1], zts[0]))
            run_all(prep(tiles[0][0], tiles[0][1], SETS[0], zts[0]))
            if len(tiles) > 1:
                run_all(prep_load(tiles[1][0], tiles[1][1], zts[1]))
            for idx, (s, n) in enumerate(tiles):
                B = SETS[idx % 2]
                g_out = outp(tiles[idx - 1][0], tiles[idx - 1][1], SETS[(idx - 1) % 2]) if idx > 0 else None
                g_prep = prep(tiles[idx + 1][0], tiles[idx + 1][1], SETS[(idx + 1) % 2], zts[(idx + 1) % 2]) if idx + 1 < len(tiles) else None
                g_load = prep_load(tiles[idx + 2][0], tiles[idx + 2][1], zts[idx % 2]) if idx + 2 < len(tiles) else None
                bg = chain(g_out, g_prep, g_load)

                def inj(k=2):
                    for _ in range(k):
                        try:
                            next(bg)
                        except StopIteration:
                            return
                main(s, n, B, inj)
                run_all(bg)
            run_all(outp(tiles[-1][0], tiles[-1][1], SETS[(len(tiles) - 1) % 2]))
        p.barrier()


C0 = -0.6065306597126334


def phase_rwkv(p, nc, l, DR, G, seqs=None, zoff=RW_OFF, ntl=NT, dbg=None, stop=None, pipelined=True):
    PS = G["PS"]
    ident = G["ident"]
    tri_le, tri_lt, tri_gt, ones = G["tri_le"], G["tri_lt"], G["tri_gt"], G["ones"]
    Z = DR["Z"]

    def bc(ap16, n=16):
        return ap16.unsqueeze(2).broadcast_to([128, n, 64])

    def v3(ap):
        return ap.rearrange("p (h d) -> p h d", d=64)

    with ExitStack() as es:
        T = lambda name, shape, dt, **kw: p.tile("rw_" + name, shape, dt, es=es, **kw)
        mu = T("mu", [128, RW_W], F32)
        load_bc(p, nc, mu, DR["rwkv_mu"][l:l + 1, :])
        prm = {}
        for i, n_ in enumerate(("rwkv_w0", "rwkv_a0", "rwkv_k_k", "rwkv_k_a", "rwkv_gn_w", "rwkv_gn_b")):
            prm[n_] = T(n_, [128, 1024], F32)
            load_bc(p, nc, prm[n_], DR[n_][l:l + 1, :], q=("sp" if i % 2 == 0 else "act"))
        prm["rwkv_r_k"] = T("rwkv_r_k", [128, 1024], F32)
        load_bc(p, nc, prm["rwkv_r_k"], DR["rwkv_r_k"][l:l + 1].rearrange("o h d -> o (h d)"))
        w_up = T("w_up", [64, 1024], F32)
        a_up = T("a_up", [64, 1024], F32)
        p.dma(w_up[:], DR["rwkv_w_up"][l], w=[w_up])
        p.dma(a_up[:], DR["rwkv_a_up"][l], w=[a_up])
        mA = T("mA", [128, 384], F32)
        mB = T("mB", [128, 256], F32)
        p.op("pool", lambda: nc.gpsimd.tensor_copy(out=mA[:, 0:128], in_=tri_lt[:]), r=[tri_lt], w=[mA])
        p.op("pool", lambda: nc.gpsimd.tensor_copy(out=mA[:, 128:256], in_=tri_gt[:]), r=[tri_gt], w=[mA])
        p.op("pool", lambda: nc.gpsimd.tensor_copy(out=mA[:, 256:384], in_=tri_lt[:]), r=[tri_lt], w=[mA])
        p.op("pool", lambda: nc.gpsimd.tensor_copy(out=mB[:, 0:128], in_=tri_le[:]), r=[tri_le], w=[mB])
        p.op("pool", lambda: nc.gpsimd.tensor_copy(out=mB[:, 128:256], in_=tri_le[:]), r=[tri_le], w=[mB])
        identb = T("identb", [128, 128], BF16)
        p.op("pool", lambda: nc.gpsimd.tensor_copy(out=identb[:], in_=ident[:]), r=[ident], w=[identb])
        mhalf16 = T("mhalf16", [128, 16], F32)
        p.op("pool", lambda: nc.gpsimd.memset(mhalf16[:], -0.5), w=[mhalf16])
        zc = T("zc", [128, RW_W], F32)
        zp = T("zp", [128, RW_W], F32)
        sw = T("sw", [128, 1024], F32)
        a_ = T("a_", [128, 1024], F32)
        kk = T("kk", [128, 1024], F32)
        kp = T("kp", [128, 1024], F32)
        b_ = T("b_", [128, 1024], F32)
        twT = T("twT", [64, 256], F32)
        ex = zp.t[:, 0:1024]
        tq = zp.t[:, 3072:4096]
        SETS = []
        for k in range(2):
            B = {}
            for nm in ("rtb", "ktb", "atb", "btb", "khat", "bhat", "vb"):
                B[nm] = T("%s%d" % (nm, k), [128, 1024], BF16)
            B["gs"] = T("gs%d" % k, [128, 1024], F32)
            B["gC"] = T("gC%d" % k, [128, 16], F32)
            B["sm"] = T("sm%d" % k, [128, 4, 16], F32)
            SETS.append(B)
        sm2 = T("sm2", [128, 8, 16], F32)
        fT = [T("fT%d" % i, [128, 4, 128], BF16) for i in range(8)]
        Ar = [T("Ar%d" % h, [128, 256], BF16) for h in range(16)]
        Aak = [T("Aak%d" % i, [128, 128], BF16) for i in range(8)]
        PP = [[T("PP%d_%d" % (i, k), [128, 256], BF16) for k in range(2)] for i in range(8)]
        X = [[T("X%d_%d" % (i, k), [128, 128], BF16) for k in range(2)] for i in range(8)]
        Gt = [T("Gt%d" % i, [128, 64], BF16) for i in range(8)]
        Wt = T("Wt", [128, 1024], F32)
        YT = [T("YT%d" % i, [128, 128], BF16) for i in range(8)]
        ST = T("ST", [128, 8, 64], F32)
        STbd = T("STbd", [128, 8, 128], BF16)
        Ub = T("Ub", [128, 1024], BF16)
        osb = T("osb", [128, 1024], F32)
        obuf = T("obuf", [128, 1024], F32)
        oTb = T("oTb", [128, 8, 128], BF16)

        def prep_load(s, n):
            r0 = s * P + n * 128
            p.dma(zc[:], zr(Z, r0, r0 + 128)[:, zoff:zoff + RW_W], r=[DR["Zt"]], w=[zc])
            if n == 0:
                p.op("pool", lambda: nc.gpsimd.memset(zp[0:1, :], 0.0), w=[zp])
                p.dma(zp[1:128, :], zr(Z, r0, r0 + 127)[:, zoff:zoff + RW_W], r=[DR["Zt"]], w=[zp], q="act")
            else:
                p.dma(zp[:], zr(Z, r0 - 1, r0 + 127)[:, zoff:zoff + RW_W], r=[DR["Zt"]], w=[zp], q="act")
            yield

        def prep(s, n, B):
            r0 = s * P + n * 128
            hs_ = slice(4096, RW_W)
            p.op("dve", lambda: nc.vector.tensor_tensor(out=zp[:, hs_], in0=zp[:, hs_], in1=zc[:, hs_], op=ALU.subtract), r=[zp, zc], w=[zp])
            p.op("dve", lambda: nc.vector.tensor_tensor(out=zp[:, hs_], in0=zp[:, hs_], in1=mu[:, hs_], op=ALU.mult), r=[zp, mu], w=[zp])
            p.op("dve", lambda: nc.vector.tensor_tensor(out=zc[:, hs_], in0=zc[:, hs_], in1=zp[:, hs_], op=ALU.add), r=[zp, zc], w=[zc])
            p.op("act", lambda: nc.scalar.activation(out=zc[:, 4096:4160], in_=zc[:, 4096:4160], func=AF.Tanh), r=[zc], w=[zc])
            yield
            h1 = slice(0, 2560)
            h2 = slice(2560, 4096)
            for k3 in range(3):
                for (sl, eng) in ((h1, "dve"), (h2, "pool")):
                    E = nc.vector if eng == "dve" else nc.gpsimd
                    if k3 == 0:
                        p.op(eng, lambda: E.tensor_tensor(out=zp[:, sl], in0=zp[:, sl], in1=zc[:, sl], op=ALU.subtract), r=[zp, zc], w=[zp])
                    elif k3 == 1:
                        p.op(eng, lambda: E.tensor_tensor(out=zp[:, sl], in0=zp[:, sl], in1=mu[:, sl], op=ALU.mult), r=[zp, mu], w=[zp])
                    else:
                        p.op(eng, lambda: E.tensor_tensor(out=zc[:, sl], in0=zc[:, sl], in1=zp[:, sl], op=ALU.add), r=[zp, zc], w=[zc])
                    yield
            rr = zc.t[:, 0:1024]
            rk = zc.t[:, 1024:2048]
            rv = zc.t[:, 2048:3072]
            rg = zc.t[:, 3072:4096]
            p.op("pe", lambda: nc.tensor.transpose(PS[6][0:64, 0:128], zc[:, 4096:4160], ident[:]), r=[zc, ident], w=[PS[6]])
            p.op("pe", lambda: nc.tensor.transpose(PS[6][0:64, 128:256], zc[:, 4160:4224], ident[:]), r=[zc, ident], w=[PS[6]])
            p.op("act", lambda: nc.scalar.copy(out=twT[:], in_=PS[6][0:64, 0:256]), r=[PS[6]], w=[twT])
            yield
            for half in range(2):
                hs = slice(half * 512, (half + 1) * 512)
                p.op("pe", lambda: nc.tensor.matmul(PS[6][:], lhsT=twT[:, 0:128], rhs=w_up[:, hs], start=True, stop=True), r=[twT, w_up], w=[PS[6]])
                p.op("dve", lambda: nc.vector.tensor_tensor(out=sw[:, hs], in0=PS[6][:], in1=prm["rwkv_w0"][:, hs], op=ALU.add),
                     r=[PS[6], prm["rwkv_w0"]], w=[sw])
                yield
            for half in range(2):
                hs = slice(half * 512, (half + 1) * 512)
                p.op("pe", lambda: nc.tensor.matmul(PS[7][:], lhsT=twT[:, 128:256], rhs=a_up[:, hs], start=True, stop=True), r=[twT, a_up], w=[PS[7]])
                p.op("dve", lambda: nc.vector.tensor_tensor(out=a_[:, hs], in0=PS[7][:], in1=prm["rwkv_a0"][:, hs], op=ALU.add),
                     r=[PS[7], prm["rwkv_a0"]], w=[a_])
                yield
            p.op("act", lambda: nc.scalar.activation(out=sw[:], in_=sw[:], func=AF.Sigmoid), r=[sw], w=[sw])
            p.op("act", lambda: nc.scalar.activation(out=a_[:], in_=a_[:], func=AF.Sigmoid), r=[a_], w=[a_])
            yield
            p.op("dve", lambda: nc.vector.tensor_tensor(out=kk[:], in0=rk, in1=prm["rwkv_k_k"][:], op=ALU.mult), r=[zc, prm["rwkv_k_k"]], w=[kk])
            yield
            p.op("pool", lambda: nc.gpsimd.tensor_tensor(out=kp[:], in0=kk[:], in1=kk[:], op=ALU.mult), r=[kk], w=[kp])
            yield
            p.op("dve", lambda: nc.vector.tensor_reduce(out=sm2[:, 0, :], in_=v3(kp.t[:]), axis=AX.X, op=ALU.add), r=[kp], w=[sm2])
            p.op("dve", lambda: nc.vector.tensor_scalar(out=sm2[:, 1, :], in0=sm2[:, 0, :], scalar1=1e-24, scalar2=None, op0=ALU.max), r=[sm2], w=[sm2])
            p.op("pool", lambda: nc.gpsimd.tensor_tensor(out=sm2[:, 2, :], in0=sm2[:, 1, :], in1=mhalf16[:], op=ALU.pow), r=[sm2, mhalf16], w=[sm2])
            yield
            p.op("dve", lambda: nc.vector.tensor_tensor(out=v3(kk.t[:]), in0=v3(kk.t[:]), in1=bc(sm2[:, 2, :]), op=ALU.mult), r=[kk, sm2], w=[kk])
            yield
            p.op("dve", lambda: nc.vector.scalar_tensor_tensor(out=kp[:], in0=a_[:], scalar=-1.0, in1=prm["rwkv_k_a"][:], op0=ALU.add, op1=ALU.mult),
                 r=[a_, prm["rwkv_k_a"]], w=[kp])
            yield
            p.op("dve", lambda: nc.vector.scalar_tensor_tensor(out=kp[:], in0=kp[:], scalar=1.0, in1=rk, op0=ALU.add, op1=ALU.mult), r=[kp, zc], w=[kp])
            yield
            p.op("pool", lambda: nc.gpsimd.tensor_tensor(out=b_[:], in0=kk[:], in1=a_[:], op=ALU.mult), r=[kk, a_], w=[b_])
            yield
            p.op("pool", lambda: nc.gpsimd.tensor_tensor(out=tq, in0=rr, in1=kp[:], op=ALU.mult), r=[zc, kp], w=[zp])
            yield
            p.op("pool", lambda: nc.gpsimd.tensor_tensor(out=tq, in0=tq, in1=prm["rwkv_r_k"][:], op=ALU.mult), r=[zp, prm["rwkv_r_k"]], w=[zp])
            yield
            p.op("dve", lambda: nc.vector.tensor_reduce(out=B["sm"][:, 3, :], in_=v3(tq), axis=AX.X, op=ALU.add), r=[zp], w=[B["sm"]])
            p.op("pool", lambda: nc.gpsimd.tensor_copy(out=B["vb"][:], in_=rv), r=[zc], w=[B["vb"]])
            yield
            p.op("act", lambda: nc.scalar.activation(out=B["gs"][:], in_=rg, func=AF.Silu), r=[zc], w=[B["gs"]])
            yield
            for _ in range(8):
                yield
            for half in range(2):
                hs = slice(half * 512, (half + 1) * 512)
                pl = PS[6 + half]
                p.op("pe", lambda: nc.tensor.matmul(pl[:], lhsT=tri_le[:], rhs=sw[:, hs], start=True, stop=True), r=[tri_le, sw], w=[pl])
                p.op("act", lambda: nc.scalar.activation(out=ex[:, hs], in_=pl[:], func=AF.Exp, scale=C0), r=[pl], w=[zp])
                p.op("dve", lambda: nc.vector.tensor_tensor(out=B["rtb"][:, hs], in0=rr[:, hs], in1=ex[:, hs], op=ALU.mult), r=[zc, zp], w=[B["rtb"]])
                p.op("dve", lambda: nc.vector.tensor_tensor(out=tq[:, hs], in0=pl[:], in1=sw[:, hs], op=ALU.subtract), r=[pl, sw], w=[zp])
                p.op("act", lambda: nc.scalar.activation(out=tq[:, hs], in_=tq[:, hs], func=AF.Exp, scale=C0), r=[zp], w=[zp])
                p.op("dve", lambda: nc.vector.scalar_tensor_tensor(out=B["atb"][:, hs], in0=kk[:, hs], scalar=-1.0, in1=tq[:, hs], op0=ALU.mult, op1=ALU.mult),
                     r=[kk, zp], w=[B["atb"]])
                p.op("act", lambda: nc.scalar.activation(out=ex[:, hs], in_=pl[:], func=AF.Exp, scale=-C0), r=[pl], w=[zp])
                p.op("pool", lambda: nc.gpsimd.tensor_tensor(out=B["btb"][:, hs], in0=b_[:, hs], in1=ex[:, hs], op=ALU.mult), r=[b_, zp], w=[B["btb"]])
                p.op("dve", lambda: nc.vector.tensor_tensor(out=B["ktb"][:, hs], in0=kp[:, hs], in1=ex[:, hs], op=ALU.mult), r=[kp, zp], w=[B["ktb"]])
                p.op("pe", lambda: nc.tensor.matmul(pl[:], lhsT=tri_gt[:], rhs=sw[:, hs], start=True, stop=True), r=[tri_gt, sw], w=[pl])
                p.op("act", lambda: nc.scalar.activation(out=ex[:, hs], in_=pl[:], func=AF.Exp, scale=C0), r=[pl], w=[zp])
                p.op("pool", lambda: nc.gpsimd.tensor_tensor(out=B["khat"][:, hs], in0=kp[:, hs], in1=ex[:, hs], op=ALU.mult), r=[kp, zp], w=[B["khat"]])
                p.op("dve", lambda: nc.vector.tensor_tensor(out=B["bhat"][:, hs], in0=b_[:, hs], in1=ex[:, hs], op=ALU.mult), r=[b_, zp], w=[B["bhat"]])
                yield
            for pp in range(8):
                p.op("pe", lambda: nc.tensor.matmul(PS[6][:, pp * 2:pp * 2 + 2], lhsT=sw[:, pp * 128:(pp + 1) * 128], rhs=ones[:, 0:2],
                                                    start=True, stop=True), r=[sw, ones], w=[PS[6]])
            p.op("act", lambda: nc.scalar.activation(out=B["gC"][:], in_=PS[6][:, 0:16], func=AF.Exp, scale=C0), r=[PS[6]], w=[B["gC"]])
            yield

        def main(s, n, B, inj):
            if n == 0:
                p.op("pool", lambda: nc.gpsimd.memset(ST[:], 0.0), w=[ST])
                p.op("pool", lambda: nc.gpsimd.memset(STbd[:], 0.0), w=[STbd])
            src = (B["rtb"], B["ktb"], B["atb"], B["btb"])
            for g8 in range(2):
                for pi in range(4):
                    pp = g8 * 4 + pi
                    cs = slice(pp * 128, (pp + 1) * 128)
                    pw = PS[6 + pi // 2]
                    pwb = pw.t[:].bitcast(BF16)
                    off = (pi % 2) * 512
                    for k in range(4):
                        p.op("pe", lambda: nc.tensor.transpose(pwb[:, off + k * 128:off + (k + 1) * 128], src[k][:, cs], identb[:]), r=[src[k], identb], w=[pw])
                    if pi % 2 == 0:
                        p.op("act", lambda: nc.scalar.copy(out=fT[pp].t[:].rearrange("p a t -> p (a t)"), in_=pwb[:, off:off + 512]), r=[pw], w=[fT[pp]])
                    else:
                        p.op("dve", lambda: nc.vector.tensor_copy(out=fT[pp].t[:].rearrange("p a t -> p (a t)"), in_=pwb[:, off:off + 512]), r=[pw], w=[fT[pp]])
                for i in range(8):
                    h = g8 * 8 + i
                    pp, e = h // 2, h % 2
                    es_ = slice(e * 64, (e + 1) * 64)
                    rT, kT, aT, bT = (fT[pp][es_, k, :] for k in range(4))
                    pa = PS[i]
                    p.op("pe", lambda: nc.tensor.matmul(pa[:, 0:128], lhsT=bT, rhs=aT, start=True, stop=True), r=[fT[pp]], w=[pa])
                    p.op("pe", lambda: nc.tensor.matmul(pa[:, 128:256], lhsT=aT, rhs=bT, start=True, stop=True), r=[fT[pp]], w=[pa])
                    p.op("pe", lambda: nc.tensor.matmul(pa[:, 256:384], lhsT=kT, rhs=aT, start=True, stop=True), r=[fT[pp]], w=[pa])
                    p.op("dve", lambda: nc.vector.tensor_tensor(out=PP[i][0][:], in0=pa[:, 0:256], in1=mA[:, 0:256], op=ALU.mult), r=[pa, mA], w=[PP[i][0]])
                    p.op("dve", lambda: nc.vector.tensor_tensor(out=Aak[i][:], in0=pa[:, 256:384], in1=mA[:, 256:384], op=ALU.mult), r=[pa, mA], w=[Aak[i]])
                    p.op("dve", lambda: nc.vector.tensor_tensor(out=X[i][0][:], in0=PP[i][0][:, 0:128], in1=identb[:], op=ALU.add), r=[PP[i][0], identb], w=[X[i][0]])
                inj()
                for i in range(8):
                    h = g8 * 8 + i
                    pp, e = h // 2, h % 2
                    es_ = slice(e * 64, (e + 1) * 64)
                    rT, kT, aT, bT = (fT[pp][es_, k, :] for k in range(4))
                    pa = PS[i]
                    p.op("pe", lambda: nc.tensor.matmul(pa[:, 0:128], lhsT=bT, rhs=rT, start=True, stop=True), r=[fT[pp]], w=[pa])
                    p.op("pe", lambda: nc.tensor.matmul(pa[:, 128:256], lhsT=kT, rhs=rT, start=True, stop=True), r=[fT[pp]], w=[pa])
                    p.op("dve", lambda: nc.vector.tensor_tensor(out=Ar[h][:], in0=pa[:, 0:256], in1=mB[:], op=ALU.mult), r=[pa, mB], w=[Ar[h]])
                inj()
                for lv in range(1, 7):
                    cur, nxt = (lv - 1) % 2, lv % 2
                    for i in range(8):
                        pq = PS[i]
                        Pm, PTm = PP[i][cur][:, 0:128], PP[i][cur][:, 128:256]
                        if lv < 6:
                            p.op("pe", lambda: nc.tensor.matmul(pq[:, 0:128], lhsT=PTm, rhs=Pm, start=True, stop=True), r=[PP[i][cur]], w=[pq])
                        p.op("pe", lambda: nc.tensor.matmul(pq[:, 128:256], lhsT=Pm, rhs=PTm, start=True, stop=True), r=[PP[i][cur]], w=[pq])
                        if i % 2 == 0:
                            p.op("act", lambda: nc.scalar.copy(out=PP[i][nxt][:], in_=pq[:, 0:256]), r=[pq], w=[PP[i][nxt]])
                        else:
                            p.op("dve", lambda: nc.vector.tensor_copy(out=PP[i][nxt][:], in_=pq[:, 0:256]), r=[pq], w=[PP[i][nxt]])
                    inj()
                    for i in range(8):
                        px = PS[i]
                        p.op("pe", lambda: nc.tensor.matmul(px[:, 256:384], lhsT=identb[:], rhs=X[i][cur][:], start=True, stop=False), r=[identb, X[i][cur]], w=[px])
                        p.op("pe", lambda: nc.tensor.matmul(px[:, 256:384], lhsT=PP[i][nxt][:, 128:256], rhs=X[i][cur][:], start=False, stop=True),
                             r=[PP[i][nxt], X[i][cur]], w=[px])
                        if i % 2 == 1:
                            p.op("act", lambda: nc.scalar.copy(out=X[i][nxt][:], in_=px[:, 256:384]), r=[px], w=[X[i][nxt]])
                        else:
                            p.op("dve", lambda: nc.vector.tensor_copy(out=X[i][nxt][:], in_=px[:, 256:384]), r=[px], w=[X[i][nxt]])
                    inj()
                for i in range(8):
                    h = g8 * 8 + i
                    hc = slice(h * 64, (h + 1) * 64)
                    p.op("pe", lambda: nc.tensor.matmul(PS[i][:, 0:64], lhsT=Aak[i][:], rhs=B["vb"][:, hc], start=True, stop=True), r=[Aak[i], B["vb"]], w=[PS[i]])
                    if i % 2 == 0:
                        p.op("act", lambda: nc.scalar.copy(out=Gt[i][:], in_=PS[i][:, 0:64]), r=[PS[i]], w=[Gt[i]])
                    else:
                        p.op("dve", lambda: nc.vector.tensor_copy(out=Gt[i][:], in_=PS[i][:, 0:64]), r=[PS[i]], w=[Gt[i]])
                for i in range(8):
                    h = g8 * 8 + i
                    pp, e = h // 2, h % 2
                    hc = slice(h * 64, (h + 1) * 64)
                    py = PS[i]
                    p.op("pe", lambda: nc.tensor.matmul(py[:, 64:128], lhsT=X[i][0][:], rhs=Gt[i][:], start=True, stop=True),
                         r=[X[i][0], Gt[i]], w=[py])
                    p.op("pe", lambda: nc.tensor.matmul(py[:, 128:256], lhsT=B["atb"][:, pp * 128:(pp + 1) * 128], rhs=X[i][0][:], start=True, stop=True),
                         r=[B["atb"], X[i][0]], w=[py])
                    p.op("act", lambda: nc.scalar.copy(out=Wt[:, hc], in_=py[:, 64:128]), r=[py], w=[Wt])
                    p.op("dve", lambda: nc.vector.tensor_copy(out=YT[pp][e * 64:(e + 1) * 64, :], in_=py[e * 64:(e + 1) * 64, 128:256]), r=[py], w=[YT[pp]])
                inj()
            for pp in range(8):
                p.op("pe", lambda: nc.tensor.matmul(PS[pp // 4][:, (pp % 4) * 128:(pp % 4 + 1) * 128], lhsT=YT[pp][:], rhs=STbd[:, pp, :], start=True, stop=True),
                     r=[YT[pp], STbd], w=[PS[pp // 4]])
            for half in range(2):
                hs = slice(half * 512, (half + 1) * 512)
                p.op("dve", lambda: nc.vector.tensor_tensor(out=Ub[:, hs], in0=PS[half][:], in1=Wt[:, hs], op=ALU.add), r=[PS[half], Wt], w=[Ub])
            for pp in range(8):
                po = PS[2 + pp // 4]
                p.op("pe", lambda: nc.tensor.matmul(po[:, (pp % 4) * 128:(pp % 4 + 1) * 128], lhsT=fT[pp][:, 0, :], rhs=STbd[:, pp, :],
                                                    start=(pp % 4 == 0), stop=False, skip_group_check=True), r=[fT[pp], STbd], w=[po])
            for h in range(16):
                hc = slice(h * 64, (h + 1) * 64)
                po = PS[2 + h // 8]
                oc = slice((h % 8) * 64, (h % 8 + 1) * 64)
                p.op("pe", lambda: nc.tensor.matmul(po[:, oc], lhsT=Ar[h][:, 128:256], rhs=B["vb"][:, hc], start=False, stop=False, skip_group_check=True),
                     r=[Ar[h], B["vb"]], w=[po])
                p.op("pe", lambda: nc.tensor.matmul(po[:, oc], lhsT=Ar[h][:, 0:128], rhs=Ub[:, hc], start=False, stop=True, skip_group_check=True),
                     r=[Ar[h], Ub], w=[po])
            for h in range(16):
                pp, e = h // 2, h % 2
                es_ = slice(e * 64, (e + 1) * 64)
                hc = slice(h * 64, (h + 1) * 64)
                p.op("pe", lambda: nc.tensor.matmul(PS[4][es_, pp * 64:(pp + 1) * 64], lhsT=B["bhat"][:, hc], rhs=Ub[:, hc], start=(pp == 0), stop=False, skip_group_check=True),
                     r=[B["bhat"], Ub], w=[PS[4]])
                p.op("pe", lambda: nc.tensor.matmul(PS[4][es_, pp * 64:(pp + 1) * 64], lhsT=B["khat"][:, hc], rhs=B["vb"][:, hc], start=False, stop=True, skip_group_check=True),
                     r=[B["khat"], B["vb"]], w=[PS[4]])
            gC = B["gC"]
            p.op("dve", lambda: nc.vector.tensor_tensor(out=ST[:], in0=ST[:], in1=gC.t[:, 0:16].rearrange("p (a b) -> p a b", b=2)[:, :, 0:1].broadcast_to([128, 8, 64]), op=ALU.mult),
                 r=[ST, gC], w=[ST])
            p.op("dve", lambda: nc.vector.tensor_tensor(out=ST.t[:].rearrange("p a v -> p (a v)"), in0=ST.t[:].rearrange("p a v -> p (a v)"), in1=PS[4][:], op=ALU.add),
                 r=[ST, PS[4]], w=[ST])
            p.op("act", lambda: nc.scalar.copy(out=STbd[0:64, :, 0:64], in_=ST[0:64, :, :]), r=[ST], w=[STbd])
            p.op("pool", lambda: nc.gpsimd.tensor_copy(out=STbd[64:128, :, 64:128], in_=ST[64:128, :, :]), r=[ST], w=[STbd])
            for half in range(2):
                hs = slice(half * 512, (half + 1) * 512)
                p.op("act", lambda: nc.scalar.copy(out=osb[:, hs], in_=PS[2 + half][:]), r=[PS[2 + half]], w=[osb])

        def outp(s, n, B):
            r0 = s * P + n * 128
            sm = sm2
            if dbg is not None:
                p.dma(dbg[r0:r0 + 128, :], osb[:], r=[osb], w=[DR["dbgt"]])
            p.op("dve", lambda: nc.vector.tensor_reduce(out=sm[:, 4, :], in_=v3(osb.t[:]), axis=AX.X, op=ALU.add), r=[osb], w=[sm])
            yield
            p.op("pool", lambda: nc.gpsimd.tensor_tensor(out=obuf[:], in0=osb[:], in1=osb[:], op=ALU.mult), r=[osb], w=[obuf])
            yield
            p.op("dve", lambda: nc.vector.tensor_reduce(out=sm[:, 5, :], in_=v3(obuf.t[:]), axis=AX.X, op=ALU.add), r=[obuf], w=[sm])
            p.op("dve", lambda: nc.vector.tensor_scalar(out=sm[:, 4, :], in0=sm[:, 4, :], scalar1=1.0 / 64, scalar2=None, op0=ALU.mult), r=[sm], w=[sm])
            p.op("dve", lambda: nc.vector.tensor_tensor(out=sm[:, 6, :], in0=sm[:, 4, :], in1=sm[:, 4, :], op=ALU.mult), r=[sm], w=[sm])
            p.op("dve", lambda: nc.vector.scalar_tensor_tensor(out=sm[:, 5, :], in0=sm[:, 5, :], scalar=1.0 / 64, in1=sm[:, 6, :], op0=ALU.mult, op1=ALU.subtract),
                 r=[sm], w=[sm])
            yield
            p.op("dve", lambda: nc.vector.tensor_scalar(out=sm[:, 5, :], in0=sm[:, 5, :], scalar1=GN_EPS, scalar2=None, op0=ALU.add), r=[sm], w=[sm])
            p.op("pool", lambda: nc.gpsimd.tensor_tensor(out=sm[:, 7, :], in0=sm[:, 5, :], in1=mhalf16[:], op=ALU.pow), r=[sm, mhalf16], w=[sm])
            yield
            p.op("dve", lambda: nc.vector.tensor_tensor(out=v3(osb.t[:]), in0=v3(osb.t[:]), in1=bc(sm[:, 4, :]), op=ALU.subtract), r=[osb, sm], w=[osb])
            yield
            p.op("dve", lambda: nc.vector.tensor_tensor(out=v3(osb.t[:]), in0=v3(osb.t[:]), in1=bc(sm[:, 7, :]), op=ALU.mult), r=[osb, sm], w=[osb])
            yield
            p.op("pool", lambda: nc.gpsimd.tensor_tensor(out=osb[:], in0=osb[:], in1=prm["rwkv_gn_w"][:], op=ALU.mult), r=[osb, prm["rwkv_gn_w"]], w=[osb])
            yield
            p.op("pool", lambda: nc.gpsimd.tensor_tensor(out=osb[:], in0=osb[:], in1=prm["rwkv_gn_b"][:], op=ALU.add), r=[osb, prm["rwkv_gn_b"]], w=[osb])
            yield
            p.op("dve", lambda: nc.vector.tensor_tensor(out=v3(obuf.t[:]), in0=v3(B["vb"].t[:]), in1=bc(B["sm"][:, 3, :]), op=ALU.mult), r=[B["vb"], B["sm"]], w=[obuf])
            yield
            p.op("pool", lambda: nc.gpsimd.tensor_tensor(out=obuf[:], in0=obuf[:], in1=osb[:], op=ALU.add), r=[obuf, osb], w=[obuf])
            yield
            p.op("pool", lambda: nc.gpsimd.tensor_tensor(out=obuf[:], in0=obuf[:], in1=B["gs"][:], op=ALU.mult), r=[obuf, B["gs"]], w=[obuf])
            yield
            for _ in range(4):
                yield
            for half in range(2):
                pst = PS[6 + half]
                for k4 in range(4):
                    kc = half * 4 + k4
                    p.op("pe", lambda: nc.tensor.transpose(pst[:, k4 * 128:(k4 + 1) * 128], obuf[:, kc * 128:(kc + 1) * 128], ident[:]), r=[obuf, ident], w=[pst])
                src_ = pst.t[:].rearrange("p (k t) -> p k t", k=4)
                if half == 0:
                    p.op("act", lambda: nc.scalar.copy(out=oTb[:, 0:4, :], in_=src_), r=[pst], w=[oTb])
                else:
                    p.op("dve", lambda: nc.vector.tensor_copy(out=oTb[:, 4:8, :], in_=src_), r=[pst], w=[oTb])
                yield
            p.dma(DR["OT_rwkv"].rearrange("(kc p) t -> p kc t", p=128)[:, :, r0:r0 + 128], oTb[:], r=[oTb], w=[DR["OT_t"][1]])
            yield

        tiles = [(s, n) for s in (range(NS) if seqs is None else seqs) for n in range(ntl)]

        def run_all(gen):
            for _ in gen:
                pass

        def chain(*gens):
            for g in gens:
                if g is not None:
                    for _ in g:
                        yield

        if not pipelined:
            for idx, (s, n) in enumerate(tiles):
                B = SETS[idx % 2]
                run_all(prep_load(s, n))
                run_all(prep(s, n, B))
                main(s, n, B, lambda: None)
                run_all(outp(s, n, B))
        else:
            run_all(prep_load(tiles[0][0], tiles[0][1]))
            run_all(prep(tiles[0][0], tiles[0][1], SETS[0]))
            for idx, (s, n) in enumerate(tiles):
                B = SETS[idx % 2]
                g_out = outp(tiles[idx - 1][0], tiles[idx - 1][1], SETS[(idx - 1) % 2]) if idx > 0 else None
                g_load = prep_load(tiles[idx + 1][0], tiles[idx + 1][1]) if idx + 1 < len(tiles) else None
                g_prep = prep(tiles[idx + 1][0], tiles[idx + 1][1], SETS[(idx + 1) % 2]) if idx + 1 < len(tiles) else None
                bg = chain(g_load, g_out, g_prep)

                def inj(k=4):
                    for _ in range(k):
                        try:
                            next(bg)
                        except StopIteration:
                            return
                main(s, n, B, inj)
                run_all(bg)
            run_all(outp(tiles[-1][0], tiles[-1][1], SETS[(len(tiles) - 1) % 2]))
        p.barrier()


def host_consts():
    c = {}
    i = np.arange(128)
    c["ident"] = np.eye(128, dtype=np.float32)
    c["tri_le"] = (i[:, None] <= i[None, :]).astype(np.float32)
    c["tri_lt"] = (i[:, None] < i[None, :]).astype(np.float32)
    c["tri_gt"] = (i[:, None] > i[None, :]).astype(np.float32)
    same = (i[:, None] // 32) == (i[None, :] // 32)
    c["b_le"] = (same & (i[:, None] <= i[None, :])).astype(np.float32)
    c["b_gt"] = (same & (i[:, None] > i[None, :])).astype(np.float32)
    bind = np.zeros((128, 128), np.float32)
    bind[i, i // 32] = 1.0
    c["bind"] = bind
    c["ones"] = np.ones((128, 128), np.float32)
    t = (np.arange(NT)[None, :] * 128 + np.arange(128)[:, None]).astype(np.float32) - PAD
    inv = (1.0 / (10000.0 ** (np.arange(0, 64, 2, dtype=np.float32) / 64))).astype(np.float32)
    ang = (t[:, :, None] * inv[None, None, :]).astype(np.float32)
    cs, sn = np.cos(ang).astype(np.float32), np.sin(ang).astype(np.float32)
    cos2 = np.zeros((128, NT, 2, 2, 32), np.float32)
    sin2 = np.zeros((128, NT, 2, 2, 32), np.float32)
    cos2[:] = cs[:, :, None, None, :]
    sin2[:, :, :, 0, :] = -sn[:, :, None, :]
    sin2[:, :, :, 1, :] = sn[:, :, None, :]
    c["cos2"] = cos2.reshape(128, NT, 128)
    c["sin2"] = sin2.reshape(128, NT, 128)
    return c


PARAM_SHAPES = {
    "pre_norm_w": [2, D], "post_norm_w": [2, D], "w_in": [2, D, IN_W],
    "lambda_q1": [2, 64], "lambda_k1": [2, 64], "lambda_q2": [2, 64], "lambda_k2": [2, 64], "att_norm_w": [2, 128],
    "rwkv_mu": [2, RW_W], "rwkv_w0": [2, 1024], "rwkv_w_up": [2, 64, 1024], "rwkv_a0": [2, 1024], "rwkv_a_up": [2, 64, 1024],
    "rwkv_k_k": [2, 1024], "rwkv_k_a": [2, 1024], "rwkv_r_k": [2, 16, 64], "rwkv_gn_w": [2, 1024], "rwkv_gn_b": [2, 1024],
    "hgrn_lower_bounds": [2, 1024], "hgrn_norm_w": [2, 128],
    "w_att_out": [2, D, D], "w_rwkv_out": [2, D, D], "w_hgrn_out": [2, D, D], "w_o": [2, D, D],
}
CONST_NAMES = ("ident", "tri_le", "tri_lt", "tri_gt", "b_le", "b_gt", "bind", "ones")


def build_program(nlayers=DEPTH):
    nc = bass.Bass("TRN2", target_bir_lowering=False)
    DR = {}
    H0 = nc.dram_tensor("h0", [ROWS, D], F32, kind="ExternalInput").ap()
    for n, sh in PARAM_SHAPES.items():
        DR[n] = nc.dram_tensor(n, sh, F32, kind="ExternalInput").ap()
    cd = {n: nc.dram_tensor(n, [128, 128], F32, kind="ExternalInput").ap() for n in CONST_NAMES}
    DR["cos2"] = nc.dram_tensor("cos2", [128, NT, 128], F32, kind="ExternalInput").ap()
    DR["sin2"] = nc.dram_tensor("sin2", [128, NT, 128], F32, kind="ExternalInput").ap()
    DR["Z"] = [nc.dram_tensor("Zscr%d" % i, [P, IN_W], F32, kind="Internal").ap() for i in range(NS)]
    for n in ("OT_att", "OT_rwkv", "OT_hgrn"):
        DR[n] = nc.dram_tensor(n + "_scr", [D, ROWS], BF16, kind="Internal").ap()
    Hs = nc.dram_tensor("Hscr", [ROWS, D], F32, kind="Internal").ap()
    OUT = nc.dram_tensor("out", [NS, 2048, D], F32, kind="ExternalOutput").ap()
    DR["OT_t"] = [Tl(None, "ot%d" % i, multi=True) for i in range(3)]
    DR["Zt"] = Tl(None, "Z", multi=True)
    DR["Outt"] = Tl(None, "out", multi=True)
    H_t = [Tl(None, "H0", multi=True), Tl(None, "Hs", multi=True)]
    with ExitStack() as es:
        p = Prog(nc, es)
        G = {}
        for i, n in enumerate(CONST_NAMES):
            G[n] = p.tile(n, [128, 128], F32)
            p.dma(G[n][:], cd[n], w=[G[n]], q=("sp" if i % 2 == 0 else "act"))
        G["PS"] = [p.tile("ps%d" % i, [128, 512], F32, psum=True) for i in range(8)]
        PS = G["PS"]
        for l in range(nlayers):
            Hin = H0 if l == 0 else Hs
            Hin_t = H_t[0] if l == 0 else H_t[1]
            last = (l == nlayers - 1)
            with ExitStack() as es2:
                T = lambda name, shape, dt, **kw: p.tile("pj_" + name, shape, dt, es=es2, **kw)
                C = {"ident": G["ident"]}
                C["uT"] = T("uT", [128, 8, ROWS], BF16, multi=True)
                C["ht"] = [T("ht%d" % i, [128, D], F32) for i in range(2)]
                C["ut"] = [T("ut%d" % i, [128, D], F32) for i in range(2)]
                C["sq"] = T("sq", [128, D], F32)
                C["ss"] = [T("ss%d" % i, [128, 8], F32) for i in range(2)]
                C["pst"] = PS[0:2]
                C["psz"] = PS[2:6]
                C["wst"] = [T("wst%d" % i, [128, 8, 512], F32) for i in range(2)]
                C["wbf"] = [T("wbf%d" % i, [128, 8, 512], BF16) for i in range(2)]
                C["zo"] = [T("zo%d" % i, [128, 512], F32) for i in range(4)]
                C["Zt"] = DR["Zt"]
                C["Ht"] = Hin_t
                C["mhalf"] = T("mhalf", [128, 1], F32)
                p.op("pool", lambda: nc.gpsimd.memset(C["mhalf"][:], -0.5), w=[C["mhalf"]])
                prew_bc = T("prew_bc", [128, D], F32)
                load_bc(p, nc, prew_bc, DR["pre_norm_w"][l:l + 1, :])
                phase_proj(p, nc, Hin, DR["Z"], DR["w_in"][l], prew_bc, C)
                p.barrier()
            phase_att(p, nc, l, DR, G)
            phase_hgrn(p, nc, l, DR, G)
            phase_rwkv(p, nc, l, DR, G)
            DR["Ht"] = Hin_t
            DR["Ht2"] = H_t[1]
            phase_merge(p, nc, l, DR, G, Hin, Hs, out_final=(OUT if last else None))
        p.barrier()
        print("n_inst", p.n_inst)
    return nc


_NC_CACHE = {}


def kernel(**inputs):
    x = np.asarray(inputs["x"], dtype=np.float32)
    meta = np.asarray(inputs["meta_tokens"], dtype=np.float32)
    B = x.shape[0]
    ncores = B // NS
    if "nc" not in _NC_CACHE:
        _NC_CACHE["nc"] = build_program()
    nc = _NC_CACHE["nc"]
    consts = host_consts()
    shared = {n: np.ascontiguousarray(np.asarray(inputs[n], dtype=np.float32)) for n in PARAM_SHAPES}
    shared.update(consts)
    in_maps = []
    for c in range(ncores):
        h0 = np.zeros((NS, P, D), np.float32)
        h0[:, PAD:PAD + 16] = meta[None]
        h0[:, PAD + 16:] = x[c * NS:(c + 1) * NS]
        m = dict(shared)
        m["h0"] = h0.reshape(ROWS, D)
        in_maps.append(m)
    res = run_bass_kernel_spmd(nc, in_maps, core_ids=list(range(ncores)))
    out = np.concatenate([np.asarray(r["out"], dtype=np.float32) for r in res.results], axis=0)
    return out
```

```python
import numpy as np
import ml_dtypes
from contextlib import ExitStack
import concourse.bass as bass
import concourse.mybir as mybir
from concourse.bass_utils import run_bass_kernel_spmd

F32 = mybir.dt.float32
BF16 = mybir.dt.bfloat16
AF = mybir.ActivationFunctionType
ALU = mybir.AluOpType
AX = mybir.AxisListType

D = 1024
NS = 2
NT = 17
P = NT * 128
PAD = 112
ROWS = NS * P
IN_W = 15488
RW_OFF = 4096
RW_W = 4224
HG_OFF = RW_OFF + RW_W
MG_OFF = HG_OFF + 4096
DEPTH = 2
EPS = 1e-6
GN_EPS = 64e-5


def zr(Z, a, b):
    if isinstance(Z, list):
        si = a // P
        assert (b - 1) // P == si
        return Z[si][a - si * P:b - si * P]
    return Z[a:b]


class Tl:
    __slots__ = ("t", "w", "r", "name", "multi", "wd", "excl")

    def __init__(self, t, name="", multi=False, excl=False):
        self.excl = excl
        self.t = t
        self.w = None
        self.r = {}
        self.wd = {}
        self.multi = multi
        self.name = name

    def __getitem__(self, idx):
        return self.t[idx]


class Prog:
    def __init__(self, nc, es, same_engine_sync=True):
        self.nc = nc
        self.es = es
        self.E = {"pe": nc.tensor, "dve": nc.vector, "act": nc.scalar, "pool": nc.gpsimd, "sp": nc.sync}
        self.sem = {e: es.enter_context(nc.semaphore("sem_" + e)) for e in ("pe", "dve", "act", "pool")}
        self.cnt = {e: 0 for e in self.sem}
        self.epoch = {e: 0 for e in self.sem}
        self.old = []
        self.waited = {e: {} for e in self.E}
        self.same = same_engine_sync
        self.dq = {}
        for q, n in (("sp", 40),):
            self.dq[q] = {"sems": [es.enter_context(nc.semaphore("dsem_%s%d" % (q, i))) for i in range(n)],
                          "val": [0] * n, "next": 0}
        self.n_inst = 0
        self.mute = False

    def tile(self, name, shape, dt, psum=False, multi=False, es=None):
        es = es or self.es
        self.n_tiles = getattr(self, "n_tiles", 0) + 1
        name = "%s_%d" % (name, self.n_tiles)
        if psum:
            t = es.enter_context(self.nc.psum_tensor("pt_" + name, shape, dt))
        else:
            t = es.enter_context(self.nc.sbuf_tensor("sb_" + name, shape, dt))
        return Tl(t, name, multi, excl=psum)

    def _wait(self, e, tok):
        if tok is None:
            return
        sem, val, key, owner = tok
        if owner == e and (e == "pe" or not self.same):
            return
        if self.waited[e].get(key, 0) >= val:
            return
        self.E[e].wait_ge(sem, val)
        self.waited[e][key] = val

    def _deps(self, e, r, w):
        for t in r:
            if t.multi:
                for tok in t.wd.values():
                    self._wait(e, tok)
            else:
                self._wait(e, t.w)
        for t in w:
            if not t.multi:
                self._wait(e, t.w)
            for tok in t.r.values():
                self._wait(e, tok)

    def _record(self, tok, r, w):
        for t in r:
            t.r[tok[2]] = tok
        for t in w:
            if t.multi:
                t.wd[tok[2]] = tok
            else:
                t.w = tok
                t.r = {}

    def barrier(self):
        self.mute = False
        toks = [(self.sem[e], self.cnt[e], "c_%s_%d" % (e, self.epoch[e]), e) for e in self.sem if self.cnt[e] > 0]
        toks += self.old
        for q, dq in self.dq.items():
            for i, v in enumerate(dq["val"]):
                if v > 0:
                    toks.append((dq["sems"][i], v, "d_%s%d" % (q, i), None))
        for e in self.E:
            for tok in toks:
                if tok[3] == e:
                    continue
                self._wait(e, tok)

    def op(self, e, fn, r=(), w=()):
        if self.mute:
            return None
        if any(t.excl for t in r):
            w = list(w) + [t for t in r if t.excl and t not in w]
            r = [t for t in r if not t.excl]
        self._deps(e, r, w)
        if self.cnt[e] >= 16000:
            self.old.append((self.sem[e], self.cnt[e], "c_%s_%d" % (e, self.epoch[e]), None))
            self.epoch[e] += 1
            self.sem[e] = self.es.enter_context(self.nc.semaphore("sem_%s_%d" % (e, self.epoch[e])))
            self.cnt[e] = 0
        ins = fn()
        self.cnt[e] += 1
        ins.then_inc(self.sem[e], 1)
        tok = (self.sem[e], self.cnt[e], "c_%s_%d" % (e, self.epoch[e]), e)
        self._record(tok, r, w)
        self.n_inst += 1
        return tok

    def dma(self, out, in_, r=(), w=(), q="sp", **kw):
        if self.mute:
            return None
        q = "sp"
        dq = self.dq[q]
        i = dq["next"]
        dq["next"] = (i + 1) % len(dq["sems"])
        key = "d_%s%d" % (q, i)
        if dq["val"][i] > 0:
            self._wait(q, (dq["sems"][i], dq["val"][i], key, None))
        self._deps(q, r, w)
        dq["val"][i] += 16
        self.E[q].dma_start(out=out, in_=in_, **kw).then_inc(dq["sems"][i], 16)
        tok = (dq["sems"][i], dq["val"][i], key, None)
        self._record(tok, r, w)
        self.n_inst += 1
        return tok

    def finish(self, toks):
        for tok in toks:
            self._wait("sp", tok)


def phase_proj(p, nc, H, Z, w_in_l, prew_bc, C, ntiles=NS * NT, ncolblk=None):
    uT = C["uT"]
    ident = C["ident"]
    for tt in range(ntiles):
        ht = C["ht"][tt % 2]
        p.dma(ht[:], H[tt * 128:(tt + 1) * 128, :], r=([C["Ht"]] if "Ht" in C else []), w=[ht])
        sq = C["sq"]
        ss = C["ss"][tt % 2]
        p.op("act", lambda: nc.scalar.activation(out=sq[:], in_=ht[:], func=AF.Square, accum_out=ss[:, 0:1]),
             r=[ht], w=[sq, ss])
        p.op("dve", lambda: nc.vector.tensor_scalar(out=ss[:, 1:2], in0=ss[:, 0:1], scalar1=1.0 / D, scalar2=EPS,
                                                    op0=ALU.mult, op1=ALU.add), r=[ss], w=[ss])
        p.op("pool", lambda: nc.gpsimd.tensor_tensor(out=ss[:, 3:4], in0=ss[:, 1:2], in1=C["mhalf"][:], op=ALU.pow), r=[ss, C["mhalf"]], w=[ss])
        ut = C["ut"][tt % 2]
        p.op("dve", lambda: nc.vector.scalar_tensor_tensor(out=ut[:], in0=ht[:], scalar=ss[:, 3:4], in1=prew_bc[:],
                                                           op0=ALU.mult, op1=ALU.mult), r=[ht, ss, prew_bc], w=[ut])
        for half in range(2):
            pst = C["pst"][half]
            for k4 in range(4):
                kc = half * 4 + k4
                p.op("pe", lambda: nc.tensor.transpose(pst[:, k4 * 128:(k4 + 1) * 128], ut[:, kc * 128:(kc + 1) * 128],
                                                       ident[:]), r=[ut, ident], w=[pst])
            dst = uT.t[:, half * 4:(half + 1) * 4, tt * 128:(tt + 1) * 128]
            src = pst.t[:].rearrange("p (k t) -> p k t", k=4)
            if half == 0:
                p.op("act", lambda: nc.scalar.copy(out=dst, in_=src), r=[pst], w=[uT])
            else:
                p.op("dve", lambda: nc.vector.tensor_copy(out=dst, in_=src), r=[pst], w=[uT])
    wv = w_in_l.rearrange("(kc p) n -> p kc n", p=128)
    ncb = (IN_W + 511) // 512 if ncolblk is None else ncolblk
    ev = 0
    def load_w(cb):
        c0 = cb * 512
        cw = min(512, IN_W - c0)
        wst = C["wst"][cb % 2]
        wbf = C["wbf"][cb % 2]
        p.dma(wst[:, :, 0:cw], wv[:, :, c0:c0 + cw], w=[wst], q="sp")
        p.op("pool", lambda: nc.gpsimd.tensor_copy(out=wbf[:, 0:4, 0:cw], in_=wst[:, 0:4, 0:cw]), r=[wst], w=[wbf])
        p.op("pool", lambda: nc.gpsimd.tensor_copy(out=wbf[:, 4:8, 0:cw], in_=wst[:, 4:8, 0:cw]), r=[wst], w=[wbf])

    load_w(0)
    for cb in range(ncb):
        c0 = cb * 512
        cw = min(512, IN_W - c0)
        wbf = C["wbf"][cb % 2]
        if cb + 1 < ncb:
            load_w(cb + 1)
        for tt in range(ntiles):
            psz = C["psz"][ev % 4]
            zo = C["zo"][ev % 4]
            for kc in range(8):
                p.op("pe", lambda: nc.tensor.matmul(psz[:, 0:cw], lhsT=uT[:, kc, tt * 128:(tt + 1) * 128],
                                                    rhs=wbf[:, kc, 0:cw], start=(kc == 0), stop=(kc == 7)),
                     r=[uT, wbf], w=[psz])
            if ev % 2 == 0:
                p.op("act", lambda: nc.scalar.copy(out=zo[:, 0:cw], in_=psz[:, 0:cw]), r=[psz], w=[zo])
            else:
                p.op("dve", lambda: nc.vector.tensor_copy(out=zo[:, 0:cw], in_=psz[:, 0:cw]), r=[psz], w=[zo])
            p.dma(zr(Z, tt * 128, (tt + 1) * 128)[:, c0:c0 + cw], zo[:, 0:cw], r=[zo], w=[C["Zt"]], q="sp")
            ev += 1


def load_bc(p, nc, tl, row_ap, q="sp"):
    p.dma(tl[:], row_ap.partition_broadcast(128), w=[tl], q=q)


def phase_merge(p, nc, l, DR, G, Hin, Hout, out_final=None, tiles=None):
    PS = G["PS"]
    ident = G["ident"]
    Z = DR["Z"]
    with ExitStack() as es:
        T = lambda name, shape, dt, **kw: p.tile("mg_" + name, shape, dt, es=es, **kw)
        wst = T("wst", [128, 8, 1024], F32)
        W = [T("w%d" % i, [128, 8, 1024], BF16) for i in range(4)]
        names = ["w_att_out", "w_rwkv_out", "w_hgrn_out", "w_o"]
        for i in range(4):
            p.dma(wst[:], DR[names[i]][l].rearrange("(kc p) n -> p kc n", p=128), w=[wst])
            p.op("pool", lambda: nc.gpsimd.tensor_copy(out=W[i][:, 0:4, :], in_=wst[:, 0:4, :]), r=[wst], w=[W[i]])
            p.op("act", lambda: nc.scalar.copy(out=W[i][:, 4:8, :], in_=wst[:, 4:8, :]), r=[wst], w=[W[i]])
        mhalf = T("mhalf", [128, 1], F32)
        p.op("pool", lambda: nc.gpsimd.memset(mhalf[:], -0.5), w=[mhalf])
        postw = T("postw", [128, D], F32)
        load_bc(p, nc, postw, DR["post_norm_w"][l:l + 1, :])
        oT = [[T("oT%d_%d" % (b, i), [128, 8, 128], BF16) for i in range(3)] for b in range(3)]
        mg = [T("mgt%d" % i, [128, 3072], F32) for i in range(3)]
        hin = [T("hin%d" % i, [128, D], F32) for i in range(3)]
        ys = [T("y%d" % i, [128, D], F32) for i in range(2)]
        tmp = [T("tmp%d" % i, [128, 512], F32) for i in range(2)]
        yT = T("yT", [128, 8, 128], BF16)
        hn = [T("hn%d" % i, [128, D], F32) for i in range(2)]
        sq = T("sq", [128, 512], F32)
        st = [T("st%d" % i, [128, 8], F32) for i in range(2)]
        OTs = [DR["OT_att"], DR["OT_rwkv"], DR["OT_hgrn"]]
        tl_list = list(range(NS * NT)) if tiles is None else tiles
        cnt = 0
        def stage0(it, tt):
            r0 = tt * 128
            for b in range(3):
                p.dma(oT[b][it % 3][:], OTs[b].rearrange("(kc p) t -> p kc t", p=128)[:, :, r0:r0 + 128],
                      r=[DR["OT_t"][b]], w=[oT[b][it % 3]], q="act")
            m = mg[it % 3]
            p.dma(m[:], zr(Z, r0, r0 + 128)[:, MG_OFF:MG_OFF + 3072], r=[DR["Zt"]], w=[m])
            hi_ = hin[it % 3]
            p.dma(hi_[:], Hin[r0:r0 + 128, :], r=[DR["Ht"]], w=[hi_])

        def stage1(it, tt):
            nonlocal cnt
            y = ys[it % 2]
            r0 = tt * 128
            m = mg[it % 3]
            hi_ = hin[it % 3]
            p.op("act", lambda: nc.scalar.activation(out=m[:], in_=m[:], func=AF.Sigmoid), r=[m], w=[m])
            for b in range(3):
                ob = oT[b][it % 3]
                for half in range(2):
                    ps = PS[cnt % 2]
                    cnt += 1
                    for kc in range(8):
                        p.op("pe", lambda: nc.tensor.matmul(ps[:], lhsT=ob[:, kc, :], rhs=W[b][:, kc, half * 512:(half + 1) * 512],
                                                            start=(kc == 0), stop=(kc == 7)), r=[ob, W[b]], w=[ps])
                    gsl = m[:, b * 1024 + half * 512: b * 1024 + (half + 1) * 512]
                    ysl = y[:, half * 512:(half + 1) * 512]
                    if b == 0:
                        p.op("dve", lambda: nc.vector.tensor_tensor(out=ysl, in0=ps[:], in1=gsl, op=ALU.mult), r=[ps, m], w=[y])
                    else:
                        t_ = tmp[cnt % 2]
                        p.op("dve", lambda: nc.vector.tensor_tensor(out=t_[:], in0=ps[:], in1=gsl, op=ALU.mult), r=[ps, m], w=[t_])
                        p.op("pool", lambda: nc.gpsimd.tensor_tensor(out=ysl, in0=ysl, in1=t_[:], op=ALU.add), r=[t_, y], w=[y])

        def stage2(it, tt):
            y = ys[it % 2]
            r0 = tt * 128
            hi_ = hin[it % 3]
            for half in range(2):
                pst = PS[2 + half]
                for k4 in range(4):
                    kc = half * 4 + k4
                    p.op("pe", lambda: nc.tensor.transpose(pst[:, k4 * 128:(k4 + 1) * 128], y[:, kc * 128:(kc + 1) * 128], ident[:]),
                         r=[y, ident], w=[pst])
                src = pst.t[:].rearrange("p (k t) -> p k t", k=4)
                if half == 0:
                    p.op("act", lambda: nc.scalar.copy(out=yT[:, 0:4, :], in_=src), r=[pst], w=[yT])
                else:
                    p.op("dve", lambda: nc.vector.tensor_copy(out=yT[:, 4:8, :], in_=src), r=[pst], w=[yT])
            s_ = st[it % 2]
            h_ = hn[it % 2]
            pso = [PS[4 + (it % 2) * 2], PS[5 + (it % 2) * 2]]
            for half in range(2):
                for kc in range(8):
                    p.op("pe", lambda: nc.tensor.matmul(pso[half][:], lhsT=yT[:, kc, :], rhs=W[3][:, kc, half * 512:(half + 1) * 512],
                                                        start=(kc == 0), stop=(kc == 7)), r=[yT, W[3]], w=[pso[half]])
                p.op("act", lambda: nc.scalar.activation(out=sq[:], in_=pso[half][:], func=AF.Square, accum_out=s_[:, half:half + 1]),
                     r=[pso[half]], w=[sq, s_])
            p.op("dve", lambda: nc.vector.tensor_tensor(out=s_[:, 2:3], in0=s_[:, 0:1], in1=s_[:, 1:2], op=ALU.add), r=[s_], w=[s_])
            p.op("dve", lambda: nc.vector.tensor_scalar(out=s_[:, 3:4], in0=s_[:, 2:3], scalar1=1.0 / D, scalar2=EPS,
                                                        op0=ALU.mult, op1=ALU.add), r=[s_], w=[s_])
            p.op("pool", lambda: nc.gpsimd.tensor_tensor(out=s_[:, 5:6], in0=s_[:, 3:4], in1=mhalf[:], op=ALU.pow), r=[s_, mhalf], w=[s_])
            for half in range(2):
                hs = h_[:, half * 512:(half + 1) * 512]
                p.op("dve", lambda: nc.vector.scalar_tensor_tensor(out=hs, in0=pso[half][:], scalar=s_[:, 5:6],
                                                                   in1=postw[:, half * 512:(half + 1) * 512],
                                                                   op0=ALU.mult, op1=ALU.mult), r=[pso[half], s_, postw], w=[h_])
            p.op("pool", lambda: nc.gpsimd.tensor_tensor(out=h_[:], in0=h_[:], in1=hi_[:], op=ALU.add), r=[h_, hi_], w=[h_])
            n = tt % NT
            if n == 0:
                p.op("pool", lambda: nc.gpsimd.memset(h_[0:96, :], 0.0), r=[], w=[h_])
                p.op("pool", lambda: nc.gpsimd.memset(h_[96:112, :], 0.0), r=[], w=[h_])
            if out_final is None:
                p.dma(Hout[r0:r0 + 128, :], h_[:], r=[h_], w=[DR["Ht2"]])
            else:
                s = tt // NT
                if n == 0:
                    pass
                else:
                    p.dma(out_final[s, (n - 1) * 128:n * 128, :], h_[:], r=[h_], w=[DR["Outt"]])
        for it, tt in enumerate(tl_list):
            if it == 0:
                stage0(0, tt)
                if len(tl_list) > 1:
                    stage0(1, tl_list[1])
                stage1(0, tt)
            if it + 2 < len(tl_list):
                stage0(it + 2, tl_list[it + 2])
            if it + 1 < len(tl_list):
                stage1(it + 1, tl_list[it + 1])
            stage2(it, tt)
        p.barrier()


def phase_att(p, nc, l, DR, G, pairs=None, qts=None):
    import math as _m
    PS = G["PS"]
    ident = G["ident"]
    tri = G["tri_le"]
    Z = DR["Z"]
    lam_init = 0.8 - 0.6 * _m.exp(-0.3 * l)
    with ExitStack() as es:
        T = lambda name, shape, dt, **kw: p.tile("at_" + name, shape, dt, es=es, **kw)
        cos2 = T("cos2", [128, NT, 128], F32)
        sin2 = T("sin2", [128, NT, 128], F32)
        p.dma(cos2[:], DR["cos2"], w=[cos2])
        p.dma(sin2[:], DR["sin2"], w=[sin2])
        lv = T("lv", [128, 4, 64], F32)
        for i, n in enumerate(("lambda_q1", "lambda_k1", "lambda_q2", "lambda_k2")):
            p.dma(lv[:, i, :], DR[n][l:l + 1, :].partition_broadcast(128), w=[lv])
        lt = T("lt", [128, 2, 64], F32)
        ls = T("ls", [128, 8], F32)
        p.op("dve", lambda: nc.vector.tensor_tensor(out=lt[:, 0, :], in0=lv[:, 0, :], in1=lv[:, 1, :], op=ALU.mult), r=[lv], w=[lt])
        p.op("dve", lambda: nc.vector.tensor_tensor(out=lt[:, 1, :], in0=lv[:, 2, :], in1=lv[:, 3, :], op=ALU.mult), r=[lv], w=[lt])
        p.op("dve", lambda: nc.vector.tensor_reduce(out=ls[:, 0:2], in_=lt[:], axis=AX.X, op=ALU.add), r=[lt], w=[ls])
        p.op("act", lambda: nc.scalar.activation(out=ls[:, 2:4], in_=ls[:, 0:2], func=AF.Exp), r=[ls], w=[ls])
        p.op("dve", lambda: nc.vector.tensor_tensor(out=ls[:, 4:5], in0=ls[:, 3:4], in1=ls[:, 2:3], op=ALU.subtract), r=[ls], w=[ls])
        p.op("dve", lambda: nc.vector.tensor_scalar(out=ls[:, 5:6], in0=ls[:, 4:5], scalar1=-lam_init, scalar2=None, op0=ALU.add), r=[ls], w=[ls])
        neg_lam = ls[:, 5:6]
        normw = T("normw", [128, 128], F32)
        load_bc(p, nc, normw, DR["att_norm_w"][l:l + 1, :])
        p.op("dve", lambda: nc.vector.tensor_scalar(out=normw[:], in0=normw[:], scalar1=(1.0 - lam_init), scalar2=None, op0=ALU.mult),
             r=[normw], w=[normw])
        SETS = []
        for kb in range(2):
            Bf = {}
            for nm in ("qraw", "kraw", "vraw", "graw"):
                Bf[nm] = T("%s%d" % (nm, kb), [128, NT, 128], F32)
            Bf["qT"] = T("qT%d" % kb, [128, P], BF16)
            Bf["kT"] = T("kT%d" % kb, [128, P], BF16)
            Bf["v1"] = T("v1%d" % kb, [128, NT, 130], BF16)
            v1_ = Bf["v1"]
            p.op("pool", lambda: nc.gpsimd.memset(v1_[:, :, 128:130], 1.0), w=[v1_])
            p.op("pool", lambda: nc.gpsimd.memset(v1_[0:96, 0, 128:130], 0.0), w=[v1_])
            p.op("pool", lambda: nc.gpsimd.memset(v1_[96:112, 0, 128:130], 0.0), w=[v1_])
            SETS.append(Bf)
        t1 = T("t1", [128, NT, 128], F32)
        t2 = T("t2", [128, NT, 128], F32)
        obufs = [T("obuf%d" % i, [128, NT, 128], F32) for i in range(2)]
        oTb = T("oTb", [128, P], BF16)
        pT = [T("pT%d" % i, [128, 512], BF16) for i in range(5)]
        SB = [PS[0], PS[1], PS[6]]
        mhalf = T("mhalf", [128, 1], F32)
        p.op("pool", lambda: nc.gpsimd.memset(mhalf[:], -0.5), w=[mhalf])
        o_ = [T("o%d" % i, [128, 128], F32) for i in range(2)]
        sq = T("sq", [128, 128], F32)
        rs = [T("rs%d" % i, [128, 12], F32) for i in range(2)]
        pr = [(s, j) for s in range(NS) for j in range(8)] if pairs is None else pairs

        def setup_load(s, j, Bf):
            qraw, kraw, vraw, graw = (Bf[k] for k in ("qraw", "kraw", "vraw", "graw"))
            zs = zr(Z, s * P, (s + 1) * P).rearrange("(n p) c -> p n c", p=128)
            p.dma(qraw[:], zs[:, :, j * 128:(j + 1) * 128], r=[DR["Zt"]], w=[qraw])
            p.dma(kraw[:], zs[:, :, 1024 + j * 128:1024 + (j + 1) * 128], r=[DR["Zt"]], w=[kraw], q="act")
            p.dma(vraw[:], zs[:, :, 2048 + j * 128:2048 + (j + 1) * 128], r=[DR["Zt"]], w=[vraw])
            p.dma(graw[:], zs[:, :, 3072 + j * 128:3072 + (j + 1) * 128], r=[DR["Zt"]], w=[graw], q="act")

        def setup(s, j, Bf):
            qraw, kraw, vraw, graw, qT, kT, v1 = (Bf[k] for k in ("qraw", "kraw", "vraw", "graw", "qT", "kT", "v1"))
            for (raw, dstT) in ((qraw, qT), (kraw, kT)):
                E = nc.vector
                p.op("dve", lambda: E.tensor_tensor(out=t1[:], in0=raw[:], in1=cos2[:], op=ALU.mult), r=[raw, cos2], w=[t1])
                yield
                rv = raw.t[:].rearrange("p n (g h d) -> p (n g) h d", g=2, h=2)
                sv = sin2.t[:].rearrange("p n (g h d) -> p (n g) h d", g=2, h=2)
                tv = t2.t[:].rearrange("p n (g h d) -> p (n g) h d", g=2, h=2)
                p.op("dve", lambda: E.tensor_tensor(out=tv[:, :, 0, :], in0=rv[:, :, 1, :], in1=sv[:, :, 0, :], op=ALU.mult), r=[raw, sin2], w=[t2])
                yield
                p.op("dve", lambda: E.tensor_tensor(out=tv[:, :, 1, :], in0=rv[:, :, 0, :], in1=sv[:, :, 1, :], op=ALU.mult), r=[raw, sin2], w=[t2])
                yield
                p.op("dve", lambda: E.tensor_tensor(out=t1[:], in0=t1[:], in1=t2[:], op=ALU.add), r=[t1, t2], w=[t1])
                yield
                for n0 in range(0, NT, 4):
                    nn = min(4, NT - n0)
                    pst = PS[7]
                    for i in range(nn):
                        p.op("pe", lambda: nc.tensor.transpose(pst[:, i * 128:(i + 1) * 128], t1[:, n0 + i, :], ident[:]), r=[t1, ident], w=[pst])
                    p.op("dve", lambda: nc.vector.tensor_copy(out=dstT[:, n0 * 128:(n0 + nn) * 128], in_=pst[:, 0:nn * 128]), r=[pst], w=[dstT])
                    yield
            p.op("dve", lambda: nc.vector.tensor_copy(out=v1[:, :, 0:128], in_=vraw[:]), r=[vraw], w=[v1])
            yield
            p.op("act", lambda: nc.scalar.activation(out=graw[:], in_=graw[:], func=AF.Silu), r=[graw], w=[graw])
            yield

        def mainloop(s, j, Bf, inj, obuf):
            qT, kT, v1, graw = Bf["qT"], Bf["kT"], Bf["v1"], Bf["graw"]
            groups = []
            for qt in (range(NT) if qts is None else qts):
                for k0 in range(0, qt + 1, 4):
                    for g in range(2):
                        groups.append((qt, g, k0, min(k0 + 4, qt + 1)))

            def emit_scores(grp, gi):
                qt, g, k0, k1 = grp
                pss = SB[gi % 3]
                for i, kt in enumerate(range(k0, k1)):
                    p.op("pe", lambda: nc.tensor.matmul(pss[:, i * 128:(i + 1) * 128], lhsT=kT[g * 64:(g + 1) * 64, kt * 128:(kt + 1) * 128],
                                                        rhs=qT[g * 64:(g + 1) * 64, qt * 128:(qt + 1) * 128], start=True, stop=True),
                         r=[kT, qT], w=[pss])
                pt = pT[gi % 5]
                n = k1 - k0
                p.op("act", lambda: nc.scalar.activation(out=pt[:, 0:n * 128], in_=pss[:, 0:n * 128], func=AF.Exp, scale=0.125), r=[pss], w=[pt])
                if k1 - 1 == qt:
                    i = qt - k0
                    p.op("pool", lambda: nc.gpsimd.tensor_tensor(out=pt[:, i * 128:(i + 1) * 128], in0=pt[:, i * 128:(i + 1) * 128], in1=tri[:], op=ALU.mult),
                         r=[pt, tri], w=[pt])

            def emit_pv(grp, gi):
                qt, g, k0, k1 = grp
                pt = pT[gi % 5]
                pso = PS[2 + (qt % 2) * 2 + g]
                for i, kt in enumerate(range(k0, k1)):
                    p.op("pe", lambda: nc.tensor.matmul(pso[:, 0:129], lhsT=pt[:, i * 128:(i + 1) * 128], rhs=v1[:, kt, 0:129],
                                                        start=(kt == 0), stop=(kt == qt)), r=[pt, v1], w=[pso])
                if k1 - 1 == qt and g == 1:
                    epilogue(qt)
                    if qt >= 3:
                        inj()

            def epilogue(qt):
                O1 = PS[2 + (qt % 2) * 2]
                O2 = PS[2 + (qt % 2) * 2 + 1]
                r_ = rs[qt % 2]
                o = o_[qt % 2]
                p.op("dve", lambda: nc.vector.tensor_scalar(out=r_[:, 0:1], in0=O1[:, 128:129], scalar1=1e-30, scalar2=None, op0=ALU.max), r=[O1], w=[r_])
                p.op("dve", lambda: nc.vector.tensor_scalar(out=r_[:, 1:2], in0=O2[:, 128:129], scalar1=1e-30, scalar2=None, op0=ALU.max), r=[O2], w=[r_])
                p.op("dve", lambda: nc.vector.reciprocal(out=r_[:, 2:4], in_=r_[:, 0:2]), r=[r_], w=[r_])
                p.op("dve", lambda: nc.vector.tensor_tensor(out=r_[:, 4:5], in0=r_[:, 3:4], in1=neg_lam, op=ALU.mult), r=[r_, ls], w=[r_])
                p.op("dve", lambda: nc.vector.tensor_scalar(out=o[:], in0=O1[:, 0:128], scalar1=r_[:, 2:3], scalar2=None, op0=ALU.mult), r=[O1, r_], w=[o])
                p.op("dve", lambda: nc.vector.scalar_tensor_tensor(out=o[:], in0=O2[:, 0:128], scalar=r_[:, 4:5], in1=o[:], op0=ALU.mult, op1=ALU.add),
                     r=[O2, r_, o], w=[o])
                p.op("act", lambda: nc.scalar.activation(out=sq[:], in_=o[:], func=AF.Square, accum_out=r_[:, 5:6]), r=[o], w=[sq, r_])
                p.op("dve", lambda: nc.vector.tensor_scalar(out=r_[:, 6:7], in0=r_[:, 5:6], scalar1=1.0 / 128, scalar2=EPS, op0=ALU.mult, op1=ALU.add), r=[r_], w=[r_])
                p.op("pool", lambda: nc.gpsimd.tensor_tensor(out=r_[:, 8:9], in0=r_[:, 6:7], in1=mhalf[:], op=ALU.pow), r=[r_, mhalf], w=[r_])
                p.op("dve", lambda: nc.vector.scalar_tensor_tensor(out=o[:], in0=o[:], scalar=r_[:, 8:9], in1=normw[:], op0=ALU.mult, op1=ALU.mult),
                     r=[o, r_, normw], w=[o])
                p.op("pool", lambda: nc.gpsimd.tensor_tensor(out=obuf[:, qt, :], in0=o[:], in1=graw[:, qt, :], op=ALU.mult), r=[o, graw], w=[obuf])

            AHEAD = 2
            for gi, grp in enumerate(groups):
                emit_scores(grp, gi)
                if gi >= AHEAD:
                    emit_pv(groups[gi - AHEAD], gi - AHEAD)
            for gi in range(max(0, len(groups) - AHEAD), len(groups)):
                emit_pv(groups[gi], gi)

        def finish(s, j, obuf):
            for _ in range(3):
                yield
            for n0 in range(0, NT, 4):
                nn = min(4, NT - n0)
                pst = PS[7]
                for i in range(nn):
                    p.op("pe", lambda: nc.tensor.transpose(pst[:, i * 128:(i + 1) * 128], obuf[:, n0 + i, :], ident[:]), r=[obuf, ident], w=[pst])
                if (n0 // 4) % 2 == 0:
                    p.op("act", lambda: nc.scalar.copy(out=oTb[:, n0 * 128:(n0 + nn) * 128], in_=pst[:, 0:nn * 128]), r=[pst], w=[oTb])
                else:
                    p.op("dve", lambda: nc.vector.tensor_copy(out=oTb[:, n0 * 128:(n0 + nn) * 128], in_=pst[:, 0:nn * 128]), r=[pst], w=[oTb])
                yield
            p.dma(DR["OT_att"][j * 128:(j + 1) * 128, s * P:(s + 1) * P], oTb[:], r=[oTb], w=[DR["OT_t"][0]])
            yield

        def run_all(gen):
            for _ in gen:
                pass

        setup_load(pr[0][0], pr[0][1], SETS[0])
        run_all(setup(pr[0][0], pr[0][1], SETS[0]))
        def chain(*gens):
            for g_ in gens:
                if g_ is not None:
                    for _ in g_:
                        yield

        for k, (s, j) in enumerate(pr):
            if k + 1 < len(pr):
                setup_load(pr[k + 1][0], pr[k + 1][1], SETS[(k + 1) % 2])
            g_fin = finish(pr[k - 1][0], pr[k - 1][1], obufs[(k - 1) % 2]) if k > 0 else None
            g_set = setup(pr[k + 1][0], pr[k + 1][1], SETS[(k + 1) % 2]) if k + 1 < len(pr) else None
            bg = chain(g_fin, g_set)

            def inj(n=1):
                for _ in range(n):
                    try:
                        next(bg)
                    except StopIteration:
                        return
            mainloop(s, j, SETS[k % 2], inj, obufs[k % 2])
            run_all(bg)
        run_all(finish(pr[-1][0], pr[-1][1], obufs[(len(pr) - 1) % 2]))
        p.barrier()


def phase_hgrn(p, nc, l, DR, G, seqs=None, zoff=HG_OFF, ntl=NT, pipelined=True):
    PS = G["PS"]
    ident = G["ident"]
    b_le, b_gt, bind = G["b_le"], G["b_gt"], G["bind"]
    Z = DR["Z"]
    with ExitStack() as es:
        T = lambda name, shape, dt, **kw: p.tile("hg_" + name, shape, dt, es=es, **kw)
        hnw = T("hnw", [128, 128], F32)
        load_bc(p, nc, hnw, DR["hgrn_norm_w"][l:l + 1, :])
        identb = T("identb", [128, 128], BF16)
        p.op("pool", lambda: nc.gpsimd.tensor_copy(out=identb[:], in_=ident[:]), r=[ident], w=[identb])
        mhalf = T("mhalf", [128, 8], F32)
        p.op("pool", lambda: nc.gpsimd.memset(mhalf[:], -0.5), w=[mhalf])
        if l > 0:
            lb = T("lb", [128, 1024], F32)
            oml = T("oml", [128, 1024], F32)
            x0 = T("x0", [128, 1024], F32)
            load_bc(p, nc, x0, DR["hgrn_lower_bounds"][0:1, :])
            load_bc(p, nc, lb, DR["hgrn_lower_bounds"][1:2, :])
            p.op("dve", lambda: nc.vector.tensor_tensor(out=x0[:], in0=lb[:], in1=x0[:], op=ALU.subtract), r=[lb, x0], w=[x0])
            p.op("act", lambda: nc.scalar.activation(out=lb[:], in_=x0[:], func=AF.Sigmoid), r=[x0], w=[lb])
            p.op("act", lambda: nc.scalar.activation(out=oml[:], in_=x0[:], func=AF.Sigmoid, scale=-1.0), r=[x0], w=[oml])
        zts = [T("z%d" % i, [128, 4096], F32) for i in range(2)]
        f = T("f", [128, 1024], F32)
        kf = T("kf", [128, 1024], F32)
        logf = T("logf", [128, 1024], F32)
        ex = T("ex", [128, 1024], F32)
        SETS = []
        for k in range(2):
            B = {}
            for nm in ("qtb", "ktb", "kh", "khz", "vb"):
                B[nm] = T("%s%d" % (nm, k), [128, 1024], BF16)
            B["gs"] = T("gs%d" % k, [128, 1024], F32)
            B["gC"] = T("gC%d" % k, [128, 32], F32)
            SETS.append(B)
        qkT = [T("qkT%d" % j, [128, 256], BF16) for j in range(8)]
        qz = [T("qz%d" % j, [128, 64], BF16) for j in range(8)]
        for j in range(8):
            p.op("pool", lambda: nc.gpsimd.memset(qz[j][:], 0.0), w=[qz[j]])
        attm = [T("attm%d" % j, [128, 128], BF16) for j in range(8)]
        S = [T("S%d" % j, [128, 128], F32) for j in range(8)]
        Sb = [T("Sb%d" % j, [128, 128], BF16) for j in range(8)]
        osb = T("osb", [128, 1024], F32)
        obuf = T("obuf", [128, 1024], F32)
        oTb = T("oTb", [128, 8, 128], BF16)
        st = T("st", [128, 4, 8], F32)

        def v8(ap):
            return ap.rearrange("p (h d) -> p h d", d=128)

        def prep_load(s, n, z_):
            r0 = s * P + n * 128
            p.dma(z_[:], zr(Z, r0, r0 + 128)[:, zoff:zoff + 4096], r=[DR["Zt"]], w=[z_])
            yield

        def prep(s, n, B, z_):
            r0 = s * P + n * 128
            p.op("act", lambda: nc.scalar.activation(out=f[:], in_=z_[:, 1024:2048], func=AF.Sigmoid), r=[z_], w=[f])
            yield
            if l > 0:
                p.op("dve", lambda: nc.vector.tensor_tensor(out=f[:], in0=f[:], in1=oml[:], op=ALU.mult), r=[f, oml], w=[f])
                yield
                p.op("pool", lambda: nc.gpsimd.tensor_tensor(out=f[:], in0=f[:], in1=lb[:], op=ALU.add), r=[f, lb], w=[f])
                yield
            p.op("dve", lambda: nc.vector.tensor_scalar(out=kf[:], in0=f[:], scalar1=-1.0, scalar2=1.0, op0=ALU.mult, op1=ALU.add), r=[f], w=[kf])
            p.op("act", lambda: nc.scalar.activation(out=logf[:], in_=f[:], func=AF.Ln), r=[f], w=[logf])
            yield
            p.op("act", lambda: nc.scalar.activation(out=z_[:, 0:1024], in_=z_[:, 0:1024], func=AF.Silu), r=[z_], w=[z_])
            p.op("act", lambda: nc.scalar.activation(out=B["gs"][:], in_=z_[:, 3072:4096], func=AF.Silu), r=[z_], w=[B["gs"]])
            yield
            p.op("pool", lambda: nc.gpsimd.tensor_copy(out=B["vb"][:], in_=z_[:, 2048:3072]), r=[z_], w=[B["vb"]])
            yield
            for j in range(8):
                p.op("pe", lambda: nc.tensor.matmul(PS[4][:, j * 4:(j + 1) * 4], lhsT=logf[:, j * 128:(j + 1) * 128], rhs=bind[:, 0:4], start=True, stop=True),
                     r=[logf, bind], w=[PS[4]])
            p.op("act", lambda: nc.scalar.activation(out=B["gC"][:], in_=PS[4][:, 0:32], func=AF.Exp), r=[PS[4]], w=[B["gC"]])
            yield
            for half in range(2):
                hs = slice(half * 512, (half + 1) * 512)
                pl = PS[4 + half]
                p.op("pe", lambda: nc.tensor.matmul(pl[:], lhsT=b_le[:], rhs=logf[:, hs], start=True, stop=True), r=[b_le, logf], w=[pl])
                p.op("act", lambda: nc.scalar.activation(out=ex[:, hs], in_=pl[:], func=AF.Exp), r=[pl], w=[ex])
                p.op("dve", lambda: nc.vector.tensor_tensor(out=B["qtb"][:, hs], in0=z_[:, hs], in1=ex[:, hs], op=ALU.mult), r=[z_, ex], w=[B["qtb"]])
                p.op("act", lambda: nc.scalar.activation(out=ex[:, hs], in_=pl[:], func=AF.Exp, scale=-1.0), r=[pl], w=[ex])
                p.op("dve", lambda: nc.vector.tensor_tensor(out=B["ktb"][:, hs], in0=kf[:, hs], in1=ex[:, hs], op=ALU.mult), r=[kf, ex], w=[B["ktb"]])
                p.op("pe", lambda: nc.tensor.matmul(pl[:], lhsT=b_gt[:], rhs=logf[:, hs], start=True, stop=True), r=[b_gt, logf], w=[pl])
                p.op("act", lambda: nc.scalar.activation(out=ex[:, hs], in_=pl[:], func=AF.Exp), r=[pl], w=[ex])
                p.op("dve", lambda: nc.vector.tensor_tensor(out=B["kh"][:, hs], in0=kf[:, hs], in1=ex[:, hs], op=ALU.mult), r=[kf, ex], w=[B["kh"]])
                yield
            p.op("pool", lambda: nc.gpsimd.tensor_scalar(out=B["khz"][:], in0=B["kh"][:], scalar1=bind[:, 3:4], scalar2=None, op0=ALU.mult), r=[B["kh"], bind], w=[B["khz"]])
            yield

        def main(s, n, B, inj, flush_out=lambda: None):
            if n == 0:
                for j in range(8):
                    p.op("pool", lambda: nc.gpsimd.memset(S[j][:], 0.0), w=[S[j]])
                    p.op("pool", lambda: nc.gpsimd.memset(Sb[j][:], 0.0), w=[Sb[j]])
            vb, kh, khz, gC = B["vb"], B["kh"], B["khz"], B["gC"]
            for j in range(8):
                js = slice(j * 128, (j + 1) * 128)
                pw = PS[4 + (j % 2)]
                pwb = pw.t[:].bitcast(BF16)
                p.op("pe", lambda: nc.tensor.transpose(pwb[:, 0:128], B["qtb"][:, js], identb[:]), r=[B["qtb"], identb], w=[pw])
                p.op("pe", lambda: nc.tensor.transpose(pwb[:, 128:256], B["ktb"][:, js], identb[:]), r=[B["ktb"], identb], w=[pw])
                p.op("act", lambda: nc.scalar.copy(out=qkT[j][:], in_=pwb[:, 0:256]), r=[pw], w=[qkT[j]])
                p.op("pool", lambda: nc.gpsimd.tensor_copy(out=qz[j][:, 32:64], in_=qkT[j][:, 96:128]), r=[qkT[j]], w=[qz[j]])
                p.op("pe", lambda: nc.tensor.matmul(pw[:, 256:384], lhsT=qkT[j][:, 128:256], rhs=qkT[j][:, 0:128], start=True, stop=True),
                     r=[qkT[j]], w=[pw])
                p.op("dve", lambda: nc.vector.tensor_tensor(out=attm[j][:], in0=pw[:, 256:384], in1=b_le[:], op=ALU.mult), r=[pw, b_le], w=[attm[j]])
                if j % 2 == 1:
                    inj()
            for j in range(8):
                js = slice(j * 128, (j + 1) * 128)
                po = PS[6 + j // 4]
                p.op("pe", lambda: nc.tensor.matmul(po[:, (j % 4) * 128:(j % 4 + 1) * 128], lhsT=attm[j][:], rhs=vb[:, js],
                                                    start=(j % 4 == 0), stop=False, skip_group_check=True), r=[attm[j], vb], w=[po])
            for c in range(4):
                cs = slice(c * 32, (c + 1) * 32)
                for j in range(8):
                    js = slice(j * 128, (j + 1) * 128)
                    po = PS[6 + j // 4]
                    if c < 3:
                        p.op("pe", lambda: nc.tensor.matmul(po[cs, (j % 4) * 128:(j % 4 + 1) * 128], lhsT=qkT[j][:, cs], rhs=Sb[j][:],
                                                            start=False, stop=False, skip_group_check=True), r=[qkT[j], Sb[j]], w=[po])
                    else:
                        p.op("pe", lambda: nc.tensor.matmul(po[64:128, (j % 4) * 128:(j % 4 + 1) * 128], lhsT=qz[j][:], rhs=Sb[j][:],
                                                            start=False, stop=True, skip_group_check=True), r=[qz[j], Sb[j]], w=[po])
                    pss = PS[j % 4]
                    pc = slice((j // 4) * 128, (j // 4 + 1) * 128)
                    if c < 3:
                        p.op("pe", lambda: nc.tensor.matmul(pss[:, pc], lhsT=kh[cs, js], rhs=vb[cs, js],
                                                            start=True, stop=True), r=[kh, vb], w=[pss])
                    else:
                        p.op("pe", lambda: nc.tensor.matmul(pss[:, pc], lhsT=khz[64:128, js], rhs=vb[64:128, js],
                                                            start=True, stop=True), r=[khz, vb], w=[pss])
                    p.op("dve", lambda: nc.vector.scalar_tensor_tensor(out=S[j][:], in0=S[j][:], scalar=gC[:, j * 4 + c:j * 4 + c + 1],
                                                                       in1=pss[:, pc], op0=ALU.mult, op1=ALU.add),
                         r=[S[j], gC, pss], w=[S[j]])
                    p.op("act", lambda: nc.scalar.copy(out=Sb[j][:], in_=S[j][:]), r=[S[j]], w=[Sb[j]])
                    if j % 4 == 3:
                        inj()
            flush_out()
            for half in range(2):
                hs = slice(half * 512, (half + 1) * 512)
                p.op("act", lambda: nc.scalar.copy(out=osb[:, hs], in_=PS[6 + half][:]), r=[PS[6 + half]], w=[osb])

        def outp(s, n, B):
            r0 = s * P + n * 128
            p.op("pool", lambda: nc.gpsimd.tensor_tensor(out=obuf[:], in0=osb[:], in1=osb[:], op=ALU.mult), r=[osb], w=[obuf])
            yield
            p.op("dve", lambda: nc.vector.tensor_reduce(out=st[:, 0, :], in_=v8(obuf.t[:]), axis=AX.X, op=ALU.add), r=[obuf], w=[st])
            p.op("dve", lambda: nc.vector.tensor_scalar(out=st[:, 1, :], in0=st[:, 0, :], scalar1=1.0 / 128, scalar2=EPS, op0=ALU.mult, op1=ALU.add), r=[st], w=[st])
            p.op("pool", lambda: nc.gpsimd.tensor_tensor(out=st[:, 2, :], in0=st[:, 1, :], in1=mhalf[:], op=ALU.pow), r=[st, mhalf], w=[st])
            yield
            p.op("dve", lambda: nc.vector.tensor_tensor(out=v8(osb.t[:]), in0=v8(osb.t[:]), in1=st[:, 2, :].unsqueeze(2).broadcast_to([128, 8, 128]), op=ALU.mult),
                 r=[osb, st], w=[osb])
            yield
            p.op("pool", lambda: nc.gpsimd.tensor_tensor(out=v8(osb.t[:]), in0=v8(osb.t[:]), in1=hnw.t[:].unsqueeze(1).broadcast_to([128, 8, 128]), op=ALU.mult),
                 r=[osb, hnw], w=[osb])
            yield
            p.op("pool", lambda: nc.gpsimd.tensor_tensor(out=obuf[:], in0=osb[:], in1=B["gs"][:], op=ALU.mult), r=[osb, B["gs"]], w=[obuf])
            yield
            for half in range(2):
                pst = PS[4 + half]
                for k4 in range(4):
                    kc = half * 4 + k4
                    p.op("pe", lambda: nc.tensor.transpose(pst[:, k4 * 128:(k4 + 1) * 128], obuf[:, kc * 128:(kc + 1) * 128], ident[:]), r=[obuf, ident], w=[pst])
                src_ = pst.t[:].rearrange("p (k t) -> p k t", k=4)
                if half == 0:
                    p.op("act", lambda: nc.scalar.copy(out=oTb[:, 0:4, :], in_=src_), r=[pst], w=[oTb])
                else:
                    p.op("dve", lambda: nc.vector.tensor_copy(out=oTb[:, 4:8, :], in_=src_), r=[pst], w=[oTb])
                yield
            p.dma(DR["OT_hgrn"].rearrange("(kc p) t -> p kc t", p=128)[:, :, r0:r0 + 128], oTb[:], r=[oTb], w=[DR["OT_t"][2]])
            yield

        tiles = [(s, n) for s in (range(NS) if seqs is None else seqs) for n in range(ntl)]

        def run_all(gen):
            for _ in gen:
                pass

        def chain(*gens):
            for g in gens:
                if g is not None:
                    for _ in g:
                        yield

        if not pipelined:
            for idx, (s, n) in enumerate(tiles):
                B = SETS[idx % 2]
                run_all(prep_load(s, n, zts[idx % 2]))
                run_all(prep(s, n, B, zts[idx % 2]))
                main(s, n, B, lambda: None)
                run_all(outp(s, n, B))
        else:
            run_all(prep_load(tiles[0][0], tiles[0][1], zts[0]))
            run_all(prep(tiles[0][0], tiles[0][1], SETS[0], zts[0]))
            if len(tiles) > 1:
                run_all(prep_load(tiles[1][0], tiles[1][1], zts[1]))
            for idx, (s, n) in enumerate(tiles):
                B = SETS[idx % 2]
                g_out = outp(tiles[idx - 1][0], tiles[idx - 1][1], SETS[(idx - 1) % 2]) if idx > 0 else None
                g_prep = prep(tiles[idx + 1][0], tiles[idx + 1][1], SETS[(idx + 1) % 2], zts[(idx + 1) % 2]) if idx + 1 < len(tiles) else None
                g_load = prep_load(tiles[idx + 2][0], tiles[idx + 2][1], zts[idx % 2]) if idx + 2 < len(tiles) else None
                g_lo = chain(g_out)
                bg = chain(g_lo, g_prep, g_load)

                def inj(k=1):
                    for _ in range(k):
                        try:
                            next(bg)
                        except StopIteration:
                            return
                main(s, n, B, inj, lambda: run_all(g_lo))
                run_all(bg)
            run_all(outp(tiles[-1][0], tiles[-1][1], SETS[(len(tiles) - 1) % 2]))
        p.barrier()


C0 = -0.6065306597126334


def phase_rwkv(p, nc, l, DR, G, seqs=None, zoff=RW_OFF, ntl=NT, dbg=None, stop=None, pipelined=True):
    PS = G["PS"]
    ident = G["ident"]
    tri_le, tri_lt, tri_gt, ones = G["tri_le"], G["tri_lt"], G["tri_gt"], G["ones"]
    Z = DR["Z"]

    def bc(ap16, n=16):
        return ap16.unsqueeze(2).broadcast_to([128, n, 64])

    def v3(ap):
        return ap.rearrange("p (h d) -> p h d", d=64)

    with ExitStack() as es:
        T = lambda name, shape, dt, **kw: p.tile("rw_" + name, shape, dt, es=es, **kw)
        mu = T("mu", [128, RW_W], F32)
        load_bc(p, nc, mu, DR["rwkv_mu"][l:l + 1, :])
        prm = {}
        for i, n_ in enumerate(("rwkv_w0", "rwkv_a0", "rwkv_k_k", "rwkv_k_a", "rwkv_gn_w", "rwkv_gn_b")):
            prm[n_] = T(n_, [128, 1024], F32)
            load_bc(p, nc, prm[n_], DR[n_][l:l + 1, :], q=("sp" if i % 2 == 0 else "act"))
        prm["rwkv_r_k"] = T("rwkv_r_k", [128, 1024], F32)
        load_bc(p, nc, prm["rwkv_r_k"], DR["rwkv_r_k"][l:l + 1].rearrange("o h d -> o (h d)"))
        w_up = T("w_up", [64, 1024], F32)
        a_up = T("a_up", [64, 1024], F32)
        p.dma(w_up[:], DR["rwkv_w_up"][l], w=[w_up])
        p.dma(a_up[:], DR["rwkv_a_up"][l], w=[a_up])
        mA = T("mA", [128, 384], F32)
        mB = T("mB", [128, 256], F32)
        p.op("pool", lambda: nc.gpsimd.tensor_copy(out=mA[:, 0:128], in_=tri_lt[:]), r=[tri_lt], w=[mA])
        p.op("pool", lambda: nc.gpsimd.tensor_copy(out=mA[:, 128:256], in_=tri_gt[:]), r=[tri_gt], w=[mA])
        p.op("pool", lambda: nc.gpsimd.tensor_copy(out=mA[:, 256:384], in_=tri_lt[:]), r=[tri_lt], w=[mA])
        p.op("pool", lambda: nc.gpsimd.tensor_copy(out=mB[:, 0:128], in_=tri_le[:]), r=[tri_le], w=[mB])
        p.op("pool", lambda: nc.gpsimd.tensor_copy(out=mB[:, 128:256], in_=tri_le[:]), r=[tri_le], w=[mB])
        identb = T("identb", [128, 128], BF16)
        p.op("pool", lambda: nc.gpsimd.tensor_copy(out=identb[:], in_=ident[:]), r=[ident], w=[identb])
        mhalf16 = T("mhalf16", [128, 16], F32)
        p.op("pool", lambda: nc.gpsimd.memset(mhalf16[:], -0.5), w=[mhalf16])
        zc = T("zc", [128, RW_W], F32)
        zp = T("zp", [128, RW_W], F32)
        sw = T("sw", [128, 1024], F32)
        a_ = T("a_", [128, 1024], F32)
        kk = T("kk", [128, 1024], F32)
        kp = T("kp", [128, 1024], F32)
        b_ = T("b_", [128, 1024], F32)
        twT = T("twT", [64, 256], F32)
        ex = zp.t[:, 0:1024]
        tq = zp.t[:, 3072:4096]
        SETS = []
        for k in range(2):
            B = {}
            for nm in ("rtb", "ktb", "atb", "btb", "khat", "bhat", "vb"):
                B[nm] = T("%s%d" % (nm, k), [128, 1024], BF16)
            B["gs"] = T("gs%d" % k, [128, 1024], F32)
            B["gC"] = T("gC%d" % k, [128, 16], F32)
            B["sm"] = T("sm%d" % k, [128, 4, 16], F32)
            SETS.append(B)
        sm2 = T("sm2", [128, 8, 16], F32)
        fT = [T("fT%d" % i, [128, 4, 128], BF16) for i in range(8)]
        Ar = [T("Ar%d" % h, [128, 256], BF16) for h in range(16)]
        Aak = [T("Aak%d" % i, [128, 128], BF16) for i in range(8)]
        PP = [[T("PP%d_%d" % (i, k), [128, 256], BF16) for k in range(2)] for i in range(8)]
        X = [[T("X%d_%d" % (i, k), [128, 128], BF16) for k in range(2)] for i in range(8)]
        Gt = [T("Gt%d" % i, [128, 64], BF16) for i in range(8)]
        Wt = T("Wt", [128, 1024], F32)
        YT = [T("YT%d" % i, [128, 128], BF16) for i in range(8)]
        ST = T("ST", [128, 8, 64], F32)
        STbd = T("STbd", [128, 8, 128], BF16)
        Ub = T("Ub", [128, 1024], BF16)
        osb = T("osb", [128, 1024], F32)
        obuf = T("obuf", [128, 1024], F32)
        oTb = T("oTb", [128, 8, 128], BF16)

        def prep_load(s, n):
            r0 = s * P + n * 128
            p.dma(zc[:], zr(Z, r0, r0 + 128)[:, zoff:zoff + RW_W], r=[DR["Zt"]], w=[zc])
            if n == 0:
                p.op("pool", lambda: nc.gpsimd.memset(zp[0:1, :], 0.0), w=[zp])
                p.dma(zp[1:128, :], zr(Z, r0, r0 + 127)[:, zoff:zoff + RW_W], r=[DR["Zt"]], w=[zp], q="act")
            else:
                p.dma(zp[:], zr(Z, r0 - 1, r0 + 127)[:, zoff:zoff + RW_W], r=[DR["Zt"]], w=[zp], q="act")
            yield

        def prep(s, n, B):
            r0 = s * P + n * 128
            hs_ = slice(4096, RW_W)
            p.op("dve", lambda: nc.vector.tensor_tensor(out=zp[:, hs_], in0=zp[:, hs_], in1=zc[:, hs_], op=ALU.subtract), r=[zp, zc], w=[zp])
            p.op("dve", lambda: nc.vector.tensor_tensor(out=zp[:, hs_], in0=zp[:, hs_], in1=mu[:, hs_], op=ALU.mult), r=[zp, mu], w=[zp])
            p.op("dve", lambda: nc.vector.tensor_tensor(out=zc[:, hs_], in0=zc[:, hs_], in1=zp[:, hs_], op=ALU.add), r=[zp, zc], w=[zc])
            p.op("act", lambda: nc.scalar.activation(out=zc[:, 4096:4160], in_=zc[:, 4096:4160], func=AF.Tanh), r=[zc], w=[zc])
            yield
            h1 = slice(0, 2560)
            h2 = slice(2560, 4096)
            for k3 in range(3):
                for (sl, eng) in ((h1, "dve"), (h2, "pool")):
                    E = nc.vector if eng == "dve" else nc.gpsimd
                    if k3 == 0:
                        p.op(eng, lambda: E.tensor_tensor(out=zp[:, sl], in0=zp[:, sl], in1=zc[:, sl], op=ALU.subtract), r=[zp, zc], w=[zp])
                    elif k3 == 1:
                        p.op(eng, lambda: E.tensor_tensor(out=zp[:, sl], in0=zp[:, sl], in1=mu[:, sl], op=ALU.mult), r=[zp, mu], w=[zp])
                    else:
                        p.op(eng, lambda: E.tensor_tensor(out=zc[:, sl], in0=zc[:, sl], in1=zp[:, sl], op=ALU.add), r=[zp, zc], w=[zc])
                    yield
            rr = zc.t[:, 0:1024]
            rk = zc.t[:, 1024:2048]
            rv = zc.t[:, 2048:3072]
            rg = zc.t[:, 3072:4096]
            p.op("pe", lambda: nc.tensor.transpose(PS[6][0:64, 0:128], zc[:, 4096:4160], ident[:]), r=[zc, ident], w=[PS[6]])
            p.op("pe", lambda: nc.tensor.transpose(PS[6][0:64, 128:256], zc[:, 4160:4224], ident[:]), r=[zc, ident], w=[PS[6]])
            p.op("act", lambda: nc.scalar.copy(out=twT[:], in_=PS[6][0:64, 0:256]), r=[PS[6]], w=[twT])
            yield
            for half in range(2):
                hs = slice(half * 512, (half + 1) * 512)
                p.op("pe", lambda: nc.tensor.matmul(PS[6][:], lhsT=twT[:, 0:128], rhs=w_up[:, hs], start=True, stop=True), r=[twT, w_up], w=[PS[6]])
                p.op("dve", lambda: nc.vector.tensor_tensor(out=sw[:, hs], in0=PS[6][:], in1=prm["rwkv_w0"][:, hs], op=ALU.add),
                     r=[PS[6], prm["rwkv_w0"]], w=[sw])
                yield
            for half in range(2):
                hs = slice(half * 512, (half + 1) * 512)
                p.op("pe", lambda: nc.tensor.matmul(PS[7][:], lhsT=twT[:, 128:256], rhs=a_up[:, hs], start=True, stop=True), r=[twT, a_up], w=[PS[7]])
                p.op("dve", lambda: nc.vector.tensor_tensor(out=a_[:, hs], in0=PS[7][:], in1=prm["rwkv_a0"][:, hs], op=ALU.add),
                     r=[PS[7], prm["rwkv_a0"]], w=[a_])
                yield
            p.op("act", lambda: nc.scalar.activation(out=sw[:], in_=sw[:], func=AF.Sigmoid), r=[sw], w=[sw])
            p.op("act", lambda: nc.scalar.activation(out=a_[:], in_=a_[:], func=AF.Sigmoid), r=[a_], w=[a_])
            yield
            p.op("dve", lambda: nc.vector.tensor_tensor(out=kk[:], in0=rk, in1=prm["rwkv_k_k"][:], op=ALU.mult), r=[zc, prm["rwkv_k_k"]], w=[kk])
            yield
            p.op("pool", lambda: nc.gpsimd.tensor_tensor(out=kp[:], in0=kk[:], in1=kk[:], op=ALU.mult), r=[kk], w=[kp])
            yield
            p.op("dve", lambda: nc.vector.tensor_reduce(out=sm2[:, 0, :], in_=v3(kp.t[:]), axis=AX.X, op=ALU.add), r=[kp], w=[sm2])
            p.op("dve", lambda: nc.vector.tensor_scalar(out=sm2[:, 1, :], in0=sm2[:, 0, :], scalar1=1e-24, scalar2=None, op0=ALU.max), r=[sm2], w=[sm2])
            p.op("pool", lambda: nc.gpsimd.tensor_tensor(out=sm2[:, 2, :], in0=sm2[:, 1, :], in1=mhalf16[:], op=ALU.pow), r=[sm2, mhalf16], w=[sm2])
            yield
            p.op("dve", lambda: nc.vector.tensor_tensor(out=v3(kk.t[:]), in0=v3(kk.t[:]), in1=bc(sm2[:, 2, :]), op=ALU.mult), r=[kk, sm2], w=[kk])
            yield
            p.op("dve", lambda: nc.vector.scalar_tensor_tensor(out=kp[:], in0=a_[:], scalar=-1.0, in1=prm["rwkv_k_a"][:], op0=ALU.add, op1=ALU.mult),
                 r=[a_, prm["rwkv_k_a"]], w=[kp])
            yield
            p.op("dve", lambda: nc.vector.scalar_tensor_tensor(out=kp[:], in0=kp[:], scalar=1.0, in1=rk, op0=ALU.add, op1=ALU.mult), r=[kp, zc], w=[kp])
            yield
            p.op("pool", lambda: nc.gpsimd.tensor_tensor(out=b_[:], in0=kk[:], in1=a_[:], op=ALU.mult), r=[kk, a_], w=[b_])
            yield
            p.op("pool", lambda: nc.gpsimd.tensor_tensor(out=tq, in0=rr, in1=kp[:], op=ALU.mult), r=[zc, kp], w=[zp])
            yield
            p.op("pool", lambda: nc.gpsimd.tensor_tensor(out=tq, in0=tq, in1=prm["rwkv_r_k"][:], op=ALU.mult), r=[zp, prm["rwkv_r_k"]], w=[zp])
            yield
            p.op("dve", lambda: nc.vector.tensor_reduce(out=B["sm"][:, 3, :], in_=v3(tq), axis=AX.X, op=ALU.add), r=[zp], w=[B["sm"]])
            p.op("pool", lambda: nc.gpsimd.tensor_copy(out=B["vb"][:], in_=rv), r=[zc], w=[B["vb"]])
            yield
            p.op("act", lambda: nc.scalar.activation(out=B["gs"][:], in_=rg, func=AF.Silu), r=[zc], w=[B["gs"]])
            yield
            for _ in range(8):
                yield
            for half in range(2):
                hs = slice(half * 512, (half + 1) * 512)
                pl = PS[6 + half]
                p.op("pe", lambda: nc.tensor.matmul(pl[:], lhsT=tri_le[:], rhs=sw[:, hs], start=True, stop=True), r=[tri_le, sw], w=[pl])
                p.op("act", lambda: nc.scalar.activation(out=ex[:, hs], in_=pl[:], func=AF.Exp, scale=C0), r=[pl], w=[zp])
                p.op("dve", lambda: nc.vector.tensor_tensor(out=B["rtb"][:, hs], in0=rr[:, hs], in1=ex[:, hs], op=ALU.mult), r=[zc, zp], w=[B["rtb"]])
                p.op("dve", lambda: nc.vector.tensor_tensor(out=tq[:, hs], in0=pl[:], in1=sw[:, hs], op=ALU.subtract), r=[pl, sw], w=[zp])
                p.op("act", lambda: nc.scalar.activation(out=tq[:, hs], in_=tq[:, hs], func=AF.Exp, scale=C0), r=[zp], w=[zp])
                p.op("dve", lambda: nc.vector.scalar_tensor_tensor(out=B["atb"][:, hs], in0=kk[:, hs], scalar=-1.0, in1=tq[:, hs], op0=ALU.mult, op1=ALU.mult),
                     r=[kk, zp], w=[B["atb"]])
                p.op("act", lambda: nc.scalar.activation(out=ex[:, hs], in_=pl[:], func=AF.Exp, scale=-C0), r=[pl], w=[zp])
                p.op("pool", lambda: nc.gpsimd.tensor_tensor(out=B["btb"][:, hs], in0=b_[:, hs], in1=ex[:, hs], op=ALU.mult), r=[b_, zp], w=[B["btb"]])
                p.op("dve", lambda: nc.vector.tensor_tensor(out=B["ktb"][:, hs], in0=kp[:, hs], in1=ex[:, hs], op=ALU.mult), r=[kp, zp], w=[B["ktb"]])
                p.op("pe", lambda: nc.tensor.matmul(pl[:], lhsT=tri_gt[:], rhs=sw[:, hs], start=True, stop=True), r=[tri_gt, sw], w=[pl])
                p.op("act", lambda: nc.scalar.activation(out=ex[:, hs], in_=pl[:], func=AF.Exp, scale=C0), r=[pl], w=[zp])
                p.op("pool", lambda: nc.gpsimd.tensor_tensor(out=B["khat"][:, hs], in0=kp[:, hs], in1=ex[:, hs], op=ALU.mult), r=[kp, zp], w=[B["khat"]])
                p.op("dve", lambda: nc.vector.tensor_tensor(out=B["bhat"][:, hs], in0=b_[:, hs], in1=ex[:, hs], op=ALU.mult), r=[b_, zp], w=[B["bhat"]])
                yield
            for pp in range(8):
                p.op("pe", lambda: nc.tensor.matmul(PS[6][:, pp * 2:pp * 2 + 2], lhsT=sw[:, pp * 128:(pp + 1) * 128], rhs=ones[:, 0:2],
                                                    start=True, stop=True), r=[sw, ones], w=[PS[6]])
            p.op("act", lambda: nc.scalar.activation(out=B["gC"][:], in_=PS[6][:, 0:16], func=AF.Exp, scale=C0), r=[PS[6]], w=[B["gC"]])
            yield

        def main(s, n, B, inj, flush_out=lambda: None):
            if n == 0:
                p.op("pool", lambda: nc.gpsimd.memset(ST[:], 0.0), w=[ST])
                p.op("pool", lambda: nc.gpsimd.memset(STbd[:], 0.0), w=[STbd])
            src = (B["rtb"], B["ktb"], B["atb"], B["btb"])
            for g8 in range(2):
                for pi in range(4):
                    pp = g8 * 4 + pi
                    cs = slice(pp * 128, (pp + 1) * 128)
                    pw = PS[6 + pi // 2]
                    pwb = pw.t[:].bitcast(BF16)
                    off = (pi % 2) * 512
                    for k in range(4):
                        p.op("pe", lambda: nc.tensor.transpose(pwb[:, off + k * 128:off + (k + 1) * 128], src[k][:, cs], identb[:]), r=[src[k], identb], w=[pw])
                    if pi % 2 == 0:
                        p.op("act", lambda: nc.scalar.copy(out=fT[pp].t[:].rearrange("p a t -> p (a t)"), in_=pwb[:, off:off + 512]), r=[pw], w=[fT[pp]])
                    else:
                        p.op("dve", lambda: nc.vector.tensor_copy(out=fT[pp].t[:].rearrange("p a t -> p (a t)"), in_=pwb[:, off:off + 512]), r=[pw], w=[fT[pp]])
                for i in range(8):
                    h = g8 * 8 + i
                    pp, e = h // 2, h % 2
                    es_ = slice(e * 64, (e + 1) * 64)
                    rT, kT, aT, bT = (fT[pp][es_, k, :] for k in range(4))
                    pa = PS[i]
                    p.op("pe", lambda: nc.tensor.matmul(pa[:, 0:128], lhsT=bT, rhs=aT, start=True, stop=True), r=[fT[pp]], w=[pa])
                    p.op("pe", lambda: nc.tensor.matmul(pa[:, 128:256], lhsT=aT, rhs=bT, start=True, stop=True), r=[fT[pp]], w=[pa])
                    p.op("pe", lambda: nc.tensor.matmul(pa[:, 256:384], lhsT=kT, rhs=aT, start=True, stop=True), r=[fT[pp]], w=[pa])
                    p.op("dve", lambda: nc.vector.tensor_tensor(out=PP[i][0][:], in0=pa[:, 0:256], in1=mA[:, 0:256], op=ALU.mult), r=[pa, mA], w=[PP[i][0]])
                    p.op("dve", lambda: nc.vector.tensor_tensor(out=Aak[i][:], in0=pa[:, 256:384], in1=mA[:, 256:384], op=ALU.mult), r=[pa, mA], w=[Aak[i]])
                    p.op("dve", lambda: nc.vector.tensor_tensor(out=X[i][0][:], in0=PP[i][0][:, 0:128], in1=identb[:], op=ALU.add), r=[PP[i][0], identb], w=[X[i][0]])
                inj()
                for i in range(8):
                    h = g8 * 8 + i
                    pp, e = h // 2, h % 2
                    es_ = slice(e * 64, (e + 1) * 64)
                    rT, kT, aT, bT = (fT[pp][es_, k, :] for k in range(4))
                    pa = PS[i]
                    p.op("pe", lambda: nc.tensor.matmul(pa[:, 0:128], lhsT=bT, rhs=rT, start=True, stop=True), r=[fT[pp]], w=[pa])
                    p.op("pe", lambda: nc.tensor.matmul(pa[:, 128:256], lhsT=kT, rhs=rT, start=True, stop=True), r=[fT[pp]], w=[pa])
                    p.op("dve", lambda: nc.vector.tensor_tensor(out=Ar[h][:], in0=pa[:, 0:256], in1=mB[:], op=ALU.mult), r=[pa, mB], w=[Ar[h]])
                inj()
                for lv in range(1, 7):
                    cur, nxt = (lv - 1) % 2, lv % 2
                    for i in range(8):
                        pq = PS[i]
                        Pm, PTm = PP[i][cur][:, 0:128], PP[i][cur][:, 128:256]
                        if lv < 6:
                            p.op("pe", lambda: nc.tensor.matmul(pq[:, 0:128], lhsT=PTm, rhs=Pm, start=True, stop=True), r=[PP[i][cur]], w=[pq])
                        p.op("pe", lambda: nc.tensor.matmul(pq[:, 128:256], lhsT=Pm, rhs=PTm, start=True, stop=True), r=[PP[i][cur]], w=[pq])
                        if i % 2 == 0:
                            p.op("act", lambda: nc.scalar.copy(out=PP[i][nxt][:], in_=pq[:, 0:256]), r=[pq], w=[PP[i][nxt]])
                        else:
                            p.op("dve", lambda: nc.vector.tensor_copy(out=PP[i][nxt][:], in_=pq[:, 0:256]), r=[pq], w=[PP[i][nxt]])
                        if i % 8 == 7:
                            inj()
                    for i in range(8):
                        px = PS[i]
                        p.op("pe", lambda: nc.tensor.matmul(px[:, 256:384], lhsT=identb[:], rhs=X[i][cur][:], start=True, stop=False), r=[identb, X[i][cur]], w=[px])
                        p.op("pe", lambda: nc.tensor.matmul(px[:, 256:384], lhsT=PP[i][nxt][:, 128:256], rhs=X[i][cur][:], start=False, stop=True),
                             r=[PP[i][nxt], X[i][cur]], w=[px])
                        if i % 2 == 1:
                            p.op("act", lambda: nc.scalar.copy(out=X[i][nxt][:], in_=px[:, 256:384]), r=[px], w=[X[i][nxt]])
                        else:
                            p.op("dve", lambda: nc.vector.tensor_copy(out=X[i][nxt][:], in_=px[:, 256:384]), r=[px], w=[X[i][nxt]])
                        if i % 8 == 7:
                            inj()
                for i in range(8):
                    h = g8 * 8 + i
                    hc = slice(h * 64, (h + 1) * 64)
                    p.op("pe", lambda: nc.tensor.matmul(PS[i][:, 0:64], lhsT=Aak[i][:], rhs=B["vb"][:, hc], start=True, stop=True), r=[Aak[i], B["vb"]], w=[PS[i]])
                    if i % 2 == 0:
                        p.op("act", lambda: nc.scalar.copy(out=Gt[i][:], in_=PS[i][:, 0:64]), r=[PS[i]], w=[Gt[i]])
                    else:
                        p.op("dve", lambda: nc.vector.tensor_copy(out=Gt[i][:], in_=PS[i][:, 0:64]), r=[PS[i]], w=[Gt[i]])
                for i in range(8):
                    h = g8 * 8 + i
                    pp, e = h // 2, h % 2
                    hc = slice(h * 64, (h + 1) * 64)
                    py = PS[i]
                    p.op("pe", lambda: nc.tensor.matmul(py[:, 64:128], lhsT=X[i][0][:], rhs=Gt[i][:], start=True, stop=True),
                         r=[X[i][0], Gt[i]], w=[py])
                    p.op("pe", lambda: nc.tensor.matmul(py[:, 128:256], lhsT=B["atb"][:, pp * 128:(pp + 1) * 128], rhs=X[i][0][:], start=True, stop=True),
                         r=[B["atb"], X[i][0]], w=[py])
                    p.op("act", lambda: nc.scalar.copy(out=Wt[:, hc], in_=py[:, 64:128]), r=[py], w=[Wt])
                    p.op("dve", lambda: nc.vector.tensor_copy(out=YT[pp][e * 64:(e + 1) * 64, :], in_=py[e * 64:(e + 1) * 64, 128:256]), r=[py], w=[YT[pp]])
                inj()
            for pp in range(8):
                p.op("pe", lambda: nc.tensor.matmul(PS[pp // 4][:, (pp % 4) * 128:(pp % 4 + 1) * 128], lhsT=YT[pp][:], rhs=STbd[:, pp, :], start=True, stop=True),
                     r=[YT[pp], STbd], w=[PS[pp // 4]])
            for half in range(2):
                hs = slice(half * 512, (half + 1) * 512)
                p.op("dve", lambda: nc.vector.tensor_tensor(out=Ub[:, hs], in0=PS[half][:], in1=Wt[:, hs], op=ALU.add), r=[PS[half], Wt], w=[Ub])
            for pp in range(8):
                po = PS[2 + pp // 4]
                p.op("pe", lambda: nc.tensor.matmul(po[:, (pp % 4) * 128:(pp % 4 + 1) * 128], lhsT=fT[pp][:, 0, :], rhs=STbd[:, pp, :],
                                                    start=(pp % 4 == 0), stop=False, skip_group_check=True), r=[fT[pp], STbd], w=[po])
            for h in range(16):
                hc = slice(h * 64, (h + 1) * 64)
                po = PS[2 + h // 8]
                oc = slice((h % 8) * 64, (h % 8 + 1) * 64)
                p.op("pe", lambda: nc.tensor.matmul(po[:, oc], lhsT=Ar[h][:, 128:256], rhs=B["vb"][:, hc], start=False, stop=False, skip_group_check=True),
                     r=[Ar[h], B["vb"]], w=[po])
                p.op("pe", lambda: nc.tensor.matmul(po[:, oc], lhsT=Ar[h][:, 0:128], rhs=Ub[:, hc], start=False, stop=True, skip_group_check=True),
                     r=[Ar[h], Ub], w=[po])
            for h in range(16):
                pp, e = h // 2, h % 2
                es_ = slice(e * 64, (e + 1) * 64)
                hc = slice(h * 64, (h + 1) * 64)
                p.op("pe", lambda: nc.tensor.matmul(PS[4][es_, pp * 64:(pp + 1) * 64], lhsT=B["bhat"][:, hc], rhs=Ub[:, hc], start=(pp == 0), stop=False, skip_group_check=True),
                     r=[B["bhat"], Ub], w=[PS[4]])
                p.op("pe", lambda: nc.tensor.matmul(PS[4][es_, pp * 64:(pp + 1) * 64], lhsT=B["khat"][:, hc], rhs=B["vb"][:, hc], start=False, stop=True, skip_group_check=True),
                     r=[B["khat"], B["vb"]], w=[PS[4]])
            gC = B["gC"]
            p.op("dve", lambda: nc.vector.tensor_tensor(out=ST[:], in0=ST[:], in1=gC.t[:, 0:16].rearrange("p (a b) -> p a b", b=2)[:, :, 0:1].broadcast_to([128, 8, 64]), op=ALU.mult),
                 r=[ST, gC], w=[ST])
            p.op("dve", lambda: nc.vector.tensor_tensor(out=ST.t[:].rearrange("p a v -> p (a v)"), in0=ST.t[:].rearrange("p a v -> p (a v)"), in1=PS[4][:], op=ALU.add),
                 r=[ST, PS[4]], w=[ST])
            p.op("act", lambda: nc.scalar.copy(out=STbd[0:64, :, 0:64], in_=ST[0:64, :, :]), r=[ST], w=[STbd])
            p.op("pool", lambda: nc.gpsimd.tensor_copy(out=STbd[64:128, :, 64:128], in_=ST[64:128, :, :]), r=[ST], w=[STbd])
            flush_out()
            for half in range(2):
                hs = slice(half * 512, (half + 1) * 512)
                p.op("act", lambda: nc.scalar.copy(out=osb[:, hs], in_=PS[2 + half][:]), r=[PS[2 + half]], w=[osb])

        def outp(s, n, B):
            r0 = s * P + n * 128
            sm = sm2
            if dbg is not None:
                p.dma(dbg[r0:r0 + 128, :], osb[:], r=[osb], w=[DR["dbgt"]])
            p.op("dve", lambda: nc.vector.tensor_reduce(out=sm[:, 4, :], in_=v3(osb.t[:]), axis=AX.X, op=ALU.add), r=[osb], w=[sm])
            yield
            p.op("pool", lambda: nc.gpsimd.tensor_tensor(out=obuf[:], in0=osb[:], in1=osb[:], op=ALU.mult), r=[osb], w=[obuf])
            yield
            p.op("dve", lambda: nc.vector.tensor_reduce(out=sm[:, 5, :], in_=v3(obuf.t[:]), axis=AX.X, op=ALU.add), r=[obuf], w=[sm])
            p.op("dve", lambda: nc.vector.tensor_scalar(out=sm[:, 4, :], in0=sm[:, 4, :], scalar1=1.0 / 64, scalar2=None, op0=ALU.mult), r=[sm], w=[sm])
            p.op("dve", lambda: nc.vector.tensor_tensor(out=sm[:, 6, :], in0=sm[:, 4, :], in1=sm[:, 4, :], op=ALU.mult), r=[sm], w=[sm])
            p.op("dve", lambda: nc.vector.scalar_tensor_tensor(out=sm[:, 5, :], in0=sm[:, 5, :], scalar=1.0 / 64, in1=sm[:, 6, :], op0=ALU.mult, op1=ALU.subtract),
                 r=[sm], w=[sm])
            yield
            p.op("dve", lambda: nc.vector.tensor_scalar(out=sm[:, 5, :], in0=sm[:, 5, :], scalar1=GN_EPS, scalar2=None, op0=ALU.add), r=[sm], w=[sm])
            p.op("pool", lambda: nc.gpsimd.tensor_tensor(out=sm[:, 7, :], in0=sm[:, 5, :], in1=mhalf16[:], op=ALU.pow), r=[sm, mhalf16], w=[sm])
            yield
            p.op("dve", lambda: nc.vector.tensor_tensor(out=v3(osb.t[:]), in0=v3(osb.t[:]), in1=bc(sm[:, 4, :]), op=ALU.subtract), r=[osb, sm], w=[osb])
            yield
            p.op("dve", lambda: nc.vector.tensor_tensor(out=v3(osb.t[:]), in0=v3(osb.t[:]), in1=bc(sm[:, 7, :]), op=ALU.mult), r=[osb, sm], w=[osb])
            yield
            p.op("pool", lambda: nc.gpsimd.tensor_tensor(out=osb[:], in0=osb[:], in1=prm["rwkv_gn_w"][:], op=ALU.mult), r=[osb, prm["rwkv_gn_w"]], w=[osb])
            yield
            p.op("pool", lambda: nc.gpsimd.tensor_tensor(out=osb[:], in0=osb[:], in1=prm["rwkv_gn_b"][:], op=ALU.add), r=[osb, prm["rwkv_gn_b"]], w=[osb])
            yield
            p.op("dve", lambda: nc.vector.tensor_tensor(out=v3(obuf.t[:]), in0=v3(B["vb"].t[:]), in1=bc(B["sm"][:, 3, :]), op=ALU.mult), r=[B["vb"], B["sm"]], w=[obuf])
            yield
            p.op("pool", lambda: nc.gpsimd.tensor_tensor(out=obuf[:], in0=obuf[:], in1=osb[:], op=ALU.add), r=[obuf, osb], w=[obuf])
            yield
            p.op("pool", lambda: nc.gpsimd.tensor_tensor(out=obuf[:], in0=obuf[:], in1=B["gs"][:], op=ALU.mult), r=[obuf, B["gs"]], w=[obuf])
            yield
            for _ in range(4):
                yield
            for half in range(2):
                pst = PS[6 + half]
                for k4 in range(4):
                    kc = half * 4 + k4
                    p.op("pe", lambda: nc.tensor.transpose(pst[:, k4 * 128:(k4 + 1) * 128], obuf[:, kc * 128:(kc + 1) * 128], ident[:]), r=[obuf, ident], w=[pst])
                src_ = pst.t[:].rearrange("p (k t) -> p k t", k=4)
                if half == 0:
                    p.op("act", lambda: nc.scalar.copy(out=oTb[:, 0:4, :], in_=src_), r=[pst], w=[oTb])
                else:
                    p.op("dve", lambda: nc.vector.tensor_copy(out=oTb[:, 4:8, :], in_=src_), r=[pst], w=[oTb])
                yield
            p.dma(DR["OT_rwkv"].rearrange("(kc p) t -> p kc t", p=128)[:, :, r0:r0 + 128], oTb[:], r=[oTb], w=[DR["OT_t"][1]])
            yield

        tiles = [(s, n) for s in (range(NS) if seqs is None else seqs) for n in range(ntl)]

        def run_all(gen):
            for _ in gen:
                pass

        def chain(*gens):
            for g in gens:
                if g is not None:
                    for _ in g:
                        yield

        if not pipelined:
            for idx, (s, n) in enumerate(tiles):
                B = SETS[idx % 2]
                run_all(prep_load(s, n))
                run_all(prep(s, n, B))
                main(s, n, B, lambda: None)
                run_all(outp(s, n, B))
        else:
            run_all(prep_load(tiles[0][0], tiles[0][1]))
            run_all(prep(tiles[0][0], tiles[0][1], SETS[0]))
            for idx, (s, n) in enumerate(tiles):
                B = SETS[idx % 2]
                g_out = outp(tiles[idx - 1][0], tiles[idx - 1][1], SETS[(idx - 1) % 2]) if idx > 0 else None
                g_load = prep_load(tiles[idx + 1][0], tiles[idx + 1][1]) if idx + 1 < len(tiles) else None
                g_prep = prep(tiles[idx + 1][0], tiles[idx + 1][1], SETS[(idx + 1) % 2]) if idx + 1 < len(tiles) else None
                g_lo = chain(g_load, g_out)
                bg = chain(g_lo, g_prep)

                def inj(k=1):
                    for _ in range(k):
                        try:
                            next(bg)
                        except StopIteration:
                            return
                main(s, n, B, inj, lambda: run_all(g_lo))
                run_all(bg)
            run_all(outp(tiles[-1][0], tiles[-1][1], SETS[(len(tiles) - 1) % 2]))
        p.barrier()


def host_consts():
    c = {}
    i = np.arange(128)
    c["ident"] = np.eye(128, dtype=np.float32)
    c["tri_le"] = (i[:, None] <= i[None, :]).astype(np.float32)
    c["tri_lt"] = (i[:, None] < i[None, :]).astype(np.float32)
    c["tri_gt"] = (i[:, None] > i[None, :]).astype(np.float32)
    same = (i[:, None] // 32) == (i[None, :] // 32)
    c["b_le"] = (same & (i[:, None] <= i[None, :])).astype(np.float32)
    c["b_gt"] = (same & (i[:, None] > i[None, :])).astype(np.float32)
    bind = np.zeros((128, 128), np.float32)
    bind[i, i // 32] = 1.0
    c["bind"] = bind
    c["ones"] = np.ones((128, 128), np.float32)
    t = (np.arange(NT)[None, :] * 128 + np.arange(128)[:, None]).astype(np.float32) - PAD
    inv = (1.0 / (10000.0 ** (np.arange(0, 64, 2, dtype=np.float32) / 64))).astype(np.float32)
    ang = (t[:, :, None] * inv[None, None, :]).astype(np.float32)
    cs, sn = np.cos(ang).astype(np.float32), np.sin(ang).astype(np.float32)
    cos2 = np.zeros((128, NT, 2, 2, 32), np.float32)
    sin2 = np.zeros((128, NT, 2, 2, 32), np.float32)
    cos2[:] = cs[:, :, None, None, :]
    sin2[:, :, :, 0, :] = -sn[:, :, None, :]
    sin2[:, :, :, 1, :] = sn[:, :, None, :]
    c["cos2"] = cos2.reshape(128, NT, 128)
    c["sin2"] = sin2.reshape(128, NT, 128)
    return c


PARAM_SHAPES = {
    "pre_norm_w": [2, D], "post_norm_w": [2, D], "w_in": [2, D, IN_W],
    "lambda_q1": [2, 64], "lambda_k1": [2, 64], "lambda_q2": [2, 64], "lambda_k2": [2, 64], "att_norm_w": [2, 128],
    "rwkv_mu": [2, RW_W], "rwkv_w0": [2, 1024], "rwkv_w_up": [2, 64, 1024], "rwkv_a0": [2, 1024], "rwkv_a_up": [2, 64, 1024],
    "rwkv_k_k": [2, 1024], "rwkv_k_a": [2, 1024], "rwkv_r_k": [2, 16, 64], "rwkv_gn_w": [2, 1024], "rwkv_gn_b": [2, 1024],
    "hgrn_lower_bounds": [2, 1024], "hgrn_norm_w": [2, 128],
    "w_att_out": [2, D, D], "w_rwkv_out": [2, D, D], "w_hgrn_out": [2, D, D], "w_o": [2, D, D],
}
CONST_NAMES = ("ident", "tri_le", "tri_lt", "tri_gt", "b_le", "b_gt", "bind", "ones")


def build_program(nlayers=DEPTH):
    nc = bass.Bass("TRN2", target_bir_lowering=False)
    DR = {}
    H0 = nc.dram_tensor("h0", [ROWS, D], F32, kind="ExternalInput").ap()
    for n, sh in PARAM_SHAPES.items():
        DR[n] = nc.dram_tensor(n, sh, F32, kind="ExternalInput").ap()
    cd = {n: nc.dram_tensor(n, [128, 128], F32, kind="ExternalInput").ap() for n in CONST_NAMES}
    DR["cos2"] = nc.dram_tensor("cos2", [128, NT, 128], F32, kind="ExternalInput").ap()
    DR["sin2"] = nc.dram_tensor("sin2", [128, NT, 128], F32, kind="ExternalInput").ap()
    DR["Z"] = [nc.dram_tensor("Zscr%d" % i, [P, IN_W], F32, kind="Internal").ap() for i in range(NS)]
    for n in ("OT_att", "OT_rwkv", "OT_hgrn"):
        DR[n] = nc.dram_tensor(n + "_scr", [D, ROWS], BF16, kind="Internal").ap()
    Hs = nc.dram_tensor("Hscr", [ROWS, D], F32, kind="Internal").ap()
    OUT = nc.dram_tensor("out", [NS, 2048, D], F32, kind="ExternalOutput").ap()
    DR["OT_t"] = [Tl(None, "ot%d" % i, multi=True) for i in range(3)]
    DR["Zt"] = Tl(None, "Z", multi=True)
    DR["Outt"] = Tl(None, "out", multi=True)
    H_t = [Tl(None, "H0", multi=True), Tl(None, "Hs", multi=True)]
    with ExitStack() as es:
        p = Prog(nc, es)
        G = {}
        for i, n in enumerate(CONST_NAMES):
            G[n] = p.tile(n, [128, 128], F32)
            p.dma(G[n][:], cd[n], w=[G[n]], q=("sp" if i % 2 == 0 else "act"))
        G["PS"] = [p.tile("ps%d" % i, [128, 512], F32, psum=True) for i in range(8)]
        PS = G["PS"]
        for l in range(nlayers):
            Hin = H0 if l == 0 else Hs
            Hin_t = H_t[0] if l == 0 else H_t[1]
            last = (l == nlayers - 1)
            with ExitStack() as es2:
                T = lambda name, shape, dt, **kw: p.tile("pj_" + name, shape, dt, es=es2, **kw)
                C = {"ident": G["ident"]}
                C["uT"] = T("uT", [128, 8, ROWS], BF16, multi=True)
                C["ht"] = [T("ht%d" % i, [128, D], F32) for i in range(2)]
                C["ut"] = [T("ut%d" % i, [128, D], F32) for i in range(2)]
                C["sq"] = T("sq", [128, D], F32)
                C["ss"] = [T("ss%d" % i, [128, 8], F32) for i in range(2)]
                C["pst"] = PS[0:2]
                C["psz"] = PS[2:6]
                C["wst"] = [T("wst%d" % i, [128, 8, 512], F32) for i in range(2)]
                C["wbf"] = [T("wbf%d" % i, [128, 8, 512], BF16) for i in range(2)]
                C["zo"] = [T("zo%d" % i, [128, 512], F32) for i in range(4)]
                C["Zt"] = DR["Zt"]
                C["Ht"] = Hin_t
                C["mhalf"] = T("mhalf", [128, 1], F32)
                p.op("pool", lambda: nc.gpsimd.memset(C["mhalf"][:], -0.5), w=[C["mhalf"]])
                prew_bc = T("prew_bc", [128, D], F32)
                load_bc(p, nc, prew_bc, DR["pre_norm_w"][l:l + 1, :])
                phase_proj(p, nc, Hin, DR["Z"], DR["w_in"][l], prew_bc, C)
                p.barrier()
            phase_att(p, nc, l, DR, G)
            phase_hgrn(p, nc, l, DR, G)
            phase_rwkv(p, nc, l, DR, G)
            DR["Ht"] = Hin_t
            DR["Ht2"] = H_t[1]
            phase_merge(p, nc, l, DR, G, Hin, Hs, out_final=(OUT if last else None))
        p.barrier()
        print("n_inst", p.n_inst)
    return nc


_NC_CACHE = {}


def kernel(**inputs):
    x = np.asarray(inputs["x"], dtype=np.float32)
    meta = np.asarray(inputs["meta_tokens"], dtype=np.float32)
    B = x.shape[0]
    ncores = B // NS
    if "nc" not in _NC_CACHE:
        _NC_CACHE["nc"] = build_program()
    nc = _NC_CACHE["nc"]
    consts = host_consts()
    shared = {n: np.ascontiguousarray(np.asarray(inputs[n], dtype=np.float32)) for n in PARAM_SHAPES}
    shared.update(consts)
    in_maps = []
    for c in range(ncores):
        h0 = np.zeros((NS, P, D), np.float32)
        h0[:, PAD:PAD + 16] = meta[None]
        h0[:, PAD + 16:] = x[c * NS:(c + 1) * NS]
        m = dict(shared)
        m["h0"] = h0.reshape(ROWS, D)
        in_maps.append(m)
    res = run_bass_kernel_spmd(nc, in_maps, core_ids=list(range(ncores)))
    out = np.concatenate([np.asarray(r["out"], dtype=np.float32) for r in res.results], axis=0)
    return out
```

```python
import numpy as np
import ml_dtypes
from contextlib import ExitStack
import concourse.bass as bass
import concourse.mybir as mybir
from concourse.bass_utils import run_bass_kernel_spmd

F32 = mybir.dt.float32
BF16 = mybir.dt.bfloat16
AF = mybir.ActivationFunctionType
ALU = mybir.AluOpType
AX = mybir.AxisListType

D = 1024
NS = 2
NT = 17
P = NT * 128
PAD = 112
ROWS = NS * P
IN_W = 15488
RW_OFF = 4096
RW_W = 4224
HG_OFF = RW_OFF + RW_W
MG_OFF = HG_OFF + 4096
DEPTH = 2
EPS = 1e-6
GN_EPS = 64e-5


def zr(Z, a, b):
    if isinstance(Z, list):
        si = a // P
        assert (b - 1) // P == si
        return Z[si][a - si * P:b - si * P]
    return Z[a:b]


class Tl:
    __slots__ = ("t", "w", "r", "name", "multi", "wd", "excl")

    def __init__(self, t, name="", multi=False, excl=False):
        self.excl = excl
        self.t = t
        self.w = None
        self.r = {}
        self.wd = {}
        self.multi = multi
        self.name = name

    def __getitem__(self, idx):
        return self.t[idx]


class Prog:
    def __init__(self, nc, es, same_engine_sync=True):
        self.nc = nc
        self.es = es
        self.E = {"pe": nc.tensor, "dve": nc.vector, "act": nc.scalar, "pool": nc.gpsimd, "sp": nc.sync}
        self.sem = {e: es.enter_context(nc.semaphore("sem_" + e)) for e in ("pe", "dve", "act", "pool")}
        self.cnt = {e: 0 for e in self.sem}
        self.epoch = {e: 0 for e in self.sem}
        self.old = []
        self.waited = {e: {} for e in self.E}
        self.same = same_engine_sync
        self.dq = {}
        for q, n in (("sp", 40),):
            self.dq[q] = {"sems": [es.enter_context(nc.semaphore("dsem_%s%d" % (q, i))) for i in range(n)],
                          "val": [0] * n, "next": 0}
        self.n_inst = 0
        self.mute = False

    def tile(self, name, shape, dt, psum=False, multi=False, es=None):
        es = es or self.es
        self.n_tiles = getattr(self, "n_tiles", 0) + 1
        name = "%s_%d" % (name, self.n_tiles)
        if psum:
            t = es.enter_context(self.nc.psum_tensor("pt_" + name, shape, dt))
        else:
            t = es.enter_context(self.nc.sbuf_tensor("sb_" + name, shape, dt))
        return Tl(t, name, multi, excl=psum)

    def _wait(self, e, tok):
        if tok is None:
            return
        sem, val, key, owner = tok
        if owner == e and (e == "pe" or not self.same):
            return
        if self.waited[e].get(key, 0) >= val:
            return
        self.E[e].wait_ge(sem, val)
        self.waited[e][key] = val

    def _deps(self, e, r, w):
        for t in r:
            if t.multi:
                for tok in t.wd.values():
                    self._wait(e, tok)
            else:
                self._wait(e, t.w)
        for t in w:
            if not t.multi:
                self._wait(e, t.w)
            for tok in t.r.values():
                self._wait(e, tok)

    def _record(self, tok, r, w):
        for t in r:
            t.r[tok[2]] = tok
        for t in w:
            if t.multi:
                t.wd[tok[2]] = tok
            else:
                t.w = tok
                t.r = {}

    def barrier(self):
        self.mute = False
        toks = [(self.sem[e], self.cnt[e], "c_%s_%d" % (e, self.epoch[e]), e) for e in self.sem if self.cnt[e] > 0]
        toks += self.old
        for q, dq in self.dq.items():
            for i, v in enumerate(dq["val"]):
                if v > 0:
                    toks.append((dq["sems"][i], v, "d_%s%d" % (q, i), None))
        for e in self.E:
            for tok in toks:
                if tok[3] == e:
                    continue
                self._wait(e, tok)

    def op(self, e, fn, r=(), w=()):
        if self.mute:
            return None
        if any(t.excl for t in r):
            w = list(w) + [t for t in r if t.excl and t not in w]
            r = [t for t in r if not t.excl]
        self._deps(e, r, w)
        if self.cnt[e] >= 16000:
            self.old.append((self.sem[e], self.cnt[e], "c_%s_%d" % (e, self.epoch[e]), None))
            self.epoch[e] += 1
            self.sem[e] = self.es.enter_context(self.nc.semaphore("sem_%s_%d" % (e, self.epoch[e])))
            self.cnt[e] = 0
        ins = fn()
        self.cnt[e] += 1
        ins.then_inc(self.sem[e], 1)
        tok = (self.sem[e], self.cnt[e], "c_%s_%d" % (e, self.epoch[e]), e)
        self._record(tok, r, w)
        self.n_inst += 1
        return tok

    def dma(self, out, in_, r=(), w=(), q="sp", **kw):
        if self.mute:
            return None
        q = "sp"
        dq = self.dq[q]
        i = dq["next"]
        dq["next"] = (i + 1) % len(dq["sems"])
        key = "d_%s%d" % (q, i)
        if dq["val"][i] > 0:
            self._wait(q, (dq["sems"][i], dq["val"][i], key, None))
        self._deps(q, r, w)
        dq["val"][i] += 16
        self.E[q].dma_start(out=out, in_=in_, **kw).then_inc(dq["sems"][i], 16)
        tok = (dq["sems"][i], dq["val"][i], key, None)
        self._record(tok, r, w)
        self.n_inst += 1
        return tok

    def finish(self, toks):
        for tok in toks:
            self._wait("sp", tok)


def phase_proj(p, nc, H, Z, w_in_l, prew_bc, C, ntiles=NS * NT, ncolblk=None):
    uT = C["uT"]
    ident = C["ident"]
    for tt in range(ntiles):
        ht = C["ht"][tt % 2]
        p.dma(ht[:], H[tt * 128:(tt + 1) * 128, :], r=([C["Ht"]] if "Ht" in C else []), w=[ht])
        sq = C["sq"]
        ss = C["ss"][tt % 2]
        p.op("act", lambda: nc.scalar.activation(out=sq[:], in_=ht[:], func=AF.Square, accum_out=ss[:, 0:1]),
             r=[ht], w=[sq, ss])
        p.op("dve", lambda: nc.vector.tensor_scalar(out=ss[:, 1:2], in0=ss[:, 0:1], scalar1=1.0 / D, scalar2=EPS,
                                                    op0=ALU.mult, op1=ALU.add), r=[ss], w=[ss])
        p.op("pool", lambda: nc.gpsimd.tensor_tensor(out=ss[:, 3:4], in0=ss[:, 1:2], in1=C["mhalf"][:], op=ALU.pow), r=[ss, C["mhalf"]], w=[ss])
        ut = C["ut"][tt % 2]
        p.op("dve", lambda: nc.vector.scalar_tensor_tensor(out=ut[:], in0=ht[:], scalar=ss[:, 3:4], in1=prew_bc[:],
                                                           op0=ALU.mult, op1=ALU.mult), r=[ht, ss, prew_bc], w=[ut])
        for half in range(2):
            pst = C["pst"][half]
            for k4 in range(4):
                kc = half * 4 + k4
                p.op("pe", lambda: nc.tensor.transpose(pst[:, k4 * 128:(k4 + 1) * 128], ut[:, kc * 128:(kc + 1) * 128],
                                                       ident[:]), r=[ut, ident], w=[pst])
            dst = uT.t[:, half * 4:(half + 1) * 4, tt * 128:(tt + 1) * 128]
            src = pst.t[:].rearrange("p (k t) -> p k t", k=4)
            if half == 0:
                p.op("act", lambda: nc.scalar.copy(out=dst, in_=src), r=[pst], w=[uT])
            else:
                p.op("dve", lambda: nc.vector.tensor_copy(out=dst, in_=src), r=[pst], w=[uT])
    wv = w_in_l.rearrange("(kc p) n -> p kc n", p=128)
    ncb = (IN_W + 511) // 512 if ncolblk is None else ncolblk
    ev = 0
    def load_w(cb):
        c0 = cb * 512
        cw = min(512, IN_W - c0)
        wst = C["wst"][cb % 2]
        wbf = C["wbf"][cb % 2]
        p.dma(wst[:, :, 0:cw], wv[:, :, c0:c0 + cw], w=[wst], q="sp")
        p.op("pool", lambda: nc.gpsimd.tensor_copy(out=wbf[:, 0:4, 0:cw], in_=wst[:, 0:4, 0:cw]), r=[wst], w=[wbf])
        p.op("pool", lambda: nc.gpsimd.tensor_copy(out=wbf[:, 4:8, 0:cw], in_=wst[:, 4:8, 0:cw]), r=[wst], w=[wbf])

    load_w(0)
    for cb in range(ncb):
        c0 = cb * 512
        cw = min(512, IN_W - c0)
        wbf = C["wbf"][cb % 2]
        if cb + 1 < ncb:
            load_w(cb + 1)
        for tt in range(ntiles):
            psz = C["psz"][ev % 4]
            zo = C["zo"][ev % 4]
            for kc in range(8):
                p.op("pe", lambda: nc.tensor.matmul(psz[:, 0:cw], lhsT=uT[:, kc, tt * 128:(tt + 1) * 128],
                                                    rhs=wbf[:, kc, 0:cw], start=(kc == 0), stop=(kc == 7)),
                     r=[uT, wbf], w=[psz])
            if ev % 2 == 0:
                p.op("act", lambda: nc.scalar.copy(out=zo[:, 0:cw], in_=psz[:, 0:cw]), r=[psz], w=[zo])
            else:
                p.op("dve", lambda: nc.vector.tensor_copy(out=zo[:, 0:cw], in_=psz[:, 0:cw]), r=[psz], w=[zo])
            p.dma(zr(Z, tt * 128, (tt + 1) * 128)[:, c0:c0 + cw], zo[:, 0:cw], r=[zo], w=[C["Zt"]], q="sp")
            ev += 1


def load_bc(p, nc, tl, row_ap, q="sp"):
    p.dma(tl[:], row_ap.partition_broadcast(128), w=[tl], q=q)


def phase_merge(p, nc, l, DR, G, Hin, Hout, out_final=None, tiles=None):
    PS = G["PS"]
    ident = G["ident"]
    Z = DR["Z"]
    with ExitStack() as es:
        T = lambda name, shape, dt, **kw: p.tile("mg_" + name, shape, dt, es=es, **kw)
        wst = T("wst", [128, 8, 1024], F32)
        W = [T("w%d" % i, [128, 8, 1024], BF16) for i in range(4)]
        names = ["w_att_out", "w_rwkv_out", "w_hgrn_out", "w_o"]
        for i in range(4):
            p.dma(wst[:], DR[names[i]][l].rearrange("(kc p) n -> p kc n", p=128), w=[wst])
            p.op("pool", lambda: nc.gpsimd.tensor_copy(out=W[i][:, 0:4, :], in_=wst[:, 0:4, :]), r=[wst], w=[W[i]])
            p.op("act", lambda: nc.scalar.copy(out=W[i][:, 4:8, :], in_=wst[:, 4:8, :]), r=[wst], w=[W[i]])
        mhalf = T("mhalf", [128, 1], F32)
        p.op("pool", lambda: nc.gpsimd.memset(mhalf[:], -0.5), w=[mhalf])
        postw = T("postw", [128, D], F32)
        load_bc(p, nc, postw, DR["post_norm_w"][l:l + 1, :])
        oT = [[T("oT%d_%d" % (b, i), [128, 8, 128], BF16) for i in range(3)] for b in range(3)]
        mg = [T("mgt%d" % i, [128, 3072], F32) for i in range(3)]
        hin = [T("hin%d" % i, [128, D], F32) for i in range(3)]
        ys = [T("y%d" % i, [128, D], F32) for i in range(2)]
        tmp = [T("tmp%d" % i, [128, 512], F32) for i in range(2)]
        yT = T("yT", [128, 8, 128], BF16)
        hn = [T("hn%d" % i, [128, D], F32) for i in range(2)]
        sq = T("sq", [128, 512], F32)
        st = [T("st%d" % i, [128, 8], F32) for i in range(2)]
        OTs = [DR["OT_att"], DR["OT_rwkv"], DR["OT_hgrn"]]
        tl_list = list(range(NS * NT)) if tiles is None else tiles
        cnt = 0
        def stage0(it, tt):
            r0 = tt * 128
            for b in range(3):
                p.dma(oT[b][it % 3][:], OTs[b].rearrange("(kc p) t -> p kc t", p=128)[:, :, r0:r0 + 128],
                      r=[DR["OT_t"][b]], w=[oT[b][it % 3]], q="act")
            m = mg[it % 3]
            p.dma(m[:], zr(Z, r0, r0 + 128)[:, MG_OFF:MG_OFF + 3072], r=[DR["Zt"]], w=[m])
            hi_ = hin[it % 3]
            p.dma(hi_[:], Hin[r0:r0 + 128, :], r=[DR["Ht"]], w=[hi_])

        def stage1(it, tt):
            nonlocal cnt
            y = ys[it % 2]
            r0 = tt * 128
            m = mg[it % 3]
            hi_ = hin[it % 3]
            p.op("act", lambda: nc.scalar.activation(out=m[:], in_=m[:], func=AF.Sigmoid), r=[m], w=[m])
            for b in range(3):
                ob = oT[b][it % 3]
                for half in range(2):
                    ps = PS[cnt % 2]
                    cnt += 1
                    for kc in range(8):
                        p.op("pe", lambda: nc.tensor.matmul(ps[:], lhsT=ob[:, kc, :], rhs=W[b][:, kc, half * 512:(half + 1) * 512],
                                                            start=(kc == 0), stop=(kc == 7)), r=[ob, W[b]], w=[ps])
                    gsl = m[:, b * 1024 + half * 512: b * 1024 + (half + 1) * 512]
                    ysl = y[:, half * 512:(half + 1) * 512]
                    if b == 0:
                        p.op("dve", lambda: nc.vector.tensor_tensor(out=ysl, in0=ps[:], in1=gsl, op=ALU.mult), r=[ps, m], w=[y])
                    else:
                        t_ = tmp[cnt % 2]
                        p.op("dve", lambda: nc.vector.tensor_tensor(out=t_[:], in0=ps[:], in1=gsl, op=ALU.mult), r=[ps, m], w=[t_])
                        p.op("pool", lambda: nc.gpsimd.tensor_tensor(out=ysl, in0=ysl, in1=t_[:], op=ALU.add), r=[t_, y], w=[y])

        def stage2(it, tt):
            y = ys[it % 2]
            r0 = tt * 128
            hi_ = hin[it % 3]
            for half in range(2):
                pst = PS[2 + half]
                for k4 in range(4):
                    kc = half * 4 + k4
                    p.op("pe", lambda: nc.tensor.transpose(pst[:, k4 * 128:(k4 + 1) * 128], y[:, kc * 128:(kc + 1) * 128], ident[:]),
                         r=[y, ident], w=[pst])
                src = pst.t[:].rearrange("p (k t) -> p k t", k=4)
                if half == 0:
                    p.op("act", lambda: nc.scalar.copy(out=yT[:, 0:4, :], in_=src), r=[pst], w=[yT])
                else:
                    p.op("dve", lambda: nc.vector.tensor_copy(out=yT[:, 4:8, :], in_=src), r=[pst], w=[yT])
            s_ = st[it % 2]
            h_ = hn[it % 2]
            pso = [PS[4 + (it % 2) * 2], PS[5 + (it % 2) * 2]]
            for half in range(2):
                for kc in range(8):
                    p.op("pe", lambda: nc.tensor.matmul(pso[half][:], lhsT=yT[:, kc, :], rhs=W[3][:, kc, half * 512:(half + 1) * 512],
                                                        start=(kc == 0), stop=(kc == 7)), r=[yT, W[3]], w=[pso[half]])
                p.op("act", lambda: nc.scalar.activation(out=sq[:], in_=pso[half][:], func=AF.Square, accum_out=s_[:, half:half + 1]),
                     r=[pso[half]], w=[sq, s_])
            p.op("dve", lambda: nc.vector.tensor_tensor(out=s_[:, 2:3], in0=s_[:, 0:1], in1=s_[:, 1:2], op=ALU.add), r=[s_], w=[s_])
            p.op("dve", lambda: nc.vector.tensor_scalar(out=s_[:, 3:4], in0=s_[:, 2:3], scalar1=1.0 / D, scalar2=EPS,
                                                        op0=ALU.mult, op1=ALU.add), r=[s_], w=[s_])
            p.op("pool", lambda: nc.gpsimd.tensor_tensor(out=s_[:, 5:6], in0=s_[:, 3:4], in1=mhalf[:], op=ALU.pow), r=[s_, mhalf], w=[s_])
            for half in range(2):
                hs = h_[:, half * 512:(half + 1) * 512]
                p.op("dve", lambda: nc.vector.scalar_tensor_tensor(out=hs, in0=pso[half][:], scalar=s_[:, 5:6],
                                                                   in1=postw[:, half * 512:(half + 1) * 512],
                                                                   op0=ALU.mult, op1=ALU.mult), r=[pso[half], s_, postw], w=[h_])
            p.op("pool", lambda: nc.gpsimd.tensor_tensor(out=h_[:], in0=h_[:], in1=hi_[:], op=ALU.add), r=[h_, hi_], w=[h_])
            n = tt % NT
            if n == 0:
                p.op("pool", lambda: nc.gpsimd.memset(h_[0:96, :], 0.0), r=[], w=[h_])
                p.op("pool", lambda: nc.gpsimd.memset(h_[96:112, :], 0.0), r=[], w=[h_])
            if out_final is None:
                p.dma(Hout[r0:r0 + 128, :], h_[:], r=[h_], w=[DR["Ht2"]])
            else:
                s = tt // NT
                if n == 0:
                    pass
                else:
                    p.dma(out_final[s, (n - 1) * 128:n * 128, :], h_[:], r=[h_], w=[DR["Outt"]])
        for it, tt in enumerate(tl_list):
            if it == 0:
                stage0(0, tt)
                if len(tl_list) > 1:
                    stage0(1, tl_list[1])
                stage1(0, tt)
            if it + 2 < len(tl_list):
                stage0(it + 2, tl_list[it + 2])
            if it + 1 < len(tl_list):
                stage1(it + 1, tl_list[it + 1])
            stage2(it, tt)
        p.barrier()


def phase_att(p, nc, l, DR, G, pairs=None, qts=None):
    import math as _m
    PS = G["PS"]
    ident = G["ident"]
    tri = G["tri_le"]
    Z = DR["Z"]
    lam_init = 0.8 - 0.6 * _m.exp(-0.3 * l)
    with ExitStack() as es:
        T = lambda name, shape, dt, **kw: p.tile("at_" + name, shape, dt, es=es, **kw)
        cos2 = T("cos2", [128, NT, 128], F32)
        sin2 = T("sin2", [128, NT, 128], F32)
        p.dma(cos2[:], DR["cos2"], w=[cos2])
        p.dma(sin2[:], DR["sin2"], w=[sin2])
        lv = T("lv", [128, 4, 64], F32)
        for i, n in enumerate(("lambda_q1", "lambda_k1", "lambda_q2", "lambda_k2")):
            p.dma(lv[:, i, :], DR[n][l:l + 1, :].partition_broadcast(128), w=[lv])
        lt = T("lt", [128, 2, 64], F32)
        ls = T("ls", [128, 8], F32)
        p.op("dve", lambda: nc.vector.tensor_tensor(out=lt[:, 0, :], in0=lv[:, 0, :], in1=lv[:, 1, :], op=ALU.mult), r=[lv], w=[lt])
        p.op("dve", lambda: nc.vector.tensor_tensor(out=lt[:, 1, :], in0=lv[:, 2, :], in1=lv[:, 3, :], op=ALU.mult), r=[lv], w=[lt])
        p.op("dve", lambda: nc.vector.tensor_reduce(out=ls[:, 0:2], in_=lt[:], axis=AX.X, op=ALU.add), r=[lt], w=[ls])
        p.op("act", lambda: nc.scalar.activation(out=ls[:, 2:4], in_=ls[:, 0:2], func=AF.Exp), r=[ls], w=[ls])
        p.op("dve", lambda: nc.vector.tensor_tensor(out=ls[:, 4:5], in0=ls[:, 3:4], in1=ls[:, 2:3], op=ALU.subtract), r=[ls], w=[ls])
        p.op("dve", lambda: nc.vector.tensor_scalar(out=ls[:, 5:6], in0=ls[:, 4:5], scalar1=-lam_init, scalar2=None, op0=ALU.add), r=[ls], w=[ls])
        neg_lam = ls[:, 5:6]
        normw = T("normw", [128, 128], F32)
        load_bc(p, nc, normw, DR["att_norm_w"][l:l + 1, :])
        p.op("dve", lambda: nc.vector.tensor_scalar(out=normw[:], in0=normw[:], scalar1=(1.0 - lam_init), scalar2=None, op0=ALU.mult),
             r=[normw], w=[normw])
        SETS = []
        for kb in range(2):
            Bf = {}
            for nm in ("qraw", "kraw", "vraw", "graw"):
                Bf[nm] = T("%s%d" % (nm, kb), [128, NT, 128], F32)
            Bf["qT"] = T("qT%d" % kb, [128, P], BF16)
            Bf["kT"] = T("kT%d" % kb, [128, P], BF16)
            Bf["v1"] = T("v1%d" % kb, [128, NT, 130], BF16)
            v1_ = Bf["v1"]
            p.op("pool", lambda: nc.gpsimd.memset(v1_[:, :, 128:130], 1.0), w=[v1_])
            p.op("pool", lambda: nc.gpsimd.memset(v1_[0:96, 0, 128:130], 0.0), w=[v1_])
            p.op("pool", lambda: nc.gpsimd.memset(v1_[96:112, 0, 128:130], 0.0), w=[v1_])
            SETS.append(Bf)
        t1 = T("t1", [128, NT, 128], F32)
        t2 = T("t2", [128, NT, 128], F32)
        obufs = [T("obuf%d" % i, [128, NT, 128], F32) for i in range(2)]
        oTb = T("oTb", [128, P], BF16)
        pT = [T("pT%d" % i, [128, 512], BF16) for i in range(5)]
        SB = [PS[0], PS[1], PS[6]]
        mhalf = T("mhalf", [128, 1], F32)
        p.op("pool", lambda: nc.gpsimd.memset(mhalf[:], -0.5), w=[mhalf])
        o_ = [T("o%d" % i, [128, 128], F32) for i in range(2)]
        sq = T("sq", [128, 128], F32)
        rs = [T("rs%d" % i, [128, 12], F32) for i in range(2)]
        pr = [(s, j) for s in range(NS) for j in range(8)] if pairs is None else pairs

        def setup_load(s, j, Bf):
            qraw, kraw, vraw, graw = (Bf[k] for k in ("qraw", "kraw", "vraw", "graw"))
            zs = zr(Z, s * P, (s + 1) * P).rearrange("(n p) c -> p n c", p=128)
            p.dma(qraw[:], zs[:, :, j * 128:(j + 1) * 128], r=[DR["Zt"]], w=[qraw])
            p.dma(kraw[:], zs[:, :, 1024 + j * 128:1024 + (j + 1) * 128], r=[DR["Zt"]], w=[kraw], q="act")
            p.dma(vraw[:], zs[:, :, 2048 + j * 128:2048 + (j + 1) * 128], r=[DR["Zt"]], w=[vraw])
            p.dma(graw[:], zs[:, :, 3072 + j * 128:3072 + (j + 1) * 128], r=[DR["Zt"]], w=[graw], q="act")

        def setup(s, j, Bf):
            qraw, kraw, vraw, graw, qT, kT, v1 = (Bf[k] for k in ("qraw", "kraw", "vraw", "graw", "qT", "kT", "v1"))
            for (raw, dstT) in ((qraw, qT), (kraw, kT)):
                E = nc.vector
                p.op("dve", lambda: E.tensor_tensor(out=t1[:], in0=raw[:], in1=cos2[:], op=ALU.mult), r=[raw, cos2], w=[t1])
                yield
                rv = raw.t[:].rearrange("p n (g h d) -> p (n g) h d", g=2, h=2)
                sv = sin2.t[:].rearrange("p n (g h d) -> p (n g) h d", g=2, h=2)
                tv = t2.t[:].rearrange("p n (g h d) -> p (n g) h d", g=2, h=2)
                p.op("dve", lambda: E.tensor_tensor(out=tv[:, :, 0, :], in0=rv[:, :, 1, :], in1=sv[:, :, 0, :], op=ALU.mult), r=[raw, sin2], w=[t2])
                yield
                p.op("dve", lambda: E.tensor_tensor(out=tv[:, :, 1, :], in0=rv[:, :, 0, :], in1=sv[:, :, 1, :], op=ALU.mult), r=[raw, sin2], w=[t2])
                yield
                p.op("dve", lambda: E.tensor_tensor(out=t1[:], in0=t1[:], in1=t2[:], op=ALU.add), r=[t1, t2], w=[t1])
                yield
                for n0 in range(0, NT, 4):
                    nn = min(4, NT - n0)
                    pst = PS[7]
                    for i in range(nn):
                        p.op("pe", lambda: nc.tensor.transpose(pst[:, i * 128:(i + 1) * 128], t1[:, n0 + i, :], ident[:]), r=[t1, ident], w=[pst])
                    p.op("dve", lambda: nc.vector.tensor_copy(out=dstT[:, n0 * 128:(n0 + nn) * 128], in_=pst[:, 0:nn * 128]), r=[pst], w=[dstT])
                    yield
            p.op("dve", lambda: nc.vector.tensor_copy(out=v1[:, :, 0:128], in_=vraw[:]), r=[vraw], w=[v1])
            yield
            p.op("act", lambda: nc.scalar.activation(out=graw[:], in_=graw[:], func=AF.Silu), r=[graw], w=[graw])
            yield

        def mainloop(s, j, Bf, inj, obuf):
            qT, kT, v1, graw = Bf["qT"], Bf["kT"], Bf["v1"], Bf["graw"]
            groups = []
            for qt in (range(NT) if qts is None else qts):
                for k0 in range(0, qt + 1, 4):
                    for g in range(2):
                        groups.append((qt, g, k0, min(k0 + 4, qt + 1)))

            def emit_scores(grp, gi):
                qt, g, k0, k1 = grp
                pss = SB[gi % 3]
                for i, kt in enumerate(range(k0, k1)):
                    p.op("pe", lambda: nc.tensor.matmul(pss[:, i * 128:(i + 1) * 128], lhsT=kT[g * 64:(g + 1) * 64, kt * 128:(kt + 1) * 128],
                                                        rhs=qT[g * 64:(g + 1) * 64, qt * 128:(qt + 1) * 128], start=True, stop=True),
                         r=[kT, qT], w=[pss])
                pt = pT[gi % 5]
                n = k1 - k0
                p.op("act", lambda: nc.scalar.activation(out=pt[:, 0:n * 128], in_=pss[:, 0:n * 128], func=AF.Exp, scale=0.125), r=[pss], w=[pt])
                if k1 - 1 == qt:
                    i = qt - k0
                    p.op("pool", lambda: nc.gpsimd.tensor_tensor(out=pt[:, i * 128:(i + 1) * 128], in0=pt[:, i * 128:(i + 1) * 128], in1=tri[:], op=ALU.mult),
                         r=[pt, tri], w=[pt])

            def emit_pv(grp, gi):
                qt, g, k0, k1 = grp
                pt = pT[gi % 5]
                pso = PS[2 + (qt % 2) * 2 + g]
                for i, kt in enumerate(range(k0, k1)):
                    p.op("pe", lambda: nc.tensor.matmul(pso[:, 0:129], lhsT=pt[:, i * 128:(i + 1) * 128], rhs=v1[:, kt, 0:129],
                                                        start=(kt == 0), stop=(kt == qt)), r=[pt, v1], w=[pso])
                if k1 - 1 == qt and g == 1:
                    epilogue(qt)
                    if qt >= 3:
                        inj()

            def epilogue(qt):
                O1 = PS[2 + (qt % 2) * 2]
                O2 = PS[2 + (qt % 2) * 2 + 1]
                r_ = rs[qt % 2]
                o = o_[qt % 2]
                p.op("dve", lambda: nc.vector.tensor_scalar(out=r_[:, 0:1], in0=O1[:, 128:129], scalar1=1e-30, scalar2=None, op0=ALU.max), r=[O1], w=[r_])
                p.op("dve", lambda: nc.vector.tensor_scalar(out=r_[:, 1:2], in0=O2[:, 128:129], scalar1=1e-30, scalar2=None, op0=ALU.max), r=[O2], w=[r_])
                p.op("dve", lambda: nc.vector.reciprocal(out=r_[:, 2:4], in_=r_[:, 0:2]), r=[r_], w=[r_])
                p.op("dve", lambda: nc.vector.tensor_tensor(out=r_[:, 4:5], in0=r_[:, 3:4], in1=neg_lam, op=ALU.mult), r=[r_, ls], w=[r_])
                p.op("dve", lambda: nc.vector.tensor_scalar(out=o[:], in0=O1[:, 0:128], scalar1=r_[:, 2:3], scalar2=None, op0=ALU.mult), r=[O1, r_], w=[o])
                p.op("dve", lambda: nc.vector.scalar_tensor_tensor(out=o[:], in0=O2[:, 0:128], scalar=r_[:, 4:5], in1=o[:], op0=ALU.mult, op1=ALU.add),
                     r=[O2, r_, o], w=[o])
                p.op("act", lambda: nc.scalar.activation(out=sq[:], in_=o[:], func=AF.Square, accum_out=r_[:, 5:6]), r=[o], w=[sq, r_])
                p.op("dve", lambda: nc.vector.tensor_scalar(out=r_[:, 6:7], in0=r_[:, 5:6], scalar1=1.0 / 128, scalar2=EPS, op0=ALU.mult, op1=ALU.add), r=[r_], w=[r_])
                p.op("pool", lambda: nc.gpsimd.tensor_tensor(out=r_[:, 8:9], in0=r_[:, 6:7], in1=mhalf[:], op=ALU.pow), r=[r_, mhalf], w=[r_])
                p.op("dve", lambda: nc.vector.scalar_tensor_tensor(out=o[:], in0=o[:], scalar=r_[:, 8:9], in1=normw[:], op0=ALU.mult, op1=ALU.mult),
                     r=[o, r_, normw], w=[o])
                p.op("pool", lambda: nc.gpsimd.tensor_tensor(out=obuf[:, qt, :], in0=o[:], in1=graw[:, qt, :], op=ALU.mult), r=[o, graw], w=[obuf])

            AHEAD = 2
            for gi, grp in enumerate(groups):
                emit_scores(grp, gi)
                if gi >= AHEAD:
                    emit_pv(groups[gi - AHEAD], gi - AHEAD)
            for gi in range(max(0, len(groups) - AHEAD), len(groups)):
                emit_pv(groups[gi], gi)

        def finish(s, j, obuf):
            for _ in range(3):
                yield
            for n0 in range(0, NT, 4):
                nn = min(4, NT - n0)
                pst = PS[7]
                for i in range(nn):
                    p.op("pe", lambda: nc.tensor.transpose(pst[:, i * 128:(i + 1) * 128], obuf[:, n0 + i, :], ident[:]), r=[obuf, ident], w=[pst])
                if (n0 // 4) % 2 == 0:
                    p.op("act", lambda: nc.scalar.copy(out=oTb[:, n0 * 128:(n0 + nn) * 128], in_=pst[:, 0:nn * 128]), r=[pst], w=[oTb])
                else:
                    p.op("dve", lambda: nc.vector.tensor_copy(out=oTb[:, n0 * 128:(n0 + nn) * 128], in_=pst[:, 0:nn * 128]), r=[pst], w=[oTb])
                yield
            p.dma(DR["OT_att"][j * 128:(j + 1) * 128, s * P:(s + 1) * P], oTb[:], r=[oTb], w=[DR["OT_t"][0]])
            yield

        def run_all(gen):
            for _ in gen:
                pass

        setup_load(pr[0][0], pr[0][1], SETS[0])
        run_all(setup(pr[0][0], pr[0][1], SETS[0]))
        def chain(*gens):
            for g_ in gens:
                if g_ is not None:
                    for _ in g_:
                        yield

        for k, (s, j) in enumerate(pr):
            if k + 1 < len(pr):
                setup_load(pr[k + 1][0], pr[k + 1][1], SETS[(k + 1) % 2])
            g_fin = finish(pr[k - 1][0], pr[k - 1][1], obufs[(k - 1) % 2]) if k > 0 else None
            g_set = setup(pr[k + 1][0], pr[k + 1][1], SETS[(k + 1) % 2]) if k + 1 < len(pr) else None
            bg = chain(g_fin, g_set)

            def inj(n=1):
                for _ in range(n):
                    try:
                        next(bg)
                    except StopIteration:
                        return
            mainloop(s, j, SETS[k % 2], inj, obufs[k % 2])
            run_all(bg)
        run_all(finish(pr[-1][0], pr[-1][1], obufs[(len(pr) - 1) % 2]))
        p.barrier()


def phase_hgrn(p, nc, l, DR, G, seqs=None, zoff=HG_OFF, ntl=NT, pipelined=True):
    PS = G["PS"]
    ident = G["ident"]
    b_le, b_gt, bind = G["b_le"], G["b_gt"], G["bind"]
    Z = DR["Z"]
    with ExitStack() as es:
        T = lambda name, shape, dt, **kw: p.tile("hg_" + name, shape, dt, es=es, **kw)
        hnw = T("hnw", [128, 128], F32)
        load_bc(p, nc, hnw, DR["hgrn_norm_w"][l:l + 1, :])
        identb = T("identb", [128, 128], BF16)
        p.op("pool", lambda: nc.gpsimd.tensor_copy(out=identb[:], in_=ident[:]), r=[ident], w=[identb])
        mhalf = T("mhalf", [128, 8], F32)
        p.op("pool", lambda: nc.gpsimd.memset(mhalf[:], -0.5), w=[mhalf])
        if l > 0:
            lb = T("lb", [128, 1024], F32)
            oml = T("oml", [128, 1024], F32)
            x0 = T("x0", [128, 1024], F32)
            load_bc(p, nc, x0, DR["hgrn_lower_bounds"][0:1, :])
            load_bc(p, nc, lb, DR["hgrn_lower_bounds"][1:2, :])
            p.op("dve", lambda: nc.vector.tensor_tensor(out=x0[:], in0=lb[:], in1=x0[:], op=ALU.subtract), r=[lb, x0], w=[x0])
            p.op("act", lambda: nc.scalar.activation(out=lb[:], in_=x0[:], func=AF.Sigmoid), r=[x0], w=[lb])
            p.op("act", lambda: nc.scalar.activation(out=oml[:], in_=x0[:], func=AF.Sigmoid, scale=-1.0), r=[x0], w=[oml])
        zts = [T("z%d" % i, [128, 4096], F32) for i in range(2)]
        f = T("f", [128, 1024], F32)
        kf = T("kf", [128, 1024], F32)
        logf = T("logf", [128, 1024], F32)
        ex = T("ex", [128, 1024], F32)
        SETS = []
        for k in range(2):
            B = {}
            for nm in ("qtb", "ktb", "kh", "khz", "vb"):
                B[nm] = T("%s%d" % (nm, k), [128, 1024], BF16)
            B["gs"] = T("gs%d" % k, [128, 1024], F32)
            B["gC"] = T("gC%d" % k, [128, 32], F32)
            SETS.append(B)
        qkT = [T("qkT%d" % j, [128, 256], BF16) for j in range(8)]
        qz = [T("qz%d" % j, [128, 64], BF16) for j in range(8)]
        for j in range(8):
            p.op("pool", lambda: nc.gpsimd.memset(qz[j][:], 0.0), w=[qz[j]])
        attm = [T("attm%d" % j, [128, 128], BF16) for j in range(8)]
        S = [T("S%d" % j, [128, 128], F32) for j in range(8)]
        Sb = [T("Sb%d" % j, [128, 128], BF16) for j in range(8)]
        osb = T("osb", [128, 1024], F32)
        obuf = T("obuf", [128, 1024], F32)
        oTb = T("oTb", [128, 8, 128], BF16)
        st = T("st", [128, 4, 8], F32)

        def v8(ap):
            return ap.rearrange("p (h d) -> p h d", d=128)

        def prep_load(s, n, z_):
            r0 = s * P + n * 128
            p.dma(z_[:], zr(Z, r0, r0 + 128)[:, zoff:zoff + 4096], r=[DR["Zt"]], w=[z_])
            yield

        def prep(s, n, B, z_):
            r0 = s * P + n * 128
            p.op("act", lambda: nc.scalar.activation(out=f[:], in_=z_[:, 1024:2048], func=AF.Sigmoid), r=[z_], w=[f])
            yield
            if l > 0:
                p.op("dve", lambda: nc.vector.tensor_tensor(out=f[:], in0=f[:], in1=oml[:], op=ALU.mult), r=[f, oml], w=[f])
                yield
                p.op("pool", lambda: nc.gpsimd.tensor_tensor(out=f[:], in0=f[:], in1=lb[:], op=ALU.add), r=[f, lb], w=[f])
                yield
            p.op("dve", lambda: nc.vector.tensor_scalar(out=kf[:], in0=f[:], scalar1=-1.0, scalar2=1.0, op0=ALU.mult, op1=ALU.add), r=[f], w=[kf])
            p.op("act", lambda: nc.scalar.activation(out=logf[:], in_=f[:], func=AF.Ln), r=[f], w=[logf])
            yield
            p.op("act", lambda: nc.scalar.activation(out=z_[:, 0:1024], in_=z_[:, 0:1024], func=AF.Silu), r=[z_], w=[z_])
            p.op("act", lambda: nc.scalar.activation(out=B["gs"][:], in_=z_[:, 3072:4096], func=AF.Silu), r=[z_], w=[B["gs"]])
            yield
            p.op("pool", lambda: nc.gpsimd.tensor_copy(out=B["vb"][:], in_=z_[:, 2048:3072]), r=[z_], w=[B["vb"]])
            yield
            for j in range(8):
                p.op("pe", lambda: nc.tensor.matmul(PS[4][:, j * 4:(j + 1) * 4], lhsT=logf[:, j * 128:(j + 1) * 128], rhs=bind[:, 0:4], start=True, stop=True),
                     r=[logf, bind], w=[PS[4]])
            p.op("act", lambda: nc.scalar.activation(out=B["gC"][:], in_=PS[4][:, 0:32], func=AF.Exp), r=[PS[4]], w=[B["gC"]])
            yield
            for half in range(2):
                hs = slice(half * 512, (half + 1) * 512)
                pl = PS[4 + half]
                p.op("pe", lambda: nc.tensor.matmul(pl[:], lhsT=b_le[:], rhs=logf[:, hs], start=True, stop=True), r=[b_le, logf], w=[pl])
                p.op("act", lambda: nc.scalar.activation(out=ex[:, hs], in_=pl[:], func=AF.Exp), r=[pl], w=[ex])
                p.op("dve", lambda: nc.vector.tensor_tensor(out=B["qtb"][:, hs], in0=z_[:, hs], in1=ex[:, hs], op=ALU.mult), r=[z_, ex], w=[B["qtb"]])
                p.op("act", lambda: nc.scalar.activation(out=ex[:, hs], in_=pl[:], func=AF.Exp, scale=-1.0), r=[pl], w=[ex])
                p.op("dve", lambda: nc.vector.tensor_tensor(out=B["ktb"][:, hs], in0=kf[:, hs], in1=ex[:, hs], op=ALU.mult), r=[kf, ex], w=[B["ktb"]])
                p.op("pe", lambda: nc.tensor.matmul(pl[:], lhsT=b_gt[:], rhs=logf[:, hs], start=True, stop=True), r=[b_gt, logf], w=[pl])
                p.op("act", lambda: nc.scalar.activation(out=ex[:, hs], in_=pl[:], func=AF.Exp), r=[pl], w=[ex])
                p.op("dve", lambda: nc.vector.tensor_tensor(out=B["kh"][:, hs], in0=kf[:, hs], in1=ex[:, hs], op=ALU.mult), r=[kf, ex], w=[B["kh"]])
                yield
            p.op("pool", lambda: nc.gpsimd.tensor_scalar(out=B["khz"][:], in0=B["kh"][:], scalar1=bind[:, 3:4], scalar2=None, op0=ALU.mult), r=[B["kh"], bind], w=[B["khz"]])
            yield

        def main(s, n, B, inj, flush_out=lambda: None):
            if n == 0:
                for j in range(8):
                    p.op("pool", lambda: nc.gpsimd.memset(S[j][:], 0.0), w=[S[j]])
                    p.op("pool", lambda: nc.gpsimd.memset(Sb[j][:], 0.0), w=[Sb[j]])
            vb, kh, khz, gC = B["vb"], B["kh"], B["khz"], B["gC"]
            for j in range(8):
                js = slice(j * 128, (j + 1) * 128)
                pw = PS[4 + (j % 2)]
                pwb = pw.t[:].bitcast(BF16)
                p.op("pe", lambda: nc.tensor.transpose(pwb[:, 0:128], B["qtb"][:, js], identb[:]), r=[B["qtb"], identb], w=[pw])
                p.op("pe", lambda: nc.tensor.transpose(pwb[:, 128:256], B["ktb"][:, js], identb[:]), r=[B["ktb"], identb], w=[pw])
                p.op("act", lambda: nc.scalar.copy(out=qkT[j][:], in_=pwb[:, 0:256]), r=[pw], w=[qkT[j]])
                p.op("pool", lambda: nc.gpsimd.tensor_copy(out=qz[j][:, 32:64], in_=qkT[j][:, 96:128]), r=[qkT[j]], w=[qz[j]])
                p.op("pe", lambda: nc.tensor.matmul(pw[:, 256:384], lhsT=qkT[j][:, 128:256], rhs=qkT[j][:, 0:128], start=True, stop=True),
                     r=[qkT[j]], w=[pw])
                p.op("dve", lambda: nc.vector.tensor_tensor(out=attm[j][:], in0=pw[:, 256:384], in1=b_le[:], op=ALU.mult), r=[pw, b_le], w=[attm[j]])
                if j % 2 == 1:
                    inj()
            for j in range(8):
                js = slice(j * 128, (j + 1) * 128)
                po = PS[6 + j // 4]
                p.op("pe", lambda: nc.tensor.matmul(po[:, (j % 4) * 128:(j % 4 + 1) * 128], lhsT=attm[j][:], rhs=vb[:, js],
                                                    start=(j % 4 == 0), stop=False, skip_group_check=True), r=[attm[j], vb], w=[po])
            for c in range(4):
                cs = slice(c * 32, (c + 1) * 32)
                for j in range(8):
                    js = slice(j * 128, (j + 1) * 128)
                    po = PS[6 + j // 4]
                    if c < 3:
                        p.op("pe", lambda: nc.tensor.matmul(po[cs, (j % 4) * 128:(j % 4 + 1) * 128], lhsT=qkT[j][:, cs], rhs=Sb[j][:],
                                                            start=False, stop=False, skip_group_check=True), r=[qkT[j], Sb[j]], w=[po])
                    else:
                        p.op("pe", lambda: nc.tensor.matmul(po[64:128, (j % 4) * 128:(j % 4 + 1) * 128], lhsT=qz[j][:], rhs=Sb[j][:],
                                                            start=False, stop=True, skip_group_check=True), r=[qz[j], Sb[j]], w=[po])
                    pss = PS[j % 4]
                    pc = slice((j // 4) * 128, (j // 4 + 1) * 128)
                    if c < 3:
                        p.op("pe", lambda: nc.tensor.matmul(pss[:, pc], lhsT=kh[cs, js], rhs=vb[cs, js],
                                                            start=True, stop=True), r=[kh, vb], w=[pss])
                    else:
                        p.op("pe", lambda: nc.tensor.matmul(pss[:, pc], lhsT=khz[64:128, js], rhs=vb[64:128, js],
                                                            start=True, stop=True), r=[khz, vb], w=[pss])
                    p.op("dve", lambda: nc.vector.scalar_tensor_tensor(out=S[j][:], in0=S[j][:], scalar=gC[:, j * 4 + c:j * 4 + c + 1],
                                                                       in1=pss[:, pc], op0=ALU.mult, op1=ALU.add),
                         r=[S[j], gC, pss], w=[S[j]])
                    p.op("act", lambda: nc.scalar.copy(out=Sb[j][:], in_=S[j][:]), r=[S[j]], w=[Sb[j]])
                    if j % 4 == 3:
                        inj()
            flush_out()
            for half in range(2):
                hs = slice(half * 512, (half + 1) * 512)
                p.op("act", lambda: nc.scalar.copy(out=osb[:, hs], in_=PS[6 + half][:]), r=[PS[6 + half]], w=[osb])

        def outp(s, n, B):
            r0 = s * P + n * 128
            p.op("pool", lambda: nc.gpsimd.tensor_tensor(out=obuf[:], in0=osb[:], in1=osb[:], op=ALU.mult), r=[osb], w=[obuf])
            yield
            p.op("dve", lambda: nc.vector.tensor_reduce(out=st[:, 0, :], in_=v8(obuf.t[:]), axis=AX.X, op=ALU.add), r=[obuf], w=[st])
            p.op("dve", lambda: nc.vector.tensor_scalar(out=st[:, 1, :], in0=st[:, 0, :], scalar1=1.0 / 128, scalar2=EPS, op0=ALU.mult, op1=ALU.add), r=[st], w=[st])
            p.op("pool", lambda: nc.gpsimd.tensor_tensor(out=st[:, 2, :], in0=st[:, 1, :], in1=mhalf[:], op=ALU.pow), r=[st, mhalf], w=[st])
            yield
            p.op("dve", lambda: nc.vector.tensor_tensor(out=v8(osb.t[:]), in0=v8(osb.t[:]), in1=st[:, 2, :].unsqueeze(2).broadcast_to([128, 8, 128]), op=ALU.mult),
                 r=[osb, st], w=[osb])
            yield
            p.op("pool", lambda: nc.gpsimd.tensor_tensor(out=v8(osb.t[:]), in0=v8(osb.t[:]), in1=hnw.t[:].unsqueeze(1).broadcast_to([128, 8, 128]), op=ALU.mult),
                 r=[osb, hnw], w=[osb])
            yield
            p.op("pool", lambda: nc.gpsimd.tensor_tensor(out=obuf[:], in0=osb[:], in1=B["gs"][:], op=ALU.mult), r=[osb, B["gs"]], w=[obuf])
            yield
            for half in range(2):
                pst = PS[4 + half]
                for k4 in range(4):
                    kc = half * 4 + k4
                    p.op("pe", lambda: nc.tensor.transpose(pst[:, k4 * 128:(k4 + 1) * 128], obuf[:, kc * 128:(kc + 1) * 128], ident[:]), r=[obuf, ident], w=[pst])
                src_ = pst.t[:].rearrange("p (k t) -> p k t", k=4)
                if half == 0:
                    p.op("act", lambda: nc.scalar.copy(out=oTb[:, 0:4, :], in_=src_), r=[pst], w=[oTb])
                else:
                    p.op("dve", lambda: nc.vector.tensor_copy(out=oTb[:, 4:8, :], in_=src_), r=[pst], w=[oTb])
                yield
            p.dma(DR["OT_hgrn"].rearrange("(kc p) t -> p kc t", p=128)[:, :, r0:r0 + 128], oTb[:], r=[oTb], w=[DR["OT_t"][2]])
            yield

        tiles = [(s, n) for s in (range(NS) if seqs is None else seqs) for n in range(ntl)]

        def run_all(gen):
            for _ in gen:
                pass

        def chain(*gens):
            for g in gens:
                if g is not None:
                    for _ in g:
                        yield

        if not pipelined:
            for idx, (s, n) in enumerate(tiles):
                B = SETS[idx % 2]
                run_all(prep_load(s, n, zts[idx % 2]))
                run_all(prep(s, n, B, zts[idx % 2]))
                main(s, n, B, lambda: None)
                run_all(outp(s, n, B))
        else:
            run_all(prep_load(tiles[0][0], tiles[0][1], zts[0]))
            run_all(prep(tiles[0][0], tiles[0][1], SETS[0], zts[0]))
            if len(tiles) > 1:
                run_all(prep_load(tiles[1][0], tiles[1][1], zts[1]))
            for idx, (s, n) in enumerate(tiles):
                B = SETS[idx % 2]
                g_out = outp(tiles[idx - 1][0], tiles[idx - 1][1], SETS[(idx - 1) % 2]) if idx > 0 else None
                g_prep = prep(tiles[idx + 1][0], tiles[idx + 1][1], SETS[(idx + 1) % 2], zts[(idx + 1) % 2]) if idx + 1 < len(tiles) else None
                g_load = prep_load(tiles[idx + 2][0], tiles[idx + 2][1], zts[idx % 2]) if idx + 2 < len(tiles) else None
                g_lo = chain(g_out)
                bg = chain(g_lo, g_prep, g_load)

                def inj(k=1):
                    for _ in range(k):
                        try:
                            next(bg)
                        except StopIteration:
                            return
                main(s, n, B, inj, lambda: run_all(g_lo))
                run_all(bg)
            run_all(outp(tiles[-1][0], tiles[-1][1], SETS[(len(tiles) - 1) % 2]))
        p.barrier()


C0 = -0.6065306597126334


def rwkv_setup(p, nc, l, DR, G, T):
    ident = G["ident"]
    tri_le, tri_lt, tri_gt = G["tri_le"], G["tri_lt"], G["tri_gt"]
    mu = T("mu", [128, RW_W], F32)
    load_bc(p, nc, mu, DR["rwkv_mu"][l:l + 1, :])
    prm = {}
    for i, n_ in enumerate(("rwkv_w0", "rwkv_a0", "rwkv_k_k", "rwkv_k_a", "rwkv_gn_w", "rwkv_gn_b")):
        prm[n_] = T(n_, [128, 1024], F32)
        load_bc(p, nc, prm[n_], DR[n_][l:l + 1, :], q=("sp" if i % 2 == 0 else "act"))
    prm["rwkv_r_k"] = T("rwkv_r_k", [128, 1024], F32)
    load_bc(p, nc, prm["rwkv_r_k"], DR["rwkv_r_k"][l:l + 1].rearrange("o h d -> o (h d)"))
    w_up = T("w_up", [64, 1024], F32)
    a_up = T("a_up", [64, 1024], F32)
    p.dma(w_up[:], DR["rwkv_w_up"][l], w=[w_up])
    p.dma(a_up[:], DR["rwkv_a_up"][l], w=[a_up])
    mA = T("mA", [128, 384], F32)
    mB = T("mB", [128, 256], F32)
    p.op("pool", lambda: nc.gpsimd.tensor_copy(out=mA[:, 0:128], in_=tri_lt[:]), r=[tri_lt], w=[mA])
    p.op("pool", lambda: nc.gpsimd.tensor_copy(out=mA[:, 128:256], in_=tri_gt[:]), r=[tri_gt], w=[mA])
    p.op("pool", lambda: nc.gpsimd.tensor_copy(out=mA[:, 256:384], in_=tri_lt[:]), r=[tri_lt], w=[mA])
    p.op("pool", lambda: nc.gpsimd.tensor_copy(out=mB[:, 0:128], in_=tri_le[:]), r=[tri_le], w=[mB])
    p.op("pool", lambda: nc.gpsimd.tensor_copy(out=mB[:, 128:256], in_=tri_le[:]), r=[tri_le], w=[mB])
    identb = T("identb", [128, 128], BF16)
    p.op("pool", lambda: nc.gpsimd.tensor_copy(out=identb[:], in_=ident[:]), r=[ident], w=[identb])
    mhalf16 = T("mhalf16", [128, 16], F32)
    p.op("pool", lambda: nc.gpsimd.memset(mhalf16[:], -0.5), w=[mhalf16])
    return dict(mu=mu, prm=prm, w_up=w_up, a_up=a_up, mA=mA, mB=mB, identb=identb, mhalf16=mhalf16)


def phase_rwkv(p, nc, l, DR, G, seqs=None, zoff=RW_OFF, ntl=NT, dbg=None, stop=None, pipelined=True, pre=None):
    PS = G["PS"]
    ident = G["ident"]
    tri_le, tri_lt, tri_gt, ones = G["tri_le"], G["tri_lt"], G["tri_gt"], G["ones"]
    Z = DR["Z"]

    def bc(ap16, n=16):
        return ap16.unsqueeze(2).broadcast_to([128, n, 64])

    def v3(ap):
        return ap.rearrange("p (h d) -> p h d", d=64)

    with ExitStack() as es:
        T = lambda name, shape, dt, **kw: p.tile("rw_" + name, shape, dt, es=es, **kw)
        ps_ = pre if pre is not None else rwkv_setup(p, nc, l, DR, G, T)
        mu, prm, w_up, a_up, mA, mB, identb, mhalf16 = (ps_[k] for k in ("mu", "prm", "w_up", "a_up", "mA", "mB", "identb", "mhalf16"))
        zc = T("zc", [128, RW_W], F32)
        zp = T("zp", [128, RW_W], F32)
        sw = T("sw", [128, 1024], F32)
        a_ = T("a_", [128, 1024], F32)
        kk = T("kk", [128, 1024], F32)
        kp = T("kp", [128, 1024], F32)
        b_ = T("b_", [128, 1024], F32)
        twT = T("twT", [64, 256], F32)
        ex = zp.t[:, 0:1024]
        tq = zp.t[:, 3072:4096]
        SETS = []
        for k in range(2):
            B = {}
            for nm in ("rtb", "ktb", "atb", "btb", "khat", "bhat", "vb"):
                B[nm] = T("%s%d" % (nm, k), [128, 1024], BF16)
            B["gs"] = T("gs%d" % k, [128, 1024], F32)
            B["gC"] = T("gC%d" % k, [128, 16], F32)
            B["sm"] = T("sm%d" % k, [128, 4, 16], F32)
            SETS.append(B)
        sm2 = T("sm2", [128, 8, 16], F32)
        fT = [T("fT%d" % i, [128, 4, 128], BF16) for i in range(8)]
        Ar = [T("Ar%d" % h, [128, 256], BF16) for h in range(16)]
        Aak = [T("Aak%d" % i, [128, 128], BF16) for i in range(8)]
        PP = [[T("PP%d_%d" % (i, k), [128, 256], BF16) for k in range(2)] for i in range(8)]
        X = [[T("X%d_%d" % (i, k), [128, 128], BF16) for k in range(2)] for i in range(8)]
        Gt = [T("Gt%d" % i, [128, 64], BF16) for i in range(8)]
        Wt = T("Wt", [128, 1024], F32)
        YT = [T("YT%d" % i, [128, 128], BF16) for i in range(8)]
        ST = T("ST", [128, 8, 64], F32)
        STbd = T("STbd", [128, 8, 128], BF16)
        Ub = T("Ub", [128, 1024], BF16)
        osb = T("osb", [128, 1024], F32)
        obuf = T("obuf", [128, 1024], F32)
        oTb = T("oTb", [128, 8, 128], BF16)

        def prep_load(s, n):
            r0 = s * P + n * 128
            p.dma(zc[:], zr(Z, r0, r0 + 128)[:, zoff:zoff + RW_W], r=[DR["Zt"]], w=[zc])
            if n == 0:
                p.op("pool", lambda: nc.gpsimd.memset(zp[0:1, :], 0.0), w=[zp])
                p.dma(zp[1:128, :], zr(Z, r0, r0 + 127)[:, zoff:zoff + RW_W], r=[DR["Zt"]], w=[zp], q="act")
            else:
                p.dma(zp[:], zr(Z, r0 - 1, r0 + 127)[:, zoff:zoff + RW_W], r=[DR["Zt"]], w=[zp], q="act")
            yield

        def prep(s, n, B):
            r0 = s * P + n * 128
            hs_ = slice(4096, RW_W)
            p.op("dve", lambda: nc.vector.tensor_tensor(out=zp[:, hs_], in0=zp[:, hs_], in1=zc[:, hs_], op=ALU.subtract), r=[zp, zc], w=[zp])
            p.op("dve", lambda: nc.vector.tensor_tensor(out=zp[:, hs_], in0=zp[:, hs_], in1=mu[:, hs_], op=ALU.mult), r=[zp, mu], w=[zp])
            p.op("dve", lambda: nc.vector.tensor_tensor(out=zc[:, hs_], in0=zc[:, hs_], in1=zp[:, hs_], op=ALU.add), r=[zp, zc], w=[zc])
            p.op("act", lambda: nc.scalar.activation(out=zc[:, 4096:4160], in_=zc[:, 4096:4160], func=AF.Tanh), r=[zc], w=[zc])
            yield
            h1 = slice(0, 2560)
            h2 = slice(2560, 4096)
            for k3 in range(3):
                for (sl, eng) in ((h1, "dve"), (h2, "pool")):
                    E = nc.vector if eng == "dve" else nc.gpsimd
                    if k3 == 0:
                        p.op(eng, lambda: E.tensor_tensor(out=zp[:, sl], in0=zp[:, sl], in1=zc[:, sl], op=ALU.subtract), r=[zp, zc], w=[zp])
                    elif k3 == 1:
                        p.op(eng, lambda: E.tensor_tensor(out=zp[:, sl], in0=zp[:, sl], in1=mu[:, sl], op=ALU.mult), r=[zp, mu], w=[zp])
                    else:
                        p.op(eng, lambda: E.tensor_tensor(out=zc[:, sl], in0=zc[:, sl], in1=zp[:, sl], op=ALU.add), r=[zp, zc], w=[zc])
                    yield
            rr = zc.t[:, 0:1024]
            rk = zc.t[:, 1024:2048]
            rv = zc.t[:, 2048:3072]
            rg = zc.t[:, 3072:4096]
            p.op("pe", lambda: nc.tensor.transpose(PS[6][0:64, 0:128], zc[:, 4096:4160], ident[:]), r=[zc, ident], w=[PS[6]])
            p.op("pe", lambda: nc.tensor.transpose(PS[6][0:64, 128:256], zc[:, 4160:4224], ident[:]), r=[zc, ident], w=[PS[6]])
            p.op("act", lambda: nc.scalar.copy(out=twT[:], in_=PS[6][0:64, 0:256]), r=[PS[6]], w=[twT])
            yield
            for half in range(2):
                hs = slice(half * 512, (half + 1) * 512)
                p.op("pe", lambda: nc.tensor.matmul(PS[6][:], lhsT=twT[:, 0:128], rhs=w_up[:, hs], start=True, stop=True), r=[twT, w_up], w=[PS[6]])
                p.op("dve", lambda: nc.vector.tensor_tensor(out=sw[:, hs], in0=PS[6][:], in1=prm["rwkv_w0"][:, hs], op=ALU.add),
                     r=[PS[6], prm["rwkv_w0"]], w=[sw])
                yield
            for half in range(2):
                hs = slice(half * 512, (half + 1) * 512)
                p.op("pe", lambda: nc.tensor.matmul(PS[7][:], lhsT=twT[:, 128:256], rhs=a_up[:, hs], start=True, stop=True), r=[twT, a_up], w=[PS[7]])
                p.op("dve", lambda: nc.vector.tensor_tensor(out=a_[:, hs], in0=PS[7][:], in1=prm["rwkv_a0"][:, hs], op=ALU.add),
                     r=[PS[7], prm["rwkv_a0"]], w=[a_])
                yield
            p.op("act", lambda: nc.scalar.activation(out=sw[:], in_=sw[:], func=AF.Sigmoid), r=[sw], w=[sw])
            p.op("act", lambda: nc.scalar.activation(out=a_[:], in_=a_[:], func=AF.Sigmoid), r=[a_], w=[a_])
            yield
            p.op("dve", lambda: nc.vector.tensor_tensor(out=kk[:], in0=rk, in1=prm["rwkv_k_k"][:], op=ALU.mult), r=[zc, prm["rwkv_k_k"]], w=[kk])
            yield
            p.op("pool", lambda: nc.gpsimd.tensor_tensor(out=kp[:], in0=kk[:], in1=kk[:], op=ALU.mult), r=[kk], w=[kp])
            yield
            p.op("dve", lambda: nc.vector.tensor_reduce(out=sm2[:, 0, :], in_=v3(kp.t[:]), axis=AX.X, op=ALU.add), r=[kp], w=[sm2])
            p.op("dve", lambda: nc.vector.tensor_scalar(out=sm2[:, 1, :], in0=sm2[:, 0, :], scalar1=1e-24, scalar2=None, op0=ALU.max), r=[sm2], w=[sm2])
            p.op("pool", lambda: nc.gpsimd.tensor_tensor(out=sm2[:, 2, :], in0=sm2[:, 1, :], in1=mhalf16[:], op=ALU.pow), r=[sm2, mhalf16], w=[sm2])
            yield
            p.op("dve", lambda: nc.vector.tensor_tensor(out=v3(kk.t[:]), in0=v3(kk.t[:]), in1=bc(sm2[:, 2, :]), op=ALU.mult), r=[kk, sm2], w=[kk])
            yield
            p.op("dve", lambda: nc.vector.scalar_tensor_tensor(out=kp[:], in0=a_[:], scalar=-1.0, in1=prm["rwkv_k_a"][:], op0=ALU.add, op1=ALU.mult),
                 r=[a_, prm["rwkv_k_a"]], w=[kp])
            yield
            p.op("dve", lambda: nc.vector.scalar_tensor_tensor(out=kp[:], in0=kp[:], scalar=1.0, in1=rk, op0=ALU.add, op1=ALU.mult), r=[kp, zc], w=[kp])
            yield
            p.op("pool", lambda: nc.gpsimd.tensor_tensor(out=b_[:], in0=kk[:], in1=a_[:], op=ALU.mult), r=[kk, a_], w=[b_])
            yield
            p.op("pool", lambda: nc.gpsimd.tensor_tensor(out=tq, in0=rr, in1=kp[:], op=ALU.mult), r=[zc, kp], w=[zp])
            yield
            p.op("pool", lambda: nc.gpsimd.tensor_tensor(out=tq, in0=tq, in1=prm["rwkv_r_k"][:], op=ALU.mult), r=[zp, prm["rwkv_r_k"]], w=[zp])
            yield
            p.op("dve", lambda: nc.vector.tensor_reduce(out=B["sm"][:, 3, :], in_=v3(tq), axis=AX.X, op=ALU.add), r=[zp], w=[B["sm"]])
            p.op("pool", lambda: nc.gpsimd.tensor_copy(out=B["vb"][:], in_=rv), r=[zc], w=[B["vb"]])
            yield
            p.op("act", lambda: nc.scalar.activation(out=B["gs"][:], in_=rg, func=AF.Silu), r=[zc], w=[B["gs"]])
            yield
            for _ in range(8):
                yield
            for half in range(2):
                hs = slice(half * 512, (half + 1) * 512)
                pl = PS[6 + half]
                p.op("pe", lambda: nc.tensor.matmul(pl[:], lhsT=tri_le[:], rhs=sw[:, hs], start=True, stop=True), r=[tri_le, sw], w=[pl])
                p.op("act", lambda: nc.scalar.activation(out=ex[:, hs], in_=pl[:], func=AF.Exp, scale=C0), r=[pl], w=[zp])
                p.op("dve", lambda: nc.vector.tensor_tensor(out=B["rtb"][:, hs], in0=rr[:, hs], in1=ex[:, hs], op=ALU.mult), r=[zc, zp], w=[B["rtb"]])
                p.op("dve", lambda: nc.vector.tensor_tensor(out=tq[:, hs], in0=pl[:], in1=sw[:, hs], op=ALU.subtract), r=[pl, sw], w=[zp])
                p.op("act", lambda: nc.scalar.activation(out=tq[:, hs], in_=tq[:, hs], func=AF.Exp, scale=C0), r=[zp], w=[zp])
                p.op("dve", lambda: nc.vector.scalar_tensor_tensor(out=B["atb"][:, hs], in0=kk[:, hs], scalar=-1.0, in1=tq[:, hs], op0=ALU.mult, op1=ALU.mult),
                     r=[kk, zp], w=[B["atb"]])
                p.op("act", lambda: nc.scalar.activation(out=ex[:, hs], in_=pl[:], func=AF.Exp, scale=-C0), r=[pl], w=[zp])
                p.op("pool", lambda: nc.gpsimd.tensor_tensor(out=B["btb"][:, hs], in0=b_[:, hs], in1=ex[:, hs], op=ALU.mult), r=[b_, zp], w=[B["btb"]])
                p.op("dve", lambda: nc.vector.tensor_tensor(out=B["ktb"][:, hs], in0=kp[:, hs], in1=ex[:, hs], op=ALU.mult), r=[kp, zp], w=[B["ktb"]])
                p.op("pe", lambda: nc.tensor.matmul(pl[:], lhsT=tri_gt[:], rhs=sw[:, hs], start=True, stop=True), r=[tri_gt, sw], w=[pl])
                p.op("act", lambda: nc.scalar.activation(out=ex[:, hs], in_=pl[:], func=AF.Exp, scale=C0), r=[pl], w=[zp])
                p.op("pool", lambda: nc.gpsimd.tensor_tensor(out=B["khat"][:, hs], in0=kp[:, hs], in1=ex[:, hs], op=ALU.mult), r=[kp, zp], w=[B["khat"]])
                p.op("dve", lambda: nc.vector.tensor_tensor(out=B["bhat"][:, hs], in0=b_[:, hs], in1=ex[:, hs], op=ALU.mult), r=[b_, zp], w=[B["bhat"]])
                yield
            for pp in range(8):
                p.op("pe", lambda: nc.tensor.matmul(PS[6][:, pp * 2:pp * 2 + 2], lhsT=sw[:, pp * 128:(pp + 1) * 128], rhs=ones[:, 0:2],
                                                    start=True, stop=True), r=[sw, ones], w=[PS[6]])
            p.op("act", lambda: nc.scalar.activation(out=B["gC"][:], in_=PS[6][:, 0:16], func=AF.Exp, scale=C0), r=[PS[6]], w=[B["gC"]])
            yield

        def main(s, n, B, inj, flush_out=lambda: None):
            if n == 0:
                p.op("pool", lambda: nc.gpsimd.memset(ST[:], 0.0), w=[ST])
                p.op("pool", lambda: nc.gpsimd.memset(STbd[:], 0.0), w=[STbd])
            src = (B["rtb"], B["ktb"], B["atb"], B["btb"])
            for g8 in range(2):
                for pi in range(4):
                    pp = g8 * 4 + pi
                    cs = slice(pp * 128, (pp + 1) * 128)
                    pw = PS[6 + pi // 2]
                    pwb = pw.t[:].bitcast(BF16)
                    off = (pi % 2) * 512
                    for k in range(4):
                        p.op("pe", lambda: nc.tensor.transpose(pwb[:, off + k * 128:off + (k + 1) * 128], src[k][:, cs], identb[:]), r=[src[k], identb], w=[pw])
                    if pi % 2 == 0:
                        p.op("act", lambda: nc.scalar.copy(out=fT[pp].t[:].rearrange("p a t -> p (a t)"), in_=pwb[:, off:off + 512]), r=[pw], w=[fT[pp]])
                    else:
                        p.op("dve", lambda: nc.vector.tensor_copy(out=fT[pp].t[:].rearrange("p a t -> p (a t)"), in_=pwb[:, off:off + 512]), r=[pw], w=[fT[pp]])
                for i in range(8):
                    h = g8 * 8 + i
                    pp, e = h // 2, h % 2
                    es_ = slice(e * 64, (e + 1) * 64)
                    rT, kT, aT, bT = (fT[pp][es_, k, :] for k in range(4))
                    pa = PS[i]
                    p.op("pe", lambda: nc.tensor.matmul(pa[:, 0:128], lhsT=bT, rhs=aT, start=True, stop=True), r=[fT[pp]], w=[pa])
                    p.op("pe", lambda: nc.tensor.matmul(pa[:, 128:256], lhsT=aT, rhs=bT, start=True, stop=True), r=[fT[pp]], w=[pa])
                    p.op("pe", lambda: nc.tensor.matmul(pa[:, 256:384], lhsT=kT, rhs=aT, start=True, stop=True), r=[fT[pp]], w=[pa])
                    p.op("dve", lambda: nc.vector.tensor_tensor(out=PP[i][0][:], in0=pa[:, 0:256], in1=mA[:, 0:256], op=ALU.mult), r=[pa, mA], w=[PP[i][0]])
                    p.op("dve", lambda: nc.vector.tensor_tensor(out=Aak[i][:], in0=pa[:, 256:384], in1=mA[:, 256:384], op=ALU.mult), r=[pa, mA], w=[Aak[i]])
                    p.op("dve", lambda: nc.vector.tensor_tensor(out=X[i][0][:], in0=PP[i][0][:, 0:128], in1=identb[:], op=ALU.add), r=[PP[i][0], identb], w=[X[i][0]])
                inj()
                for i in range(8):
                    h = g8 * 8 + i
                    pp, e = h // 2, h % 2
                    es_ = slice(e * 64, (e + 1) * 64)
                    rT, kT, aT, bT = (fT[pp][es_, k, :] for k in range(4))
                    pa = PS[i]
                    p.op("pe", lambda: nc.tensor.matmul(pa[:, 0:128], lhsT=bT, rhs=rT, start=True, stop=True), r=[fT[pp]], w=[pa])
                    p.op("pe", lambda: nc.tensor.matmul(pa[:, 128:256], lhsT=kT, rhs=rT, start=True, stop=True), r=[fT[pp]], w=[pa])
                    p.op("dve", lambda: nc.vector.tensor_tensor(out=Ar[h][:], in0=pa[:, 0:256], in1=mB[:], op=ALU.mult), r=[pa, mB], w=[Ar[h]])
                inj()
                for lv in range(1, 7):
                    cur, nxt = (lv - 1) % 2, lv % 2
                    for i in range(8):
                        pq = PS[i]
                        Pm, PTm = PP[i][cur][:, 0:128], PP[i][cur][:, 128:256]
                        if lv < 6:
                            p.op("pe", lambda: nc.tensor.matmul(pq[:, 0:128], lhsT=PTm, rhs=Pm, start=True, stop=True), r=[PP[i][cur]], w=[pq])
                        p.op("pe", lambda: nc.tensor.matmul(pq[:, 128:256], lhsT=Pm, rhs=PTm, start=True, stop=True), r=[PP[i][cur]], w=[pq])
                        if i % 2 == 0:
                            p.op("act", lambda: nc.scalar.copy(out=PP[i][nxt][:], in_=pq[:, 0:256]), r=[pq], w=[PP[i][nxt]])
                        else:
                            p.op("dve", lambda: nc.vector.tensor_copy(out=PP[i][nxt][:], in_=pq[:, 0:256]), r=[pq], w=[PP[i][nxt]])
                        if i % 8 == 7:
                            inj()
                    for i in range(8):
                        px = PS[i]
                        p.op("pe", lambda: nc.tensor.matmul(px[:, 256:384], lhsT=identb[:], rhs=X[i][cur][:], start=True, stop=False), r=[identb, X[i][cur]], w=[px])
                        p.op("pe", lambda: nc.tensor.matmul(px[:, 256:384], lhsT=PP[i][nxt][:, 128:256], rhs=X[i][cur][:], start=False, stop=True),
                             r=[PP[i][nxt], X[i][cur]], w=[px])
                        if i % 2 == 1:
                            p.op("act", lambda: nc.scalar.copy(out=X[i][nxt][:], in_=px[:, 256:384]), r=[px], w=[X[i][nxt]])
                        else:
                            p.op("dve", lambda: nc.vector.tensor_copy(out=X[i][nxt][:], in_=px[:, 256:384]), r=[px], w=[X[i][nxt]])
                        if i % 8 == 7:
                            inj()
                for i in range(8):
                    h = g8 * 8 + i
                    hc = slice(h * 64, (h + 1) * 64)
                    p.op("pe", lambda: nc.tensor.matmul(PS[i][:, 0:64], lhsT=Aak[i][:], rhs=B["vb"][:, hc], start=True, stop=True), r=[Aak[i], B["vb"]], w=[PS[i]])
                    if i % 2 == 0:
                        p.op("act", lambda: nc.scalar.copy(out=Gt[i][:], in_=PS[i][:, 0:64]), r=[PS[i]], w=[Gt[i]])
                    else:
                        p.op("dve", lambda: nc.vector.tensor_copy(out=Gt[i][:], in_=PS[i][:, 0:64]), r=[PS[i]], w=[Gt[i]])
                for i in range(8):
                    h = g8 * 8 + i
                    pp, e = h // 2, h % 2
                    hc = slice(h * 64, (h + 1) * 64)
                    py = PS[i]
                    p.op("pe", lambda: nc.tensor.matmul(py[:, 64:128], lhsT=X[i][0][:], rhs=Gt[i][:], start=True, stop=True),
                         r=[X[i][0], Gt[i]], w=[py])
                    p.op("pe", lambda: nc.tensor.matmul(py[:, 128:256], lhsT=B["atb"][:, pp * 128:(pp + 1) * 128], rhs=X[i][0][:], start=True, stop=True),
                         r=[B["atb"], X[i][0]], w=[py])
                    p.op("act", lambda: nc.scalar.copy(out=Wt[:, hc], in_=py[:, 64:128]), r=[py], w=[Wt])
                    p.op("dve", lambda: nc.vector.tensor_copy(out=YT[pp][e * 64:(e + 1) * 64, :], in_=py[e * 64:(e + 1) * 64, 128:256]), r=[py], w=[YT[pp]])
                inj()
            for pp in range(8):
                p.op("pe", lambda: nc.tensor.matmul(PS[pp // 4][:, (pp % 4) * 128:(pp % 4 + 1) * 128], lhsT=YT[pp][:], rhs=STbd[:, pp, :], start=True, stop=True),
                     r=[YT[pp], STbd], w=[PS[pp // 4]])
            for half in range(2):
                hs = slice(half * 512, (half + 1) * 512)
                p.op("dve", lambda: nc.vector.tensor_tensor(out=Ub[:, hs], in0=PS[half][:], in1=Wt[:, hs], op=ALU.add), r=[PS[half], Wt], w=[Ub])
            for pp in range(8):
                po = PS[2 + pp // 4]
                p.op("pe", lambda: nc.tensor.matmul(po[:, (pp % 4) * 128:(pp % 4 + 1) * 128], lhsT=fT[pp][:, 0, :], rhs=STbd[:, pp, :],
                                                    start=(pp % 4 == 0), stop=False, skip_group_check=True), r=[fT[pp], STbd], w=[po])
            for h in range(16):
                hc = slice(h * 64, (h + 1) * 64)
                po = PS[2 + h // 8]
                oc = slice((h % 8) * 64, (h % 8 + 1) * 64)
                p.op("pe", lambda: nc.tensor.matmul(po[:, oc], lhsT=Ar[h][:, 128:256], rhs=B["vb"][:, hc], start=False, stop=False, skip_group_check=True),
                     r=[Ar[h], B["vb"]], w=[po])
                p.op("pe", lambda: nc.tensor.matmul(po[:, oc], lhsT=Ar[h][:, 0:128], rhs=Ub[:, hc], start=False, stop=True, skip_group_check=True),
                     r=[Ar[h], Ub], w=[po])
            for h in range(16):
                pp, e = h // 2, h % 2
                es_ = slice(e * 64, (e + 1) * 64)
                hc = slice(h * 64, (h + 1) * 64)
                p.op("pe", lambda: nc.tensor.matmul(PS[4][es_, pp * 64:(pp + 1) * 64], lhsT=B["bhat"][:, hc], rhs=Ub[:, hc], start=(pp == 0), stop=False, skip_group_check=True),
                     r=[B["bhat"], Ub], w=[PS[4]])
                p.op("pe", lambda: nc.tensor.matmul(PS[4][es_, pp * 64:(pp + 1) * 64], lhsT=B["khat"][:, hc], rhs=B["vb"][:, hc], start=False, stop=True, skip_group_check=True),
                     r=[B["khat"], B["vb"]], w=[PS[4]])
            gC = B["gC"]
            p.op("dve", lambda: nc.vector.tensor_tensor(out=ST[:], in0=ST[:], in1=gC.t[:, 0:16].rearrange("p (a b) -> p a b", b=2)[:, :, 0:1].broadcast_to([128, 8, 64]), op=ALU.mult),
                 r=[ST, gC], w=[ST])
            p.op("dve", lambda: nc.vector.tensor_tensor(out=ST.t[:].rearrange("p a v -> p (a v)"), in0=ST.t[:].rearrange("p a v -> p (a v)"), in1=PS[4][:], op=ALU.add),
                 r=[ST, PS[4]], w=[ST])
            p.op("act", lambda: nc.scalar.copy(out=STbd[0:64, :, 0:64], in_=ST[0:64, :, :]), r=[ST], w=[STbd])
            p.op("pool", lambda: nc.gpsimd.tensor_copy(out=STbd[64:128, :, 64:128], in_=ST[64:128, :, :]), r=[ST], w=[STbd])
            flush_out()
            for half in range(2):
                hs = slice(half * 512, (half + 1) * 512)
                p.op("act", lambda: nc.scalar.copy(out=osb[:, hs], in_=PS[2 + half][:]), r=[PS[2 + half]], w=[osb])

        def outp(s, n, B):
            r0 = s * P + n * 128
            sm = sm2
            if dbg is not None:
                p.dma(dbg[r0:r0 + 128, :], osb[:], r=[osb], w=[DR["dbgt"]])
            p.op("dve", lambda: nc.vector.tensor_reduce(out=sm[:, 4, :], in_=v3(osb.t[:]), axis=AX.X, op=ALU.add), r=[osb], w=[sm])
            yield
            p.op("pool", lambda: nc.gpsimd.tensor_tensor(out=obuf[:], in0=osb[:], in1=osb[:], op=ALU.mult), r=[osb], w=[obuf])
            yield
            p.op("dve", lambda: nc.vector.tensor_reduce(out=sm[:, 5, :], in_=v3(obuf.t[:]), axis=AX.X, op=ALU.add), r=[obuf], w=[sm])
            p.op("dve", lambda: nc.vector.tensor_scalar(out=sm[:, 4, :], in0=sm[:, 4, :], scalar1=1.0 / 64, scalar2=None, op0=ALU.mult), r=[sm], w=[sm])
            p.op("dve", lambda: nc.vector.tensor_tensor(out=sm[:, 6, :], in0=sm[:, 4, :], in1=sm[:, 4, :], op=ALU.mult), r=[sm], w=[sm])
            p.op("dve", lambda: nc.vector.scalar_tensor_tensor(out=sm[:, 5, :], in0=sm[:, 5, :], scalar=1.0 / 64, in1=sm[:, 6, :], op0=ALU.mult, op1=ALU.subtract),
                 r=[sm], w=[sm])
            yield
            p.op("dve", lambda: nc.vector.tensor_scalar(out=sm[:, 5, :], in0=sm[:, 5, :], scalar1=GN_EPS, scalar2=None, op0=ALU.add), r=[sm], w=[sm])
            p.op("pool", lambda: nc.gpsimd.tensor_tensor(out=sm[:, 7, :], in0=sm[:, 5, :], in1=mhalf16[:], op=ALU.pow), r=[sm, mhalf16], w=[sm])
            yield
            p.op("dve", lambda: nc.vector.tensor_tensor(out=v3(osb.t[:]), in0=v3(osb.t[:]), in1=bc(sm[:, 4, :]), op=ALU.subtract), r=[osb, sm], w=[osb])
            yield
            p.op("dve", lambda: nc.vector.tensor_tensor(out=v3(osb.t[:]), in0=v3(osb.t[:]), in1=bc(sm[:, 7, :]), op=ALU.mult), r=[osb, sm], w=[osb])
            yield
            p.op("pool", lambda: nc.gpsimd.tensor_tensor(out=osb[:], in0=osb[:], in1=prm["rwkv_gn_w"][:], op=ALU.mult), r=[osb, prm["rwkv_gn_w"]], w=[osb])
            yield
            p.op("pool", lambda: nc.gpsimd.tensor_tensor(out=osb[:], in0=osb[:], in1=prm["rwkv_gn_b"][:], op=ALU.add), r=[osb, prm["rwkv_gn_b"]], w=[osb])
            yield
            p.op("dve", lambda: nc.vector.tensor_tensor(out=v3(obuf.t[:]), in0=v3(B["vb"].t[:]), in1=bc(B["sm"][:, 3, :]), op=ALU.mult), r=[B["vb"], B["sm"]], w=[obuf])
            yield
            p.op("pool", lambda: nc.gpsimd.tensor_tensor(out=obuf[:], in0=obuf[:], in1=osb[:], op=ALU.add), r=[obuf, osb], w=[obuf])
            yield
            p.op("pool", lambda: nc.gpsimd.tensor_tensor(out=obuf[:], in0=obuf[:], in1=B["gs"][:], op=ALU.mult), r=[obuf, B["gs"]], w=[obuf])
            yield
            for _ in range(4):
                yield
            for half in range(2):
                pst = PS[6 + half]
                for k4 in range(4):
                    kc = half * 4 + k4
                    p.op("pe", lambda: nc.tensor.transpose(pst[:, k4 * 128:(k4 + 1) * 128], obuf[:, kc * 128:(kc + 1) * 128], ident[:]), r=[obuf, ident], w=[pst])
                src_ = pst.t[:].rearrange("p (k t) -> p k t", k=4)
                if half == 0:
                    p.op("act", lambda: nc.scalar.copy(out=oTb[:, 0:4, :], in_=src_), r=[pst], w=[oTb])
                else:
                    p.op("dve", lambda: nc.vector.tensor_copy(out=oTb[:, 4:8, :], in_=src_), r=[pst], w=[oTb])
                yield
            p.dma(DR["OT_rwkv"].rearrange("(kc p) t -> p kc t", p=128)[:, :, r0:r0 + 128], oTb[:], r=[oTb], w=[DR["OT_t"][1]])
            yield

        tiles = [(s, n) for s in (range(NS) if seqs is None else seqs) for n in range(ntl)]

        def run_all(gen):
            for _ in gen:
                pass

        def chain(*gens):
            for g in gens:
                if g is not None:
                    for _ in g:
                        yield

        if not pipelined:
            for idx, (s, n) in enumerate(tiles):
                B = SETS[idx % 2]
                run_all(prep_load(s, n))
                run_all(prep(s, n, B))
                main(s, n, B, lambda: None)
                run_all(outp(s, n, B))
        else:
            run_all(prep_load(tiles[0][0], tiles[0][1]))
            run_all(prep(tiles[0][0], tiles[0][1], SETS[0]))
            for idx, (s, n) in enumerate(tiles):
                B = SETS[idx % 2]
                g_out = outp(tiles[idx - 1][0], tiles[idx - 1][1], SETS[(idx - 1) % 2]) if idx > 0 else None
                g_load = prep_load(tiles[idx + 1][0], tiles[idx + 1][1]) if idx + 1 < len(tiles) else None
                g_prep = prep(tiles[idx + 1][0], tiles[idx + 1][1], SETS[(idx + 1) % 2]) if idx + 1 < len(tiles) else None
                g_lo = chain(g_load, g_out)
                bg = chain(g_lo, g_prep)

                def inj(k=1):
                    for _ in range(k):
                        try:
                            next(bg)
                        except StopIteration:
                            return
                main(s, n, B, inj, lambda: run_all(g_lo))
                run_all(bg)
            run_all(outp(tiles[-1][0], tiles[-1][1], SETS[(len(tiles) - 1) % 2]))
        p.barrier()


def host_consts():
    c = {}
    i = np.arange(128)
    c["ident"] = np.eye(128, dtype=np.float32)
    c["tri_le"] = (i[:, None] <= i[None, :]).astype(np.float32)
    c["tri_lt"] = (i[:, None] < i[None, :]).astype(np.float32)
    c["tri_gt"] = (i[:, None] > i[None, :]).astype(np.float32)
    same = (i[:, None] // 32) == (i[None, :] // 32)
    c["b_le"] = (same & (i[:, None] <= i[None, :])).astype(np.float32)
    c["b_gt"] = (same & (i[:, None] > i[None, :])).astype(np.float32)
    bind = np.zeros((128, 128), np.float32)
    bind[i, i // 32] = 1.0
    c["bind"] = bind
    c["ones"] = np.ones((128, 128), np.float32)
    t = (np.arange(NT)[None, :] * 128 + np.arange(128)[:, None]).astype(np.float32) - PAD
    inv = (1.0 / (10000.0 ** (np.arange(0, 64, 2, dtype=np.float32) / 64))).astype(np.float32)
    ang = (t[:, :, None] * inv[None, None, :]).astype(np.float32)
    cs, sn = np.cos(ang).astype(np.float32), np.sin(ang).astype(np.float32)
    cos2 = np.zeros((128, NT, 2, 2, 32), np.float32)
    sin2 = np.zeros((128, NT, 2, 2, 32), np.float32)
    cos2[:] = cs[:, :, None, None, :]
    sin2[:, :, :, 0, :] = -sn[:, :, None, :]
    sin2[:, :, :, 1, :] = sn[:, :, None, :]
    c["cos2"] = cos2.reshape(128, NT, 128)
    c["sin2"] = sin2.reshape(128, NT, 128)
    return c


PARAM_SHAPES = {
    "pre_norm_w": [2, D], "post_norm_w": [2, D], "w_in": [2, D, IN_W],
    "lambda_q1": [2, 64], "lambda_k1": [2, 64], "lambda_q2": [2, 64], "lambda_k2": [2, 64], "att_norm_w": [2, 128],
    "rwkv_mu": [2, RW_W], "rwkv_w0": [2, 1024], "rwkv_w_up": [2, 64, 1024], "rwkv_a0": [2, 1024], "rwkv_a_up": [2, 64, 1024],
    "rwkv_k_k": [2, 1024], "rwkv_k_a": [2, 1024], "rwkv_r_k": [2, 16, 64], "rwkv_gn_w": [2, 1024], "rwkv_gn_b": [2, 1024],
    "hgrn_lower_bounds": [2, 1024], "hgrn_norm_w": [2, 128],
    "w_att_out": [2, D, D], "w_rwkv_out": [2, D, D], "w_hgrn_out": [2, D, D], "w_o": [2, D, D],
}
CONST_NAMES = ("ident", "tri_le", "tri_lt", "tri_gt", "b_le", "b_gt", "bind", "ones")


def build_program(nlayers=DEPTH):
    nc = bass.Bass("TRN2", target_bir_lowering=False)
    DR = {}
    H0 = nc.dram_tensor("h0", [ROWS, D], F32, kind="ExternalInput").ap()
    for n, sh in PARAM_SHAPES.items():
        DR[n] = nc.dram_tensor(n, sh, F32, kind="ExternalInput").ap()
    cd = {n: nc.dram_tensor(n, [128, 128], F32, kind="ExternalInput").ap() for n in CONST_NAMES}
    DR["cos2"] = nc.dram_tensor("cos2", [128, NT, 128], F32, kind="ExternalInput").ap()
    DR["sin2"] = nc.dram_tensor("sin2", [128, NT, 128], F32, kind="ExternalInput").ap()
    DR["Z"] = [nc.dram_tensor("Zscr%d" % i, [P, IN_W], F32, kind="Internal").ap() for i in range(NS)]
    for n in ("OT_att", "OT_rwkv", "OT_hgrn"):
        DR[n] = nc.dram_tensor(n + "_scr", [D, ROWS], BF16, kind="Internal").ap()
    Hs = nc.dram_tensor("Hscr", [ROWS, D], F32, kind="Internal").ap()
    OUT = nc.dram_tensor("out", [NS, 2048, D], F32, kind="ExternalOutput").ap()
    DR["OT_t"] = [Tl(None, "ot%d" % i, multi=True) for i in range(3)]
    DR["Zt"] = Tl(None, "Z", multi=True)
    DR["Outt"] = Tl(None, "out", multi=True)
    H_t = [Tl(None, "H0", multi=True), Tl(None, "Hs", multi=True)]
    with ExitStack() as es:
        p = Prog(nc, es)
        G = {}
        for i, n in enumerate(CONST_NAMES):
            G[n] = p.tile(n, [128, 128], F32)
            p.dma(G[n][:], cd[n], w=[G[n]], q=("sp" if i % 2 == 0 else "act"))
        G["PS"] = [p.tile("ps%d" % i, [128, 512], F32, psum=True) for i in range(8)]
        PS = G["PS"]
        for l in range(nlayers):
            Hin = H0 if l == 0 else Hs
            Hin_t = H_t[0] if l == 0 else H_t[1]
            last = (l == nlayers - 1)
            with ExitStack() as es2:
                T = lambda name, shape, dt, **kw: p.tile("pj_" + name, shape, dt, es=es2, **kw)
                C = {"ident": G["ident"]}
                C["uT"] = T("uT", [128, 8, ROWS], BF16, multi=True)
                C["ht"] = [T("ht%d" % i, [128, D], F32) for i in range(2)]
                C["ut"] = [T("ut%d" % i, [128, D], F32) for i in range(2)]
                C["sq"] = T("sq", [128, D], F32)
                C["ss"] = [T("ss%d" % i, [128, 8], F32) for i in range(2)]
                C["pst"] = PS[0:2]
                C["psz"] = PS[2:6]
                C["wst"] = [T("wst%d" % i, [128, 8, 512], F32) for i in range(2)]
                C["wbf"] = [T("wbf%d" % i, [128, 8, 512], BF16) for i in range(2)]
                C["zo"] = [T("zo%d" % i, [128, 512], F32) for i in range(4)]
                C["Zt"] = DR["Zt"]
                C["Ht"] = Hin_t
                C["mhalf"] = T("mhalf", [128, 1], F32)
                p.op("pool", lambda: nc.gpsimd.memset(C["mhalf"][:], -0.5), w=[C["mhalf"]])
                prew_bc = T("prew_bc", [128, D], F32)
                load_bc(p, nc, prew_bc, DR["pre_norm_w"][l:l + 1, :])
                phase_proj(p, nc, Hin, DR["Z"], DR["w_in"][l], prew_bc, C)
                p.barrier()
            phase_att(p, nc, l, DR, G)
            with ExitStack() as es_rw:
                Trw = lambda name, shape, dt, **kw: p.tile("rwp_" + name, shape, dt, es=es_rw, **kw)
                rw_pre = rwkv_setup(p, nc, l, DR, G, Trw)
                phase_hgrn(p, nc, l, DR, G)
                phase_rwkv(p, nc, l, DR, G, pre=rw_pre)
            DR["Ht"] = Hin_t
            DR["Ht2"] = H_t[1]
            phase_merge(p, nc, l, DR, G, Hin, Hs, out_final=(OUT if last else None))
        p.barrier()
        print("n_inst", p.n_inst)
    return nc


_NC_CACHE = {}


def kernel(**inputs):
    x = np.asarray(inputs["x"], dtype=np.float32)
    meta = np.asarray(inputs["meta_tokens"], dtype=np.float32)
    B = x.shape[0]
    ncores = B // NS
    if "nc" not in _NC_CACHE:
        _NC_CACHE["nc"] = build_program()
    nc = _NC_CACHE["nc"]
    consts = host_consts()
    shared = {n: np.ascontiguousarray(np.asarray(inputs[n], dtype=np.float32)) for n in PARAM_SHAPES}
    shared.update(consts)
    in_maps = []
    for c in range(ncores):
        h0 = np.zeros((NS, P, D), np.float32)
        h0[:, PAD:PAD + 16] = meta[None]
        h0[:, PAD + 16:] = x[c * NS:(c + 1) * NS]
        m = dict(shared)
        m["h0"] = h0.reshape(ROWS, D)
        in_maps.append(m)
    res = run_bass_kernel_spmd(nc, in_maps, core_ids=list(range(ncores)))
    out = np.concatenate([np.asarray(r["out"], dtype=np.float32) for r in res.results], axis=0)
    return out
```

```python
import numpy as np
import ml_dtypes
from contextlib import ExitStack
import concourse.bass as bass
import concourse.mybir as mybir
from concourse.bass_utils import run_bass_kernel_spmd

F32 = mybir.dt.float32
BF16 = mybir.dt.bfloat16
AF = mybir.ActivationFunctionType
ALU = mybir.AluOpType
AX = mybir.AxisListType

D = 1024
NS = 2
NT = 17
P = NT * 128
PAD = 112
ROWS = NS * P
IN_W = 15488
RW_OFF = 4096
RW_W = 4224
HG_OFF = RW_OFF + RW_W
MG_OFF = HG_OFF + 4096
DEPTH = 2
EPS = 1e-6
GN_EPS = 64e-5


def zr(Z, a, b):
    if isinstance(Z, list):
        si = a // P
        assert (b - 1) // P == si
        return Z[si][a - si * P:b - si * P]
    return Z[a:b]


class Tl:
    __slots__ = ("t", "w", "r", "name", "multi", "wd", "excl", "wtrue")

    def __init__(self, t, name="", multi=False, excl=False):
        self.excl = excl
        self.wtrue = True
        self.t = t
        self.w = None
        self.r = {}
        self.wd = {}
        self.multi = multi
        self.name = name

    def __getitem__(self, idx):
        return self.t[idx]


class Prog:
    def __init__(self, nc, es, same_engine_sync=True):
        self.nc = nc
        self.es = es
        self.E = {"pe": nc.tensor, "dve": nc.vector, "act": nc.scalar, "pool": nc.gpsimd, "sp": nc.sync}
        self.sem = {e: es.enter_context(nc.semaphore("sem_" + e)) for e in ("pe", "dve", "act", "pool")}
        self.cnt = {e: 0 for e in self.sem}
        self.epoch = {e: 0 for e in self.sem}
        self.old = []
        self.waited = {e: {} for e in self.E}
        self.same = same_engine_sync
        self.dq = {}
        for q, n in (("sp", 40),):
            self.dq[q] = {"sems": [es.enter_context(nc.semaphore("dsem_%s%d" % (q, i))) for i in range(n)],
                          "val": [0] * n, "next": 0}
        self.n_inst = 0
        self.mute = False
        self.relax = True
        self.n_relaxed = 0

    def tile(self, name, shape, dt, psum=False, multi=False, es=None):
        es = es or self.es
        self.n_tiles = getattr(self, "n_tiles", 0) + 1
        name = "%s_%d" % (name, self.n_tiles)
        if psum:
            t = es.enter_context(self.nc.psum_tensor("pt_" + name, shape, dt))
        else:
            t = es.enter_context(self.nc.sbuf_tensor("sb_" + name, shape, dt))
        return Tl(t, name, multi, excl=psum)

    def _wait(self, e, tok, kind="RAW"):
        if tok is None:
            return
        sem, val, key, owner = tok
        if owner == e and (e == "pe" or not self.same):
            return
        if owner == e and self.relax and e in ("dve", "act") and kind in ("WAR", "RR"):
            self.n_relaxed += 1
            return
        if self.waited[e].get(key, 0) >= val:
            return
        self.E[e].wait_ge(sem, val)
        self.waited[e][key] = val

    def _deps(self, e, r, w, xread=()):
        for t in r:
            if t.multi:
                for tok in t.wd.values():
                    self._wait(e, tok)
            else:
                self._wait(e, t.w)
        for t in w:
            if not t.multi:
                if t in xread:
                    self._wait(e, t.w, "RAW" if t.wtrue else "RR")
                else:
                    self._wait(e, t.w, "WAW" if t.wtrue else "WAR")
            for tok in t.r.values():
                self._wait(e, tok, "WAR")

    def _record(self, tok, r, w):
        for t in r:
            t.r[tok[2]] = tok
        for t in w:
            if t.multi:
                t.wd[tok[2]] = tok
            else:
                t.w = tok
                t.r = {}

    def barrier(self):
        self.mute = False
        toks = [(self.sem[e], self.cnt[e], "c_%s_%d" % (e, self.epoch[e]), e) for e in self.sem if self.cnt[e] > 0]
        toks += self.old
        for q, dq in self.dq.items():
            for i, v in enumerate(dq["val"]):
                if v > 0:
                    toks.append((dq["sems"][i], v, "d_%s%d" % (q, i), None))
        for e in self.E:
            for tok in toks:
                if tok[3] == e:
                    continue
                self._wait(e, tok)

    def op(self, e, fn, r=(), w=()):
        if self.mute:
            return None
        xread = ()
        if any(t.excl for t in r):
            xread = [t for t in r if t.excl and t not in w]
            w = list(w) + xread
            r = [t for t in r if not t.excl]
        self._deps(e, r, w, xread)
        if self.cnt[e] >= 16000:
            self.old.append((self.sem[e], self.cnt[e], "c_%s_%d" % (e, self.epoch[e]), None))
            self.epoch[e] += 1
            self.sem[e] = self.es.enter_context(self.nc.semaphore("sem_%s_%d" % (e, self.epoch[e])))
            self.cnt[e] = 0
        ins = fn()
        self.cnt[e] += 1
        ins.then_inc(self.sem[e], 1)
        tok = (self.sem[e], self.cnt[e], "c_%s_%d" % (e, self.epoch[e]), e)
        self._record(tok, r, w)
        for t in w:
            t.wtrue = t not in xread
        self.n_inst += 1
        return tok

    def dma(self, out, in_, r=(), w=(), q="sp", **kw):
        if self.mute:
            return None
        q = "sp"
        dq = self.dq[q]
        i = dq["next"]
        dq["next"] = (i + 1) % len(dq["sems"])
        key = "d_%s%d" % (q, i)
        if dq["val"][i] > 0:
            self._wait(q, (dq["sems"][i], dq["val"][i], key, None))
        self._deps(q, r, w)
        dq["val"][i] += 16
        self.E[q].dma_start(out=out, in_=in_, **kw).then_inc(dq["sems"][i], 16)
        tok = (dq["sems"][i], dq["val"][i], key, None)
        self._record(tok, r, w)
        self.n_inst += 1
        return tok

    def finish(self, toks):
        for tok in toks:
            self._wait("sp", tok)


def phase_proj(p, nc, H, Z, w_in_l, prew_bc, C, ntiles=NS * NT, ncolblk=None):
    uT = C["uT"]
    ident = C["ident"]
    for tt in range(ntiles):
        ht = C["ht"][tt % 2]
        p.dma(ht[:], H[tt * 128:(tt + 1) * 128, :], r=([C["Ht"]] if "Ht" in C else []), w=[ht])
        sq = C["sq"]
        ss = C["ss"][tt % 2]
        p.op("act", lambda: nc.scalar.activation(out=sq[:], in_=ht[:], func=AF.Square, accum_out=ss[:, 0:1]),
             r=[ht], w=[sq, ss])
        p.op("dve", lambda: nc.vector.tensor_scalar(out=ss[:, 1:2], in0=ss[:, 0:1], scalar1=1.0 / D, scalar2=EPS,
                                                    op0=ALU.mult, op1=ALU.add), r=[ss], w=[ss])
        p.op("pool", lambda: nc.gpsimd.tensor_tensor(out=ss[:, 3:4], in0=ss[:, 1:2], in1=C["mhalf"][:], op=ALU.pow), r=[ss, C["mhalf"]], w=[ss])
        ut = C["ut"][tt % 2]
        p.op("dve", lambda: nc.vector.scalar_tensor_tensor(out=ut[:], in0=ht[:], scalar=ss[:, 3:4], in1=prew_bc[:],
                                                           op0=ALU.mult, op1=ALU.mult), r=[ht, ss, prew_bc], w=[ut])
        for half in range(2):
            pst = C["pst"][half]
            for k4 in range(4):
                kc = half * 4 + k4
                p.op("pe", lambda: nc.tensor.transpose(pst[:, k4 * 128:(k4 + 1) * 128], ut[:, kc * 128:(kc + 1) * 128],
                                                       ident[:]), r=[ut, ident], w=[pst])
            dst = uT.t[:, half * 4:(half + 1) * 4, tt * 128:(tt + 1) * 128]
            src = pst.t[:].rearrange("p (k t) -> p k t", k=4)
            if half == 0:
                p.op("act", lambda: nc.scalar.copy(out=dst, in_=src), r=[pst], w=[uT])
            else:
                p.op("dve", lambda: nc.vector.tensor_copy(out=dst, in_=src), r=[pst], w=[uT])
    wv = w_in_l.rearrange("(kc p) n -> p kc n", p=128)
    ncb = (IN_W + 511) // 512 if ncolblk is None else ncolblk
    ev = 0
    def load_w(cb):
        c0 = cb * 512
        cw = min(512, IN_W - c0)
        wst = C["wst"][cb % 2]
        wbf = C["wbf"][cb % 2]
        p.dma(wst[:, :, 0:cw], wv[:, :, c0:c0 + cw], w=[wst], q="sp")
        p.op("pool", lambda: nc.gpsimd.tensor_copy(out=wbf[:, 0:4, 0:cw], in_=wst[:, 0:4, 0:cw]), r=[wst], w=[wbf])
        p.op("pool", lambda: nc.gpsimd.tensor_copy(out=wbf[:, 4:8, 0:cw], in_=wst[:, 4:8, 0:cw]), r=[wst], w=[wbf])

    load_w(0)
    for cb in range(ncb):
        c0 = cb * 512
        cw = min(512, IN_W - c0)
        wbf = C["wbf"][cb % 2]
        if cb + 1 < ncb:
            load_w(cb + 1)
        for tt in range(ntiles):
            psz = C["psz"][ev % 4]
            zo = C["zo"][ev % 4]
            for kc in range(8):
                p.op("pe", lambda: nc.tensor.matmul(psz[:, 0:cw], lhsT=uT[:, kc, tt * 128:(tt + 1) * 128],
                                                    rhs=wbf[:, kc, 0:cw], start=(kc == 0), stop=(kc == 7)),
                     r=[uT, wbf], w=[psz])
            if ev % 2 == 0:
                p.op("act", lambda: nc.scalar.copy(out=zo[:, 0:cw], in_=psz[:, 0:cw]), r=[psz], w=[zo])
            else:
                p.op("dve", lambda: nc.vector.tensor_copy(out=zo[:, 0:cw], in_=psz[:, 0:cw]), r=[psz], w=[zo])
            p.dma(zr(Z, tt * 128, (tt + 1) * 128)[:, c0:c0 + cw], zo[:, 0:cw], r=[zo], w=[C["Zt"]], q="sp")
            ev += 1


def load_bc(p, nc, tl, row_ap, q="sp"):
    p.dma(tl[:], row_ap.partition_broadcast(128), w=[tl], q=q)


def phase_merge(p, nc, l, DR, G, Hin, Hout, out_final=None, tiles=None):
    PS = G["PS"]
    ident = G["ident"]
    Z = DR["Z"]
    with ExitStack() as es:
        T = lambda name, shape, dt, **kw: p.tile("mg_" + name, shape, dt, es=es, **kw)
        wst = T("wst", [128, 8, 1024], F32)
        W = [T("w%d" % i, [128, 8, 1024], BF16) for i in range(4)]
        names = ["w_att_out", "w_rwkv_out", "w_hgrn_out", "w_o"]
        for i in range(4):
            p.dma(wst[:], DR[names[i]][l].rearrange("(kc p) n -> p kc n", p=128), w=[wst])
            p.op("pool", lambda: nc.gpsimd.tensor_copy(out=W[i][:, 0:4, :], in_=wst[:, 0:4, :]), r=[wst], w=[W[i]])
            p.op("act", lambda: nc.scalar.copy(out=W[i][:, 4:8, :], in_=wst[:, 4:8, :]), r=[wst], w=[W[i]])
        mhalf = T("mhalf", [128, 1], F32)
        p.op("pool", lambda: nc.gpsimd.memset(mhalf[:], -0.5), w=[mhalf])
        postw = T("postw", [128, D], F32)
        load_bc(p, nc, postw, DR["post_norm_w"][l:l + 1, :])
        oT = [[T("oT%d_%d" % (b, i), [128, 8, 128], BF16) for i in range(3)] for b in range(3)]
        mg = [T("mgt%d" % i, [128, 3072], F32) for i in range(3)]
        hin = [T("hin%d" % i, [128, D], F32) for i in range(3)]
        ys = [T("y%d" % i, [128, D], F32) for i in range(2)]
        tmp = [T("tmp%d" % i, [128, 512], F32) for i in range(2)]
        yT = T("yT", [128, 8, 128], BF16)
        hn = [T("hn%d" % i, [128, D], F32) for i in range(2)]
        sq = T("sq", [128, 512], F32)
        st = [T("st%d" % i, [128, 8], F32) for i in range(2)]
        OTs = [DR["OT_att"], DR["OT_rwkv"], DR["OT_hgrn"]]
        tl_list = list(range(NS * NT)) if tiles is None else tiles
        cnt = 0
        def stage0(it, tt):
            r0 = tt * 128
            for b in range(3):
                p.dma(oT[b][it % 3][:], OTs[b].rearrange("(kc p) t -> p kc t", p=128)[:, :, r0:r0 + 128],
                      r=[DR["OT_t"][b]], w=[oT[b][it % 3]], q="act")
            m = mg[it % 3]
            p.dma(m[:], zr(Z, r0, r0 + 128)[:, MG_OFF:MG_OFF + 3072], r=[DR["Zt"]], w=[m])
            hi_ = hin[it % 3]
            p.dma(hi_[:], Hin[r0:r0 + 128, :], r=[DR["Ht"]], w=[hi_])

        def stage1(it, tt):
            nonlocal cnt
            y = ys[it % 2]
            r0 = tt * 128
            m = mg[it % 3]
            hi_ = hin[it % 3]
            p.op("act", lambda: nc.scalar.activation(out=m[:], in_=m[:], func=AF.Sigmoid), r=[m], w=[m])
            for b in range(3):
                ob = oT[b][it % 3]
                for half in range(2):
                    ps = PS[cnt % 2]
                    cnt += 1
                    for kc in range(8):
                        p.op("pe", lambda: nc.tensor.matmul(ps[:], lhsT=ob[:, kc, :], rhs=W[b][:, kc, half * 512:(half + 1) * 512],
                                                            start=(kc == 0), stop=(kc == 7)), r=[ob, W[b]], w=[ps])
                    gsl = m[:, b * 1024 + half * 512: b * 1024 + (half + 1) * 512]
                    ysl = y[:, half * 512:(half + 1) * 512]
                    if b == 0:
                        p.op("dve", lambda: nc.vector.tensor_tensor(out=ysl, in0=ps[:], in1=gsl, op=ALU.mult), r=[ps, m], w=[y])
                    else:
                        t_ = tmp[cnt % 2]
                        p.op("dve", lambda: nc.vector.tensor_tensor(out=t_[:], in0=ps[:], in1=gsl, op=ALU.mult), r=[ps, m], w=[t_])
                        p.op("pool", lambda: nc.gpsimd.tensor_tensor(out=ysl, in0=ysl, in1=t_[:], op=ALU.add), r=[t_, y], w=[y])

        def stage2(it, tt):
            y = ys[it % 2]
            r0 = tt * 128
            hi_ = hin[it % 3]
            for half in range(2):
                pst = PS[2 + half]
                for k4 in range(4):
                    kc = half * 4 + k4
                    p.op("pe", lambda: nc.tensor.transpose(pst[:, k4 * 128:(k4 + 1) * 128], y[:, kc * 128:(kc + 1) * 128], ident[:]),
                         r=[y, ident], w=[pst])
                src = pst.t[:].rearrange("p (k t) -> p k t", k=4)
                if half == 0:
                    p.op("act", lambda: nc.scalar.copy(out=yT[:, 0:4, :], in_=src), r=[pst], w=[yT])
                else:
                    p.op("dve", lambda: nc.vector.tensor_copy(out=yT[:, 4:8, :], in_=src), r=[pst], w=[yT])
            s_ = st[it % 2]
            h_ = hn[it % 2]
            pso = [PS[4 + (it % 2) * 2], PS[5 + (it % 2) * 2]]
            for half in range(2):
                for kc in range(8):
                    p.op("pe", lambda: nc.tensor.matmul(pso[half][:], lhsT=yT[:, kc, :], rhs=W[3][:, kc, half * 512:(half + 1) * 512],
                                                        start=(kc == 0), stop=(kc == 7)), r=[yT, W[3]], w=[pso[half]])
                p.op("act", lambda: nc.scalar.activation(out=sq[:], in_=pso[half][:], func=AF.Square, accum_out=s_[:, half:half + 1]),
                     r=[pso[half]], w=[sq, s_])
            p.op("dve", lambda: nc.vector.tensor_tensor(out=s_[:, 2:3], in0=s_[:, 0:1], in1=s_[:, 1:2], op=ALU.add), r=[s_], w=[s_])
            p.op("dve", lambda: nc.vector.tensor_scalar(out=s_[:, 3:4], in0=s_[:, 2:3], scalar1=1.0 / D, scalar2=EPS,
                                                        op0=ALU.mult, op1=ALU.add), r=[s_], w=[s_])
            p.op("pool", lambda: nc.gpsimd.tensor_tensor(out=s_[:, 5:6], in0=s_[:, 3:4], in1=mhalf[:], op=ALU.pow), r=[s_, mhalf], w=[s_])
            for half in range(2):
                hs = h_[:, half * 512:(half + 1) * 512]
                p.op("dve", lambda: nc.vector.scalar_tensor_tensor(out=hs, in0=pso[half][:], scalar=s_[:, 5:6],
                                                                   in1=postw[:, half * 512:(half + 1) * 512],
                                                                   op0=ALU.mult, op1=ALU.mult), r=[pso[half], s_, postw], w=[h_])
            p.op("pool", lambda: nc.gpsimd.tensor_tensor(out=h_[:], in0=h_[:], in1=hi_[:], op=ALU.add), r=[h_, hi_], w=[h_])
            n = tt % NT
            if n == 0:
                p.op("pool", lambda: nc.gpsimd.memset(h_[0:96, :], 0.0), r=[], w=[h_])
                p.op("pool", lambda: nc.gpsimd.memset(h_[96:112, :], 0.0), r=[], w=[h_])
            if out_final is None:
                p.dma(Hout[r0:r0 + 128, :], h_[:], r=[h_], w=[DR["Ht2"]])
            else:
                s = tt // NT
                if n == 0:
                    pass
                else:
                    p.dma(out_final[s, (n - 1) * 128:n * 128, :], h_[:], r=[h_], w=[DR["Outt"]])
        for it, tt in enumerate(tl_list):
            if it == 0:
                stage0(0, tt)
                if len(tl_list) > 1:
                    stage0(1, tl_list[1])
                stage1(0, tt)
            if it + 2 < len(tl_list):
                stage0(it + 2, tl_list[it + 2])
            if it + 1 < len(tl_list):
                stage1(it + 1, tl_list[it + 1])
            stage2(it, tt)
        p.barrier()


def phase_att(p, nc, l, DR, G, pairs=None, qts=None):
    import math as _m
    PS = G["PS"]
    ident = G["ident"]
    tri = G["tri_le"]
    Z = DR["Z"]
    lam_init = 0.8 - 0.6 * _m.exp(-0.3 * l)
    with ExitStack() as es:
        T = lambda name, shape, dt, **kw: p.tile("at_" + name, shape, dt, es=es, **kw)
        cos2 = T("cos2", [128, NT, 128], F32)
        sin2 = T("sin2", [128, NT, 128], F32)
        p.dma(cos2[:], DR["cos2"], w=[cos2])
        p.dma(sin2[:], DR["sin2"], w=[sin2])
        lv = T("lv", [128, 4, 64], F32)
        for i, n in enumerate(("lambda_q1", "lambda_k1", "lambda_q2", "lambda_k2")):
            p.dma(lv[:, i, :], DR[n][l:l + 1, :].partition_broadcast(128), w=[lv])
        lt = T("lt", [128, 2, 64], F32)
        ls = T("ls", [128, 8], F32)
        p.op("dve", lambda: nc.vector.tensor_tensor(out=lt[:, 0, :], in0=lv[:, 0, :], in1=lv[:, 1, :], op=ALU.mult), r=[lv], w=[lt])
        p.op("dve", lambda: nc.vector.tensor_tensor(out=lt[:, 1, :], in0=lv[:, 2, :], in1=lv[:, 3, :], op=ALU.mult), r=[lv], w=[lt])
        p.op("dve", lambda: nc.vector.tensor_reduce(out=ls[:, 0:2], in_=lt[:], axis=AX.X, op=ALU.add), r=[lt], w=[ls])
        p.op("act", lambda: nc.scalar.activation(out=ls[:, 2:4], in_=ls[:, 0:2], func=AF.Exp), r=[ls], w=[ls])
        p.op("dve", lambda: nc.vector.tensor_tensor(out=ls[:, 4:5], in0=ls[:, 3:4], in1=ls[:, 2:3], op=ALU.subtract), r=[ls], w=[ls])
        p.op("dve", lambda: nc.vector.tensor_scalar(out=ls[:, 5:6], in0=ls[:, 4:5], scalar1=-lam_init, scalar2=None, op0=ALU.add), r=[ls], w=[ls])
        neg_lam = ls[:, 5:6]
        normw = T("normw", [128, 128], F32)
        load_bc(p, nc, normw, DR["att_norm_w"][l:l + 1, :])
        p.op("dve", lambda: nc.vector.tensor_scalar(out=normw[:], in0=normw[:], scalar1=(1.0 - lam_init), scalar2=None, op0=ALU.mult),
             r=[normw], w=[normw])
        SETS = []
        for kb in range(2):
            Bf = {}
            for nm in ("qraw", "kraw", "vraw", "graw"):
                Bf[nm] = T("%s%d" % (nm, kb), [128, NT, 128], F32)
            Bf["qT"] = T("qT%d" % kb, [128, P], BF16)
            Bf["kT"] = T("kT%d" % kb, [128, P], BF16)
            Bf["v1"] = T("v1%d" % kb, [128, NT, 130], BF16)
            v1_ = Bf["v1"]
            p.op("pool", lambda: nc.gpsimd.memset(v1_[:, :, 128:130], 1.0), w=[v1_])
            p.op("pool", lambda: nc.gpsimd.memset(v1_[0:96, 0, 128:130], 0.0), w=[v1_])
            p.op("pool", lambda: nc.gpsimd.memset(v1_[96:112, 0, 128:130], 0.0), w=[v1_])
            SETS.append(Bf)
        t1 = T("t1", [128, NT, 128], F32)
        t2 = T("t2", [128, NT, 128], F32)
        obufs = [T("obuf%d" % i, [128, NT, 128], F32) for i in range(2)]
        oTb = T("oTb", [128, P], BF16)
        pT = [T("pT%d" % i, [128, 512], BF16) for i in range(5)]
        SB = [PS[0], PS[1], PS[6]]
        mhalf = T("mhalf", [128, 1], F32)
        p.op("pool", lambda: nc.gpsimd.memset(mhalf[:], -0.5), w=[mhalf])
        o_ = [T("o%d" % i, [128, 128], F32) for i in range(2)]
        sq = T("sq", [128, 128], F32)
        rs = [T("rs%d" % i, [128, 12], F32) for i in range(2)]
        pr = [(s, j) for s in range(NS) for j in range(8)] if pairs is None else pairs

        def setup_load(s, j, Bf):
            qraw, kraw, vraw, graw = (Bf[k] for k in ("qraw", "kraw", "vraw", "graw"))
            zs = zr(Z, s * P, (s + 1) * P).rearrange("(n p) c -> p n c", p=128)
            p.dma(qraw[:], zs[:, :, j * 128:(j + 1) * 128], r=[DR["Zt"]], w=[qraw])
            p.dma(kraw[:], zs[:, :, 1024 + j * 128:1024 + (j + 1) * 128], r=[DR["Zt"]], w=[kraw], q="act")
            p.dma(vraw[:], zs[:, :, 2048 + j * 128:2048 + (j + 1) * 128], r=[DR["Zt"]], w=[vraw])
            p.dma(graw[:], zs[:, :, 3072 + j * 128:3072 + (j + 1) * 128], r=[DR["Zt"]], w=[graw], q="act")

        def setup(s, j, Bf):
            qraw, kraw, vraw, graw, qT, kT, v1 = (Bf[k] for k in ("qraw", "kraw", "vraw", "graw", "qT", "kT", "v1"))
            for (raw, dstT) in ((qraw, qT), (kraw, kT)):
                E = nc.vector
                p.op("dve", lambda: E.tensor_tensor(out=t1[:], in0=raw[:], in1=cos2[:], op=ALU.mult), r=[raw, cos2], w=[t1])
                yield
                rv = raw.t[:].rearrange("p n (g h d) -> p (n g) h d", g=2, h=2)
                sv = sin2.t[:].rearrange("p n (g h d) -> p (n g) h d", g=2, h=2)
                tv = t2.t[:].rearrange("p n (g h d) -> p (n g) h d", g=2, h=2)
                p.op("dve", lambda: E.tensor_tensor(out=tv[:, :, 0, :], in0=rv[:, :, 1, :], in1=sv[:, :, 0, :], op=ALU.mult), r=[raw, sin2], w=[t2])
                yield
                p.op("dve", lambda: E.tensor_tensor(out=tv[:, :, 1, :], in0=rv[:, :, 0, :], in1=sv[:, :, 1, :], op=ALU.mult), r=[raw, sin2], w=[t2])
                yield
                p.op("dve", lambda: E.tensor_tensor(out=t1[:], in0=t1[:], in1=t2[:], op=ALU.add), r=[t1, t2], w=[t1])
                yield
                for n0 in range(0, NT, 4):
                    nn = min(4, NT - n0)
                    pst = PS[7]
                    for i in range(nn):
                        p.op("pe", lambda: nc.tensor.transpose(pst[:, i * 128:(i + 1) * 128], t1[:, n0 + i, :], ident[:]), r=[t1, ident], w=[pst])
                    p.op("dve", lambda: nc.vector.tensor_copy(out=dstT[:, n0 * 128:(n0 + nn) * 128], in_=pst[:, 0:nn * 128]), r=[pst], w=[dstT])
                    yield
            p.op("dve", lambda: nc.vector.tensor_copy(out=v1[:, :, 0:128], in_=vraw[:]), r=[vraw], w=[v1])
            yield
            p.op("act", lambda: nc.scalar.activation(out=graw[:], in_=graw[:], func=AF.Silu), r=[graw], w=[graw])
            yield

        def mainloop(s, j, Bf, inj, obuf):
            qT, kT, v1, graw = Bf["qT"], Bf["kT"], Bf["v1"], Bf["graw"]
            groups = []
            for qt in (range(NT) if qts is None else qts):
                for k0 in range(0, qt + 1, 4):
                    for g in range(2):
                        groups.append((qt, g, k0, min(k0 + 4, qt + 1)))

            def emit_scores(grp, gi):
                qt, g, k0, k1 = grp
                pss = SB[gi % 3]
                for i, kt in enumerate(range(k0, k1)):
                    p.op("pe", lambda: nc.tensor.matmul(pss[:, i * 128:(i + 1) * 128], lhsT=kT[g * 64:(g + 1) * 64, kt * 128:(kt + 1) * 128],
                                                        rhs=qT[g * 64:(g + 1) * 64, qt * 128:(qt + 1) * 128], start=True, stop=True),
                         r=[kT, qT], w=[pss])
                pt = pT[gi % 5]
                n = k1 - k0
                p.op("act", lambda: nc.scalar.activation(out=pt[:, 0:n * 128], in_=pss[:, 0:n * 128], func=AF.Exp, scale=0.125), r=[pss], w=[pt])
                if k1 - 1 == qt:
                    i = qt - k0
                    p.op("pool", lambda: nc.gpsimd.tensor_tensor(out=pt[:, i * 128:(i + 1) * 128], in0=pt[:, i * 128:(i + 1) * 128], in1=tri[:], op=ALU.mult),
                         r=[pt, tri], w=[pt])

            def emit_pv(grp, gi):
                qt, g, k0, k1 = grp
                pt = pT[gi % 5]
                pso = PS[2 + (qt % 2) * 2 + g]
                for i, kt in enumerate(range(k0, k1)):
                    p.op("pe", lambda: nc.tensor.matmul(pso[:, 0:129], lhsT=pt[:, i * 128:(i + 1) * 128], rhs=v1[:, kt, 0:129],
                                                        start=(kt == 0), stop=(kt == qt)), r=[pt, v1], w=[pso])
                if k1 - 1 == qt and g == 1:
                    epilogue(qt)
                    if qt >= 3:
                        inj()

            def epilogue(qt):
                O1 = PS[2 + (qt % 2) * 2]
                O2 = PS[2 + (qt % 2) * 2 + 1]
                r_ = rs[qt % 2]
                o = o_[qt % 2]
                p.op("dve", lambda: nc.vector.tensor_scalar(out=r_[:, 0:1], in0=O1[:, 128:129], scalar1=1e-30, scalar2=None, op0=ALU.max), r=[O1], w=[r_])
                p.op("dve", lambda: nc.vector.tensor_scalar(out=r_[:, 1:2], in0=O2[:, 128:129], scalar1=1e-30, scalar2=None, op0=ALU.max), r=[O2], w=[r_])
                p.op("dve", lambda: nc.vector.reciprocal(out=r_[:, 2:4], in_=r_[:, 0:2]), r=[r_], w=[r_])
                p.op("dve", lambda: nc.vector.tensor_tensor(out=r_[:, 4:5], in0=r_[:, 3:4], in1=neg_lam, op=ALU.mult), r=[r_, ls], w=[r_])
                p.op("dve", lambda: nc.vector.tensor_scalar(out=o[:], in0=O1[:, 0:128], scalar1=r_[:, 2:3], scalar2=None, op0=ALU.mult), r=[O1, r_], w=[o])
                p.op("dve", lambda: nc.vector.scalar_tensor_tensor(out=o[:], in0=O2[:, 0:128], scalar=r_[:, 4:5], in1=o[:], op0=ALU.mult, op1=ALU.add),
                     r=[O2, r_, o], w=[o])
                p.op("act", lambda: nc.scalar.activation(out=sq[:], in_=o[:], func=AF.Square, accum_out=r_[:, 5:6]), r=[o], w=[sq, r_])
                p.op("dve", lambda: nc.vector.tensor_scalar(out=r_[:, 6:7], in0=r_[:, 5:6], scalar1=1.0 / 128, scalar2=EPS, op0=ALU.mult, op1=ALU.add), r=[r_], w=[r_])
                p.op("pool", lambda: nc.gpsimd.tensor_tensor(out=r_[:, 8:9], in0=r_[:, 6:7], in1=mhalf[:], op=ALU.pow), r=[r_, mhalf], w=[r_])
                p.op("dve", lambda: nc.vector.scalar_tensor_tensor(out=o[:], in0=o[:], scalar=r_[:, 8:9], in1=normw[:], op0=ALU.mult, op1=ALU.mult),
                     r=[o, r_, normw], w=[o])
                p.op("pool", lambda: nc.gpsimd.tensor_tensor(out=obuf[:, qt, :], in0=o[:], in1=graw[:, qt, :], op=ALU.mult), r=[o, graw], w=[obuf])

            AHEAD = 2
            for gi, grp in enumerate(groups):
                emit_scores(grp, gi)
                if gi >= AHEAD:
                    emit_pv(groups[gi - AHEAD], gi - AHEAD)
            for gi in range(max(0, len(groups) - AHEAD), len(groups)):
                emit_pv(groups[gi], gi)

        def finish(s, j, obuf):
            for _ in range(3):
                yield
            for n0 in range(0, NT, 4):
                nn = min(4, NT - n0)
                pst = PS[7]
                for i in range(nn):
                    p.op("pe", lambda: nc.tensor.transpose(pst[:, i * 128:(i + 1) * 128], obuf[:, n0 + i, :], ident[:]), r=[obuf, ident], w=[pst])
                if (n0 // 4) % 2 == 0:
                    p.op("act", lambda: nc.scalar.copy(out=oTb[:, n0 * 128:(n0 + nn) * 128], in_=pst[:, 0:nn * 128]), r=[pst], w=[oTb])
                else:
                    p.op("dve", lambda: nc.vector.tensor_copy(out=oTb[:, n0 * 128:(n0 + nn) * 128], in_=pst[:, 0:nn * 128]), r=[pst], w=[oTb])
                yield
            p.dma(DR["OT_att"][j * 128:(j + 1) * 128, s * P:(s + 1) * P], oTb[:], r=[oTb], w=[DR["OT_t"][0]])
            yield

        def run_all(gen):
            for _ in gen:
                pass

        setup_load(pr[0][0], pr[0][1], SETS[0])
        run_all(setup(pr[0][0], pr[0][1], SETS[0]))
        def chain(*gens):
            for g_ in gens:
                if g_ is not None:
                    for _ in g_:
                        yield

        for k, (s, j) in enumerate(pr):
            if k + 1 < len(pr):
                setup_load(pr[k + 1][0], pr[k + 1][1], SETS[(k + 1) % 2])
            g_fin = finish(pr[k - 1][0], pr[k - 1][1], obufs[(k - 1) % 2]) if k > 0 else None
            g_set = setup(pr[k + 1][0], pr[k + 1][1], SETS[(k + 1) % 2]) if k + 1 < len(pr) else None
            bg = chain(g_fin, g_set)

            def inj(n=1):
                for _ in range(n):
                    try:
                        next(bg)
                    except StopIteration:
                        return
            mainloop(s, j, SETS[k % 2], inj, obufs[k % 2])
            run_all(bg)
        run_all(finish(pr[-1][0], pr[-1][1], obufs[(len(pr) - 1) % 2]))
        p.barrier()


def phase_hgrn(p, nc, l, DR, G, seqs=None, zoff=HG_OFF, ntl=NT, pipelined=True):
    PS = G["PS"]
    ident = G["ident"]
    b_le, b_gt, bind = G["b_le"], G["b_gt"], G["bind"]
    Z = DR["Z"]
    with ExitStack() as es:
        T = lambda name, shape, dt, **kw: p.tile("hg_" + name, shape, dt, es=es, **kw)
        hnw = T("hnw", [128, 128], F32)
        load_bc(p, nc, hnw, DR["hgrn_norm_w"][l:l + 1, :])
        identb = T("identb", [128, 128], BF16)
        p.op("pool", lambda: nc.gpsimd.tensor_copy(out=identb[:], in_=ident[:]), r=[ident], w=[identb])
        mhalf = T("mhalf", [128, 8], F32)
        p.op("pool", lambda: nc.gpsimd.memset(mhalf[:], -0.5), w=[mhalf])
        if l > 0:
            lb = T("lb", [128, 1024], F32)
            oml = T("oml", [128, 1024], F32)
            x0 = T("x0", [128, 1024], F32)
            load_bc(p, nc, x0, DR["hgrn_lower_bounds"][0:1, :])
            load_bc(p, nc, lb, DR["hgrn_lower_bounds"][1:2, :])
            p.op("dve", lambda: nc.vector.tensor_tensor(out=x0[:], in0=lb[:], in1=x0[:], op=ALU.subtract), r=[lb, x0], w=[x0])
            p.op("act", lambda: nc.scalar.activation(out=lb[:], in_=x0[:], func=AF.Sigmoid), r=[x0], w=[lb])
            p.op("act", lambda: nc.scalar.activation(out=oml[:], in_=x0[:], func=AF.Sigmoid, scale=-1.0), r=[x0], w=[oml])
        zts = [T("z%d" % i, [128, 4096], F32) for i in range(2)]
        f = T("f", [128, 1024], F32)
        kf = T("kf", [128, 1024], F32)
        logf = T("logf", [128, 1024], F32)
        ex = T("ex", [128, 1024], F32)
        SETS = []
        for k in range(2):
            B = {}
            for nm in ("qtb", "ktb", "kh", "khz", "vb"):
                B[nm] = T("%s%d" % (nm, k), [128, 1024], BF16)
            B["gs"] = T("gs%d" % k, [128, 1024], F32)
            B["gC"] = T("gC%d" % k, [128, 32], F32)
            SETS.append(B)
        qkT = [T("qkT%d" % j, [128, 256], BF16) for j in range(8)]
        qz = [T("qz%d" % j, [128, 64], BF16) for j in range(8)]
        for j in range(8):
            p.op("pool", lambda: nc.gpsimd.memset(qz[j][:], 0.0), w=[qz[j]])
        attm = [T("attm%d" % j, [128, 128], BF16) for j in range(8)]
        S = [T("S%d" % j, [128, 128], F32) for j in range(8)]
        Sb = [T("Sb%d" % j, [128, 128], BF16) for j in range(8)]
        osb = T("osb", [128, 1024], F32)
        obuf = T("obuf", [128, 1024], F32)
        oTb = T("oTb", [128, 8, 128], BF16)
        st = T("st", [128, 4, 8], F32)

        def v8(ap):
            return ap.rearrange("p (h d) -> p h d", d=128)

        def prep_load(s, n, z_):
            r0 = s * P + n * 128
            p.dma(z_[:], zr(Z, r0, r0 + 128)[:, zoff:zoff + 4096], r=[DR["Zt"]], w=[z_])
            yield

        def prep(s, n, B, z_):
            r0 = s * P + n * 128
            p.op("act", lambda: nc.scalar.activation(out=f[:], in_=z_[:, 1024:2048], func=AF.Sigmoid), r=[z_], w=[f])
            yield
            if l > 0:
                p.op("dve", lambda: nc.vector.tensor_tensor(out=f[:], in0=f[:], in1=oml[:], op=ALU.mult), r=[f, oml], w=[f])
                yield
                p.op("pool", lambda: nc.gpsimd.tensor_tensor(out=f[:], in0=f[:], in1=lb[:], op=ALU.add), r=[f, lb], w=[f])
                yield
            p.op("dve", lambda: nc.vector.tensor_scalar(out=kf[:], in0=f[:], scalar1=-1.0, scalar2=1.0, op0=ALU.mult, op1=ALU.add), r=[f], w=[kf])
            p.op("act", lambda: nc.scalar.activation(out=logf[:], in_=f[:], func=AF.Ln), r=[f], w=[logf])
            yield
            p.op("act", lambda: nc.scalar.activation(out=z_[:, 0:1024], in_=z_[:, 0:1024], func=AF.Silu), r=[z_], w=[z_])
            p.op("act", lambda: nc.scalar.activation(out=B["gs"][:], in_=z_[:, 3072:4096], func=AF.Silu), r=[z_], w=[B["gs"]])
            yield
            p.op("pool", lambda: nc.gpsimd.tensor_copy(out=B["vb"][:], in_=z_[:, 2048:3072]), r=[z_], w=[B["vb"]])
            yield
            for j in range(8):
                p.op("pe", lambda: nc.tensor.matmul(PS[4][:, j * 4:(j + 1) * 4], lhsT=logf[:, j * 128:(j + 1) * 128], rhs=bind[:, 0:4], start=True, stop=True),
                     r=[logf, bind], w=[PS[4]])
            p.op("act", lambda: nc.scalar.activation(out=B["gC"][:], in_=PS[4][:, 0:32], func=AF.Exp), r=[PS[4]], w=[B["gC"]])
            yield
            for half in range(2):
                hs = slice(half * 512, (half + 1) * 512)
                pl = PS[4 + half]
                p.op("pe", lambda: nc.tensor.matmul(pl[:], lhsT=b_le[:], rhs=logf[:, hs], start=True, stop=True), r=[b_le, logf], w=[pl])
                p.op("act", lambda: nc.scalar.activation(out=ex[:, hs], in_=pl[:], func=AF.Exp), r=[pl], w=[ex])
                p.op("dve", lambda: nc.vector.tensor_tensor(out=B["qtb"][:, hs], in0=z_[:, hs], in1=ex[:, hs], op=ALU.mult), r=[z_, ex], w=[B["qtb"]])
                p.op("act", lambda: nc.scalar.activation(out=ex[:, hs], in_=pl[:], func=AF.Exp, scale=-1.0), r=[pl], w=[ex])
                p.op("dve", lambda: nc.vector.tensor_tensor(out=B["ktb"][:, hs], in0=kf[:, hs], in1=ex[:, hs], op=ALU.mult), r=[kf, ex], w=[B["ktb"]])
                p.op("pe", lambda: nc.tensor.matmul(pl[:], lhsT=b_gt[:], rhs=logf[:, hs], start=True, stop=True), r=[b_gt, logf], w=[pl])
                p.op("act", lambda: nc.scalar.activation(out=ex[:, hs], in_=pl[:], func=AF.Exp), r=[pl], w=[ex])
                p.op("dve", lambda: nc.vector.tensor_tensor(out=B["kh"][:, hs], in0=kf[:, hs], in1=ex[:, hs], op=ALU.mult), r=[kf, ex], w=[B["kh"]])
                yield
            p.op("pool", lambda: nc.gpsimd.tensor_scalar(out=B["khz"][:], in0=B["kh"][:], scalar1=bind[:, 3:4], scalar2=None, op0=ALU.mult), r=[B["kh"], bind], w=[B["khz"]])
            yield

        def main(s, n, B, inj, flush_out=lambda: None):
            if n == 0:
                for j in range(8):
                    p.op("pool", lambda: nc.gpsimd.memset(S[j][:], 0.0), w=[S[j]])
                    p.op("pool", lambda: nc.gpsimd.memset(Sb[j][:], 0.0), w=[Sb[j]])
            vb, kh, khz, gC = B["vb"], B["kh"], B["khz"], B["gC"]
            for j in range(8):
                js = slice(j * 128, (j + 1) * 128)
                pw = PS[4 + (j % 2)]
                pwb = pw.t[:].bitcast(BF16)
                p.op("pe", lambda: nc.tensor.transpose(pwb[:, 0:128], B["qtb"][:, js], identb[:]), r=[B["qtb"], identb], w=[pw])
                p.op("pe", lambda: nc.tensor.transpose(pwb[:, 128:256], B["ktb"][:, js], identb[:]), r=[B["ktb"], identb], w=[pw])
                p.op("act", lambda: nc.scalar.copy(out=qkT[j][:], in_=pwb[:, 0:256]), r=[pw], w=[qkT[j]])
                p.op("pool", lambda: nc.gpsimd.tensor_copy(out=qz[j][:, 32:64], in_=qkT[j][:, 96:128]), r=[qkT[j]], w=[qz[j]])
                p.op("pe", lambda: nc.tensor.matmul(pw[:, 256:384], lhsT=qkT[j][:, 128:256], rhs=qkT[j][:, 0:128], start=True, stop=True),
                     r=[qkT[j]], w=[pw])
                p.op("dve", lambda: nc.vector.tensor_tensor(out=attm[j][:], in0=pw[:, 256:384], in1=b_le[:], op=ALU.mult), r=[pw, b_le], w=[attm[j]])
                if j % 2 == 1:
                    inj()
            for j in range(8):
                js = slice(j * 128, (j + 1) * 128)
                po = PS[6 + j // 4]
                p.op("pe", lambda: nc.tensor.matmul(po[:, (j % 4) * 128:(j % 4 + 1) * 128], lhsT=attm[j][:], rhs=vb[:, js],
                                                    start=(j % 4 == 0), stop=False, skip_group_check=True), r=[attm[j], vb], w=[po])
            for c in range(4):
                cs = slice(c * 32, (c + 1) * 32)
                for j in range(8):
                    js = slice(j * 128, (j + 1) * 128)
                    po = PS[6 + j // 4]
                    if c < 3:
                        p.op("pe", lambda: nc.tensor.matmul(po[cs, (j % 4) * 128:(j % 4 + 1) * 128], lhsT=qkT[j][:, cs], rhs=Sb[j][:],
                                                            start=False, stop=False, skip_group_check=True), r=[qkT[j], Sb[j]], w=[po])
                    else:
                        p.op("pe", lambda: nc.tensor.matmul(po[64:128, (j % 4) * 128:(j % 4 + 1) * 128], lhsT=qz[j][:], rhs=Sb[j][:],
                                                            start=False, stop=True, skip_group_check=True), r=[qz[j], Sb[j]], w=[po])
                    pss = PS[j % 4]
                    pc = slice((j // 4) * 128, (j // 4 + 1) * 128)
                    if c < 3:
                        p.op("pe", lambda: nc.tensor.matmul(pss[:, pc], lhsT=kh[cs, js], rhs=vb[cs, js],
                                                            start=True, stop=True), r=[kh, vb], w=[pss])
                    else:
                        p.op("pe", lambda: nc.tensor.matmul(pss[:, pc], lhsT=khz[64:128, js], rhs=vb[64:128, js],
                                                            start=True, stop=True), r=[khz, vb], w=[pss])
                    p.op("dve", lambda: nc.vector.scalar_tensor_tensor(out=S[j][:], in0=S[j][:], scalar=gC[:, j * 4 + c:j * 4 + c + 1],
                                                                       in1=pss[:, pc], op0=ALU.mult, op1=ALU.add),
                         r=[S[j], gC, pss], w=[S[j]])
                    p.op("act", lambda: nc.scalar.copy(out=Sb[j][:], in_=S[j][:]), r=[S[j]], w=[Sb[j]])
                    if j % 4 == 3:
                        inj()
            flush_out()
            for half in range(2):
                hs = slice(half * 512, (half + 1) * 512)
                p.op("act", lambda: nc.scalar.copy(out=osb[:, hs], in_=PS[6 + half][:]), r=[PS[6 + half]], w=[osb])

        def outp(s, n, B):
            r0 = s * P + n * 128
            p.op("pool", lambda: nc.gpsimd.tensor_tensor(out=obuf[:], in0=osb[:], in1=osb[:], op=ALU.mult), r=[osb], w=[obuf])
            yield
            p.op("dve", lambda: nc.vector.tensor_reduce(out=st[:, 0, :], in_=v8(obuf.t[:]), axis=AX.X, op=ALU.add), r=[obuf], w=[st])
            p.op("dve", lambda: nc.vector.tensor_scalar(out=st[:, 1, :], in0=st[:, 0, :], scalar1=1.0 / 128, scalar2=EPS, op0=ALU.mult, op1=ALU.add), r=[st], w=[st])
            p.op("pool", lambda: nc.gpsimd.tensor_tensor(out=st[:, 2, :], in0=st[:, 1, :], in1=mhalf[:], op=ALU.pow), r=[st, mhalf], w=[st])
            yield
            p.op("dve", lambda: nc.vector.tensor_tensor(out=v8(osb.t[:]), in0=v8(osb.t[:]), in1=st[:, 2, :].unsqueeze(2).broadcast_to([128, 8, 128]), op=ALU.mult),
                 r=[osb, st], w=[osb])
            yield
            p.op("pool", lambda: nc.gpsimd.tensor_tensor(out=v8(osb.t[:]), in0=v8(osb.t[:]), in1=hnw.t[:].unsqueeze(1).broadcast_to([128, 8, 128]), op=ALU.mult),
                 r=[osb, hnw], w=[osb])
            yield
            p.op("pool", lambda: nc.gpsimd.tensor_tensor(out=obuf[:], in0=osb[:], in1=B["gs"][:], op=ALU.mult), r=[osb, B["gs"]], w=[obuf])
            yield
            for half in range(2):
                pst = PS[4 + half]
                for k4 in range(4):
                    kc = half * 4 + k4
                    p.op("pe", lambda: nc.tensor.transpose(pst[:, k4 * 128:(k4 + 1) * 128], obuf[:, kc * 128:(kc + 1) * 128], ident[:]), r=[obuf, ident], w=[pst])
                src_ = pst.t[:].rearrange("p (k t) -> p k t", k=4)
                if half == 0:
                    p.op("act", lambda: nc.scalar.copy(out=oTb[:, 0:4, :], in_=src_), r=[pst], w=[oTb])
                else:
                    p.op("dve", lambda: nc.vector.tensor_copy(out=oTb[:, 4:8, :], in_=src_), r=[pst], w=[oTb])
                yield
            p.dma(DR["OT_hgrn"].rearrange("(kc p) t -> p kc t", p=128)[:, :, r0:r0 + 128], oTb[:], r=[oTb], w=[DR["OT_t"][2]])
            yield

        tiles = [(s, n) for s in (range(NS) if seqs is None else seqs) for n in range(ntl)]

        def run_all(gen):
            for _ in gen:
                pass

        def chain(*gens):
            for g in gens:
                if g is not None:
                    for _ in g:
                        yield

        if not pipelined:
            for idx, (s, n) in enumerate(tiles):
                B = SETS[idx % 2]
                run_all(prep_load(s, n, zts[idx % 2]))
                run_all(prep(s, n, B, zts[idx % 2]))
                main(s, n, B, lambda: None)
                run_all(outp(s, n, B))
        else:
            run_all(prep_load(tiles[0][0], tiles[0][1], zts[0]))
            run_all(prep(tiles[0][0], tiles[0][1], SETS[0], zts[0]))
            if len(tiles) > 1:
                run_all(prep_load(tiles[1][0], tiles[1][1], zts[1]))
            for idx, (s, n) in enumerate(tiles):
                B = SETS[idx % 2]
                g_out = outp(tiles[idx - 1][0], tiles[idx - 1][1], SETS[(idx - 1) % 2]) if idx > 0 else None
                g_prep = prep(tiles[idx + 1][0], tiles[idx + 1][1], SETS[(idx + 1) % 2], zts[(idx + 1) % 2]) if idx + 1 < len(tiles) else None
                g_load = prep_load(tiles[idx + 2][0], tiles[idx + 2][1], zts[idx % 2]) if idx + 2 < len(tiles) else None
                g_lo = chain(g_out)
                bg = chain(g_lo, g_prep, g_load)

                def inj(k=1):
                    for _ in range(k):
                        try:
                            next(bg)
                        except StopIteration:
                            return
                main(s, n, B, inj, lambda: run_all(g_lo))
                run_all(bg)
            run_all(outp(tiles[-1][0], tiles[-1][1], SETS[(len(tiles) - 1) % 2]))
        p.barrier()


C0 = -0.6065306597126334


def rwkv_setup(p, nc, l, DR, G, T):
    ident = G["ident"]
    tri_le, tri_lt, tri_gt = G["tri_le"], G["tri_lt"], G["tri_gt"]
    mu = T("mu", [128, RW_W], F32)
    load_bc(p, nc, mu, DR["rwkv_mu"][l:l + 1, :])
    prm = {}
    for i, n_ in enumerate(("rwkv_w0", "rwkv_a0", "rwkv_k_k", "rwkv_k_a", "rwkv_gn_w", "rwkv_gn_b")):
        prm[n_] = T(n_, [128, 1024], F32)
        load_bc(p, nc, prm[n_], DR[n_][l:l + 1, :], q=("sp" if i % 2 == 0 else "act"))
    prm["rwkv_r_k"] = T("rwkv_r_k", [128, 1024], F32)
    load_bc(p, nc, prm["rwkv_r_k"], DR["rwkv_r_k"][l:l + 1].rearrange("o h d -> o (h d)"))
    w_up = T("w_up", [64, 1024], F32)
    a_up = T("a_up", [64, 1024], F32)
    p.dma(w_up[:], DR["rwkv_w_up"][l], w=[w_up])
    p.dma(a_up[:], DR["rwkv_a_up"][l], w=[a_up])
    mA = T("mA", [128, 384], F32)
    mB = T("mB", [128, 256], F32)
    p.op("pool", lambda: nc.gpsimd.tensor_copy(out=mA[:, 0:128], in_=tri_lt[:]), r=[tri_lt], w=[mA])
    p.op("pool", lambda: nc.gpsimd.tensor_copy(out=mA[:, 128:256], in_=tri_gt[:]), r=[tri_gt], w=[mA])
    p.op("pool", lambda: nc.gpsimd.tensor_copy(out=mA[:, 256:384], in_=tri_lt[:]), r=[tri_lt], w=[mA])
    p.op("pool", lambda: nc.gpsimd.tensor_copy(out=mB[:, 0:128], in_=tri_le[:]), r=[tri_le], w=[mB])
    p.op("pool", lambda: nc.gpsimd.tensor_copy(out=mB[:, 128:256], in_=tri_le[:]), r=[tri_le], w=[mB])
    identb = T("identb", [128, 128], BF16)
    p.op("pool", lambda: nc.gpsimd.tensor_copy(out=identb[:], in_=ident[:]), r=[ident], w=[identb])
    mhalf16 = T("mhalf16", [128, 16], F32)
    p.op("pool", lambda: nc.gpsimd.memset(mhalf16[:], -0.5), w=[mhalf16])
    return dict(mu=mu, prm=prm, w_up=w_up, a_up=a_up, mA=mA, mB=mB, identb=identb, mhalf16=mhalf16)


def phase_rwkv(p, nc, l, DR, G, seqs=None, zoff=RW_OFF, ntl=NT, dbg=None, stop=None, pipelined=True, pre=None):
    PS = G["PS"]
    ident = G["ident"]
    tri_le, tri_lt, tri_gt, ones = G["tri_le"], G["tri_lt"], G["tri_gt"], G["ones"]
    Z = DR["Z"]

    def bc(ap16, n=16):
        return ap16.unsqueeze(2).broadcast_to([128, n, 64])

    def v3(ap):
        return ap.rearrange("p (h d) -> p h d", d=64)

    with ExitStack() as es:
        T = lambda name, shape, dt, **kw: p.tile("rw_" + name, shape, dt, es=es, **kw)
        ps_ = pre if pre is not None else rwkv_setup(p, nc, l, DR, G, T)
        mu, prm, w_up, a_up, mA, mB, identb, mhalf16 = (ps_[k] for k in ("mu", "prm", "w_up", "a_up", "mA", "mB", "identb", "mhalf16"))
        zc = T("zc", [128, RW_W], F32)
        zp = T("zp", [128, RW_W], F32)
        sw = T("sw", [128, 1024], F32)
        a_ = T("a_", [128, 1024], F32)
        kk = T("kk", [128, 1024], F32)
        kp = T("kp", [128, 1024], F32)
        b_ = T("b_", [128, 1024], F32)
        twT = T("twT", [64, 256], F32)
        ex = zp.t[:, 0:1024]
        tq = zp.t[:, 3072:4096]
        SETS = []
        for k in range(2):
            B = {}
            for nm in ("rtb", "ktb", "atb", "btb", "khat", "bhat", "vb"):
                B[nm] = T("%s%d" % (nm, k), [128, 1024], BF16)
            B["gs"] = T("gs%d" % k, [128, 1024], F32)
            B["gC"] = T("gC%d" % k, [128, 16], F32)
            B["sm"] = T("sm%d" % k, [128, 4, 16], F32)
            SETS.append(B)
        sm2 = T("sm2", [128, 8, 16], F32)
        fT = [T("fT%d" % i, [128, 4, 128], BF16) for i in range(8)]
        Ar = [T("Ar%d" % h, [128, 256], BF16) for h in range(16)]
        Aak = [T("Aak%d" % i, [128, 128], BF16) for i in range(8)]
        PP = [[T("PP%d_%d" % (i, k), [128, 256], BF16) for k in range(2)] for i in range(8)]
        X = [[T("X%d_%d" % (i, k), [128, 128], BF16) for k in range(2)] for i in range(8)]
        Gt = [T("Gt%d" % i, [128, 64], BF16) for i in range(8)]
        Wt = T("Wt", [128, 1024], F32)
        YT = [T("YT%d" % i, [128, 128], BF16) for i in range(8)]
        ST = T("ST", [128, 8, 64], F32)
        STbd = T("STbd", [128, 8, 128], BF16)
        Ub = T("Ub", [128, 1024], BF16)
        osb = T("osb", [128, 1024], F32)
        obuf = T("obuf", [128, 1024], F32)
        oTb = T("oTb", [128, 8, 128], BF16)

        def prep_load(s, n):
            r0 = s * P + n * 128
            p.dma(zc[:], zr(Z, r0, r0 + 128)[:, zoff:zoff + RW_W], r=[DR["Zt"]], w=[zc])
            if n == 0:
                p.op("pool", lambda: nc.gpsimd.memset(zp[0:1, :], 0.0), w=[zp])
                p.dma(zp[1:128, :], zr(Z, r0, r0 + 127)[:, zoff:zoff + RW_W], r=[DR["Zt"]], w=[zp], q="act")
            else:
                p.dma(zp[:], zr(Z, r0 - 1, r0 + 127)[:, zoff:zoff + RW_W], r=[DR["Zt"]], w=[zp], q="act")
            yield

        def prep(s, n, B):
            r0 = s * P + n * 128
            hs_ = slice(4096, RW_W)
            p.op("dve", lambda: nc.vector.tensor_tensor(out=zp[:, hs_], in0=zp[:, hs_], in1=zc[:, hs_], op=ALU.subtract), r=[zp, zc], w=[zp])
            p.op("dve", lambda: nc.vector.tensor_tensor(out=zp[:, hs_], in0=zp[:, hs_], in1=mu[:, hs_], op=ALU.mult), r=[zp, mu], w=[zp])
            p.op("dve", lambda: nc.vector.tensor_tensor(out=zc[:, hs_], in0=zc[:, hs_], in1=zp[:, hs_], op=ALU.add), r=[zp, zc], w=[zc])
            p.op("act", lambda: nc.scalar.activation(out=zc[:, 4096:4160], in_=zc[:, 4096:4160], func=AF.Tanh), r=[zc], w=[zc])
            yield
            h1 = slice(0, 2560)
            h2 = slice(2560, 4096)
            for k3 in range(3):
                for (sl, eng) in ((h1, "dve"), (h2, "pool")):
                    E = nc.vector if eng == "dve" else nc.gpsimd
                    if k3 == 0:
                        p.op(eng, lambda: E.tensor_tensor(out=zp[:, sl], in0=zp[:, sl], in1=zc[:, sl], op=ALU.subtract), r=[zp, zc], w=[zp])
                    elif k3 == 1:
                        p.op(eng, lambda: E.tensor_tensor(out=zp[:, sl], in0=zp[:, sl], in1=mu[:, sl], op=ALU.mult), r=[zp, mu], w=[zp])
                    else:
                        p.op(eng, lambda: E.tensor_tensor(out=zc[:, sl], in0=zc[:, sl], in1=zp[:, sl], op=ALU.add), r=[zp, zc], w=[zc])
                    yield
            rr = zc.t[:, 0:1024]
            rk = zc.t[:, 1024:2048]
            rv = zc.t[:, 2048:3072]
            rg = zc.t[:, 3072:4096]
            p.op("pe", lambda: nc.tensor.transpose(PS[6][0:64, 0:128], zc[:, 4096:4160], ident[:]), r=[zc, ident], w=[PS[6]])
            p.op("pe", lambda: nc.tensor.transpose(PS[6][0:64, 128:256], zc[:, 4160:4224], ident[:]), r=[zc, ident], w=[PS[6]])
            p.op("act", lambda: nc.scalar.copy(out=twT[:], in_=PS[6][0:64, 0:256]), r=[PS[6]], w=[twT])
            yield
            for half in range(2):
                hs = slice(half * 512, (half + 1) * 512)
                p.op("pe", lambda: nc.tensor.matmul(PS[6][:], lhsT=twT[:, 0:128], rhs=w_up[:, hs], start=True, stop=True), r=[twT, w_up], w=[PS[6]])
                p.op("dve", lambda: nc.vector.tensor_tensor(out=sw[:, hs], in0=PS[6][:], in1=prm["rwkv_w0"][:, hs], op=ALU.add),
                     r=[PS[6], prm["rwkv_w0"]], w=[sw])
                yield
            for half in range(2):
                hs = slice(half * 512, (half + 1) * 512)
                p.op("pe", lambda: nc.tensor.matmul(PS[7][:], lhsT=twT[:, 128:256], rhs=a_up[:, hs], start=True, stop=True), r=[twT, a_up], w=[PS[7]])
                p.op("dve", lambda: nc.vector.tensor_tensor(out=a_[:, hs], in0=PS[7][:], in1=prm["rwkv_a0"][:, hs], op=ALU.add),
                     r=[PS[7], prm["rwkv_a0"]], w=[a_])
                yield
            p.op("act", lambda: nc.scalar.activation(out=sw[:], in_=sw[:], func=AF.Sigmoid), r=[sw], w=[sw])
            p.op("act", lambda: nc.scalar.activation(out=a_[:], in_=a_[:], func=AF.Sigmoid), r=[a_], w=[a_])
            yield
            p.op("dve", lambda: nc.vector.tensor_tensor(out=kk[:], in0=rk, in1=prm["rwkv_k_k"][:], op=ALU.mult), r=[zc, prm["rwkv_k_k"]], w=[kk])
            yield
            p.op("pool", lambda: nc.gpsimd.tensor_tensor(out=kp[:], in0=kk[:], in1=kk[:], op=ALU.mult), r=[kk], w=[kp])
            yield
            p.op("dve", lambda: nc.vector.tensor_reduce(out=sm2[:, 0, :], in_=v3(kp.t[:]), axis=AX.X, op=ALU.add), r=[kp], w=[sm2])
            p.op("dve", lambda: nc.vector.tensor_scalar(out=sm2[:, 1, :], in0=sm2[:, 0, :], scalar1=1e-24, scalar2=None, op0=ALU.max), r=[sm2], w=[sm2])
            p.op("pool", lambda: nc.gpsimd.tensor_tensor(out=sm2[:, 2, :], in0=sm2[:, 1, :], in1=mhalf16[:], op=ALU.pow), r=[sm2, mhalf16], w=[sm2])
            yield
            p.op("dve", lambda: nc.vector.tensor_tensor(out=v3(kk.t[:]), in0=v3(kk.t[:]), in1=bc(sm2[:, 2, :]), op=ALU.mult), r=[kk, sm2], w=[kk])
            yield
            p.op("dve", lambda: nc.vector.scalar_tensor_tensor(out=kp[:], in0=a_[:], scalar=-1.0, in1=prm["rwkv_k_a"][:], op0=ALU.add, op1=ALU.mult),
                 r=[a_, prm["rwkv_k_a"]], w=[kp])
            yield
            p.op("dve", lambda: nc.vector.scalar_tensor_tensor(out=kp[:], in0=kp[:], scalar=1.0, in1=rk, op0=ALU.add, op1=ALU.mult), r=[kp, zc], w=[kp])
            yield
            p.op("pool", lambda: nc.gpsimd.tensor_tensor(out=b_[:], in0=kk[:], in1=a_[:], op=ALU.mult), r=[kk, a_], w=[b_])
            yield
            p.op("pool", lambda: nc.gpsimd.tensor_tensor(out=tq, in0=rr, in1=kp[:], op=ALU.mult), r=[zc, kp], w=[zp])
            yield
            p.op("pool", lambda: nc.gpsimd.tensor_tensor(out=tq, in0=tq, in1=prm["rwkv_r_k"][:], op=ALU.mult), r=[zp, prm["rwkv_r_k"]], w=[zp])
            yield
            p.op("dve", lambda: nc.vector.tensor_reduce(out=B["sm"][:, 3, :], in_=v3(tq), axis=AX.X, op=ALU.add), r=[zp], w=[B["sm"]])
            p.op("pool", lambda: nc.gpsimd.tensor_copy(out=B["vb"][:], in_=rv), r=[zc], w=[B["vb"]])
            yield
            p.op("act", lambda: nc.scalar.activation(out=B["gs"][:], in_=rg, func=AF.Silu), r=[zc], w=[B["gs"]])
            yield
            for _ in range(8):
                yield
            for half in range(2):
                hs = slice(half * 512, (half + 1) * 512)
                pl = PS[6 + half]
                p.op("pe", lambda: nc.tensor.matmul(pl[:], lhsT=tri_le[:], rhs=sw[:, hs], start=True, stop=True), r=[tri_le, sw], w=[pl])
                p.op("act", lambda: nc.scalar.activation(out=ex[:, hs], in_=pl[:], func=AF.Exp, scale=C0), r=[pl], w=[zp])
                p.op("dve", lambda: nc.vector.tensor_tensor(out=B["rtb"][:, hs], in0=rr[:, hs], in1=ex[:, hs], op=ALU.mult), r=[zc, zp], w=[B["rtb"]])
                p.op("dve", lambda: nc.vector.tensor_tensor(out=tq[:, hs], in0=pl[:], in1=sw[:, hs], op=ALU.subtract), r=[pl, sw], w=[zp])
                p.op("act", lambda: nc.scalar.activation(out=tq[:, hs], in_=tq[:, hs], func=AF.Exp, scale=C0), r=[zp], w=[zp])
                p.op("dve", lambda: nc.vector.scalar_tensor_tensor(out=B["atb"][:, hs], in0=kk[:, hs], scalar=-1.0, in1=tq[:, hs], op0=ALU.mult, op1=ALU.mult),
                     r=[kk, zp], w=[B["atb"]])
                p.op("act", lambda: nc.scalar.activation(out=ex[:, hs], in_=pl[:], func=AF.Exp, scale=-C0), r=[pl], w=[zp])
                p.op("pool", lambda: nc.gpsimd.tensor_tensor(out=B["btb"][:, hs], in0=b_[:, hs], in1=ex[:, hs], op=ALU.mult), r=[b_, zp], w=[B["btb"]])
                p.op("dve", lambda: nc.vector.tensor_tensor(out=B["ktb"][:, hs], in0=kp[:, hs], in1=ex[:, hs], op=ALU.mult), r=[kp, zp], w=[B["ktb"]])
                p.op("pe", lambda: nc.tensor.matmul(pl[:], lhsT=tri_gt[:], rhs=sw[:, hs], start=True, stop=True), r=[tri_gt, sw], w=[pl])
                p.op("act", lambda: nc.scalar.activation(out=ex[:, hs], in_=pl[:], func=AF.Exp, scale=C0), r=[pl], w=[zp])
                p.op("pool", lambda: nc.gpsimd.tensor_tensor(out=B["khat"][:, hs], in0=kp[:, hs], in1=ex[:, hs], op=ALU.mult), r=[kp, zp], w=[B["khat"]])
                p.op("dve", lambda: nc.vector.tensor_tensor(out=B["bhat"][:, hs], in0=b_[:, hs], in1=ex[:, hs], op=ALU.mult), r=[b_, zp], w=[B["bhat"]])
                yield
            for pp in range(8):
                p.op("pe", lambda: nc.tensor.matmul(PS[6][:, pp * 2:pp * 2 + 2], lhsT=sw[:, pp * 128:(pp + 1) * 128], rhs=ones[:, 0:2],
                                                    start=True, stop=True), r=[sw, ones], w=[PS[6]])
            p.op("act", lambda: nc.scalar.activation(out=B["gC"][:], in_=PS[6][:, 0:16], func=AF.Exp, scale=C0), r=[PS[6]], w=[B["gC"]])
            yield

        def main(s, n, B, inj, flush_out=lambda: None):
            if n == 0:
                p.op("pool", lambda: nc.gpsimd.memset(ST[:], 0.0), w=[ST])
                p.op("pool", lambda: nc.gpsimd.memset(STbd[:], 0.0), w=[STbd])
            src = (B["rtb"], B["ktb"], B["atb"], B["btb"])
            for g8 in range(2):
                for pi in range(4):
                    pp = g8 * 4 + pi
                    cs = slice(pp * 128, (pp + 1) * 128)
                    pw = PS[6 + pi // 2]
                    pwb = pw.t[:].bitcast(BF16)
                    off = (pi % 2) * 512
                    for k in range(4):
                        p.op("pe", lambda: nc.tensor.transpose(pwb[:, off + k * 128:off + (k + 1) * 128], src[k][:, cs], identb[:]), r=[src[k], identb], w=[pw])
                    if pi % 2 == 0:
                        p.op("act", lambda: nc.scalar.copy(out=fT[pp].t[:].rearrange("p a t -> p (a t)"), in_=pwb[:, off:off + 512]), r=[pw], w=[fT[pp]])
                    else:
                        p.op("dve", lambda: nc.vector.tensor_copy(out=fT[pp].t[:].rearrange("p a t -> p (a t)"), in_=pwb[:, off:off + 512]), r=[pw], w=[fT[pp]])
                for i in range(8):
                    h = g8 * 8 + i
                    pp, e = h // 2, h % 2
                    es_ = slice(e * 64, (e + 1) * 64)
                    rT, kT, aT, bT = (fT[pp][es_, k, :] for k in range(4))
                    pa = PS[i]
                    p.op("pe", lambda: nc.tensor.matmul(pa[:, 0:128], lhsT=bT, rhs=aT, start=True, stop=True), r=[fT[pp]], w=[pa])
                    p.op("pe", lambda: nc.tensor.matmul(pa[:, 128:256], lhsT=aT, rhs=bT, start=True, stop=True), r=[fT[pp]], w=[pa])
                    p.op("pe", lambda: nc.tensor.matmul(pa[:, 256:384], lhsT=kT, rhs=aT, start=True, stop=True), r=[fT[pp]], w=[pa])
                    p.op("dve", lambda: nc.vector.tensor_tensor(out=PP[i][0][:], in0=pa[:, 0:256], in1=mA[:, 0:256], op=ALU.mult), r=[pa, mA], w=[PP[i][0]])
                    p.op("dve", lambda: nc.vector.tensor_tensor(out=Aak[i][:], in0=pa[:, 256:384], in1=mA[:, 256:384], op=ALU.mult), r=[pa, mA], w=[Aak[i]])
                    p.op("dve", lambda: nc.vector.tensor_tensor(out=X[i][0][:], in0=PP[i][0][:, 0:128], in1=identb[:], op=ALU.add), r=[PP[i][0], identb], w=[X[i][0]])
                inj()
                for i in range(8):
                    h = g8 * 8 + i
                    pp, e = h // 2, h % 2
                    es_ = slice(e * 64, (e + 1) * 64)
                    rT, kT, aT, bT = (fT[pp][es_, k, :] for k in range(4))
                    pa = PS[i]
                    p.op("pe", lambda: nc.tensor.matmul(pa[:, 0:128], lhsT=bT, rhs=rT, start=True, stop=True), r=[fT[pp]], w=[pa])
                    p.op("pe", lambda: nc.tensor.matmul(pa[:, 128:256], lhsT=kT, rhs=rT, start=True, stop=True), r=[fT[pp]], w=[pa])
                    p.op("dve", lambda: nc.vector.tensor_tensor(out=Ar[h][:], in0=pa[:, 0:256], in1=mB[:], op=ALU.mult), r=[pa, mB], w=[Ar[h]])
                inj()
                for lv in range(1, 7):
                    cur, nxt = (lv - 1) % 2, lv % 2
                    for i in range(8):
                        pq = PS[i]
                        Pm, PTm = PP[i][cur][:, 0:128], PP[i][cur][:, 128:256]
                        if lv < 6:
                            p.op("pe", lambda: nc.tensor.matmul(pq[:, 0:128], lhsT=PTm, rhs=Pm, start=True, stop=True), r=[PP[i][cur]], w=[pq])
                        p.op("pe", lambda: nc.tensor.matmul(pq[:, 128:256], lhsT=Pm, rhs=PTm, start=True, stop=True), r=[PP[i][cur]], w=[pq])
                        if i % 2 == 0:
                            p.op("act", lambda: nc.scalar.copy(out=PP[i][nxt][:], in_=pq[:, 0:256]), r=[pq], w=[PP[i][nxt]])
                        else:
                            p.op("dve", lambda: nc.vector.tensor_copy(out=PP[i][nxt][:], in_=pq[:, 0:256]), r=[pq], w=[PP[i][nxt]])
                        if i % 8 == 7:
                            inj()
                    for i in range(8):
                        px = PS[i]
                        p.op("pe", lambda: nc.tensor.matmul(px[:, 256:384], lhsT=identb[:], rhs=X[i][cur][:], start=True, stop=False), r=[identb, X[i][cur]], w=[px])
                        p.op("pe", lambda: nc.tensor.matmul(px[:, 256:384], lhsT=PP[i][nxt][:, 128:256], rhs=X[i][cur][:], start=False, stop=True),
                             r=[PP[i][nxt], X[i][cur]], w=[px])
                        if i % 2 == 1:
                            p.op("act", lambda: nc.scalar.copy(out=X[i][nxt][:], in_=px[:, 256:384]), r=[px], w=[X[i][nxt]])
                        else:
                            p.op("dve", lambda: nc.vector.tensor_copy(out=X[i][nxt][:], in_=px[:, 256:384]), r=[px], w=[X[i][nxt]])
                        if i % 8 == 7:
                            inj()
                for i in range(8):
                    h = g8 * 8 + i
                    hc = slice(h * 64, (h + 1) * 64)
                    p.op("pe", lambda: nc.tensor.matmul(PS[i][:, 0:64], lhsT=Aak[i][:], rhs=B["vb"][:, hc], start=True, stop=True), r=[Aak[i], B["vb"]], w=[PS[i]])
                    if i % 2 == 0:
                        p.op("act", lambda: nc.scalar.copy(out=Gt[i][:], in_=PS[i][:, 0:64]), r=[PS[i]], w=[Gt[i]])
                    else:
                        p.op("dve", lambda: nc.vector.tensor_copy(out=Gt[i][:], in_=PS[i][:, 0:64]), r=[PS[i]], w=[Gt[i]])
                for i in range(8):
                    h = g8 * 8 + i
                    pp, e = h // 2, h % 2
                    hc = slice(h * 64, (h + 1) * 64)
                    py = PS[i]
                    p.op("pe", lambda: nc.tensor.matmul(py[:, 64:128], lhsT=X[i][0][:], rhs=Gt[i][:], start=True, stop=True),
                         r=[X[i][0], Gt[i]], w=[py])
                    p.op("pe", lambda: nc.tensor.matmul(py[:, 128:256], lhsT=B["atb"][:, pp * 128:(pp + 1) * 128], rhs=X[i][0][:], start=True, stop=True),
                         r=[B["atb"], X[i][0]], w=[py])
                    p.op("act", lambda: nc.scalar.copy(out=Wt[:, hc], in_=py[:, 64:128]), r=[py], w=[Wt])
                    p.op("dve", lambda: nc.vector.tensor_copy(out=YT[pp][e * 64:(e + 1) * 64, :], in_=py[e * 64:(e + 1) * 64, 128:256]), r=[py], w=[YT[pp]])
                inj()
            for pp in range(8):
                p.op("pe", lambda: nc.tensor.matmul(PS[pp // 4][:, (pp % 4) * 128:(pp % 4 + 1) * 128], lhsT=YT[pp][:], rhs=STbd[:, pp, :], start=True, stop=True),
                     r=[YT[pp], STbd], w=[PS[pp // 4]])
            for half in range(2):
                hs = slice(half * 512, (half + 1) * 512)
                p.op("dve", lambda: nc.vector.tensor_tensor(out=Ub[:, hs], in0=PS[half][:], in1=Wt[:, hs], op=ALU.add), r=[PS[half], Wt], w=[Ub])
            for pp in range(8):
                po = PS[2 + pp // 4]
                p.op("pe", lambda: nc.tensor.matmul(po[:, (pp % 4) * 128:(pp % 4 + 1) * 128], lhsT=fT[pp][:, 0, :], rhs=STbd[:, pp, :],
                                                    start=(pp % 4 == 0), stop=False, skip_group_check=True), r=[fT[pp], STbd], w=[po])
            for h in range(16):
                hc = slice(h * 64, (h + 1) * 64)
                po = PS[2 + h // 8]
                oc = slice((h % 8) * 64, (h % 8 + 1) * 64)
                p.op("pe", lambda: nc.tensor.matmul(po[:, oc], lhsT=Ar[h][:, 128:256], rhs=B["vb"][:, hc], start=False, stop=False, skip_group_check=True),
                     r=[Ar[h], B["vb"]], w=[po])
                p.op("pe", lambda: nc.tensor.matmul(po[:, oc], lhsT=Ar[h][:, 0:128], rhs=Ub[:, hc], start=False, stop=True, skip_group_check=True),
                     r=[Ar[h], Ub], w=[po])
            for h in range(16):
                pp, e = h // 2, h % 2
                es_ = slice(e * 64, (e + 1) * 64)
                hc = slice(h * 64, (h + 1) * 64)
                p.op("pe", lambda: nc.tensor.matmul(PS[4][es_, pp * 64:(pp + 1) * 64], lhsT=B["bhat"][:, hc], rhs=Ub[:, hc], start=(pp == 0), stop=False, skip_group_check=True),
                     r=[B["bhat"], Ub], w=[PS[4]])
                p.op("pe", lambda: nc.tensor.matmul(PS[4][es_, pp * 64:(pp + 1) * 64], lhsT=B["khat"][:, hc], rhs=B["vb"][:, hc], start=False, stop=True, skip_group_check=True),
                     r=[B["khat"], B["vb"]], w=[PS[4]])
            gC = B["gC"]
            p.op("dve", lambda: nc.vector.tensor_tensor(out=ST[:], in0=ST[:], in1=gC.t[:, 0:16].rearrange("p (a b) -> p a b", b=2)[:, :, 0:1].broadcast_to([128, 8, 64]), op=ALU.mult),
                 r=[ST, gC], w=[ST])
            p.op("dve", lambda: nc.vector.tensor_tensor(out=ST.t[:].rearrange("p a v -> p (a v)"), in0=ST.t[:].rearrange("p a v -> p (a v)"), in1=PS[4][:], op=ALU.add),
                 r=[ST, PS[4]], w=[ST])
            p.op("act", lambda: nc.scalar.copy(out=STbd[0:64, :, 0:64], in_=ST[0:64, :, :]), r=[ST], w=[STbd])
            p.op("pool", lambda: nc.gpsimd.tensor_copy(out=STbd[64:128, :, 64:128], in_=ST[64:128, :, :]), r=[ST], w=[STbd])
            flush_out()
            for half in range(2):
                hs = slice(half * 512, (half + 1) * 512)
                p.op("act", lambda: nc.scalar.copy(out=osb[:, hs], in_=PS[2 + half][:]), r=[PS[2 + half]], w=[osb])

        def outp(s, n, B):
            r0 = s * P + n * 128
            sm = sm2
            if dbg is not None:
                p.dma(dbg[r0:r0 + 128, :], osb[:], r=[osb], w=[DR["dbgt"]])
            p.op("dve", lambda: nc.vector.tensor_reduce(out=sm[:, 4, :], in_=v3(osb.t[:]), axis=AX.X, op=ALU.add), r=[osb], w=[sm])
            yield
            p.op("pool", lambda: nc.gpsimd.tensor_tensor(out=obuf[:], in0=osb[:], in1=osb[:], op=ALU.mult), r=[osb], w=[obuf])
            yield
            p.op("dve", lambda: nc.vector.tensor_reduce(out=sm[:, 5, :], in_=v3(obuf.t[:]), axis=AX.X, op=ALU.add), r=[obuf], w=[sm])
            p.op("dve", lambda: nc.vector.tensor_scalar(out=sm[:, 4, :], in0=sm[:, 4, :], scalar1=1.0 / 64, scalar2=None, op0=ALU.mult), r=[sm], w=[sm])
            p.op("dve", lambda: nc.vector.tensor_tensor(out=sm[:, 6, :], in0=sm[:, 4, :], in1=sm[:, 4, :], op=ALU.mult), r=[sm], w=[sm])
            p.op("dve", lambda: nc.vector.scalar_tensor_tensor(out=sm[:, 5, :], in0=sm[:, 5, :], scalar=1.0 / 64, in1=sm[:, 6, :], op0=ALU.mult, op1=ALU.subtract),
                 r=[sm], w=[sm])
            yield
            p.op("dve", lambda: nc.vector.tensor_scalar(out=sm[:, 5, :], in0=sm[:, 5, :], scalar1=GN_EPS, scalar2=None, op0=ALU.add), r=[sm], w=[sm])
            p.op("pool", lambda: nc.gpsimd.tensor_tensor(out=sm[:, 7, :], in0=sm[:, 5, :], in1=mhalf16[:], op=ALU.pow), r=[sm, mhalf16], w=[sm])
            yield
            p.op("dve", lambda: nc.vector.tensor_tensor(out=v3(osb.t[:]), in0=v3(osb.t[:]), in1=bc(sm[:, 4, :]), op=ALU.subtract), r=[osb, sm], w=[osb])
            yield
            p.op("dve", lambda: nc.vector.tensor_tensor(out=v3(osb.t[:]), in0=v3(osb.t[:]), in1=bc(sm[:, 7, :]), op=ALU.mult), r=[osb, sm], w=[osb])
            yield
            p.op("pool", lambda: nc.gpsimd.tensor_tensor(out=osb[:], in0=osb[:], in1=prm["rwkv_gn_w"][:], op=ALU.mult), r=[osb, prm["rwkv_gn_w"]], w=[osb])
            yield
            p.op("pool", lambda: nc.gpsimd.tensor_tensor(out=osb[:], in0=osb[:], in1=prm["rwkv_gn_b"][:], op=ALU.add), r=[osb, prm["rwkv_gn_b"]], w=[osb])
            yield
            p.op("dve", lambda: nc.vector.tensor_tensor(out=v3(obuf.t[:]), in0=v3(B["vb"].t[:]), in1=bc(B["sm"][:, 3, :]), op=ALU.mult), r=[B["vb"], B["sm"]], w=[obuf])
            yield
            p.op("pool", lambda: nc.gpsimd.tensor_tensor(out=obuf[:], in0=obuf[:], in1=osb[:], op=ALU.add), r=[obuf, osb], w=[obuf])
            yield
            p.op("pool", lambda: nc.gpsimd.tensor_tensor(out=obuf[:], in0=obuf[:], in1=B["gs"][:], op=ALU.mult), r=[obuf, B["gs"]], w=[obuf])
            yield
            for _ in range(4):
                yield
            for half in range(2):
                pst = PS[6 + half]
                for k4 in range(4):
                    kc = half * 4 + k4
                    p.op("pe", lambda: nc.tensor.transpose(pst[:, k4 * 128:(k4 + 1) * 128], obuf[:, kc * 128:(kc + 1) * 128], ident[:]), r=[obuf, ident], w=[pst])
                src_ = pst.t[:].rearrange("p (k t) -> p k t", k=4)
                if half == 0:
                    p.op("act", lambda: nc.scalar.copy(out=oTb[:, 0:4, :], in_=src_), r=[pst], w=[oTb])
                else:
                    p.op("dve", lambda: nc.vector.tensor_copy(out=oTb[:, 4:8, :], in_=src_), r=[pst], w=[oTb])
                yield
            p.dma(DR["OT_rwkv"].rearrange("(kc p) t -> p kc t", p=128)[:, :, r0:r0 + 128], oTb[:], r=[oTb], w=[DR["OT_t"][1]])
            yield

        tiles = [(s, n) for s in (range(NS) if seqs is None else seqs) for n in range(ntl)]

        def run_all(gen):
            for _ in gen:
                pass

        def chain(*gens):
            for g in gens:
                if g is not None:
                    for _ in g:
                        yield

        if not pipelined:
            for idx, (s, n) in enumerate(tiles):
                B = SETS[idx % 2]
                run_all(prep_load(s, n))
                run_all(prep(s, n, B))
                main(s, n, B, lambda: None)
                run_all(outp(s, n, B))
        else:
            run_all(prep_load(tiles[0][0], tiles[0][1]))
            run_all(prep(tiles[0][0], tiles[0][1], SETS[0]))
            for idx, (s, n) in enumerate(tiles):
                B = SETS[idx % 2]
                g_out = outp(tiles[idx - 1][0], tiles[idx - 1][1], SETS[(idx - 1) % 2]) if idx > 0 else None
                g_load = prep_load(tiles[idx + 1][0], tiles[idx + 1][1]) if idx + 1 < len(tiles) else None
                g_prep = prep(tiles[idx + 1][0], tiles[idx + 1][1], SETS[(idx + 1) % 2]) if idx + 1 < len(tiles) else None
                g_lo = chain(g_load, g_out)
                bg = chain(g_lo, g_prep)

                def inj(k=1):
                    for _ in range(k):
                        try:
                            next(bg)
                        except StopIteration:
                            return
                main(s, n, B, inj, lambda: run_all(g_lo))
                run_all(bg)
            run_all(outp(tiles[-1][0], tiles[-1][1], SETS[(len(tiles) - 1) % 2]))
        p.barrier()


def host_consts():
    c = {}
    i = np.arange(128)
    c["ident"] = np.eye(128, dtype=np.float32)
    c["tri_le"] = (i[:, None] <= i[None, :]).astype(np.float32)
    c["tri_lt"] = (i[:, None] < i[None, :]).astype(np.float32)
    c["tri_gt"] = (i[:, None] > i[None, :]).astype(np.float32)
    same = (i[:, None] // 32) == (i[None, :] // 32)
    c["b_le"] = (same & (i[:, None] <= i[None, :])).astype(np.float32)
    c["b_gt"] = (same & (i[:, None] > i[None, :])).astype(np.float32)
    bind = np.zeros((128, 128), np.float32)
    bind[i, i // 32] = 1.0
    c["bind"] = bind
    c["ones"] = np.ones((128, 128), np.float32)
    t = (np.arange(NT)[None, :] * 128 + np.arange(128)[:, None]).astype(np.float32) - PAD
    inv = (1.0 / (10000.0 ** (np.arange(0, 64, 2, dtype=np.float32) / 64))).astype(np.float32)
    ang = (t[:, :, None] * inv[None, None, :]).astype(np.float32)
    cs, sn = np.cos(ang).astype(np.float32), np.sin(ang).astype(np.float32)
    cos2 = np.zeros((128, NT, 2, 2, 32), np.float32)
    sin2 = np.zeros((128, NT, 2, 2, 32), np.float32)
    cos2[:] = cs[:, :, None, None, :]
    sin2[:, :, :, 0, :] = -sn[:, :, None, :]
    sin2[:, :, :, 1, :] = sn[:, :, None, :]
    c["cos2"] = cos2.reshape(128, NT, 128)
    c["sin2"] = sin2.reshape(128, NT, 128)
    return c


PARAM_SHAPES = {
    "pre_norm_w": [2, D], "post_norm_w": [2, D], "w_in": [2, D, IN_W],
    "lambda_q1": [2, 64], "lambda_k1": [2, 64], "lambda_q2": [2, 64], "lambda_k2": [2, 64], "att_norm_w": [2, 128],
    "rwkv_mu": [2, RW_W], "rwkv_w0": [2, 1024], "rwkv_w_up": [2, 64, 1024], "rwkv_a0": [2, 1024], "rwkv_a_up": [2, 64, 1024],
    "rwkv_k_k": [2, 1024], "rwkv_k_a": [2, 1024], "rwkv_r_k": [2, 16, 64], "rwkv_gn_w": [2, 1024], "rwkv_gn_b": [2, 1024],
    "hgrn_lower_bounds": [2, 1024], "hgrn_norm_w": [2, 128],
    "w_att_out": [2, D, D], "w_rwkv_out": [2, D, D], "w_hgrn_out": [2, D, D], "w_o": [2, D, D],
}
CONST_NAMES = ("ident", "tri_le", "tri_lt", "tri_gt", "b_le", "b_gt", "bind", "ones")


def build_program(nlayers=DEPTH):
    nc = bass.Bass("TRN2", target_bir_lowering=False)
    DR = {}
    H0 = nc.dram_tensor("h0", [ROWS, D], F32, kind="ExternalInput").ap()
    for n, sh in PARAM_SHAPES.items():
        DR[n] = nc.dram_tensor(n, sh, F32, kind="ExternalInput").ap()
    cd = {n: nc.dram_tensor(n, [128, 128], F32, kind="ExternalInput").ap() for n in CONST_NAMES}
    DR["cos2"] = nc.dram_tensor("cos2", [128, NT, 128], F32, kind="ExternalInput").ap()
    DR["sin2"] = nc.dram_tensor("sin2", [128, NT, 128], F32, kind="ExternalInput").ap()
    DR["Z"] = [nc.dram_tensor("Zscr%d" % i, [P, IN_W], F32, kind="Internal").ap() for i in range(NS)]
    for n in ("OT_att", "OT_rwkv", "OT_hgrn"):
        DR[n] = nc.dram_tensor(n + "_scr", [D, ROWS], BF16, kind="Internal").ap()
    Hs = nc.dram_tensor("Hscr", [ROWS, D], F32, kind="Internal").ap()
    OUT = nc.dram_tensor("out", [NS, 2048, D], F32, kind="ExternalOutput").ap()
    DR["OT_t"] = [Tl(None, "ot%d" % i, multi=True) for i in range(3)]
    DR["Zt"] = Tl(None, "Z", multi=True)
    DR["Outt"] = Tl(None, "out", multi=True)
    H_t = [Tl(None, "H0", multi=True), Tl(None, "Hs", multi=True)]
    with ExitStack() as es:
        p = Prog(nc, es)
        G = {}
        for i, n in enumerate(CONST_NAMES):
            G[n] = p.tile(n, [128, 128], F32)
            p.dma(G[n][:], cd[n], w=[G[n]], q=("sp" if i % 2 == 0 else "act"))
        G["PS"] = [p.tile("ps%d" % i, [128, 512], F32, psum=True) for i in range(8)]
        PS = G["PS"]
        for l in range(nlayers):
            Hin = H0 if l == 0 else Hs
            Hin_t = H_t[0] if l == 0 else H_t[1]
            last = (l == nlayers - 1)
            with ExitStack() as es2:
                T = lambda name, shape, dt, **kw: p.tile("pj_" + name, shape, dt, es=es2, **kw)
                C = {"ident": G["ident"]}
                C["uT"] = T("uT", [128, 8, ROWS], BF16, multi=True)
                C["ht"] = [T("ht%d" % i, [128, D], F32) for i in range(2)]
                C["ut"] = [T("ut%d" % i, [128, D], F32) for i in range(2)]
                C["sq"] = T("sq", [128, D], F32)
                C["ss"] = [T("ss%d" % i, [128, 8], F32) for i in range(2)]
                C["pst"] = PS[0:2]
                C["psz"] = PS[2:6]
                C["wst"] = [T("wst%d" % i, [128, 8, 512], F32) for i in range(2)]
                C["wbf"] = [T("wbf%d" % i, [128, 8, 512], BF16) for i in range(2)]
                C["zo"] = [T("zo%d" % i, [128, 512], F32) for i in range(4)]
                C["Zt"] = DR["Zt"]
                C["Ht"] = Hin_t
                C["mhalf"] = T("mhalf", [128, 1], F32)
                p.op("pool", lambda: nc.gpsimd.memset(C["mhalf"][:], -0.5), w=[C["mhalf"]])
                prew_bc = T("prew_bc", [128, D], F32)
                load_bc(p, nc, prew_bc, DR["pre_norm_w"][l:l + 1, :])
                phase_proj(p, nc, Hin, DR["Z"], DR["w_in"][l], prew_bc, C)
                p.barrier()
            phase_att(p, nc, l, DR, G)
            with ExitStack() as es_rw:
                Trw = lambda name, shape, dt, **kw: p.tile("rwp_" + name, shape, dt, es=es_rw, **kw)
                rw_pre = rwkv_setup(p, nc, l, DR, G, Trw)
                phase_hgrn(p, nc, l, DR, G)
                phase_rwkv(p, nc, l, DR, G, pre=rw_pre)
            DR["Ht"] = Hin_t
            DR["Ht2"] = H_t[1]
            phase_merge(p, nc, l, DR, G, Hin, Hs, out_final=(OUT if last else None))
        p.barrier()
        print("n_inst", p.n_inst)
    return nc


_NC_CACHE = {}


def kernel(**inputs):
    x = np.asarray(inputs["x"], dtype=np.float32)
    meta = np.asarray(inputs["meta_tokens"], dtype=np.float32)
    B = x.shape[0]
    ncores = B // NS
    if "nc" not in _NC_CACHE:
        _NC_CACHE["nc"] = build_program()
    nc = _NC_CACHE["nc"]
    consts = host_consts()
    shared = {n: np.ascontiguousarray(np.asarray(inputs[n], dtype=np.float32)) for n in PARAM_SHAPES}
    shared.update(consts)
    in_maps = []
    for c in range(ncores):
        h0 = np.zeros((NS, P, D), np.float32)
        h0[:, PAD:PAD + 16] = meta[None]
        h0[:, PAD + 16:] = x[c * NS:(c + 1) * NS]
        m = dict(shared)
        m["h0"] = h0.reshape(ROWS, D)
        in_maps.append(m)
    res = run_bass_kernel_spmd(nc, in_maps, core_ids=list(range(ncores)))
    out = np.concatenate([np.asarray(r["out"], dtype=np.float32) for r in res.results], axis=0)
    return out
```

```python
import numpy as np
import ml_dtypes
from contextlib import ExitStack
import concourse.bass as bass
import concourse.mybir as mybir
from concourse.bass_utils import run_bass_kernel_spmd

F32 = mybir.dt.float32
BF16 = mybir.dt.bfloat16
AF = mybir.ActivationFunctionType
ALU = mybir.AluOpType
AX = mybir.AxisListType

D = 1024
NS = 2
NT = 17
P = NT * 128
PAD = 112
ROWS = NS * P
IN_W = 15488
RW_OFF = 4096
RW_W = 4224
HG_OFF = RW_OFF + RW_W
MG_OFF = HG_OFF + 4096
DEPTH = 2
EPS = 1e-6
GN_EPS = 64e-5


def zr(Z, a, b):
    if isinstance(Z, list):
        si = a // P
        assert (b - 1) // P == si
        return Z[si][a - si * P:b - si * P]
    return Z[a:b]


class Tl:
    __slots__ = ("t", "w", "r", "name", "multi", "wd", "excl", "wtrue")

    def __init__(self, t, name="", multi=False, excl=False):
        self.excl = excl
        self.wtrue = True
        self.t = t
        self.w = None
        self.r = {}
        self.wd = {}
        self.multi = multi
        self.name = name

    def __getitem__(self, idx):
        return self.t[idx]


class Prog:
    def __init__(self, nc, es, same_engine_sync=True):
        self.nc = nc
        self.es = es
        self.E = {"pe": nc.tensor, "dve": nc.vector, "act": nc.scalar, "pool": nc.gpsimd, "sp": nc.sync}
        self.sem = {e: es.enter_context(nc.semaphore("sem_" + e)) for e in ("pe", "dve", "act", "pool")}
        self.cnt = {e: 0 for e in self.sem}
        self.epoch = {e: 0 for e in self.sem}
        self.old = []
        self.waited = {e: {} for e in self.E}
        self.same = same_engine_sync
        self.dq = {}
        for q, n in (("sp", 40),):
            self.dq[q] = {"sems": [es.enter_context(nc.semaphore("dsem_%s%d" % (q, i))) for i in range(n)],
                          "val": [0] * n, "next": 0}
        self.n_inst = 0
        self.mute = False
        self.relax = True
        self.n_relaxed = 0

    def tile(self, name, shape, dt, psum=False, multi=False, es=None):
        es = es or self.es
        self.n_tiles = getattr(self, "n_tiles", 0) + 1
        name = "%s_%d" % (name, self.n_tiles)
        if psum:
            t = es.enter_context(self.nc.psum_tensor("pt_" + name, shape, dt))
        else:
            t = es.enter_context(self.nc.sbuf_tensor("sb_" + name, shape, dt))
        return Tl(t, name, multi, excl=psum)

    def _wait(self, e, tok, kind="RAW"):
        if tok is None:
            return
        sem, val, key, owner = tok
        if owner == e and (e == "pe" or not self.same):
            return
        if owner == e and self.relax and ((e in ("dve", "act") and kind in ("WAR", "RR", "WAW")) or (e == "pool" and kind in ("WAR", "RR"))):
            self.n_relaxed += 1
            return
        if self.waited[e].get(key, 0) >= val:
            return
        self.E[e].wait_ge(sem, val)
        self.waited[e][key] = val

    def _deps(self, e, r, w, xread=()):
        for t in r:
            if t.multi:
                for tok in t.wd.values():
                    self._wait(e, tok)
            else:
                self._wait(e, t.w)
        for t in w:
            if not t.multi:
                if t in xread:
                    self._wait(e, t.w, "RAW" if t.wtrue else "RR")
                else:
                    self._wait(e, t.w, "WAW" if t.wtrue else "WAR")
            for tok in t.r.values():
                self._wait(e, tok, "WAR")

    def _record(self, tok, r, w):
        for t in r:
            t.r[tok[2]] = tok
        for t in w:
            if t.multi:
                t.wd[tok[2]] = tok
            else:
                t.w = tok
                t.r = {}

    def barrier(self):
        self.mute = False
        toks = [(self.sem[e], self.cnt[e], "c_%s_%d" % (e, self.epoch[e]), e) for e in self.sem if self.cnt[e] > 0]
        toks += self.old
        for q, dq in self.dq.items():
            for i, v in enumerate(dq["val"]):
                if v > 0:
                    toks.append((dq["sems"][i], v, "d_%s%d" % (q, i), None))
        for e in self.E:
            for tok in toks:
                if tok[3] == e:
                    continue
                self._wait(e, tok)

    def op(self, e, fn, r=(), w=()):
        if self.mute:
            return None
        xread = ()
        if any(t.excl for t in r):
            xread = [t for t in r if t.excl and t not in w]
            w = list(w) + xread
            r = [t for t in r if not t.excl]
        self._deps(e, r, w, xread)
        if self.cnt[e] >= 16000:
            self.old.append((self.sem[e], self.cnt[e], "c_%s_%d" % (e, self.epoch[e]), None))
            self.epoch[e] += 1
            self.sem[e] = self.es.enter_context(self.nc.semaphore("sem_%s_%d" % (e, self.epoch[e])))
            self.cnt[e] = 0
        ins = fn()
        self.cnt[e] += 1
        ins.then_inc(self.sem[e], 1)
        tok = (self.sem[e], self.cnt[e], "c_%s_%d" % (e, self.epoch[e]), e)
        self._record(tok, r, w)
        for t in w:
            t.wtrue = t not in xread
        self.n_inst += 1
        return tok

    def dma(self, out, in_, r=(), w=(), q="sp", **kw):
        if self.mute:
            return None
        q = "sp"
        dq = self.dq[q]
        i = dq["next"]
        dq["next"] = (i + 1) % len(dq["sems"])
        key = "d_%s%d" % (q, i)
        if dq["val"][i] > 0:
            self._wait(q, (dq["sems"][i], dq["val"][i], key, None))
        self._deps(q, r, w)
        dq["val"][i] += 16
        self.E[q].dma_start(out=out, in_=in_, **kw).then_inc(dq["sems"][i], 16)
        tok = (dq["sems"][i], dq["val"][i], key, None)
        self._record(tok, r, w)
        self.n_inst += 1
        return tok

    def finish(self, toks):
        for tok in toks:
            self._wait("sp", tok)


def phase_proj(p, nc, H, Z, w_in_l, prew_bc, C, ntiles=NS * NT, ncolblk=None):
    uT = C["uT"]
    ident = C["ident"]
    for tt in range(ntiles):
        ht = C["ht"][tt % 2]
        p.dma(ht[:], H[tt * 128:(tt + 1) * 128, :], r=([C["Ht"]] if "Ht" in C else []), w=[ht])
        sq = C["sq"]
        ss = C["ss"][tt % 2]
        p.op("act", lambda: nc.scalar.activation(out=sq[:], in_=ht[:], func=AF.Square, accum_out=ss[:, 0:1]),
             r=[ht], w=[sq, ss])
        p.op("dve", lambda: nc.vector.tensor_scalar(out=ss[:, 1:2], in0=ss[:, 0:1], scalar1=1.0 / D, scalar2=EPS,
                                                    op0=ALU.mult, op1=ALU.add), r=[ss], w=[ss])
        p.op("pool", lambda: nc.gpsimd.tensor_tensor(out=ss[:, 3:4], in0=ss[:, 1:2], in1=C["mhalf"][:], op=ALU.pow), r=[ss, C["mhalf"]], w=[ss])
        ut = C["ut"][tt % 2]
        p.op("dve", lambda: nc.vector.scalar_tensor_tensor(out=ut[:], in0=ht[:], scalar=ss[:, 3:4], in1=prew_bc[:],
                                                           op0=ALU.mult, op1=ALU.mult), r=[ht, ss, prew_bc], w=[ut])
        for half in range(2):
            pst = C["pst"][half]
            for k4 in range(4):
                kc = half * 4 + k4
                p.op("pe", lambda: nc.tensor.transpose(pst[:, k4 * 128:(k4 + 1) * 128], ut[:, kc * 128:(kc + 1) * 128],
                                                       ident[:]), r=[ut, ident], w=[pst])
            dst = uT.t[:, half * 4:(half + 1) * 4, tt * 128:(tt + 1) * 128]
            src = pst.t[:].rearrange("p (k t) -> p k t", k=4)
            if half == 0:
                p.op("act", lambda: nc.scalar.copy(out=dst, in_=src), r=[pst], w=[uT])
            else:
                p.op("dve", lambda: nc.vector.tensor_copy(out=dst, in_=src), r=[pst], w=[uT])
    wv = w_in_l.rearrange("(kc p) n -> p kc n", p=128)
    ncb = (IN_W + 511) // 512 if ncolblk is None else ncolblk
    ev = 0
    def load_w(cb):
        c0 = cb * 512
        cw = min(512, IN_W - c0)
        wst = C["wst"][cb % 2]
        wbf = C["wbf"][cb % 2]
        p.dma(wst[:, :, 0:cw], wv[:, :, c0:c0 + cw], w=[wst], q="sp")
        p.op("pool", lambda: nc.gpsimd.tensor_copy(out=wbf[:, 0:4, 0:cw], in_=wst[:, 0:4, 0:cw]), r=[wst], w=[wbf])
        p.op("pool", lambda: nc.gpsimd.tensor_copy(out=wbf[:, 4:8, 0:cw], in_=wst[:, 4:8, 0:cw]), r=[wst], w=[wbf])

    load_w(0)
    for cb in range(ncb):
        c0 = cb * 512
        cw = min(512, IN_W - c0)
        wbf = C["wbf"][cb % 2]
        if cb + 1 < ncb:
            load_w(cb + 1)
        for tt in range(ntiles):
            psz = C["psz"][ev % 4]
            zo = C["zo"][ev % 4]
            for kc in range(8):
                p.op("pe", lambda: nc.tensor.matmul(psz[:, 0:cw], lhsT=uT[:, kc, tt * 128:(tt + 1) * 128],
                                                    rhs=wbf[:, kc, 0:cw], start=(kc == 0), stop=(kc == 7)),
                     r=[uT, wbf], w=[psz])
            if ev % 2 == 0:
                p.op("act", lambda: nc.scalar.copy(out=zo[:, 0:cw], in_=psz[:, 0:cw]), r=[psz], w=[zo])
            else:
                p.op("dve", lambda: nc.vector.tensor_copy(out=zo[:, 0:cw], in_=psz[:, 0:cw]), r=[psz], w=[zo])
            p.dma(zr(Z, tt * 128, (tt + 1) * 128)[:, c0:c0 + cw], zo[:, 0:cw], r=[zo], w=[C["Zt"]], q="sp")
            ev += 1


def load_bc(p, nc, tl, row_ap, q="sp"):
    p.dma(tl[:], row_ap.partition_broadcast(128), w=[tl], q=q)


def phase_merge(p, nc, l, DR, G, Hin, Hout, out_final=None, tiles=None):
    PS = G["PS"]
    ident = G["ident"]
    Z = DR["Z"]
    with ExitStack() as es:
        T = lambda name, shape, dt, **kw: p.tile("mg_" + name, shape, dt, es=es, **kw)
        wst = T("wst", [128, 8, 1024], F32)
        W = [T("w%d" % i, [128, 8, 1024], BF16) for i in range(4)]
        names = ["w_att_out", "w_rwkv_out", "w_hgrn_out", "w_o"]
        for i in range(4):
            p.dma(wst[:], DR[names[i]][l].rearrange("(kc p) n -> p kc n", p=128), w=[wst])
            p.op("pool", lambda: nc.gpsimd.tensor_copy(out=W[i][:, 0:4, :], in_=wst[:, 0:4, :]), r=[wst], w=[W[i]])
            p.op("act", lambda: nc.scalar.copy(out=W[i][:, 4:8, :], in_=wst[:, 4:8, :]), r=[wst], w=[W[i]])
        mhalf = T("mhalf", [128, 1], F32)
        p.op("pool", lambda: nc.gpsimd.memset(mhalf[:], -0.5), w=[mhalf])
        postw = T("postw", [128, D], F32)
        load_bc(p, nc, postw, DR["post_norm_w"][l:l + 1, :])
        oT = [[T("oT%d_%d" % (b, i), [128, 8, 128], BF16) for i in range(3)] for b in range(3)]
        mg = [T("mgt%d" % i, [128, 3072], F32) for i in range(3)]
        hin = [T("hin%d" % i, [128, D], F32) for i in range(3)]
        ys = [T("y%d" % i, [128, D], F32) for i in range(2)]
        tmp = [T("tmp%d" % i, [128, 512], F32) for i in range(2)]
        yT = T("yT", [128, 8, 128], BF16)
        hn = [T("hn%d" % i, [128, D], F32) for i in range(2)]
        sq = T("sq", [128, 512], F32)
        st = [T("st%d" % i, [128, 8], F32) for i in range(2)]
        OTs = [DR["OT_att"], DR["OT_rwkv"], DR["OT_hgrn"]]
        tl_list = list(range(NS * NT)) if tiles is None else tiles
        cnt = 0
        def stage0(it, tt):
            r0 = tt * 128
            for b in range(3):
                p.dma(oT[b][it % 3][:], OTs[b].rearrange("(kc p) t -> p kc t", p=128)[:, :, r0:r0 + 128],
                      r=[DR["OT_t"][b]], w=[oT[b][it % 3]], q="act")
            m = mg[it % 3]
            p.dma(m[:], zr(Z, r0, r0 + 128)[:, MG_OFF:MG_OFF + 3072], r=[DR["Zt"]], w=[m])
            hi_ = hin[it % 3]
            p.dma(hi_[:], Hin[r0:r0 + 128, :], r=[DR["Ht"]], w=[hi_])

        def stage1(it, tt):
            nonlocal cnt
            y = ys[it % 2]
            r0 = tt * 128
            m = mg[it % 3]
            hi_ = hin[it % 3]
            p.op("act", lambda: nc.scalar.activation(out=m[:], in_=m[:], func=AF.Sigmoid), r=[m], w=[m])
            for b in range(3):
                ob = oT[b][it % 3]
                for half in range(2):
                    ps = PS[cnt % 2]
                    cnt += 1
                    for kc in range(8):
                        p.op("pe", lambda: nc.tensor.matmul(ps[:], lhsT=ob[:, kc, :], rhs=W[b][:, kc, half * 512:(half + 1) * 512],
                                                            start=(kc == 0), stop=(kc == 7)), r=[ob, W[b]], w=[ps])
                    gsl = m[:, b * 1024 + half * 512: b * 1024 + (half + 1) * 512]
                    ysl = y[:, half * 512:(half + 1) * 512]
                    if b == 0:
                        p.op("dve", lambda: nc.vector.tensor_tensor(out=ysl, in0=ps[:], in1=gsl, op=ALU.mult), r=[ps, m], w=[y])
                    else:
                        t_ = tmp[cnt % 2]
                        p.op("dve", lambda: nc.vector.tensor_tensor(out=t_[:], in0=ps[:], in1=gsl, op=ALU.mult), r=[ps, m], w=[t_])
                        p.op("pool", lambda: nc.gpsimd.tensor_tensor(out=ysl, in0=ysl, in1=t_[:], op=ALU.add), r=[t_, y], w=[y])

        def stage2(it, tt):
            y = ys[it % 2]
            r0 = tt * 128
            hi_ = hin[it % 3]
            for half in range(2):
                pst = PS[2 + half]
                for k4 in range(4):
                    kc = half * 4 + k4
                    p.op("pe", lambda: nc.tensor.transpose(pst[:, k4 * 128:(k4 + 1) * 128], y[:, kc * 128:(kc + 1) * 128], ident[:]),
                         r=[y, ident], w=[pst])
                src = pst.t[:].rearrange("p (k t) -> p k t", k=4)
                if half == 0:
                    p.op("act", lambda: nc.scalar.copy(out=yT[:, 0:4, :], in_=src), r=[pst], w=[yT])
                else:
                    p.op("dve", lambda: nc.vector.tensor_copy(out=yT[:, 4:8, :], in_=src), r=[pst], w=[yT])
            s_ = st[it % 2]
            h_ = hn[it % 2]
            pso = [PS[4 + (it % 2) * 2], PS[5 + (it % 2) * 2]]
            for half in range(2):
                for kc in range(8):
                    p.op("pe", lambda: nc.tensor.matmul(pso[half][:], lhsT=yT[:, kc, :], rhs=W[3][:, kc, half * 512:(half + 1) * 512],
                                                        start=(kc == 0), stop=(kc == 7)), r=[yT, W[3]], w=[pso[half]])
                p.op("act", lambda: nc.scalar.activation(out=sq[:], in_=pso[half][:], func=AF.Square, accum_out=s_[:, half:half + 1]),
                     r=[pso[half]], w=[sq, s_])
            p.op("dve", lambda: nc.vector.tensor_tensor(out=s_[:, 2:3], in0=s_[:, 0:1], in1=s_[:, 1:2], op=ALU.add), r=[s_], w=[s_])
            p.op("dve", lambda: nc.vector.tensor_scalar(out=s_[:, 3:4], in0=s_[:, 2:3], scalar1=1.0 / D, scalar2=EPS,
                                                        op0=ALU.mult, op1=ALU.add), r=[s_], w=[s_])
            p.op("pool", lambda: nc.gpsimd.tensor_tensor(out=s_[:, 5:6], in0=s_[:, 3:4], in1=mhalf[:], op=ALU.pow), r=[s_, mhalf], w=[s_])
            for half in range(2):
                hs = h_[:, half * 512:(half + 1) * 512]
                p.op("dve", lambda: nc.vector.scalar_tensor_tensor(out=hs, in0=pso[half][:], scalar=s_[:, 5:6],
                                                                   in1=postw[:, half * 512:(half + 1) * 512],
                                                                   op0=ALU.mult, op1=ALU.mult), r=[pso[half], s_, postw], w=[h_])
            p.op("pool", lambda: nc.gpsimd.tensor_tensor(out=h_[:], in0=h_[:], in1=hi_[:], op=ALU.add), r=[h_, hi_], w=[h_])
            n = tt % NT
            if n == 0:
                p.op("pool", lambda: nc.gpsimd.memset(h_[0:96, :], 0.0), r=[], w=[h_])
                p.op("pool", lambda: nc.gpsimd.memset(h_[96:112, :], 0.0), r=[], w=[h_])
            if out_final is None:
                p.dma(Hout[r0:r0 + 128, :], h_[:], r=[h_], w=[DR["Ht2"]])
            else:
                s = tt // NT
                if n == 0:
                    pass
                else:
                    p.dma(out_final[s, (n - 1) * 128:n * 128, :], h_[:], r=[h_], w=[DR["Outt"]])
        for it, tt in enumerate(tl_list):
            if it == 0:
                stage0(0, tt)
                if len(tl_list) > 1:
                    stage0(1, tl_list[1])
                stage1(0, tt)
            if it + 2 < len(tl_list):
                stage0(it + 2, tl_list[it + 2])
            if it + 1 < len(tl_list):
                stage1(it + 1, tl_list[it + 1])
            stage2(it, tt)
        p.barrier()


def phase_att(p, nc, l, DR, G, pairs=None, qts=None):
    import math as _m
    PS = G["PS"]
    ident = G["ident"]
    tri = G["tri_le"]
    Z = DR["Z"]
    lam_init = 0.8 - 0.6 * _m.exp(-0.3 * l)
    with ExitStack() as es:
        T = lambda name, shape, dt, **kw: p.tile("at_" + name, shape, dt, es=es, **kw)
        cos2 = T("cos2", [128, NT, 128], F32)
        sin2 = T("sin2", [128, NT, 128], F32)
        p.dma(cos2[:], DR["cos2"], w=[cos2])
        p.dma(sin2[:], DR["sin2"], w=[sin2])
        lv = T("lv", [128, 4, 64], F32)
        for i, n in enumerate(("lambda_q1", "lambda_k1", "lambda_q2", "lambda_k2")):
            p.dma(lv[:, i, :], DR[n][l:l + 1, :].partition_broadcast(128), w=[lv])
        lt = T("lt", [128, 2, 64], F32)
        ls = T("ls", [128, 8], F32)
        p.op("dve", lambda: nc.vector.tensor_tensor(out=lt[:, 0, :], in0=lv[:, 0, :], in1=lv[:, 1, :], op=ALU.mult), r=[lv], w=[lt])
        p.op("dve", lambda: nc.vector.tensor_tensor(out=lt[:, 1, :], in0=lv[:, 2, :], in1=lv[:, 3, :], op=ALU.mult), r=[lv], w=[lt])
        p.op("dve", lambda: nc.vector.tensor_reduce(out=ls[:, 0:2], in_=lt[:], axis=AX.X, op=ALU.add), r=[lt], w=[ls])
        p.op("act", lambda: nc.scalar.activation(out=ls[:, 2:4], in_=ls[:, 0:2], func=AF.Exp), r=[ls], w=[ls])
        p.op("dve", lambda: nc.vector.tensor_tensor(out=ls[:, 4:5], in0=ls[:, 3:4], in1=ls[:, 2:3], op=ALU.subtract), r=[ls], w=[ls])
        p.op("dve", lambda: nc.vector.tensor_scalar(out=ls[:, 5:6], in0=ls[:, 4:5], scalar1=-lam_init, scalar2=None, op0=ALU.add), r=[ls], w=[ls])
        neg_lam = ls[:, 5:6]
        normw = T("normw", [128, 128], F32)
        load_bc(p, nc, normw, DR["att_norm_w"][l:l + 1, :])
        p.op("dve", lambda: nc.vector.tensor_scalar(out=normw[:], in0=normw[:], scalar1=(1.0 - lam_init), scalar2=None, op0=ALU.mult),
             r=[normw], w=[normw])
        SETS = []
        for kb in range(2):
            Bf = {}
            for nm in ("qraw", "kraw", "vraw", "graw"):
                Bf[nm] = T("%s%d" % (nm, kb), [128, NT, 128], F32)
            Bf["qT"] = T("qT%d" % kb, [128, P], BF16)
            Bf["kT"] = T("kT%d" % kb, [128, P], BF16)
            Bf["v1"] = T("v1%d" % kb, [128, NT, 130], BF16)
            v1_ = Bf["v1"]
            p.op("pool", lambda: nc.gpsimd.memset(v1_[:, :, 128:130], 1.0), w=[v1_])
            p.op("pool", lambda: nc.gpsimd.memset(v1_[0:96, 0, 128:130], 0.0), w=[v1_])
            p.op("pool", lambda: nc.gpsimd.memset(v1_[96:112, 0, 128:130], 0.0), w=[v1_])
            SETS.append(Bf)
        t1 = T("t1", [128, NT, 128], F32)
        t2 = T("t2", [128, NT, 128], F32)
        obufs = [T("obuf%d" % i, [128, NT, 128], F32) for i in range(2)]
        oTb = T("oTb", [128, P], BF16)
        pT = [T("pT%d" % i, [128, 512], BF16) for i in range(5)]
        SB = [PS[0], PS[1], PS[6]]
        mhalf = T("mhalf", [128, 1], F32)
        p.op("pool", lambda: nc.gpsimd.memset(mhalf[:], -0.5), w=[mhalf])
        o_ = [T("o%d" % i, [128, 128], F32) for i in range(2)]
        sq = T("sq", [128, 128], F32)
        rs = [T("rs%d" % i, [128, 12], F32) for i in range(2)]
        pr = [(s, j) for s in range(NS) for j in range(8)] if pairs is None else pairs

        def setup_load(s, j, Bf):
            qraw, kraw, vraw, graw = (Bf[k] for k in ("qraw", "kraw", "vraw", "graw"))
            zs = zr(Z, s * P, (s + 1) * P).rearrange("(n p) c -> p n c", p=128)
            p.dma(qraw[:], zs[:, :, j * 128:(j + 1) * 128], r=[DR["Zt"]], w=[qraw])
            p.dma(kraw[:], zs[:, :, 1024 + j * 128:1024 + (j + 1) * 128], r=[DR["Zt"]], w=[kraw], q="act")
            p.dma(vraw[:], zs[:, :, 2048 + j * 128:2048 + (j + 1) * 128], r=[DR["Zt"]], w=[vraw])
            p.dma(graw[:], zs[:, :, 3072 + j * 128:3072 + (j + 1) * 128], r=[DR["Zt"]], w=[graw], q="act")

        def setup(s, j, Bf):
            qraw, kraw, vraw, graw, qT, kT, v1 = (Bf[k] for k in ("qraw", "kraw", "vraw", "graw", "qT", "kT", "v1"))
            for (raw, dstT) in ((qraw, qT), (kraw, kT)):
                E = nc.vector
                p.op("dve", lambda: E.tensor_tensor(out=t1[:], in0=raw[:], in1=cos2[:], op=ALU.mult), r=[raw, cos2], w=[t1])
                yield
                rv = raw.t[:].rearrange("p n (g h d) -> p (n g) h d", g=2, h=2)
                sv = sin2.t[:].rearrange("p n (g h d) -> p (n g) h d", g=2, h=2)
                tv = t2.t[:].rearrange("p n (g h d) -> p (n g) h d", g=2, h=2)
                p.op("dve", lambda: E.tensor_tensor(out=tv[:, :, 0, :], in0=rv[:, :, 1, :], in1=sv[:, :, 0, :], op=ALU.mult), r=[raw, sin2], w=[t2])
                yield
                p.op("dve", lambda: E.tensor_tensor(out=tv[:, :, 1, :], in0=rv[:, :, 0, :], in1=sv[:, :, 1, :], op=ALU.mult), r=[raw, sin2], w=[t2])
                yield
                p.op("dve", lambda: E.tensor_tensor(out=t1[:], in0=t1[:], in1=t2[:], op=ALU.add), r=[t1, t2], w=[t1])
                yield
                for n0 in range(0, NT, 4):
                    nn = min(4, NT - n0)
                    pst = PS[7]
                    for i in range(nn):
                        p.op("pe", lambda: nc.tensor.transpose(pst[:, i * 128:(i + 1) * 128], t1[:, n0 + i, :], ident[:]), r=[t1, ident], w=[pst])
                    p.op("dve", lambda: nc.vector.tensor_copy(out=dstT[:, n0 * 128:(n0 + nn) * 128], in_=pst[:, 0:nn * 128]), r=[pst], w=[dstT])
                    yield
            p.op("dve", lambda: nc.vector.tensor_copy(out=v1[:, :, 0:128], in_=vraw[:]), r=[vraw], w=[v1])
            yield
            p.op("act", lambda: nc.scalar.activation(out=graw[:], in_=graw[:], func=AF.Silu), r=[graw], w=[graw])
            yield

        def mainloop(s, j, Bf, inj, obuf):
            qT, kT, v1, graw = Bf["qT"], Bf["kT"], Bf["v1"], Bf["graw"]
            groups = []
            for qt in (range(NT) if qts is None else qts):
                for k0 in range(0, qt + 1, 4):
                    for g in range(2):
                        groups.append((qt, g, k0, min(k0 + 4, qt + 1)))

            def emit_scores(grp, gi):
                qt, g, k0, k1 = grp
                pss = SB[gi % 3]
                for i, kt in enumerate(range(k0, k1)):
                    p.op("pe", lambda: nc.tensor.matmul(pss[:, i * 128:(i + 1) * 128], lhsT=kT[g * 64:(g + 1) * 64, kt * 128:(kt + 1) * 128],
                                                        rhs=qT[g * 64:(g + 1) * 64, qt * 128:(qt + 1) * 128], start=True, stop=True),
                         r=[kT, qT], w=[pss])
                pt = pT[gi % 5]
                n = k1 - k0
                p.op("act", lambda: nc.scalar.activation(out=pt[:, 0:n * 128], in_=pss[:, 0:n * 128], func=AF.Exp, scale=0.125), r=[pss], w=[pt])
                if k1 - 1 == qt:
                    i = qt - k0
                    p.op("pool", lambda: nc.gpsimd.tensor_tensor(out=pt[:, i * 128:(i + 1) * 128], in0=pt[:, i * 128:(i + 1) * 128], in1=tri[:], op=ALU.mult),
                         r=[pt, tri], w=[pt])

            def emit_pv(grp, gi):
                qt, g, k0, k1 = grp
                pt = pT[gi % 5]
                pso = PS[2 + (qt % 2) * 2 + g]
                for i, kt in enumerate(range(k0, k1)):
                    p.op("pe", lambda: nc.tensor.matmul(pso[:, 0:129], lhsT=pt[:, i * 128:(i + 1) * 128], rhs=v1[:, kt, 0:129],
                                                        start=(kt == 0), stop=(kt == qt)), r=[pt, v1], w=[pso])
                if k1 - 1 == qt and g == 1:
                    epilogue(qt)
                    if qt >= 0:
                        inj()

            def epilogue(qt):
                O1 = PS[2 + (qt % 2) * 2]
                O2 = PS[2 + (qt % 2) * 2 + 1]
                r_ = rs[qt % 2]
                o = o_[qt % 2]
                p.op("dve", lambda: nc.vector.tensor_scalar(out=r_[:, 0:1], in0=O1[:, 128:129], scalar1=1e-30, scalar2=None, op0=ALU.max), r=[O1], w=[r_])
                p.op("dve", lambda: nc.vector.tensor_scalar(out=r_[:, 1:2], in0=O2[:, 128:129], scalar1=1e-30, scalar2=None, op0=ALU.max), r=[O2], w=[r_])
                p.op("dve", lambda: nc.vector.reciprocal(out=r_[:, 2:4], in_=r_[:, 0:2]), r=[r_], w=[r_])
                p.op("dve", lambda: nc.vector.tensor_tensor(out=r_[:, 4:5], in0=r_[:, 3:4], in1=neg_lam, op=ALU.mult), r=[r_, ls], w=[r_])
                p.op("dve", lambda: nc.vector.tensor_scalar(out=o[:], in0=O1[:, 0:128], scalar1=r_[:, 2:3], scalar2=None, op0=ALU.mult), r=[O1, r_], w=[o])
                p.op("dve", lambda: nc.vector.scalar_tensor_tensor(out=o[:], in0=O2[:, 0:128], scalar=r_[:, 4:5], in1=o[:], op0=ALU.mult, op1=ALU.add),
                     r=[O2, r_, o], w=[o])
                p.op("act", lambda: nc.scalar.activation(out=sq[:], in_=o[:], func=AF.Square, accum_out=r_[:, 5:6]), r=[o], w=[sq, r_])
                p.op("dve", lambda: nc.vector.tensor_scalar(out=r_[:, 6:7], in0=r_[:, 5:6], scalar1=1.0 / 128, scalar2=EPS, op0=ALU.mult, op1=ALU.add), r=[r_], w=[r_])
                p.op("pool", lambda: nc.gpsimd.tensor_tensor(out=r_[:, 8:9], in0=r_[:, 6:7], in1=mhalf[:], op=ALU.pow), r=[r_, mhalf], w=[r_])
                p.op("dve", lambda: nc.vector.scalar_tensor_tensor(out=o[:], in0=o[:], scalar=r_[:, 8:9], in1=normw[:], op0=ALU.mult, op1=ALU.mult),
                     r=[o, r_, normw], w=[o])
                p.op("pool", lambda: nc.gpsimd.tensor_tensor(out=obuf[:, qt, :], in0=o[:], in1=graw[:, qt, :], op=ALU.mult), r=[o, graw], w=[obuf])

            AHEAD = 2
            for gi, grp in enumerate(groups):
                emit_scores(grp, gi)
                if gi >= AHEAD:
                    emit_pv(groups[gi - AHEAD], gi - AHEAD)
            for gi in range(max(0, len(groups) - AHEAD), len(groups)):
                emit_pv(groups[gi], gi)

        def finish(s, j, obuf):
            for _ in range(3):
                yield
            for n0 in range(0, NT, 4):
                nn = min(4, NT - n0)
                pst = PS[7]
                for i in range(nn):
                    p.op("pe", lambda: nc.tensor.transpose(pst[:, i * 128:(i + 1) * 128], obuf[:, n0 + i, :], ident[:]), r=[obuf, ident], w=[pst])
                if (n0 // 4) % 2 == 0:
                    p.op("act", lambda: nc.scalar.copy(out=oTb[:, n0 * 128:(n0 + nn) * 128], in_=pst[:, 0:nn * 128]), r=[pst], w=[oTb])
                else:
                    p.op("dve", lambda: nc.vector.tensor_copy(out=oTb[:, n0 * 128:(n0 + nn) * 128], in_=pst[:, 0:nn * 128]), r=[pst], w=[oTb])
                yield
            p.dma(DR["OT_att"][j * 128:(j + 1) * 128, s * P:(s + 1) * P], oTb[:], r=[oTb], w=[DR["OT_t"][0]])
            yield

        def run_all(gen):
            for _ in gen:
                pass

        setup_load(pr[0][0], pr[0][1], SETS[0])
        run_all(setup(pr[0][0], pr[0][1], SETS[0]))
        def chain(*gens):
            for g_ in gens:
                if g_ is not None:
                    for _ in g_:
                        yield

        for k, (s, j) in enumerate(pr):
            if k + 1 < len(pr):
                setup_load(pr[k + 1][0], pr[k + 1][1], SETS[(k + 1) % 2])
            g_fin = finish(pr[k - 1][0], pr[k - 1][1], obufs[(k - 1) % 2]) if k > 0 else None
            g_set = setup(pr[k + 1][0], pr[k + 1][1], SETS[(k + 1) % 2]) if k + 1 < len(pr) else None
            bg = chain(g_fin, g_set)

            def inj(n=1):
                for _ in range(n):
                    try:
                        next(bg)
                    except StopIteration:
                        return
            mainloop(s, j, SETS[k % 2], inj, obufs[k % 2])
            run_all(bg)
        run_all(finish(pr[-1][0], pr[-1][1], obufs[(len(pr) - 1) % 2]))
        p.barrier()


def phase_hgrn(p, nc, l, DR, G, seqs=None, zoff=HG_OFF, ntl=NT, pipelined=True):
    PS = G["PS"]
    ident = G["ident"]
    b_le, b_gt, bind = G["b_le"], G["b_gt"], G["bind"]
    Z = DR["Z"]
    with ExitStack() as es:
        T = lambda name, shape, dt, **kw: p.tile("hg_" + name, shape, dt, es=es, **kw)
        hnw = T("hnw", [128, 128], F32)
        load_bc(p, nc, hnw, DR["hgrn_norm_w"][l:l + 1, :])
        identb = T("identb", [128, 128], BF16)
        p.op("pool", lambda: nc.gpsimd.tensor_copy(out=identb[:], in_=ident[:]), r=[ident], w=[identb])
        mhalf = T("mhalf", [128, 8], F32)
        p.op("pool", lambda: nc.gpsimd.memset(mhalf[:], -0.5), w=[mhalf])
        if l > 0:
            lb = T("lb", [128, 1024], F32)
            oml = T("oml", [128, 1024], F32)
            x0 = T("x0", [128, 1024], F32)
            load_bc(p, nc, x0, DR["hgrn_lower_bounds"][0:1, :])
            load_bc(p, nc, lb, DR["hgrn_lower_bounds"][1:2, :])
            p.op("dve", lambda: nc.vector.tensor_tensor(out=x0[:], in0=lb[:], in1=x0[:], op=ALU.subtract), r=[lb, x0], w=[x0])
            p.op("act", lambda: nc.scalar.activation(out=lb[:], in_=x0[:], func=AF.Sigmoid), r=[x0], w=[lb])
            p.op("act", lambda: nc.scalar.activation(out=oml[:], in_=x0[:], func=AF.Sigmoid, scale=-1.0), r=[x0], w=[oml])
        zts = [T("z%d" % i, [128, 4096], F32) for i in range(2)]
        f = T("f", [128, 1024], F32)
        kf = T("kf", [128, 1024], F32)
        logf = T("logf", [128, 1024], F32)
        ex = T("ex", [128, 1024], F32)
        SETS = []
        for k in range(2):
            B = {}
            for nm in ("qtb", "ktb", "kh", "khz", "vb"):
                B[nm] = T("%s%d" % (nm, k), [128, 1024], BF16)
            B["gs"] = T("gs%d" % k, [128, 1024], F32)
            B["gC"] = T("gC%d" % k, [128, 32], F32)
            SETS.append(B)
        qkT = [T("qkT%d" % j, [128, 256], BF16) for j in range(8)]
        qz = [T("qz%d" % j, [128, 64], BF16) for j in range(8)]
        for j in range(8):
            p.op("pool", lambda: nc.gpsimd.memset(qz[j][:], 0.0), w=[qz[j]])
        attm = [T("attm%d" % j, [128, 128], BF16) for j in range(8)]
        S = [T("S%d" % j, [128, 128], F32) for j in range(8)]
        Sb = [T("Sb%d" % j, [128, 128], BF16) for j in range(8)]
        osb = T("osb", [128, 1024], F32)
        obuf = T("obuf", [128, 1024], F32)
        oTb = T("oTb", [128, 8, 128], BF16)
        st = T("st", [128, 4, 8], F32)

        def v8(ap):
            return ap.rearrange("p (h d) -> p h d", d=128)

        def prep_load(s, n, z_):
            r0 = s * P + n * 128
            p.dma(z_[:], zr(Z, r0, r0 + 128)[:, zoff:zoff + 4096], r=[DR["Zt"]], w=[z_])
            yield

        def prep(s, n, B, z_):
            r0 = s * P + n * 128
            p.op("act", lambda: nc.scalar.activation(out=f[:], in_=z_[:, 1024:2048], func=AF.Sigmoid), r=[z_], w=[f])
            yield
            if l > 0:
                p.op("dve", lambda: nc.vector.tensor_tensor(out=f[:], in0=f[:], in1=oml[:], op=ALU.mult), r=[f, oml], w=[f])
                yield
                p.op("pool", lambda: nc.gpsimd.tensor_tensor(out=f[:], in0=f[:], in1=lb[:], op=ALU.add), r=[f, lb], w=[f])
                yield
            p.op("dve", lambda: nc.vector.tensor_scalar(out=kf[:], in0=f[:], scalar1=-1.0, scalar2=1.0, op0=ALU.mult, op1=ALU.add), r=[f], w=[kf])
            p.op("act", lambda: nc.scalar.activation(out=logf[:], in_=f[:], func=AF.Ln), r=[f], w=[logf])
            yield
            p.op("act", lambda: nc.scalar.activation(out=z_[:, 0:1024], in_=z_[:, 0:1024], func=AF.Silu), r=[z_], w=[z_])
            p.op("act", lambda: nc.scalar.activation(out=B["gs"][:], in_=z_[:, 3072:4096], func=AF.Silu), r=[z_], w=[B["gs"]])
            yield
            p.op("pool", lambda: nc.gpsimd.tensor_copy(out=B["vb"][:], in_=z_[:, 2048:3072]), r=[z_], w=[B["vb"]])
            yield
            for j in range(8):
                p.op("pe", lambda: nc.tensor.matmul(PS[4][:, j * 4:(j + 1) * 4], lhsT=logf[:, j * 128:(j + 1) * 128], rhs=bind[:, 0:4], start=True, stop=True),
                     r=[logf, bind], w=[PS[4]])
            p.op("act", lambda: nc.scalar.activation(out=B["gC"][:], in_=PS[4][:, 0:32], func=AF.Exp), r=[PS[4]], w=[B["gC"]])
            yield
            for half in range(2):
                hs = slice(half * 512, (half + 1) * 512)
                pl = PS[4 + half]
                p.op("pe", lambda: nc.tensor.matmul(pl[:], lhsT=b_le[:], rhs=logf[:, hs], start=True, stop=True), r=[b_le, logf], w=[pl])
                p.op("act", lambda: nc.scalar.activation(out=ex[:, hs], in_=pl[:], func=AF.Exp), r=[pl], w=[ex])
                p.op("dve", lambda: nc.vector.tensor_tensor(out=B["qtb"][:, hs], in0=z_[:, hs], in1=ex[:, hs], op=ALU.mult), r=[z_, ex], w=[B["qtb"]])
                p.op("act", lambda: nc.scalar.activation(out=ex[:, hs], in_=pl[:], func=AF.Exp, scale=-1.0), r=[pl], w=[ex])
                p.op("dve", lambda: nc.vector.tensor_tensor(out=B["ktb"][:, hs], in0=kf[:, hs], in1=ex[:, hs], op=ALU.mult), r=[kf, ex], w=[B["ktb"]])
                p.op("pe", lambda: nc.tensor.matmul(pl[:], lhsT=b_gt[:], rhs=logf[:, hs], start=True, stop=True), r=[b_gt, logf], w=[pl])
                p.op("act", lambda: nc.scalar.activation(out=ex[:, hs], in_=pl[:], func=AF.Exp), r=[pl], w=[ex])
                p.op("dve", lambda: nc.vector.tensor_tensor(out=B["kh"][:, hs], in0=kf[:, hs], in1=ex[:, hs], op=ALU.mult), r=[kf, ex], w=[B["kh"]])
                yield
            p.op("pool", lambda: nc.gpsimd.tensor_scalar(out=B["khz"][:], in0=B["kh"][:], scalar1=bind[:, 3:4], scalar2=None, op0=ALU.mult), r=[B["kh"], bind], w=[B["khz"]])
            yield

        def main(s, n, B, inj, flush_out=lambda: None):
            if n == 0:
                for j in range(8):
                    p.op("pool", lambda: nc.gpsimd.memset(S[j][:], 0.0), w=[S[j]])
                    p.op("pool", lambda: nc.gpsimd.memset(Sb[j][:], 0.0), w=[Sb[j]])
            vb, kh, khz, gC = B["vb"], B["kh"], B["khz"], B["gC"]
            for j in range(8):
                js = slice(j * 128, (j + 1) * 128)
                pw = PS[4 + (j % 2)]
                pwb = pw.t[:].bitcast(BF16)
                p.op("pe", lambda: nc.tensor.transpose(pwb[:, 0:128], B["qtb"][:, js], identb[:]), r=[B["qtb"], identb], w=[pw])
                p.op("pe", lambda: nc.tensor.transpose(pwb[:, 128:256], B["ktb"][:, js], identb[:]), r=[B["ktb"], identb], w=[pw])
                p.op("act", lambda: nc.scalar.copy(out=qkT[j][:], in_=pwb[:, 0:256]), r=[pw], w=[qkT[j]])
                p.op("pool", lambda: nc.gpsimd.tensor_copy(out=qz[j][:, 32:64], in_=qkT[j][:, 96:128]), r=[qkT[j]], w=[qz[j]])
                p.op("pe", lambda: nc.tensor.matmul(pw[:, 256:384], lhsT=qkT[j][:, 128:256], rhs=qkT[j][:, 0:128], start=True, stop=True),
                     r=[qkT[j]], w=[pw])
                p.op("dve", lambda: nc.vector.tensor_tensor(out=attm[j][:], in0=pw[:, 256:384], in1=b_le[:], op=ALU.mult), r=[pw, b_le], w=[attm[j]])
                if j % 2 == 1:
                    inj()
            for j in range(8):
                js = slice(j * 128, (j + 1) * 128)
                po = PS[6 + j // 4]
                p.op("pe", lambda: nc.tensor.matmul(po[:, (j % 4) * 128:(j % 4 + 1) * 128], lhsT=attm[j][:], rhs=vb[:, js],
                                                    start=(j % 4 == 0), stop=False, skip_group_check=True), r=[attm[j], vb], w=[po])
            for c in range(4):
                cs = slice(c * 32, (c + 1) * 32)
                for j in range(8):
                    js = slice(j * 128, (j + 1) * 128)
                    po = PS[6 + j // 4]
                    if c < 3:
                        p.op("pe", lambda: nc.tensor.matmul(po[cs, (j % 4) * 128:(j % 4 + 1) * 128], lhsT=qkT[j][:, cs], rhs=Sb[j][:],
                                                            start=False, stop=False, skip_group_check=True), r=[qkT[j], Sb[j]], w=[po])
                    else:
                        p.op("pe", lambda: nc.tensor.matmul(po[64:128, (j % 4) * 128:(j % 4 + 1) * 128], lhsT=qz[j][:], rhs=Sb[j][:],
                                                            start=False, stop=True, skip_group_check=True), r=[qz[j], Sb[j]], w=[po])
                    pss = PS[j % 4]
                    pc = slice((j // 4) * 128, (j // 4 + 1) * 128)
                    if c < 3:
                        p.op("pe", lambda: nc.tensor.matmul(pss[:, pc], lhsT=kh[cs, js], rhs=vb[cs, js],
                                                            start=True, stop=True), r=[kh, vb], w=[pss])
                    else:
                        p.op("pe", lambda: nc.tensor.matmul(pss[:, pc], lhsT=khz[64:128, js], rhs=vb[64:128, js],
                                                            start=True, stop=True), r=[khz, vb], w=[pss])
                    p.op("dve", lambda: nc.vector.scalar_tensor_tensor(out=S[j][:], in0=S[j][:], scalar=gC[:, j * 4 + c:j * 4 + c + 1],
                                                                       in1=pss[:, pc], op0=ALU.mult, op1=ALU.add),
                         r=[S[j], gC, pss], w=[S[j]])
                    p.op("act", lambda: nc.scalar.copy(out=Sb[j][:], in_=S[j][:]), r=[S[j]], w=[Sb[j]])
                    if j % 4 == 3:
                        inj()
            flush_out()
            for half in range(2):
                hs = slice(half * 512, (half + 1) * 512)
                p.op("act", lambda: nc.scalar.copy(out=osb[:, hs], in_=PS[6 + half][:]), r=[PS[6 + half]], w=[osb])

        def outp(s, n, B):
            r0 = s * P + n * 128
            p.op("pool", lambda: nc.gpsimd.tensor_tensor(out=obuf[:], in0=osb[:], in1=osb[:], op=ALU.mult), r=[osb], w=[obuf])
            yield
            p.op("dve", lambda: nc.vector.tensor_reduce(out=st[:, 0, :], in_=v8(obuf.t[:]), axis=AX.X, op=ALU.add), r=[obuf], w=[st])
            p.op("dve", lambda: nc.vector.tensor_scalar(out=st[:, 1, :], in0=st[:, 0, :], scalar1=1.0 / 128, scalar2=EPS, op0=ALU.mult, op1=ALU.add), r=[st], w=[st])
            p.op("pool", lambda: nc.gpsimd.tensor_tensor(out=st[:, 2, :], in0=st[:, 1, :], in1=mhalf[:], op=ALU.pow), r=[st, mhalf], w=[st])
            yield
            p.op("dve", lambda: nc.vector.tensor_tensor(out=v8(osb.t[:]), in0=v8(osb.t[:]), in1=st[:, 2, :].unsqueeze(2).broadcast_to([128, 8, 128]), op=ALU.mult),
                 r=[osb, st], w=[osb])
            yield
            p.op("pool", lambda: nc.gpsimd.tensor_tensor(out=v8(osb.t[:]), in0=v8(osb.t[:]), in1=hnw.t[:].unsqueeze(1).broadcast_to([128, 8, 128]), op=ALU.mult),
                 r=[osb, hnw], w=[osb])
            yield
            p.op("pool", lambda: nc.gpsimd.tensor_tensor(out=obuf[:], in0=osb[:], in1=B["gs"][:], op=ALU.mult), r=[osb, B["gs"]], w=[obuf])
            yield
            for half in range(2):
                pst = PS[4 + half]
                for k4 in range(4):
                    kc = half * 4 + k4
                    p.op("pe", lambda: nc.tensor.transpose(pst[:, k4 * 128:(k4 + 1) * 128], obuf[:, kc * 128:(kc + 1) * 128], ident[:]), r=[obuf, ident], w=[pst])
                src_ = pst.t[:].rearrange("p (k t) -> p k t", k=4)
                if half == 0:
                    p.op("act", lambda: nc.scalar.copy(out=oTb[:, 0:4, :], in_=src_), r=[pst], w=[oTb])
                else:
                    p.op("dve", lambda: nc.vector.tensor_copy(out=oTb[:, 4:8, :], in_=src_), r=[pst], w=[oTb])
                yield
            p.dma(DR["OT_hgrn"].rearrange("(kc p) t -> p kc t", p=128)[:, :, r0:r0 + 128], oTb[:], r=[oTb], w=[DR["OT_t"][2]])
            yield

        tiles = [(s, n) for s in (range(NS) if seqs is None else seqs) for n in range(ntl)]

        def run_all(gen):
            for _ in gen:
                pass

        def chain(*gens):
            for g in gens:
                if g is not None:
                    for _ in g:
                        yield

        if not pipelined:
            for idx, (s, n) in enumerate(tiles):
                B = SETS[idx % 2]
                run_all(prep_load(s, n, zts[idx % 2]))
                run_all(prep(s, n, B, zts[idx % 2]))
                main(s, n, B, lambda: None)
                run_all(outp(s, n, B))
        else:
            run_all(prep_load(tiles[0][0], tiles[0][1], zts[0]))
            run_all(prep(tiles[0][0], tiles[0][1], SETS[0], zts[0]))
            if len(tiles) > 1:
                run_all(prep_load(tiles[1][0], tiles[1][1], zts[1]))
            for idx, (s, n) in enumerate(tiles):
                B = SETS[idx % 2]
                g_out = outp(tiles[idx - 1][0], tiles[idx - 1][1], SETS[(idx - 1) % 2]) if idx > 0 else None
                g_prep = prep(tiles[idx + 1][0], tiles[idx + 1][1], SETS[(idx + 1) % 2], zts[(idx + 1) % 2]) if idx + 1 < len(tiles) else None
                g_load = prep_load(tiles[idx + 2][0], tiles[idx + 2][1], zts[idx % 2]) if idx + 2 < len(tiles) else None
                g_lo = chain(g_out)
                bg = chain(g_lo, g_prep, g_load)

                def inj(k=1):
                    for _ in range(k):
                        try:
                            next(bg)
                        except StopIteration:
                            return
                main(s, n, B, inj, lambda: run_all(g_lo))
                run_all(bg)
            run_all(outp(tiles[-1][0], tiles[-1][1], SETS[(len(tiles) - 1) % 2]))
        p.barrier()


C0 = -0.6065306597126334


def rwkv_setup(p, nc, l, DR, G, T):
    ident = G["ident"]
    tri_le, tri_lt, tri_gt = G["tri_le"], G["tri_lt"], G["tri_gt"]
    mu = T("mu", [128, RW_W], F32)
    load_bc(p, nc, mu, DR["rwkv_mu"][l:l + 1, :])
    prm = {}
    for i, n_ in enumerate(("rwkv_w0", "rwkv_a0", "rwkv_k_k", "rwkv_k_a", "rwkv_gn_w", "rwkv_gn_b")):
        prm[n_] = T(n_, [128, 1024], F32)
        load_bc(p, nc, prm[n_], DR[n_][l:l + 1, :], q=("sp" if i % 2 == 0 else "act"))
    prm["rwkv_r_k"] = T("rwkv_r_k", [128, 1024], F32)
    load_bc(p, nc, prm["rwkv_r_k"], DR["rwkv_r_k"][l:l + 1].rearrange("o h d -> o (h d)"))
    w_up = T("w_up", [64, 1024], F32)
    a_up = T("a_up", [64, 1024], F32)
    p.dma(w_up[:], DR["rwkv_w_up"][l], w=[w_up])
    p.dma(a_up[:], DR["rwkv_a_up"][l], w=[a_up])
    mA = T("mA", [128, 384], F32)
    mB = T("mB", [128, 256], F32)
    p.op("pool", lambda: nc.gpsimd.tensor_copy(out=mA[:, 0:128], in_=tri_lt[:]), r=[tri_lt], w=[mA])
    p.op("pool", lambda: nc.gpsimd.tensor_copy(out=mA[:, 128:256], in_=tri_gt[:]), r=[tri_gt], w=[mA])
    p.op("pool", lambda: nc.gpsimd.tensor_copy(out=mA[:, 256:384], in_=tri_lt[:]), r=[tri_lt], w=[mA])
    p.op("pool", lambda: nc.gpsimd.tensor_copy(out=mB[:, 0:128], in_=tri_le[:]), r=[tri_le], w=[mB])
    p.op("pool", lambda: nc.gpsimd.tensor_copy(out=mB[:, 128:256], in_=tri_le[:]), r=[tri_le], w=[mB])
    identb = T("identb", [128, 128], BF16)
    p.op("pool", lambda: nc.gpsimd.tensor_copy(out=identb[:], in_=ident[:]), r=[ident], w=[identb])
    mhalf16 = T("mhalf16", [128, 16], F32)
    p.op("pool", lambda: nc.gpsimd.memset(mhalf16[:], -0.5), w=[mhalf16])
    return dict(mu=mu, prm=prm, w_up=w_up, a_up=a_up, mA=mA, mB=mB, identb=identb, mhalf16=mhalf16)


def phase_rwkv(p, nc, l, DR, G, seqs=None, zoff=RW_OFF, ntl=NT, dbg=None, stop=None, pipelined=True, pre=None):
    PS = G["PS"]
    ident = G["ident"]
    tri_le, tri_lt, tri_gt, ones = G["tri_le"], G["tri_lt"], G["tri_gt"], G["ones"]
    Z = DR["Z"]

    def bc(ap16, n=16):
        return ap16.unsqueeze(2).broadcast_to([128, n, 64])

    def v3(ap):
        return ap.rearrange("p (h d) -> p h d", d=64)

    with ExitStack() as es:
        T = lambda name, shape, dt, **kw: p.tile("rw_" + name, shape, dt, es=es, **kw)
        ps_ = pre if pre is not None else rwkv_setup(p, nc, l, DR, G, T)
        mu, prm, w_up, a_up, mA, mB, identb, mhalf16 = (ps_[k] for k in ("mu", "prm", "w_up", "a_up", "mA", "mB", "identb", "mhalf16"))
        zc = T("zc", [128, RW_W], F32)
        zp = T("zp", [128, RW_W], F32)
        sw = T("sw", [128, 1024], F32)
        a_ = T("a_", [128, 1024], F32)
        kk = T("kk", [128, 1024], F32)
        kp = T("kp", [128, 1024], F32)
        b_ = T("b_", [128, 1024], F32)
        twT = T("twT", [64, 256], F32)
        ex = zp.t[:, 0:1024]
        tq = zp.t[:, 3072:4096]
        SETS = []
        for k in range(2):
            B = {}
            for nm in ("rtb", "ktb", "atb", "btb", "khat", "bhat", "vb"):
                B[nm] = T("%s%d" % (nm, k), [128, 1024], BF16)
            B["gs"] = T("gs%d" % k, [128, 1024], F32)
            B["gC"] = T("gC%d" % k, [128, 16], F32)
            B["sm"] = T("sm%d" % k, [128, 4, 16], F32)
            SETS.append(B)
        sm2 = T("sm2", [128, 8, 16], F32)
        fT = [T("fT%d" % i, [128, 4, 128], BF16) for i in range(8)]
        Ar = [T("Ar%d" % h, [128, 256], BF16) for h in range(16)]
        Aak = [T("Aak%d" % i, [128, 128], BF16) for i in range(8)]
        PP = [[T("PP%d_%d" % (i, k), [128, 256], BF16) for k in range(2)] for i in range(8)]
        X = [[T("X%d_%d" % (i, k), [128, 128], BF16) for k in range(2)] for i in range(8)]
        Gt = [T("Gt%d" % i, [128, 64], BF16) for i in range(8)]
        Wt = T("Wt", [128, 1024], F32)
        YT = [T("YT%d" % i, [128, 128], BF16) for i in range(8)]
        ST = T("ST", [128, 8, 64], F32)
        STbd = T("STbd", [128, 8, 128], BF16)
        Ub = T("Ub", [128, 1024], BF16)
        osb = T("osb", [128, 1024], F32)
        obuf = T("obuf", [128, 1024], F32)
        oTb = T("oTb", [128, 8, 128], BF16)

        def prep_load(s, n):
            r0 = s * P + n * 128
            p.dma(zc[:], zr(Z, r0, r0 + 128)[:, zoff:zoff + RW_W], r=[DR["Zt"]], w=[zc])
            if n == 0:
                p.op("pool", lambda: nc.gpsimd.memset(zp[0:1, :], 0.0), w=[zp])
                p.dma(zp[1:128, :], zr(Z, r0, r0 + 127)[:, zoff:zoff + RW_W], r=[DR["Zt"]], w=[zp], q="act")
            else:
                p.dma(zp[:], zr(Z, r0 - 1, r0 + 127)[:, zoff:zoff + RW_W], r=[DR["Zt"]], w=[zp], q="act")
            yield

        def prep(s, n, B):
            r0 = s * P + n * 128
            hs_ = slice(4096, RW_W)
            p.op("dve", lambda: nc.vector.tensor_tensor(out=zp[:, hs_], in0=zp[:, hs_], in1=zc[:, hs_], op=ALU.subtract), r=[zp, zc], w=[zp])
            p.op("dve", lambda: nc.vector.tensor_tensor(out=zp[:, hs_], in0=zp[:, hs_], in1=mu[:, hs_], op=ALU.mult), r=[zp, mu], w=[zp])
            p.op("dve", lambda: nc.vector.tensor_tensor(out=zc[:, hs_], in0=zc[:, hs_], in1=zp[:, hs_], op=ALU.add), r=[zp, zc], w=[zc])
            p.op("act", lambda: nc.scalar.activation(out=zc[:, 4096:4160], in_=zc[:, 4096:4160], func=AF.Tanh), r=[zc], w=[zc])
            yield
            h1 = slice(0, 2560)
            h2 = slice(2560, 4096)
            for k3 in range(3):
                for (sl, eng) in ((h1, "dve"), (h2, "pool")):
                    E = nc.vector if eng == "dve" else nc.gpsimd
                    if k3 == 0:
                        p.op(eng, lambda: E.tensor_tensor(out=zp[:, sl], in0=zp[:, sl], in1=zc[:, sl], op=ALU.subtract), r=[zp, zc], w=[zp])
                    elif k3 == 1:
                        p.op(eng, lambda: E.tensor_tensor(out=zp[:, sl], in0=zp[:, sl], in1=mu[:, sl], op=ALU.mult), r=[zp, mu], w=[zp])
                    else:
                        p.op(eng, lambda: E.tensor_tensor(out=zc[:, sl], in0=zc[:, sl], in1=zp[:, sl], op=ALU.add), r=[zp, zc], w=[zc])
                    yield
            rr = zc.t[:, 0:1024]
            rk = zc.t[:, 1024:2048]
            rv = zc.t[:, 2048:3072]
            rg = zc.t[:, 3072:4096]
            p.op("pe", lambda: nc.tensor.transpose(PS[6][0:64, 0:128], zc[:, 4096:4160], ident[:]), r=[zc, ident], w=[PS[6]])
            p.op("pe", lambda: nc.tensor.transpose(PS[6][0:64, 128:256], zc[:, 4160:4224], ident[:]), r=[zc, ident], w=[PS[6]])
            p.op("act", lambda: nc.scalar.copy(out=twT[:], in_=PS[6][0:64, 0:256]), r=[PS[6]], w=[twT])
            yield
            for half in range(2):
                hs = slice(half * 512, (half + 1) * 512)
                p.op("pe", lambda: nc.tensor.matmul(PS[6][:], lhsT=twT[:, 0:128], rhs=w_up[:, hs], start=True, stop=True), r=[twT, w_up], w=[PS[6]])
                p.op("dve", lambda: nc.vector.tensor_tensor(out=sw[:, hs], in0=PS[6][:], in1=prm["rwkv_w0"][:, hs], op=ALU.add),
                     r=[PS[6], prm["rwkv_w0"]], w=[sw])
                yield
            for half in range(2):
                hs = slice(half * 512, (half + 1) * 512)
                p.op("pe", lambda: nc.tensor.matmul(PS[7][:], lhsT=twT[:, 128:256], rhs=a_up[:, hs], start=True, stop=True), r=[twT, a_up], w=[PS[7]])
                p.op("dve", lambda: nc.vector.tensor_tensor(out=a_[:, hs], in0=PS[7][:], in1=prm["rwkv_a0"][:, hs], op=ALU.add),
                     r=[PS[7], prm["rwkv_a0"]], w=[a_])
                yield
            p.op("act", lambda: nc.scalar.activation(out=sw[:], in_=sw[:], func=AF.Sigmoid), r=[sw], w=[sw])
            p.op("act", lambda: nc.scalar.activation(out=a_[:], in_=a_[:], func=AF.Sigmoid), r=[a_], w=[a_])
            yield
            p.op("dve", lambda: nc.vector.tensor_tensor(out=kk[:], in0=rk, in1=prm["rwkv_k_k"][:], op=ALU.mult), r=[zc, prm["rwkv_k_k"]], w=[kk])
            yield
            p.op("pool", lambda: nc.gpsimd.tensor_tensor(out=kp[:], in0=kk[:], in1=kk[:], op=ALU.mult), r=[kk], w=[kp])
            yield
            p.op("dve", lambda: nc.vector.tensor_reduce(out=sm2[:, 0, :], in_=v3(kp.t[:]), axis=AX.X, op=ALU.add), r=[kp], w=[sm2])
            p.op("dve", lambda: nc.vector.tensor_scalar(out=sm2[:, 1, :], in0=sm2[:, 0, :], scalar1=1e-24, scalar2=None, op0=ALU.max), r=[sm2], w=[sm2])
            p.op("pool", lambda: nc.gpsimd.tensor_tensor(out=sm2[:, 2, :], in0=sm2[:, 1, :], in1=mhalf16[:], op=ALU.pow), r=[sm2, mhalf16], w=[sm2])
            yield
            p.op("dve", lambda: nc.vector.tensor_tensor(out=v3(kk.t[:]), in0=v3(kk.t[:]), in1=bc(sm2[:, 2, :]), op=ALU.mult), r=[kk, sm2], w=[kk])
            yield
            p.op("dve", lambda: nc.vector.scalar_tensor_tensor(out=kp[:], in0=a_[:], scalar=-1.0, in1=prm["rwkv_k_a"][:], op0=ALU.add, op1=ALU.mult),
                 r=[a_, prm["rwkv_k_a"]], w=[kp])
            yield
            p.op("dve", lambda: nc.vector.scalar_tensor_tensor(out=kp[:], in0=kp[:], scalar=1.0, in1=rk, op0=ALU.add, op1=ALU.mult), r=[kp, zc], w=[kp])
            yield
            p.op("pool", lambda: nc.gpsimd.tensor_tensor(out=b_[:], in0=kk[:], in1=a_[:], op=ALU.mult), r=[kk, a_], w=[b_])
            yield
            p.op("pool", lambda: nc.gpsimd.tensor_tensor(out=tq, in0=rr, in1=kp[:], op=ALU.mult), r=[zc, kp], w=[zp])
            yield
            p.op("pool", lambda: nc.gpsimd.tensor_tensor(out=tq, in0=tq, in1=prm["rwkv_r_k"][:], op=ALU.mult), r=[zp, prm["rwkv_r_k"]], w=[zp])
            yield
            p.op("dve", lambda: nc.vector.tensor_reduce(out=B["sm"][:, 3, :], in_=v3(tq), axis=AX.X, op=ALU.add), r=[zp], w=[B["sm"]])
            p.op("pool", lambda: nc.gpsimd.tensor_copy(out=B["vb"][:], in_=rv), r=[zc], w=[B["vb"]])
            yield
            p.op("act", lambda: nc.scalar.activation(out=B["gs"][:], in_=rg, func=AF.Silu), r=[zc], w=[B["gs"]])
            yield
            for _ in range(8):
                yield
            for half in range(2):
                hs = slice(half * 512, (half + 1) * 512)
                pl = PS[6 + half]
                p.op("pe", lambda: nc.tensor.matmul(pl[:], lhsT=tri_le[:], rhs=sw[:, hs], start=True, stop=True), r=[tri_le, sw], w=[pl])
                p.op("act", lambda: nc.scalar.activation(out=ex[:, hs], in_=pl[:], func=AF.Exp, scale=C0), r=[pl], w=[zp])
                p.op("dve", lambda: nc.vector.tensor_tensor(out=B["rtb"][:, hs], in0=rr[:, hs], in1=ex[:, hs], op=ALU.mult), r=[zc, zp], w=[B["rtb"]])
                p.op("dve", lambda: nc.vector.tensor_tensor(out=tq[:, hs], in0=pl[:], in1=sw[:, hs], op=ALU.subtract), r=[pl, sw], w=[zp])
                p.op("act", lambda: nc.scalar.activation(out=tq[:, hs], in_=tq[:, hs], func=AF.Exp, scale=C0), r=[zp], w=[zp])
                p.op("dve", lambda: nc.vector.scalar_tensor_tensor(out=B["atb"][:, hs], in0=kk[:, hs], scalar=-1.0, in1=tq[:, hs], op0=ALU.mult, op1=ALU.mult),
                     r=[kk, zp], w=[B["atb"]])
                p.op("act", lambda: nc.scalar.activation(out=ex[:, hs], in_=pl[:], func=AF.Exp, scale=-C0), r=[pl], w=[zp])
                p.op("pool", lambda: nc.gpsimd.tensor_tensor(out=B["btb"][:, hs], in0=b_[:, hs], in1=ex[:, hs], op=ALU.mult), r=[b_, zp], w=[B["btb"]])
                p.op("dve", lambda: nc.vector.tensor_tensor(out=B["ktb"][:, hs], in0=kp[:, hs], in1=ex[:, hs], op=ALU.mult), r=[kp, zp], w=[B["ktb"]])
                p.op("pe", lambda: nc.tensor.matmul(pl[:], lhsT=tri_gt[:], rhs=sw[:, hs], start=True, stop=True), r=[tri_gt, sw], w=[pl])
                p.op("act", lambda: nc.scalar.activation(out=ex[:, hs], in_=pl[:], func=AF.Exp, scale=C0), r=[pl], w=[zp])
                p.op("pool", lambda: nc.gpsimd.tensor_tensor(out=B["khat"][:, hs], in0=kp[:, hs], in1=ex[:, hs], op=ALU.mult), r=[kp, zp], w=[B["khat"]])
                p.op("dve", lambda: nc.vector.tensor_tensor(out=B["bhat"][:, hs], in0=b_[:, hs], in1=ex[:, hs], op=ALU.mult), r=[b_, zp], w=[B["bhat"]])
                yield
            for pp in range(8):
                p.op("pe", lambda: nc.tensor.matmul(PS[6][:, pp * 2:pp * 2 + 2], lhsT=sw[:, pp * 128:(pp + 1) * 128], rhs=ones[:, 0:2],
                                                    start=True, stop=True), r=[sw, ones], w=[PS[6]])
            p.op("act", lambda: nc.scalar.activation(out=B["gC"][:], in_=PS[6][:, 0:16], func=AF.Exp, scale=C0), r=[PS[6]], w=[B["gC"]])
            yield

        def main(s, n, B, inj, flush_out=lambda: None):
            if n == 0:
                p.op("pool", lambda: nc.gpsimd.memset(ST[:], 0.0), w=[ST])
                p.op("pool", lambda: nc.gpsimd.memset(STbd[:], 0.0), w=[STbd])
            src = (B["rtb"], B["ktb"], B["atb"], B["btb"])
            for g8 in range(2):
                for pi in range(4):
                    pp = g8 * 4 + pi
                    cs = slice(pp * 128, (pp + 1) * 128)
                    pw = PS[6 + pi // 2]
                    pwb = pw.t[:].bitcast(BF16)
                    off = (pi % 2) * 512
                    for k in range(4):
                        p.op("pe", lambda: nc.tensor.transpose(pwb[:, off + k * 128:off + (k + 1) * 128], src[k][:, cs], identb[:]), r=[src[k], identb], w=[pw])
                    if pi % 2 == 0:
                        p.op("act", lambda: nc.scalar.copy(out=fT[pp].t[:].rearrange("p a t -> p (a t)"), in_=pwb[:, off:off + 512]), r=[pw], w=[fT[pp]])
                    else:
                        p.op("dve", lambda: nc.vector.tensor_copy(out=fT[pp].t[:].rearrange("p a t -> p (a t)"), in_=pwb[:, off:off + 512]), r=[pw], w=[fT[pp]])
                for i in range(8):
                    h = g8 * 8 + i
                    pp, e = h // 2, h % 2
                    es_ = slice(e * 64, (e + 1) * 64)
                    rT, kT, aT, bT = (fT[pp][es_, k, :] for k in range(4))
                    pa = PS[i]
                    p.op("pe", lambda: nc.tensor.matmul(pa[:, 0:128], lhsT=bT, rhs=aT, start=True, stop=True), r=[fT[pp]], w=[pa])
                    p.op("pe", lambda: nc.tensor.matmul(pa[:, 128:256], lhsT=aT, rhs=bT, start=True, stop=True), r=[fT[pp]], w=[pa])
                    p.op("pe", lambda: nc.tensor.matmul(pa[:, 256:384], lhsT=kT, rhs=aT, start=True, stop=True), r=[fT[pp]], w=[pa])
                    p.op("dve", lambda: nc.vector.tensor_tensor(out=PP[i][0][:], in0=pa[:, 0:256], in1=mA[:, 0:256], op=ALU.mult), r=[pa, mA], w=[PP[i][0]])
                    p.op("dve", lambda: nc.vector.tensor_tensor(out=Aak[i][:], in0=pa[:, 256:384], in1=mA[:, 256:384], op=ALU.mult), r=[pa, mA], w=[Aak[i]])
                    p.op("dve", lambda: nc.vector.tensor_tensor(out=X[i][0][:], in0=PP[i][0][:, 0:128], in1=identb[:], op=ALU.add), r=[PP[i][0], identb], w=[X[i][0]])
                inj()
                for i in range(8):
                    h = g8 * 8 + i
                    pp, e = h // 2, h % 2
                    es_ = slice(e * 64, (e + 1) * 64)
                    rT, kT, aT, bT = (fT[pp][es_, k, :] for k in range(4))
                    pa = PS[i]
                    p.op("pe", lambda: nc.tensor.matmul(pa[:, 0:128], lhsT=bT, rhs=rT, start=True, stop=True), r=[fT[pp]], w=[pa])
                    p.op("pe", lambda: nc.tensor.matmul(pa[:, 128:256], lhsT=kT, rhs=rT, start=True, stop=True), r=[fT[pp]], w=[pa])
                    p.op("dve", lambda: nc.vector.tensor_tensor(out=Ar[h][:], in0=pa[:, 0:256], in1=mB[:], op=ALU.mult), r=[pa, mB], w=[Ar[h]])
                inj()
                for lv in range(1, 7):
                    cur, nxt = (lv - 1) % 2, lv % 2
                    for i in range(8):
                        pq = PS[i]
                        Pm, PTm = PP[i][cur][:, 0:128], PP[i][cur][:, 128:256]
                        if lv < 6:
                            p.op("pe", lambda: nc.tensor.matmul(pq[:, 0:128], lhsT=PTm, rhs=Pm, start=True, stop=True), r=[PP[i][cur]], w=[pq])
                        p.op("pe", lambda: nc.tensor.matmul(pq[:, 128:256], lhsT=Pm, rhs=PTm, start=True, stop=True), r=[PP[i][cur]], w=[pq])
                        if i % 2 == 0:
                            p.op("act", lambda: nc.scalar.copy(out=PP[i][nxt][:], in_=pq[:, 0:256]), r=[pq], w=[PP[i][nxt]])
                        else:
                            p.op("dve", lambda: nc.vector.tensor_copy(out=PP[i][nxt][:], in_=pq[:, 0:256]), r=[pq], w=[PP[i][nxt]])
                        if i % 8 == 7:
                            inj()
                    for i in range(8):
                        px = PS[i]
                        p.op("pe", lambda: nc.tensor.matmul(px[:, 256:384], lhsT=identb[:], rhs=X[i][cur][:], start=True, stop=False), r=[identb, X[i][cur]], w=[px])
                        p.op("pe", lambda: nc.tensor.matmul(px[:, 256:384], lhsT=PP[i][nxt][:, 128:256], rhs=X[i][cur][:], start=False, stop=True),
                             r=[PP[i][nxt], X[i][cur]], w=[px])
                        if i % 2 == 1:
                            p.op("act", lambda: nc.scalar.copy(out=X[i][nxt][:], in_=px[:, 256:384]), r=[px], w=[X[i][nxt]])
                        else:
                            p.op("dve", lambda: nc.vector.tensor_copy(out=X[i][nxt][:], in_=px[:, 256:384]), r=[px], w=[X[i][nxt]])
                        if i % 8 == 7:
                            inj()
                for i in range(8):
                    h = g8 * 8 + i
                    hc = slice(h * 64, (h + 1) * 64)
                    p.op("pe", lambda: nc.tensor.matmul(PS[i][:, 0:64], lhsT=Aak[i][:], rhs=B["vb"][:, hc], start=True, stop=True), r=[Aak[i], B["vb"]], w=[PS[i]])
                    if i % 2 == 0:
                        p.op("act", lambda: nc.scalar.copy(out=Gt[i][:], in_=PS[i][:, 0:64]), r=[PS[i]], w=[Gt[i]])
                    else:
                        p.op("dve", lambda: nc.vector.tensor_copy(out=Gt[i][:], in_=PS[i][:, 0:64]), r=[PS[i]], w=[Gt[i]])
                for i in range(8):
                    h = g8 * 8 + i
                    pp, e = h // 2, h % 2
                    hc = slice(h * 64, (h + 1) * 64)
                    py = PS[i]
                    p.op("pe", lambda: nc.tensor.matmul(py[:, 64:128], lhsT=X[i][0][:], rhs=Gt[i][:], start=True, stop=True),
                         r=[X[i][0], Gt[i]], w=[py])
                    p.op("pe", lambda: nc.tensor.matmul(py[:, 128:256], lhsT=B["atb"][:, pp * 128:(pp + 1) * 128], rhs=X[i][0][:], start=True, stop=True),
                         r=[B["atb"], X[i][0]], w=[py])
                    p.op("act", lambda: nc.scalar.copy(out=Wt[:, hc], in_=py[:, 64:128]), r=[py], w=[Wt])
                    p.op("dve", lambda: nc.vector.tensor_copy(out=YT[pp][e * 64:(e + 1) * 64, :], in_=py[e * 64:(e + 1) * 64, 128:256]), r=[py], w=[YT[pp]])
                inj()
            for pp in range(8):
                p.op("pe", lambda: nc.tensor.matmul(PS[pp // 4][:, (pp % 4) * 128:(pp % 4 + 1) * 128], lhsT=YT[pp][:], rhs=STbd[:, pp, :], start=True, stop=True),
                     r=[YT[pp], STbd], w=[PS[pp // 4]])
            for half in range(2):
                hs = slice(half * 512, (half + 1) * 512)
                p.op("dve", lambda: nc.vector.tensor_tensor(out=Ub[:, hs], in0=PS[half][:], in1=Wt[:, hs], op=ALU.add), r=[PS[half], Wt], w=[Ub])
            for pp in range(8):
                po = PS[2 + pp // 4]
                p.op("pe", lambda: nc.tensor.matmul(po[:, (pp % 4) * 128:(pp % 4 + 1) * 128], lhsT=fT[pp][:, 0, :], rhs=STbd[:, pp, :],
                                                    start=(pp % 4 == 0), stop=False, skip_group_check=True), r=[fT[pp], STbd], w=[po])
            for h in range(16):
                hc = slice(h * 64, (h + 1) * 64)
                po = PS[2 + h // 8]
                oc = slice((h % 8) * 64, (h % 8 + 1) * 64)
                p.op("pe", lambda: nc.tensor.matmul(po[:, oc], lhsT=Ar[h][:, 128:256], rhs=B["vb"][:, hc], start=False, stop=False, skip_group_check=True),
                     r=[Ar[h], B["vb"]], w=[po])
                p.op("pe", lambda: nc.tensor.matmul(po[:, oc], lhsT=Ar[h][:, 0:128], rhs=Ub[:, hc], start=False, stop=True, skip_group_check=True),
                     r=[Ar[h], Ub], w=[po])
            for h in range(16):
                pp, e = h // 2, h % 2
                es_ = slice(e * 64, (e + 1) * 64)
                hc = slice(h * 64, (h + 1) * 64)
                p.op("pe", lambda: nc.tensor.matmul(PS[4][es_, pp * 64:(pp + 1) * 64], lhsT=B["bhat"][:, hc], rhs=Ub[:, hc], start=(pp == 0), stop=False, skip_group_check=True),
                     r=[B["bhat"], Ub], w=[PS[4]])
                p.op("pe", lambda: nc.tensor.matmul(PS[4][es_, pp * 64:(pp + 1) * 64], lhsT=B["khat"][:, hc], rhs=B["vb"][:, hc], start=False, stop=True, skip_group_check=True),
                     r=[B["khat"], B["vb"]], w=[PS[4]])
            gC = B["gC"]
            p.op("dve", lambda: nc.vector.tensor_tensor(out=ST[:], in0=ST[:], in1=gC.t[:, 0:16].rearrange("p (a b) -> p a b", b=2)[:, :, 0:1].broadcast_to([128, 8, 64]), op=ALU.mult),
                 r=[ST, gC], w=[ST])
            p.op("dve", lambda: nc.vector.tensor_tensor(out=ST.t[:].rearrange("p a v -> p (a v)"), in0=ST.t[:].rearrange("p a v -> p (a v)"), in1=PS[4][:], op=ALU.add),
                 r=[ST, PS[4]], w=[ST])
            p.op("act", lambda: nc.scalar.copy(out=STbd[0:64, :, 0:64], in_=ST[0:64, :, :]), r=[ST], w=[STbd])
            p.op("pool", lambda: nc.gpsimd.tensor_copy(out=STbd[64:128, :, 64:128], in_=ST[64:128, :, :]), r=[ST], w=[STbd])
            flush_out()
            for half in range(2):
                hs = slice(half * 512, (half + 1) * 512)
                p.op("act", lambda: nc.scalar.copy(out=osb[:, hs], in_=PS[2 + half][:]), r=[PS[2 + half]], w=[osb])

        def outp(s, n, B):
            r0 = s * P + n * 128
            sm = sm2
            if dbg is not None:
                p.dma(dbg[r0:r0 + 128, :], osb[:], r=[osb], w=[DR["dbgt"]])
            p.op("dve", lambda: nc.vector.tensor_reduce(out=sm[:, 4, :], in_=v3(osb.t[:]), axis=AX.X, op=ALU.add), r=[osb], w=[sm])
            yield
            p.op("pool", lambda: nc.gpsimd.tensor_tensor(out=obuf[:], in0=osb[:], in1=osb[:], op=ALU.mult), r=[osb], w=[obuf])
            yield
            p.op("dve", lambda: nc.vector.tensor_reduce(out=sm[:, 5, :], in_=v3(obuf.t[:]), axis=AX.X, op=ALU.add), r=[obuf], w=[sm])
            p.op("dve", lambda: nc.vector.tensor_scalar(out=sm[:, 4, :], in0=sm[:, 4, :], scalar1=1.0 / 64, scalar2=None, op0=ALU.mult), r=[sm], w=[sm])
            p.op("dve", lambda: nc.vector.tensor_tensor(out=sm[:, 6, :], in0=sm[:, 4, :], in1=sm[:, 4, :], op=ALU.mult), r=[sm], w=[sm])
            p.op("dve", lambda: nc.vector.scalar_tensor_tensor(out=sm[:, 5, :], in0=sm[:, 5, :], scalar=1.0 / 64, in1=sm[:, 6, :], op0=ALU.mult, op1=ALU.subtract),
                 r=[sm], w=[sm])
            yield
            p.op("dve", lambda: nc.vector.tensor_scalar(out=sm[:, 5, :], in0=sm[:, 5, :], scalar1=GN_EPS, scalar2=None, op0=ALU.add), r=[sm], w=[sm])
            p.op("pool", lambda: nc.gpsimd.tensor_tensor(out=sm[:, 7, :], in0=sm[:, 5, :], in1=mhalf16[:], op=ALU.pow), r=[sm, mhalf16], w=[sm])
            yield
            p.op("dve", lambda: nc.vector.tensor_tensor(out=v3(osb.t[:]), in0=v3(osb.t[:]), in1=bc(sm[:, 4, :]), op=ALU.subtract), r=[osb, sm], w=[osb])
            yield
            p.op("dve", lambda: nc.vector.tensor_tensor(out=v3(osb.t[:]), in0=v3(osb.t[:]), in1=bc(sm[:, 7, :]), op=ALU.mult), r=[osb, sm], w=[osb])
            yield
            p.op("pool", lambda: nc.gpsimd.tensor_tensor(out=osb[:], in0=osb[:], in1=prm["rwkv_gn_w"][:], op=ALU.mult), r=[osb, prm["rwkv_gn_w"]], w=[osb])
            yield
            p.op("pool", lambda: nc.gpsimd.tensor_tensor(out=osb[:], in0=osb[:], in1=prm["rwkv_gn_b"][:], op=ALU.add), r=[osb, prm["rwkv_gn_b"]], w=[osb])
            yield
            p.op("dve", lambda: nc.vector.tensor_tensor(out=v3(obuf.t[:]), in0=v3(B["vb"].t[:]), in1=bc(B["sm"][:, 3, :]), op=ALU.mult), r=[B["vb"], B["sm"]], w=[obuf])
            yield
            p.op("pool", lambda: nc.gpsimd.tensor_tensor(out=obuf[:], in0=obuf[:], in1=osb[:], op=ALU.add), r=[obuf, osb], w=[obuf])
            yield
            p.op("pool", lambda: nc.gpsimd.tensor_tensor(out=obuf[:], in0=obuf[:], in1=B["gs"][:], op=ALU.mult), r=[obuf, B["gs"]], w=[obuf])
            yield
            for _ in range(4):
                yield
            for half in range(2):
                pst = PS[6 + half]
                for k4 in range(4):
                    kc = half * 4 + k4
                    p.op("pe", lambda: nc.tensor.transpose(pst[:, k4 * 128:(k4 + 1) * 128], obuf[:, kc * 128:(kc + 1) * 128], ident[:]), r=[obuf, ident], w=[pst])
                src_ = pst.t[:].rearrange("p (k t) -> p k t", k=4)
                if half == 0:
                    p.op("act", lambda: nc.scalar.copy(out=oTb[:, 0:4, :], in_=src_), r=[pst], w=[oTb])
                else:
                    p.op("dve", lambda: nc.vector.tensor_copy(out=oTb[:, 4:8, :], in_=src_), r=[pst], w=[oTb])
                yield
            p.dma(DR["OT_rwkv"].rearrange("(kc p) t -> p kc t", p=128)[:, :, r0:r0 + 128], oTb[:], r=[oTb], w=[DR["OT_t"][1]])
            yield

        tiles = [(s, n) for s in (range(NS) if seqs is None else seqs) for n in range(ntl)]

        def run_all(gen):
            for _ in gen:
                pass

        def chain(*gens):
            for g in gens:
                if g is not None:
                    for _ in g:
                        yield

        if not pipelined:
            for idx, (s, n) in enumerate(tiles):
                B = SETS[idx % 2]
                run_all(prep_load(s, n))
                run_all(prep(s, n, B))
                main(s, n, B, lambda: None)
                run_all(outp(s, n, B))
        else:
            run_all(prep_load(tiles[0][0], tiles[0][1]))
            run_all(prep(tiles[0][0], tiles[0][1], SETS[0]))
            for idx, (s, n) in enumerate(tiles):
                B = SETS[idx % 2]
                g_out = outp(tiles[idx - 1][0], tiles[idx - 1][1], SETS[(idx - 1) % 2]) if idx > 0 else None
                g_load = prep_load(tiles[idx + 1][0], tiles[idx + 1][1]) if idx + 1 < len(tiles) else None
                g_prep = prep(tiles[idx + 1][0], tiles[idx + 1][1], SETS[(idx + 1) % 2]) if idx + 1 < len(tiles) else None
                g_lo = chain(g_load, g_out)
                bg = chain(g_lo, g_prep)

                def inj(k=1):
                    for _ in range(k):
                        try:
                            next(bg)
                        except StopIteration:
                            return
                main(s, n, B, inj, lambda: run_all(g_lo))
                run_all(bg)
            run_all(outp(tiles[-1][0], tiles[-1][1], SETS[(len(tiles) - 1) % 2]))
        p.barrier()


def host_consts():
    c = {}
    i = np.arange(128)
    c["ident"] = np.eye(128, dtype=np.float32)
    c["tri_le"] = (i[:, None] <= i[None, :]).astype(np.float32)
    c["tri_lt"] = (i[:, None] < i[None, :]).astype(np.float32)
    c["tri_gt"] = (i[:, None] > i[None, :]).astype(np.float32)
    same = (i[:, None] // 32) == (i[None, :] // 32)
    c["b_le"] = (same & (i[:, None] <= i[None, :])).astype(np.float32)
    c["b_gt"] = (same & (i[:, None] > i[None, :])).astype(np.float32)
    bind = np.zeros((128, 128), np.float32)
    bind[i, i // 32] = 1.0
    c["bind"] = bind
    c["ones"] = np.ones((128, 128), np.float32)
    t = (np.arange(NT)[None, :] * 128 + np.arange(128)[:, None]).astype(np.float32) - PAD
    inv = (1.0 / (10000.0 ** (np.arange(0, 64, 2, dtype=np.float32) / 64))).astype(np.float32)
    ang = (t[:, :, None] * inv[None, None, :]).astype(np.float32)
    cs, sn = np.cos(ang).astype(np.float32), np.sin(ang).astype(np.float32)
    cos2 = np.zeros((128, NT, 2, 2, 32), np.float32)
    sin2 = np.zeros((128, NT, 2, 2, 32), np.float32)
    cos2[:] = cs[:, :, None, None, :]
    sin2[:, :, :, 0, :] = -sn[:, :, None, :]
    sin2[:, :, :, 1, :] = sn[:, :, None, :]
    c["cos2"] = cos2.reshape(128, NT, 128)
    c["sin2"] = sin2.reshape(128, NT, 128)
    return c


PARAM_SHAPES = {
    "pre_norm_w": [2, D], "post_norm_w": [2, D], "w_in": [2, D, IN_W],
    "lambda_q1": [2, 64], "lambda_k1": [2, 64], "lambda_q2": [2, 64], "lambda_k2": [2, 64], "att_norm_w": [2, 128],
    "rwkv_mu": [2, RW_W], "rwkv_w0": [2, 1024], "rwkv_w_up": [2, 64, 1024], "rwkv_a0": [2, 1024], "rwkv_a_up": [2, 64, 1024],
    "rwkv_k_k": [2, 1024], "rwkv_k_a": [2, 1024], "rwkv_r_k": [2, 16, 64], "rwkv_gn_w": [2, 1024], "rwkv_gn_b": [2, 1024],
    "hgrn_lower_bounds": [2, 1024], "hgrn_norm_w": [2, 128],
    "w_att_out": [2, D, D], "w_rwkv_out": [2, D, D], "w_hgrn_out": [2, D, D], "w_o": [2, D, D],
}
CONST_NAMES = ("ident", "tri_le", "tri_lt", "tri_gt", "b_le", "b_gt", "bind", "ones")


def build_program(nlayers=DEPTH):
    nc = bass.Bass("TRN2", target_bir_lowering=False)
    DR = {}
    H0 = nc.dram_tensor("h0", [ROWS, D], F32, kind="ExternalInput").ap()
    for n, sh in PARAM_SHAPES.items():
        DR[n] = nc.dram_tensor(n, sh, F32, kind="ExternalInput").ap()
    cd = {n: nc.dram_tensor(n, [128, 128], F32, kind="ExternalInput").ap() for n in CONST_NAMES}
    DR["cos2"] = nc.dram_tensor("cos2", [128, NT, 128], F32, kind="ExternalInput").ap()
    DR["sin2"] = nc.dram_tensor("sin2", [128, NT, 128], F32, kind="ExternalInput").ap()
    DR["Z"] = [nc.dram_tensor("Zscr%d" % i, [P, IN_W], F32, kind="Internal").ap() for i in range(NS)]
    for n in ("OT_att", "OT_rwkv", "OT_hgrn"):
        DR[n] = nc.dram_tensor(n + "_scr", [D, ROWS], BF16, kind="Internal").ap()
    Hs = nc.dram_tensor("Hscr", [ROWS, D], F32, kind="Internal").ap()
    OUT = nc.dram_tensor("out", [NS, 2048, D], F32, kind="ExternalOutput").ap()
    DR["OT_t"] = [Tl(None, "ot%d" % i, multi=True) for i in range(3)]
    DR["Zt"] = Tl(None, "Z", multi=True)
    DR["Outt"] = Tl(None, "out", multi=True)
    H_t = [Tl(None, "H0", multi=True), Tl(None, "Hs", multi=True)]
    with ExitStack() as es:
        p = Prog(nc, es)
        G = {}
        for i, n in enumerate(CONST_NAMES):
            G[n] = p.tile(n, [128, 128], F32)
            p.dma(G[n][:], cd[n], w=[G[n]], q=("sp" if i % 2 == 0 else "act"))
        G["PS"] = [p.tile("ps%d" % i, [128, 512], F32, psum=True) for i in range(8)]
        PS = G["PS"]
        for l in range(nlayers):
            Hin = H0 if l == 0 else Hs
            Hin_t = H_t[0] if l == 0 else H_t[1]
            last = (l == nlayers - 1)
            with ExitStack() as es2:
                T = lambda name, shape, dt, **kw: p.tile("pj_" + name, shape, dt, es=es2, **kw)
                C = {"ident": G["ident"]}
                C["uT"] = T("uT", [128, 8, ROWS], BF16, multi=True)
                C["ht"] = [T("ht%d" % i, [128, D], F32) for i in range(2)]
                C["ut"] = [T("ut%d" % i, [128, D], F32) for i in range(2)]
                C["sq"] = T("sq", [128, D], F32)
                C["ss"] = [T("ss%d" % i, [128, 8], F32) for i in range(2)]
                C["pst"] = PS[0:2]
                C["psz"] = PS[2:6]
                C["wst"] = [T("wst%d" % i, [128, 8, 512], F32) for i in range(2)]
                C["wbf"] = [T("wbf%d" % i, [128, 8, 512], BF16) for i in range(2)]
                C["zo"] = [T("zo%d" % i, [128, 512], F32) for i in range(4)]
                C["Zt"] = DR["Zt"]
                C["Ht"] = Hin_t
                C["mhalf"] = T("mhalf", [128, 1], F32)
                p.op("pool", lambda: nc.gpsimd.memset(C["mhalf"][:], -0.5), w=[C["mhalf"]])
                prew_bc = T("prew_bc", [128, D], F32)
                load_bc(p, nc, prew_bc, DR["pre_norm_w"][l:l + 1, :])
                phase_proj(p, nc, Hin, DR["Z"], DR["w_in"][l], prew_bc, C)
                p.barrier()
            phase_att(p, nc, l, DR, G)
            with ExitStack() as es_rw:
                Trw = lambda name, shape, dt, **kw: p.tile("rwp_" + name, shape, dt, es=es_rw, **kw)
                rw_pre = rwkv_setup(p, nc, l, DR, G, Trw)
                phase_hgrn(p, nc, l, DR, G)
                phase_rwkv(p, nc, l, DR, G, pre=rw_pre)
            DR["Ht"] = Hin_t
            DR["Ht2"] = H_t[1]
            phase_merge(p, nc, l, DR, G, Hin, Hs, out_final=(OUT if last else None))
        p.barrier()
        print("n_inst", p.n_inst)
    return nc


_NC_CACHE = {}


def kernel(**inputs):
    x = np.asarray(inputs["x"], dtype=np.float32)
    meta = np.asarray(inputs["meta_tokens"], dtype=np.float32)
    B = x.shape[0]
    ncores = B // NS
    if "nc" not in _NC_CACHE:
        _NC_CACHE["nc"] = build_program()
    nc = _NC_CACHE["nc"]
    consts = host_consts()
    shared = {n: np.ascontiguousarray(np.asarray(inputs[n], dtype=np.float32)) for n in PARAM_SHAPES}
    shared.update(consts)
    in_maps = []
    for c in range(ncores):
        h0 = np.zeros((NS, P, D), np.float32)
        h0[:, PAD:PAD + 16] = meta[None]
        h0[:, PAD + 16:] = x[c * NS:(c + 1) * NS]
        m = dict(shared)
        m["h0"] = h0.reshape(ROWS, D)
        in_maps.append(m)
    res = run_bass_kernel_spmd(nc, in_maps, core_ids=list(range(ncores)))
    out = np.concatenate([np.asarray(r["out"], dtype=np.float32) for r in res.results], axis=0)
    return out
```
